# Optimizing a Trainium2 kernel written in Bass

```python
import math
import jax, jax.numpy as jnp
from jax import lax
import numpy as np

D_MODEL = 1024
BATCH = 8
SEQ = 2048
DEPTH = 2

GRID_W = 64
CTX_LEN = 256
S5_WIDTH = 512
S5_GROUP = 16
S5_GROUPS = S5_WIDTH // S5_GROUP
S5_STATE = 64
RET_HEADS = 4
RET_DK = 64
RET_DV = 128
RET_QK_WIDTH = RET_HEADS * RET_DK
RET_WIDTH = RET_HEADS * RET_DV
RET_CHUNK = 128
NA_HEADS = 8
NA_HEAD_DIM = 64
NA_WIDTH = NA_HEADS * NA_HEAD_DIM
NA_ROWS = 8
NA_COLS = 16
N_BRANCH = 3
FFN_HIDDEN = ((8 * D_MODEL + 3 * 256 - 1) // (3 * 256)) * 256
ROPE_BASE = 10000.0
RMS_EPS = 1e-6
GN_EPS = 1e-5
IN_SPLIT = (S5_WIDTH, RET_QK_WIDTH, RET_WIDTH, NA_WIDTH, NA_WIDTH,
            RET_QK_WIDTH, RET_WIDTH, NA_WIDTH, N_BRANCH * D_MODEL)
N_CTX_KV = 5
N_IN = sum(IN_SPLIT)

kernel_name = 'hybrid_s5_retention_natten_dit_block'


def rms_norm(x):
    xf = x.astype(jnp.float32)
    return (xf * lax.rsqrt(jnp.mean(xf * xf, axis=-1, keepdims=True) + RMS_EPS)).astype(x.dtype)


def split_cols(t, sizes):
    return jnp.split(t, np.cumsum(sizes)[:-1].tolist(), axis=-1)


def heads(t, n):
    return t.reshape(t.shape[:-1] + (n, t.shape[-1] // n))


def seq_order(t, reverse):
    return jnp.flip(t, axis=1) if reverse else t


def axial_rotary(t):
    L, d = t.shape[1], t.shape[-1]
    half, quarter = d // 2, d // 4
    pos = jnp.arange(L)
    inv_freq = ROPE_BASE ** (-jnp.arange(quarter, dtype=jnp.float32) / quarter)

    def rotate(part, coord):
        ang = coord.astype(jnp.float32)[:, None] * inv_freq[None, :]
        cos = jnp.cos(ang)[None, :, None, :].astype(t.dtype)
        sin = jnp.sin(ang)[None, :, None, :].astype(t.dtype)
        a, b = part[..., :quarter], part[..., quarter:]
        return jnp.concatenate([a * cos - b * sin, a * sin + b * cos], axis=-1)

    return jnp.concatenate([rotate(t[..., :half], pos // GRID_W),
                            rotate(t[..., half:], pos % GRID_W)], axis=-1)


def _ssm_combine(e1, e2):
    a1r, a1i, b1r, b1i = e1
    a2r, a2i, b2r, b2i = e2
    return (a1r * a2r - a1i * a2i, a1r * a2i + a1i * a2r,
            a2r * b1r - a2i * b1i + b2r, a2r * b1i + a2i * b1r + b2i)


def s5_discretize(lam_re, lam_im, log_dt, b_re, b_im):
    f32 = jnp.float32
    lam_re, lam_im = lam_re.astype(f32), lam_im.astype(f32)
    dt = jnp.exp(log_dt.astype(f32))[:, None]
    mag, ang = jnp.exp(lam_re * dt), lam_im * dt
    ab_re, ab_im = mag * jnp.cos(ang), mag * jnp.sin(ang)
    den = lam_re * lam_re + lam_im * lam_im
    f_re = ((ab_re - 1.0) * lam_re + ab_im * lam_im) / den
    f_im = (ab_im * lam_re - (ab_re - 1.0) * lam_im) / den
    b_re, b_im = b_re.astype(f32), b_im.astype(f32)
    bb_re = f_re[..., None] * b_re - f_im[..., None] * b_im
    bb_im = f_re[..., None] * b_im + f_im[..., None] * b_re
    return ab_re, ab_im, bb_re, bb_im


def s5_states(u, disc, x0):
    ab_re, ab_im, bb_re, bb_im = disc
    u = u.astype(jnp.float32)
    bu_re = jnp.einsum('blgh,gph->blgp', u, bb_re)
    bu_im = jnp.einsum('blgh,gph->blgp', u, bb_im)
    shape = (1, u.shape[1]) + ab_re.shape
    a_re, a_im, x_re, x_im = lax.associative_scan(
        _ssm_combine, (jnp.broadcast_to(ab_re, shape), jnp.broadcast_to(ab_im, shape), bu_re, bu_im), axis=1)
    if x0 is not None:
        x0_re, x0_im = x0[0][:, None], x0[1][:, None]
        x_re, x_im = x_re + a_re * x0_re - a_im * x0_im, x_im + a_re * x0_im + a_im * x0_re
    return x_re, x_im


def s5_readout(x_re, x_im, c_re, c_im):
    f32 = jnp.float32
    return (jnp.einsum('ghp,blgp->blgh', c_re.astype(f32), x_re)
            - jnp.einsum('ghp,blgp->blgh', c_im.astype(f32), x_im))


def s5_glu(y, w_glu, b_glu):
    g = jax.nn.gelu(y)
    return g * jax.nn.sigmoid(g @ w_glu + b_glu)


def s5_mixer(u_lat, u_ctx, lam_re, lam_im, log_dt, b_re, b_im, c_re, c_im, d_skip, w_glu, b_glu, need_ctx):
    bsz, L, _ = u_lat.shape
    n_ctx = u_ctx.shape[1]
    ul = u_lat.reshape(bsz, L, S5_GROUPS, S5_GROUP)
    uc = u_ctx.reshape(bsz, n_ctx, S5_GROUPS, S5_GROUP)
    d32 = d_skip.astype(jnp.float32)
    y_lat = d32 * u_lat.astype(jnp.float32)
    y_ctx = d32 * u_ctx.astype(jnp.float32) if need_ctx else None
    for dirn in range(2):
        rev = dirn == 1
        disc = s5_discretize(lam_re[dirn], lam_im[dirn], log_dt[dirn], b_re[dirn], b_im[dirn])
        xc_re, xc_im = s5_states(seq_order(uc, rev), disc, None)
        xl_re, xl_im = s5_states(seq_order(ul, rev), disc, (xc_re[:, -1], xc_im[:, -1]))
        y_lat = y_lat + seq_order(s5_readout(xl_re, xl_im, c_re[dirn], c_im[dirn]), rev).reshape(bsz, L, S5_WIDTH)
        if need_ctx:
            y_ctx = y_ctx + seq_order(s5_readout(xc_re, xc_im, c_re[dirn], c_im[dirn]), rev).reshape(bsz, n_ctx, S5_WIDTH)
    out_lat = s5_glu(y_lat.astype(u_lat.dtype), w_glu, b_glu)
    out_ctx = s5_glu(y_ctx.astype(u_ctx.dtype), w_glu, b_glu) if need_ctx else None
    return out_lat, out_ctx


def retention_chunkwise(q, k, v, log_gamma, s0, strict):
    bsz, L, H, dk = q.shape
    dv = v.shape[-1]
    cs = RET_CHUNK
    n = L // cs
    qc = q.reshape(bsz, n, cs, H, dk)
    kc = k.reshape(bsz, n, cs, H, dk)
    vc = v.reshape(bsz, n, cs, H, dv)
    idx = jnp.arange(cs, dtype=jnp.float32)
    diff = idx[:, None] - idx[None, :]
    mask = diff > 0 if strict else diff >= 0
    intra = jnp.where(mask[None], jnp.exp(jnp.where(mask, diff, 0.0)[None] * log_gamma[:, None, None]), 0.0)
    scores = jnp.einsum('bcihd,bcjhd->bchij', qc, kc) * intra
    o = jnp.einsum('bchij,bcjhe->bcihe', scores, vc)
    k_dec = kc * jnp.exp((cs - 1 - idx)[:, None] * log_gamma[None, :])[:, :, None]
    kv = jnp.einsum('bcjhd,bcjhe->bchde', k_dec, vc)
    chunk_decay = jnp.exp(cs * log_gamma)[:, None, None]

    def step(s, kv_c):
        return chunk_decay * s + kv_c, s

    s_final, s_in = lax.scan(step, s0, jnp.moveaxis(kv, 1, 0))
    q_dec = qc * jnp.exp((idx + 1.0)[:, None] * log_gamma[None, :])[:, :, None]
    o = o + jnp.einsum('bcihd,cbhde->bcihe', q_dec, s_in)
    return o.reshape(bsz, L, H, dv), s_final


def retention_final_state(k, v, log_gamma):
    L = k.shape[1]
    w = jnp.exp((L - 1 - jnp.arange(L, dtype=jnp.float32))[:, None] * log_gamma[None, :])
    return jnp.einsum('blhd,blhe->bhde', k * w[:, :, None], v)


def head_norm(o):
    mu = jnp.mean(o, axis=-1, keepdims=True)
    var = jnp.mean(jnp.square(o - mu), axis=-1, keepdims=True)
    return (o - mu) * lax.rsqrt(var + GN_EPS)


def retention_mixer(q, k, v, g, q_c, k_c, v_c, g_c, theta, need_ctx):
    f32 = jnp.float32
    scale = RET_DK ** -0.5
    q = axial_rotary(q).astype(f32)
    k = (axial_rotary(k) * scale).astype(f32)
    v = v.astype(f32)
    k_c = (k_c * scale).astype(f32)
    v_c = v_c.astype(f32)
    log_gamma = jax.nn.log_sigmoid(theta.astype(f32))
    bsz = q.shape[0]
    o_lat, o_ctx = [], []
    for dirn in range(2):
        rev = dirn == 1
        lg = log_gamma[dirn]
        if need_ctx:
            s0 = jnp.zeros((bsz, RET_HEADS, RET_DK, RET_DV), f32)
            oc, s_ctx = retention_chunkwise(seq_order(q_c.astype(f32), rev), seq_order(k_c, rev),
                                            seq_order(v_c, rev), lg, s0, rev)
            o_ctx.append(seq_order(oc, rev))
        else:
            s_ctx = retention_final_state(seq_order(k_c, rev), seq_order(v_c, rev), lg)
        ol, _ = retention_chunkwise(seq_order(q, rev), seq_order(k, rev), seq_order(v, rev), lg, s_ctx, rev)
        o_lat.append(seq_order(ol, rev))
    y_lat = jax.nn.silu(g) * head_norm(o_lat[0] + o_lat[1]).reshape(g.shape).astype(g.dtype)
    y_ctx = (jax.nn.silu(g_c) * head_norm(o_ctx[0] + o_ctx[1]).reshape(g_c.shape).astype(g_c.dtype)
             if need_ctx else None)
    return y_lat, y_ctx


def neighborhood_attention(q, k, v, k_c, v_c, rpb):
    f32 = jnp.float32
    bsz, L, H, d = q.shape
    rows = L // GRID_W
    wr = min(NA_ROWS, rows)
    n_win = wr * NA_COLS
    qg = (q * d ** -0.5).reshape(bsz, rows, GRID_W, H, d)
    kg = k.reshape(bsz, rows, GRID_W, H, d)
    vg = v.reshape(bsz, rows, GRID_W, H, d)
    row_start = jnp.clip(jnp.arange(rows) - wr // 2, 0, rows - wr)
    cols = jnp.arange(GRID_W)
    col_idx = jnp.clip(cols - NA_COLS // 2, 0, GRID_W - NA_COLS)[:, None] + jnp.arange(NA_COLS)[None, :]
    col_bias_idx = col_idx - cols[:, None] + NA_COLS - 1
    rpb32 = rpb.astype(f32)

    def row_block(args):
        q_r, r = args
        rs = row_start[r]
        k_win = lax.dynamic_slice_in_dim(kg, rs, wr, axis=1)[:, :, col_idx]
        v_win = lax.dynamic_slice_in_dim(vg, rs, wr, axis=1)[:, :, col_idx]
        row_bias_idx = rs + jnp.arange(wr) - r + NA_ROWS - 1
        bias = rpb32[:, row_bias_idx[:, None, None], col_bias_idx[None]]
        s_win = jnp.einsum('bqhd,brqchd->bhqrc', q_r, k_win).astype(f32) + jnp.transpose(bias, (0, 2, 1, 3))[None]
        s_ctx = jnp.einsum('bqhd,bkhd->bhqk', q_r, k_c).astype(f32)
        s = jnp.concatenate([s_win.reshape(bsz, H, GRID_W, n_win), s_ctx], axis=-1)
        p = jax.nn.softmax(s, axis=-1).astype(v.dtype)
        p_win = p[..., :n_win].reshape(bsz, H, GRID_W, wr, NA_COLS)
        return (jnp.einsum('bhqrc,brqchd->bqhd', p_win, v_win)
                + jnp.einsum('bhqk,bkhd->bqhd', p[..., n_win:], v_c))

    out = lax.map(row_block, (jnp.moveaxis(qg, 1, 0), jnp.arange(rows)))
    return jnp.moveaxis(out, 0, 1).reshape(bsz, L, H * d)


def context_attention(q, k, v):
    bsz, n, H, d = q.shape
    s = jnp.einsum('bqhd,bkhd->bhqk', q * d ** -0.5, k).astype(jnp.float32)
    p = jax.nn.softmax(s, axis=-1).astype(v.dtype)
    return jnp.einsum('bhqk,bkhd->bqhd', p, v).reshape(bsz, n, H * d)


def merge_branches(y_s5, y_ret, y_na, gates, w_bs5, w_bret, w_bna, w_out):
    g_s5, g_ret, g_na = jnp.split(jax.nn.sigmoid(gates), N_BRANCH, axis=-1)
    m = g_s5 * (y_s5 @ w_bs5) + g_ret * (y_ret @ w_bret) + g_na * (y_na @ w_bna)
    return m @ w_out


def swiglu(h, w_gate, w_up, w_down):
    return (jax.nn.silu(h @ w_gate) * (h @ w_up)) @ w_down


def setup_inputs(seed: int = 0) -> dict:
    key = jax.random.key(seed)
    ks = iter(jax.random.split(key, 32))
    f32 = jnp.float32

    def normal(shape, std):
        return jax.random.normal(next(ks), shape, f32) * std

    G, P, HG = S5_GROUPS, S5_STATE, S5_GROUP
    ret_init = jnp.log(2.0 ** (5.0 + jnp.arange(RET_HEADS, dtype=f32)) - 1.0)
    return {
        'x': normal((BATCH, SEQ, D_MODEL), 1.0),
        'c': normal((BATCH, D_MODEL), 1.0),
        'ctx': normal((BATCH, CTX_LEN, D_MODEL), 1.0),
        'c_ctx': normal((D_MODEL,), 1.0),
        'w_ada': normal((DEPTH, D_MODEL, 6 * D_MODEL), 0.5 * D_MODEL ** -0.5),
        'b_ada': normal((DEPTH, 6 * D_MODEL), 0.01),
        'w_in': normal((DEPTH, D_MODEL, N_IN), D_MODEL ** -0.5),
        's5_lam_re': -0.5 + normal((DEPTH, 2, G, P), 0.01),
        's5_lam_im': jnp.pi * jnp.arange(P, dtype=f32) + normal((DEPTH, 2, G, P), 0.01),
        's5_log_dt': jax.random.uniform(next(ks), (DEPTH, 2, G), f32, math.log(1e-3), math.log(1e-1)),
        's5_b_re': normal((DEPTH, 2, G, P, HG), (2.0 * HG) ** -0.5),
        's5_b_im': normal((DEPTH, 2, G, P, HG), (2.0 * HG) ** -0.5),
        's5_c_re': normal((DEPTH, 2, G, HG, P), (2.0 * P) ** -0.5),
        's5_c_im': normal((DEPTH, 2, G, HG, P), (2.0 * P) ** -0.5),
        's5_d': normal((DEPTH, S5_WIDTH), 1.0),
        's5_w_glu': normal((DEPTH, S5_WIDTH, S5_WIDTH), S5_WIDTH ** -0.5),
        's5_b_glu': normal((DEPTH, S5_WIDTH), 0.01),
        'ret_theta': ret_init + normal((DEPTH, 2, RET_HEADS), 0.01),
        'na_rpb': normal((DEPTH, NA_HEADS, 2 * NA_ROWS - 1, 2 * NA_COLS - 1), 0.02),
        'w_branch_s5': normal((DEPTH, S5_WIDTH, D_MODEL), S5_WIDTH ** -0.5),
        'w_branch_ret': normal((DEPTH, RET_WIDTH, D_MODEL), RET_WIDTH ** -0.5),
        'w_branch_na': normal((DEPTH, NA_WIDTH, D_MODEL), NA_WIDTH ** -0.5),
        'w_out': normal((DEPTH, D_MODEL, D_MODEL), D_MODEL ** -0.5),
        'w_ffn_gate': normal((DEPTH, D_MODEL, FFN_HIDDEN), D_MODEL ** -0.5),
        'w_ffn_up': normal((DEPTH, D_MODEL, FFN_HIDDEN), D_MODEL ** -0.5),
        'w_ffn_down': normal((DEPTH, FFN_HIDDEN, D_MODEL), FFN_HIDDEN ** -0.5),
        'final_norm': 1.0 + normal((D_MODEL,), 0.01),
    }


def reference(x, c, ctx, c_ctx, w_ada, b_ada, w_in, s5_lam_re, s5_lam_im, s5_log_dt, s5_b_re, s5_b_im,
              s5_c_re, s5_c_im, s5_d, s5_w_glu, s5_b_glu, ret_theta, na_rpb, w_branch_s5, w_branch_ret,
              w_branch_na, w_out, w_ffn_gate, w_ffn_up, w_ffn_down, final_norm):
    silu_c = jax.nn.silu(c)
    silu_cc = jax.nn.silu(c_ctx)
    for l in range(DEPTH):
        need_ctx = l < DEPTH - 1
        mod = (silu_c @ w_ada[l] + b_ada[l])[:, None, :]
        mod_c = silu_cc @ w_ada[l] + b_ada[l]
        sh1, sc1, g1, sh2, sc2, g2 = jnp.split(mod, 6, axis=-1)
        csh1, csc1, cg1, csh2, csc2, cg2 = jnp.split(mod_c, 6, axis=-1)

        h = rms_norm(x) * (1.0 + sc1) + sh1
        hc = rms_norm(ctx) * (1.0 + csc1) + csh1
        u, rk, rv, nk, nv, rq, rg, nq, gates = split_cols(h @ w_in[l], IN_SPLIT)
        n_pieces = len(IN_SPLIT) if need_ctx else N_CTX_KV
        ctx_cols = sum(IN_SPLIT[:n_pieces])
        cp = split_cols(hc @ w_in[l][:, :ctx_cols], IN_SPLIT[:n_pieces])
        cu, crk, crv, cnk, cnv = cp[:N_CTX_KV]
        crq, crg, cnq, cgates = cp[N_CTX_KV:] if need_ctx else (None, None, None, None)

        y_s5, y_s5_c = s5_mixer(u, cu, s5_lam_re[l], s5_lam_im[l], s5_log_dt[l], s5_b_re[l], s5_b_im[l],
                                s5_c_re[l], s5_c_im[l], s5_d[l], s5_w_glu[l], s5_b_glu[l], need_ctx)
        y_ret, y_ret_c = retention_mixer(
            heads(rq, RET_HEADS), heads(rk, RET_HEADS), heads(rv, RET_HEADS), rg,
            heads(crq, RET_HEADS) if need_ctx else None, heads(crk, RET_HEADS), heads(crv, RET_HEADS),
            crg, ret_theta[l], need_ctx)
        k_na_c, v_na_c = heads(cnk, NA_HEADS), heads(cnv, NA_HEADS)
        y_na = neighborhood_attention(heads(nq, NA_HEADS), heads(nk, NA_HEADS), heads(nv, NA_HEADS),
                                      k_na_c, v_na_c, na_rpb[l])
        x = x + g1 * merge_branches(y_s5, y_ret, y_na, gates, w_branch_s5[l], w_branch_ret[l],
                                    w_branch_na[l], w_out[l])
        if need_ctx:
            y_na_c = context_attention(heads(cnq, NA_HEADS), k_na_c, v_na_c)
            ctx = ctx + cg1 * merge_branches(y_s5_c, y_ret_c, y_na_c, cgates, w_branch_s5[l],
                                             w_branch_ret[l], w_branch_na[l], w_out[l])

        x = x + g2 * swiglu(rms_norm(x) * (1.0 + sc2) + sh2, w_ffn_gate[l], w_ffn_up[l], w_ffn_down[l])
        if need_ctx:
            ctx = ctx + cg2 * swiglu(rms_norm(ctx) * (1.0 + csc2) + csh2, w_ffn_gate[l], w_ffn_up[l],
                                     w_ffn_down[l])
    return rms_norm(x) * final_norm
```

```python
import numpy as np
from contextlib import ExitStack
import concourse.bass as bass
import concourse.mybir as mybir
from concourse.bass_utils import run_bass_kernel_spmd

F32 = mybir.dt.float32
BF16 = mybir.dt.bfloat16
AF = mybir.ActivationFunctionType
ALU = mybir.AluOpType

D = 1024
LAT = 2048
NCTX = 256
NT = LAT + NCTX
NTT = NT // 128
DEPTH = 2
NIN = 6656
FFH = 2816
TB = [(0, 256), (256, 512), (768, 512), (1280, 512), (1792, 512)]
RMS_EPS = 1e-6
GN_EPS = 1e-5


class KB:
    def __init__(self, nc, es):
        self.nc = nc
        self.es = es
        self.eng = {'pe': nc.tensor, 'act': nc.scalar, 'dve': nc.vector, 'pool': nc.gpsimd, 'sp': nc.sync}
        self.sem = {e: es.enter_context(nc.semaphore('sem_' + e)) for e in self.eng}
        self.cnt = {e: 0 for e in self.eng}
        self.seen = {e: {} for e in self.eng}
        self.ND = 24
        self.dsem = [es.enter_context(nc.semaphore('dsem%d' % i)) for i in range(self.ND)]
        self.dcnt = [0] * self.ND
        self.dslots = {'sp': list(range(0, 14)), 'pool': list(range(14, 24))}
        self.dnext = {'sp': 0, 'pool': 0}
        self.w = {}
        self.r = {}
        self.nps = 0
        self.ninst = 0

    def _semh(self, s):
        return self.sem[s] if isinstance(s, str) else self.dsem[s[1]]

    def _wait(self, e, s, v):
        if s == e and e == 'pe':
            return
        if self.seen[e].get(s, 0) >= v:
            return
        self.eng[e].wait_ge(self._semh(s), v)
        self.seen[e][s] = v

    def _deps(self, e, R, W):
        for k in list(R) + list(W):
            for s, v in self.w.get(k, {}).items():
                self._wait(e, s, v)
        for k in W:
            for s, v in self.r.get(k, {}).items():
                self._wait(e, s, v)

    def _mark(self, tok, R, W):
        for k in W:
            self.w[k] = {tok[0]: tok[1]}
            self.r[k] = {}
        for k in R:
            d = self.r.setdefault(k, {})
            if d.get(tok[0], 0) < tok[1]:
                d[tok[0]] = tok[1]

    def op(self, e, fn, R=(), W=()):
        self._deps(e, R, W)
        inst = fn()
        self.cnt[e] += 1
        inst.then_inc(self.sem[e], 1)
        self._mark((e, self.cnt[e]), R, W)
        self.ninst += 1

    def dma(self, out, in_, R=(), W=(), q='sp', **kw):
        sl = self.dslots[q]
        i = sl[self.dnext[q] % len(sl)]
        self.dnext[q] += 1
        if self.dcnt[i]:
            self._wait(q, ('d', i), 16 * self.dcnt[i])
        self._deps(q, R, W)
        self.eng[q].dma_start(out=out, in_=in_, **kw).then_inc(self.dsem[i], 16)
        self.dcnt[i] += 1
        self._mark((('d', i), 16 * self.dcnt[i]), R, W)
        self.ninst += 1

    def merge(self, newkey, keys):
        d = {}
        for k in keys:
            for s, v in self.w.get(k, {}).items():
                if d.get(s, 0) < v:
                    d[s] = v
        self.w[newkey] = d
        self.r[newkey] = {}

    def barrier(self):
        for e in self.eng:
            for e2 in self.eng:
                if e2 != e and self.cnt[e2]:
                    self._wait(e, e2, self.cnt[e2])
            for i in range(self.ND):
                if self.dcnt[i]:
                    self._wait(e, ('d', i), 16 * self.dcnt[i])
        self.w = {}
        self.r = {}

    def finish(self):
        for i in range(self.ND):
            if self.dcnt[i]:
                self._wait('sp', ('d', i), 16 * self.dcnt[i])


def pbc(ap, n):
    return ap.to_broadcast([ap.shape[0], n])


def build(dbg=None):
    nc = bass.Bass("TRN2", target_bir_lowering=False)
    dr = {}

    def din(name, shape):
        dr[name] = nc.dram_tensor(name, list(shape), F32, kind="ExternalInput").ap()
        return dr[name]

    xin = din("xin", [NT, D])
    cT = din("cT", [128, 8, 2])
    w_ada = din("w_ada", [DEPTH, D, 6 * D])
    b_adaT = din("b_adaT", [DEPTH, 128, 48])
    w_in = din("w_in", [DEPTH, D, NIN])
    out = nc.dram_tensor("out", [LAT, D], F32, kind="ExternalOutput").ap()
    xw = nc.dram_tensor("xw", [NT, D], F32, kind="Internal").ap()
    if dbg:
        dbg_out = nc.dram_tensor("dbg", list(dbg[1]), F32, kind="ExternalOutput").ap()

    with ExitStack() as es:
        kb = KB(nc, es)

        cur = [es]

        nuniq = [0]

        def sb(name, shape, dt):
            nuniq[0] += 1
            return cur[0].enter_context(nc.sbuf_tensor("%s_%d" % (name, nuniq[0]), list(shape), dt))

        PS = [es.enter_context(nc.psum_tensor("ps%d" % i, [128, 512], F32)) for i in range(8)]

        def psum():
            i = kb.nps % 8
            kb.nps += 1
            return PS[i], 'ps%d' % i

        ident_f = sb("ident_f", [128, 128], F32)
        ident = sb("ident", [128, 128], BF16)
        hT = sb("hT", [128, 8, NT], BF16)
        modT = sb("modT", [128, 48, 2], F32)
        siluT = sb("siluT", [128, 8, 2], F32)
        NWB = 4
        wbf = [sb("wbf%d" % i, [128, 2048], BF16) for i in range(NWB)]
        ss = sb("ss", [128, 4], F32)
        badaT = sb("badaT", [128, 48], F32)
        ident_d = din("ident_d", [128, 128])

        kb.dma(ident_f[:], ident_d[:, :], W=['ident_f'])
        kb.op('dve', lambda: nc.vector.tensor_copy(out=ident[:], in_=ident_f[:]), R=['ident_f'], W=['ident'])
        kb.dma(siluT[:], cT[:, :, :], W=['siluT'])
        kb.op('act', lambda: nc.scalar.activation(out=siluT[:], in_=siluT[:], func=AF.Silu), R=['siluT'], W=['siluT'])

        wctr = [0, 0]

        def load_wk(src_ap, kt, ncols, cast=True):
            if not cast:
                i = wctr[1] % 2
                wctr[1] += 1
                sv = wst[i][:, 0:kt * ncols].rearrange("p (k n) -> p k n", n=ncols)
                kb.dma(sv, src_ap.rearrange("(k p) n -> p k n", p=128), W=['wst%d' % i])
                return sv, 'wst%d' % i
            i = wctr[0] % NWB
            wctr[0] += 1
            bv = wbf[i][:, 0:kt * ncols].rearrange("p (k n) -> p k n", n=ncols)
            kb.dma(bv, src_ap.rearrange("(k p) n -> p k n", p=128), W=['wbf%d' % i], q='pool')
            return bv, 'wbf%d' % i

        def load_w(src_ap, ncols, cast=True):
            return load_wk(src_ap, 8, ncols, cast)

        def adaln(l):
            kb.dma(badaT[:], b_adaT[l], W=['badaT'])
            ps, pk = psum()
            psv = ps[:, 0:96].rearrange("p (j v) -> p j v", v=2)
            for cb in range(24):
                wt, wk = load_w(w_ada[l][:, cb * 256:(cb + 1) * 256], 256, cast=False)
                for jj in range(2):
                    j = cb * 2 + jj

                    def f(j=j, jj=jj, wt=wt):
                        for k in range(8):
                            ins = nc.tensor.matmul(psv[:, j, :], lhsT=wt[:, k, jj * 128:(jj + 1) * 128],
                                                   rhs=siluT[:, k, :], start=(k == 0), stop=(k == 7))
                        return ins
                    kb.op('pe', f, R=[wk, 'siluT'], W=[pk])
            for v in range(2):
                kb.op('dve', lambda v=v: nc.vector.tensor_tensor(out=modT[:, :, v], in0=psv[:, :, v], in1=badaT[:],
                                                                 op=ALU.add), R=[pk, 'badaT'], W=['modT'])
            for a in (8, 32):
                kb.op('dve', lambda a=a: nc.vector.tensor_scalar(out=modT[:, a:a + 8, :], in0=modT[:, a:a + 8, :],
                                                                 scalar1=1.0, scalar2=None, op0=ALU.add),
                      R=['modT'], W=['modT'])

        def rstd_tile(i):
            kb.op('act', lambda i=i: nc.scalar.activation(out=junk[:], in_=xt[i][:], func=AF.Square),
                  R=['xt%d' % i], W=['junk'])
            kb.op('dve', lambda: nc.vector.tensor_reduce(out=ss[:, 0:1], in_=junk[:], axis=mybir.AxisListType.X, op=ALU.add),
                  R=['junk', 'ss'], W=['ss'])
            kb.op('act', lambda: nc.scalar.activation(out=ss[:, 1:2], in_=ss[:, 0:1], func=AF.Sqrt,
                                                      scale=1.0 / D, bias=eps_t[:, 0:1]), R=['ss', 'eps_t'], W=['ss'])
            kb.op('dve', lambda: nc.vector.reciprocal(out=ss[:, 2:3], in_=ss[:, 1:2]), R=['ss'], W=['ss'])

        def norm_tile(i, t, sh_off, sc_off):
            v = 1 if t < 2 else 0
            rstd_tile(i)
            kb.op('dve', lambda i=i: nc.vector.tensor_scalar(out=xn[i][:], in0=xt[i][:], scalar1=ss[:, 2:3],
                                                             scalar2=None, op0=ALU.mult),
                  R=['xt%d' % i, 'ss'], W=['xn%d' % i])
            ps, pk = psum()
            pb = ps[:].bitcast(BF16)

            def f(i=i, pb=pb):
                for k in range(8):
                    ins = nc.tensor.transpose(pb[:, k * 128:(k + 1) * 128], xn[i][:, k * 128:(k + 1) * 128], ident[:])
                return ins
            kb.op('pe', f, R=['xn%d' % i, 'ident'], W=[pk])
            for k in range(8):
                kb.op('dve', lambda k=k, pb=pb, t=t, v=v: nc.vector.tensor_scalar(
                    out=hT[:, k, t * 128:(t + 1) * 128], in0=pb[:, k * 128:(k + 1) * 128],
                    scalar1=modT[:, sc_off + k, v:v + 1], scalar2=modT[:, sh_off + k, v:v + 1],
                    op0=ALU.mult, op1=ALU.add), R=[pk, 'modT'], W=[('hT', t)])

        def norm_to_hT(src_dram, sh_off, sc_off):
            for t in range(NTT):
                i = t % 2
                kb.dma(xt[i][:], src_dram[t * 128:(t + 1) * 128, :], W=['xt%d' % i])
                norm_tile(i, t, sh_off, sc_off)

        def norm_pre(src_dram):
            for t in range(NTT):
                i = t % 2
                kb.dma(xt[i][:], src_dram[t * 128:(t + 1) * 128, :], W=['xt%d' % i])
                rstd_tile(i)
                kb.op('dve', lambda i=i, t=t: nc.vector.tensor_scalar(out=xnall[:, t, :], in0=xt[i][:], scalar1=ss[:, 2:3],
                                                                      scalar2=None, op0=ALU.mult),
                      R=['xt%d' % i, 'ss'], W=[('xnall', t)])

        def norm_post(sh_off, sc_off):
            for t in range(NTT):
                v = 1 if t < 2 else 0
                ps, pk = psum()
                pb = ps[:].bitcast(BF16)

                def f(pb=pb, t=t):
                    for k in range(8):
                        ins = nc.tensor.transpose(pb[:, k * 128:(k + 1) * 128], xnall[:, t, k * 128:(k + 1) * 128], ident[:])
                    return ins
                kb.op('pe', f, R=[('xnall', t), 'ident'], W=[pk])
                for k in range(8):
                    kb.op('dve', lambda k=k, pb=pb, t=t, v=v: nc.vector.tensor_scalar(
                        out=hT[:, k, t * 128:(t + 1) * 128], in0=pb[:, k * 128:(k + 1) * 128],
                        scalar1=modT[:, sc_off + k, v:v + 1], scalar2=modT[:, sh_off + k, v:v + 1],
                        op0=ALU.mult, op1=ALU.add), R=[pk, 'modT'], W=[('hT', t)])

        eps_t = sb("eps_t", [128, 2], F32)
        kb.op('dve', lambda: nc.vector.memset(eps_t[:, 0:1], RMS_EPS), W=['eps_t'])
        kb.op('dve', lambda: nc.vector.memset(eps_t[:, 1:2], GN_EPS), R=['eps_t'], W=['eps_t'])


        HT_KEYS = [('hT', t) for t in range(NTT)]

        def proj_fm(wsrc, col0, ncols, src, srckeys, kt, evac, blocks=TB):
            for cb in range(0, ncols, 256):
                nc_ = min(256, ncols - cb)
                wt, wk = load_wk(wsrc[:, col0 + cb:col0 + cb + nc_], kt, nc_)
                for mm in range(nc_ // 128):
                    m = (cb // 128) + mm
                    for (tc0, n) in blocks:
                        ps, pk = psum()

                        def f(ps=ps, wt=wt, mm=mm, tc0=tc0, n=n):
                            for k in range(kt):
                                ins = nc.tensor.matmul(ps[:, 0:n], lhsT=wt[:, k, mm * 128:(mm + 1) * 128],
                                                       rhs=src[:, k, tc0:tc0 + n], start=(k == 0), stop=(k == kt - 1))
                            return ins
                        kb.op('pe', f, R=[wk] + list(srckeys), W=[pk])
                        evac(m, tc0, n, ps, pk)

        s5p = din("s5p", [DEPTH, 2, 128, 3, 16])
        s5c = din("s5c", [DEPTH, 2, 128, 2, 16, 16])
        s5ba = din("s5ba", [DEPTH, 16, 128, 4, 32])
        s5dT_d = din("s5dT", [DEPTH, 128, 4])
        bgluT_d = din("bgluT", [DEPTH, 128, 4])
        w_glu = din("w_glu", [DEPTH, 512, 512])

        PI = float(np.pi)

        def dv(fn, R, W):
            kb.op('dve', fn, R=R, W=W)

        def pv(fn, R, W):
            kb.op('pool', fn, R=R, W=W)

        def s5_prep(l):
            kb.dma(prm[:], s5p[l].rearrange("d p a b -> p d a b"), W=['prm'])
            kb.dma(cfl[:], s5c[l].rearrange("d p r a b -> p d r a b"), W=['cfl'])
            kb.dma(s5dT[:], s5dT_d[l], W=['s5dT'])
            kb.dma(bgluT[:], bgluT_d[l], W=['bgluT'])
            S = ['sm']
            for d in range(2):
                lre, lim, ldt = prm[:, d, 0, :], prm[:, d, 1, :], prm[:, d, 2, :]
                dtv, mag, th, tmp, sn, cs, th2 = (sm[:, i, :] for i in range(7))
                kb.op('act', lambda: nc.scalar.activation(out=dtv, in_=ldt, func=AF.Exp), R=['prm'], W=S)
                dv(lambda: nc.vector.tensor_tensor(out=mag, in0=lre, in1=dtv, op=ALU.mult), ['prm'] + S, S)
                kb.op('act', lambda: nc.scalar.activation(out=mag, in_=mag, func=AF.Exp), R=S, W=S)
                dv(lambda: nc.vector.tensor_tensor(out=th, in0=lim, in1=dtv, op=ALU.mult), ['prm'] + S, S)
                for _ in range(6):
                    dv(lambda: nc.vector.tensor_scalar(out=tmp, in0=th, scalar1=PI, scalar2=2 * PI, op0=ALU.is_gt,
                                                       op1=ALU.mult), S, S)
                    dv(lambda: nc.vector.tensor_tensor(out=th, in0=th, in1=tmp, op=ALU.subtract), S, S)
                for _ in range(2):
                    dv(lambda: nc.vector.tensor_scalar(out=tmp, in0=th, scalar1=-PI, scalar2=2 * PI, op0=ALU.is_lt,
                                                       op1=ALU.mult), S, S)
                    dv(lambda: nc.vector.tensor_tensor(out=th, in0=th, in1=tmp, op=ALU.add), S, S)
                dv(lambda: nc.vector.tensor_scalar(out=th2, in0=th, scalar1=PI / 2, scalar2=None, op0=ALU.add), S, S)
                dv(lambda: nc.vector.tensor_scalar(out=tmp, in0=th2, scalar1=PI, scalar2=2 * PI, op0=ALU.is_gt,
                                                   op1=ALU.mult), S, S)
                dv(lambda: nc.vector.tensor_tensor(out=th2, in0=th2, in1=tmp, op=ALU.subtract), S, S)
                kb.op('act', lambda: nc.scalar.activation(out=sn, in_=th, func=AF.Sin), R=S, W=S)
                kb.op('act', lambda: nc.scalar.activation(out=cs, in_=th2, func=AF.Sin), R=S, W=S)
                A = ['APW']
                dv(lambda: nc.vector.tensor_tensor(out=APre[:, d, 0, :], in0=mag, in1=cs, op=ALU.mult), S, A)
                dv(lambda: nc.vector.tensor_tensor(out=APim[:, d, 0, :], in0=mag, in1=sn, op=ALU.mult), S, A)
                t1, t2 = sm[:, 7, :], sm[:, 8, :]
                for k in range(1, 12):
                    re0, im0 = APre[:, d, k - 1, :], APim[:, d, k - 1, :]
                    dv(lambda re0=re0: nc.vector.tensor_tensor(out=t1, in0=re0, in1=re0, op=ALU.mult), A + S, S)
                    dv(lambda im0=im0: nc.vector.tensor_tensor(out=t2, in0=im0, in1=im0, op=ALU.mult), A + S, S)
                    dv(lambda k=k: nc.vector.tensor_tensor(out=APre[:, d, k, :], in0=t1, in1=t2, op=ALU.subtract), S, A)
                    dv(lambda re0=re0, im0=im0: nc.vector.tensor_tensor(out=t1, in0=re0, in1=im0, op=ALU.mult), A + S, S)
                    dv(lambda k=k: nc.vector.tensor_scalar(out=APim[:, d, k, :], in0=t1, scalar1=2.0, scalar2=None,
                                                           op0=ALU.mult), S, A)
                am1, den, fr, fi = sm[:, 7, :], sm[:, 8, :], sm[:, 9, :], sm[:, 10, :]
                t3 = sm[:, 11, :]
                dv(lambda: nc.vector.tensor_scalar(out=am1, in0=APre[:, d, 0, :], scalar1=-1.0, scalar2=None,
                                                   op0=ALU.add), A + S, S)
                dv(lambda: nc.vector.tensor_tensor(out=den, in0=lre, in1=lre, op=ALU.mult), ['prm'] + S, S)
                dv(lambda: nc.vector.tensor_tensor(out=t3, in0=lim, in1=lim, op=ALU.mult), ['prm'] + S, S)
                dv(lambda: nc.vector.tensor_tensor(out=den, in0=den, in1=t3, op=ALU.add), S, S)
                dv(lambda: nc.vector.reciprocal(out=den, in_=den), S, S)
                dv(lambda: nc.vector.tensor_tensor(out=fr, in0=am1, in1=lre, op=ALU.mult), ['prm'] + S, S)
                dv(lambda: nc.vector.tensor_tensor(out=t3, in0=APim[:, d, 0, :], in1=lim, op=ALU.mult), ['prm'] + A + S, S)
                dv(lambda: nc.vector.tensor_tensor(out=fr, in0=fr, in1=t3, op=ALU.add), S, S)
                dv(lambda: nc.vector.tensor_tensor(out=fr, in0=fr, in1=den, op=ALU.mult), S, S)
                dv(lambda: nc.vector.tensor_tensor(out=fi, in0=APim[:, d, 0, :], in1=lre, op=ALU.mult), ['prm'] + A + S, S)
                dv(lambda: nc.vector.tensor_tensor(out=t3, in0=am1, in1=lim, op=ALU.mult), ['prm'] + S, S)
                dv(lambda: nc.vector.tensor_tensor(out=fi, in0=fi, in1=t3, op=ALU.subtract), S, S)
                dv(lambda: nc.vector.tensor_tensor(out=fi, in0=fi, in1=den, op=ALU.mult), S, S)

                def bc16(v):
                    return bass.AP(v.tensor, v.offset, [list(v.ap[0]), list(v.ap[1]), [0, 16]])
                cre, cim = cfl[:, d, 0], cfl[:, d, 1]
                cfr, cfi, ta, tb_ = cfk[:, d, 0], cfk[:, d, 1], cfs[:, 0], cfs[:, 1]
                C = ['cfs']
                dv(lambda: nc.vector.tensor_tensor(out=cfr, in0=cre, in1=bc16(fr), op=ALU.mult), ['cfl'] + S + C, C)
                dv(lambda: nc.vector.tensor_tensor(out=ta, in0=cim, in1=bc16(fi), op=ALU.mult), ['cfl'] + S + C, C)
                dv(lambda: nc.vector.tensor_tensor(out=cfr, in0=cfr, in1=ta, op=ALU.subtract), C, C)
                dv(lambda: nc.vector.tensor_tensor(out=cfi, in0=cre, in1=bc16(fi), op=ALU.mult), ['cfl'] + S + C, C)
                dv(lambda: nc.vector.tensor_tensor(out=ta, in0=cim, in1=bc16(fr), op=ALU.mult), ['cfl'] + S + C, C)
                dv(lambda: nc.vector.tensor_tensor(out=cfi, in0=cfi, in1=ta, op=ALU.add), C, C)
            dv(lambda: nc.vector.tensor_scalar(out=APnim[:], in0=APim[:], scalar1=-1.0, scalar2=None, op0=ALU.mult),
               ['APW'], ['APW'])
            P_ = ['Epw']
            for d in range(2):
                dv(lambda d=d: nc.vector.memset(Epw_re[:, d, 0, :], 1.0), P_, P_)
                dv(lambda d=d: nc.vector.memset(Epw_im[:, d, 0, :], 0.0), P_, P_)
                dv(lambda d=d: nc.vector.tensor_copy(out=Epw_re[:, d, 1, :], in_=APre[:, d, 0, :]), ['APW'] + P_, P_)
                dv(lambda d=d: nc.vector.tensor_copy(out=Epw_im[:, d, 1, :], in_=APim[:, d, 0, :]), ['APW'] + P_, P_)
                t1, t2 = sm[:, 7, :], sm[:, 8, :]
                S = ['sm']
                for j in range(2, 9):
                    ar, ai = Epw_re[:, d, j - 1, :], Epw_im[:, d, j - 1, :]
                    br, bi_ = APre[:, d, 0, :], APim[:, d, 0, :]
                    dv(lambda ar=ar, br=br: nc.vector.tensor_tensor(out=t1, in0=ar, in1=br, op=ALU.mult), P_ + S + ['APW'], S)
                    dv(lambda ai=ai, bi_=bi_: nc.vector.tensor_tensor(out=t2, in0=ai, in1=bi_, op=ALU.mult), P_ + S + ['APW'], S)
                    dv(lambda d=d, j=j: nc.vector.tensor_tensor(out=Epw_re[:, d, j, :], in0=t1, in1=t2, op=ALU.subtract), S + P_, P_)
                    dv(lambda ar=ar, bi_=bi_: nc.vector.tensor_tensor(out=t1, in0=ar, in1=bi_, op=ALU.mult), P_ + S + ['APW'], S)
                    dv(lambda ai=ai, br=br: nc.vector.tensor_tensor(out=t2, in0=ai, in1=br, op=ALU.mult), P_ + S + ['APW'], S)
                    dv(lambda d=d, j=j: nc.vector.tensor_tensor(out=Epw_im[:, d, j, :], in0=t1, in1=t2, op=ALU.add), S + P_, P_)
                for t in range(8):
                    j = t + 1 if d == 0 else 8 - t
                    dv(lambda d=d, t=t, j=j: nc.vector.tensor_copy(out=Eg_re[:, d, t, :], in_=Epw_re[:, d, j, :]), P_, P_)
                    dv(lambda d=d, t=t, j=j: nc.vector.tensor_copy(out=Eg_im[:, d, t, :], in_=Epw_im[:, d, j, :]), P_, P_)

        def bk_scan_ops(d, pi_, XR, XI, key, n, koff=0):
            rev = (d == 1)
            X = [key]
            ops = []

            def lvl(k, dst0, m):
                s = 1 << k
                step = 2 * s
                if m <= 0:
                    return
                if not rev:
                    d0 = dst0
                    s0 = dst0 - s
                else:
                    d0 = n - 1 - (dst0 + step * (m - 1))
                    s0 = d0 + s
                dsl = slice(d0, d0 + step * (m - 1) + 1, step)
                ssl = slice(s0, s0 + step * (m - 1) + 1, step)
                ar = APre[:, d, k + koff, pi_:pi_ + 1]
                ai = APim[:, d, k + koff, pi_:pi_ + 1]
                nai = APnim[:, d, k + koff, pi_:pi_ + 1]
                for (dst, src, sc) in ((XR, XR, ar), (XR, XI, nai), (XI, XI, ar), (XI, XR, ai)):
                    ops.append(lambda dst=dst, src=src, sc=sc: dv(lambda: nc.vector.scalar_tensor_tensor(
                        out=dst[:, dsl], in0=src[:, ssl], scalar=sc, in1=dst[:, dsl], op0=ALU.mult, op1=ALU.add),
                        X + ['APW'], X))
            kmax = 0
            while (2 << kmax) - 1 < n:
                kmax += 1
            for k in range(kmax):
                s = 1 << k
                m = (n - (2 * s - 1) - 1) // (2 * s) + 1
                lvl(k, 2 * s - 1, m)
            for k in range(kmax - 1, -1, -1):
                s = 1 << k
                if n - 1 < 3 * s - 1:
                    continue
                m = (n - 1 - (3 * s - 1)) // (2 * s) + 1
                lvl(k, 3 * s - 1, m)
            return ops

        def s5_mixer(l):
            s5_prep(l)
            def ev_u(m, tc0, n, ps, pk):
                kb.op('act', lambda: nc.scalar.copy(out=uT[:, m, tc0:tc0 + n], in_=ps[:, 0:n]), R=[pk], W=[('uT', m)])
            proj_fm(w_in[l], 0, 512, hT, HT_KEYS, 8, ev_u)
            CH = NT // 8
            pstep = Z[:].ap[0][0]
            zoff = Z[:].offset
            cqoff = CQ[:].offset

            def z3(start, step, m):
                return bass.AP(Z[:].tensor, zoff + start, [[pstep, 128], [2 * CH, 8], [CH, 2], [step, m]])

            def z2(comp, start, step, m):
                return bass.AP(Z[:].tensor, zoff + comp * CH + start, [[pstep, 128], [2 * CH, 8], [step, m]])

            def cq3(k, c, m):
                return bass.AP(CQ[:].tensor, cqoff + k * 24 + c, [[CQ[:].ap[0][0], 128], [3, 8], [0, 2], [0, m]])

            def cq2(k, c, m):
                return bass.AP(CQ[:].tensor, cqoff + k * 24 + c, [[CQ[:].ap[0][0], 128], [3, 8], [0, m]])

            TQ = [None]

            def scan_level(k, dst0, m):
                s_ = 1 << k
                step = 2 * s_
                src0 = dst0 - s_
                T1v = gsc[:, 0:16 * 144].rearrange("p (a c m) -> p a c m", a=8, c=2)[:, :, :, 0:m]
                T2v = T2s[0][:, :, 0:m]
                ZK = ['Z']
                dv(lambda: nc.vector.tensor_tensor(out=T1v, in0=z3(src0, step, m), in1=cq3(k + 3, 0, m), op=ALU.mult),
                   ZK + ['CQ', 'gsc'], ['gsc'])
                dv(lambda: nc.vector.tensor_tensor(out=T2v, in0=z2(1, src0, step, m), in1=cq2(k + 3, 2, m), op=ALU.mult),
                   ZK + ['CQ', TQ[0]], [TQ[0]])
                dv(lambda: nc.vector.tensor_tensor(out=T1v[:, :, 0, :], in0=T1v[:, :, 0, :], in1=T2v, op=ALU.add),
                   ['gsc', TQ[0]], ['gsc'])
                dv(lambda: nc.vector.tensor_tensor(out=T2v, in0=z2(0, src0, step, m), in1=cq2(k + 3, 1, m), op=ALU.mult),
                   ZK + ['CQ', TQ[0]], [TQ[0]])
                dv(lambda: nc.vector.tensor_tensor(out=T1v[:, :, 1, :], in0=T1v[:, :, 1, :], in1=T2v, op=ALU.add),
                   ['gsc', TQ[0]], ['gsc'])
                dv(lambda: nc.vector.tensor_tensor(out=z3(dst0, step, m), in0=z3(dst0, step, m), in1=T1v, op=ALU.add),
                   ZK + ['gsc'], ZK)

            for q in range(4):
                T2s[0] = uT[:, q, :].bitcast(F32).rearrange("p (a m) -> p a m", a=8)
                TQ[0] = ('uT', q)
                kb.op('pool', lambda q=q: nc.gpsimd.tensor_copy(out=uS[:], in_=uT[:, q, :].rearrange("p (c s) -> p s c", s=8)),
                      R=[('uT', q)], W=['uS'])
                kb.op('act', lambda q=q: nc.scalar.activation(out=yacc[:].rearrange("p (s c) -> p s c", s=8), in_=uS[:],
                                                              func=AF.Copy, scale=s5dT[:, q:q + 1]),
                      R=['uS', 's5dT'], W=['yacc'])
                for d in range(2):
                    for c, tab in enumerate((APre, APim, APnim)):
                        ov = bass.AP(CQ[:].tensor, cqoff + d * 3 + c, [[CQ[:].ap[0][0], 128], [24, 12], [6, 4]])
                        pv(lambda ov=ov, tab=tab, d=d, q=q: nc.gpsimd.tensor_copy(out=ov, in_=tab[:, d, :, q * 4:(q + 1) * 4]),
                           ['APW', 'CQ'], ['CQ'])
                UQ = ['uS']
                TK = ['s5tmp']
                for pl in range(4):
                    pi_ = q * 4 + pl
                    j4 = pi_ % 4
                    bi = pi_ % 2
                    XGb, XGk = XG[bi], 'XG%d' % bi
                    kb.dma(Bt[bi][:], s5ba[l, pi_], W=['Bt%d' % bi])
                    kb.op('pool', lambda XGb=XGb: nc.gpsimd.memset(XGb[:], 0.0), W=[XGk])
                    for d in range(2):
                        def e3(tab, d=d, pi_=pi_):
                            v = tab[:, d, 0:8, pi_]
                            return bass.AP(v.tensor, v.offset, [list(v.ap[0]), list(v.ap[1]), [0, 32]])

                        def b3(r, d=d, bi=bi):
                            v = Bt[bi][:, d * 2 + r, :]
                            return bass.AP(v.tensor, v.offset, [list(v.ap[0]), [0, 8], list(v.ap[1])])
                        T0, T1 = s5tmpP[:, 0], s5tmpP[:, 1]
                        RK = ['Epw', 'Bt%d' % bi] + ['s5tmpP']
                        dv(lambda e3=e3, b3=b3: nc.vector.tensor_tensor(out=T0, in0=e3(Epw_re), in1=b3(0), op=ALU.mult), RK, ['s5tmpP'])
                        dv(lambda e3=e3, b3=b3: nc.vector.tensor_tensor(out=T1, in0=e3(Epw_im), in1=b3(1), op=ALU.mult), RK, ['s5tmpP'])
                        dv(lambda d=d, XGb=XGb, j4=j4: nc.vector.tensor_tensor(out=XGb[:, d, 0, :, j4 * 32:(j4 + 1) * 32], in0=T0, in1=T1,
                                                                               op=ALU.subtract), ['s5tmpP', XGk], [XGk])
                        dv(lambda e3=e3, b3=b3: nc.vector.tensor_tensor(out=T0, in0=e3(Epw_re), in1=b3(1), op=ALU.mult), RK, ['s5tmpP'])
                        dv(lambda e3=e3, b3=b3: nc.vector.tensor_tensor(out=T1, in0=e3(Epw_im), in1=b3(0), op=ALU.mult), RK, ['s5tmpP'])
                        dv(lambda d=d, XGb=XGb, j4=j4: nc.vector.tensor_tensor(out=XGb[:, d, 1, :, j4 * 32:(j4 + 1) * 32], in0=T0, in1=T1,
                                                                               op=ALU.add), ['s5tmpP', XGk], [XGk])
                    RWk = 'Rwp%d' % bi
                    kb.op('pool', lambda bi=bi: nc.gpsimd.memset(Rwp[bi][:], 0.0), W=[RWk])
                    for d in range(2):
                        for r in range(2):
                            for gl in range(2):
                                c0 = j4 * 32 + gl * 16
                                pv(lambda d=d, r=r, gl=gl, c0=c0, bi=bi, pi_=pi_: nc.gpsimd.tensor_scalar(
                                    out=Rwp[bi][gl * 64:(gl + 1) * 64, d, r, c0:c0 + 16], in0=cfk[gl * 64:(gl + 1) * 64, d, r, pi_, :],
                                    scalar1=(1.0 if r == 0 else -1.0), scalar2=0.0, op0=ALU.mult, op1=ALU.add), ['cfs', RWk], [RWk])
                    for g4 in range(4):
                        ps, pk = psum()
                        pb = ps[:].bitcast(BF16)

                        def f(pb=pb, g4=g4, XGb=XGb):
                            for i8 in range(8):
                                idx = g4 * 8 + i8
                                ins = nc.tensor.transpose(pb[:, i8 * 128:(i8 + 1) * 128],
                                                          XGb[:, idx // 16, (idx // 8) % 2, idx % 8, :], ident[:])
                            return ins
                        kb.op('pe', f, R=[XGk, 'ident'], W=[pk])
                        kb.op('act', lambda pb=pb, g4=g4: nc.scalar.copy(
                            out=Sx[:, g4 * 8:(g4 + 1) * 8, :].rearrange("p a b -> p (a b)"), in_=pb[:, :]), R=[pk, 'Sx'], W=['Sx'])
                    for dd in range(2):
                        for jh in range(2):
                            ps, pk = psum()

                            def f(ps=ps, dd=dd, jh=jh, XGb=XGb, bi=bi):
                                for jj in range(4):
                                    j = jh * 4 + jj
                                    nc.tensor.matmul(ps[:, jj * 128:(jj + 1) * 128], lhsT=XGb[:, dd, 0, j, :], rhs=Rwp[bi][:, dd, 0, :],
                                                     start=True, stop=False)
                                    ins = nc.tensor.matmul(ps[:, jj * 128:(jj + 1) * 128], lhsT=XGb[:, dd, 1, j, :],
                                                           rhs=Rwp[bi][:, dd, 1, :], start=False, stop=True)
                                return ins
                            kb.op('pe', f, R=[XGk, RWk], W=[pk])
                            kb.op('act', lambda ps=ps, dd=dd, jh=jh, j4=j4: nc.scalar.copy(
                                out=BDq[:, dd, jh * 4:(jh + 1) * 4, j4 * 32:(j4 + 1) * 32],
                                in_=ps[:, :].rearrange("p (a b) -> p a b", b=128)[:, :, j4 * 32:(j4 + 1) * 32]),
                                R=[pk, 'BDq'], W=['BDq'])
                    for d in range(2):
                        for r in range(2):
                            ps, pk = psum()

                            def f(ps=ps, d=d, r=r):
                                if d == 0:
                                    for s_ in range(8):
                                        ins = nc.tensor.matmul(ps[:, 0:CH], lhsT=Sx[:, r * 8 + (7 - s_), :], rhs=uS[:, s_, :],
                                                               start=(s_ == 0), stop=(s_ == 7))
                                else:
                                    for s_ in range(8):
                                        nc.tensor.matmul(ps[:, 0:256], lhsT=Sx[:, 16 + r * 8 + s_, :], rhs=uS[:, s_, 32:CH],
                                                         start=(s_ == 0), stop=(s_ == 7))
                                    for s_ in range(8):
                                        ins = nc.tensor.matmul(ps[:, 256:CH], lhsT=Sx[:, 16 + r * 8 + s_, :], rhs=uS[:, s_, 0:32],
                                                               start=(s_ == 0), stop=(s_ == 7), skip_group_check=True)
                                return ins
                            kb.op('pe', f, R=['Sx'] + UQ, W=[pk])
                            if d == 0:
                                src = ps[:, 0:CH]
                            else:
                                v = ps[:, 0:CH]
                                src = bass.AP(v.tensor, v.offset + CH - 1, [list(v.ap[0]), [-1, CH]])
                            kb.op('act', lambda src=src, pl=pl, d=d, r=r: nc.scalar.copy(out=Z[:, pl, d, r, :], in_=src),
                                  R=[pk, 'Z'], W=['Z'])
                kmax = 0
                while (2 << kmax) - 1 < CH:
                    kmax += 1
                for k in range(kmax):
                    s_ = 1 << k
                    scan_level(k, 2 * s_ - 1, (CH - (2 * s_ - 1) - 1) // (2 * s_) + 1)
                for k in range(kmax - 1, -1, -1):
                    s_ = 1 << k
                    if CH - 1 < 3 * s_ - 1:
                        continue
                    scan_level(k, 3 * s_ - 1, (CH - 1 - (3 * s_ - 1)) // (2 * s_) + 1)
                kb.op('pool', lambda: nc.gpsimd.memset(XsQ[:], 0.0), W=['XsQ'])
                kb.op('act', lambda: nc.scalar.copy(out=XsQ[:, :, 0, :, 1:CH], in_=Z[:, :, 0, :, 0:CH - 1]), R=['Z', 'XsQ'], W=['XsQ'])
                zr = bass.AP(Z[:].tensor, zoff + 2 * CH + CH - 2, [[pstep, 128], [4 * CH, 4], [CH, 2], [-1, CH - 1]])
                kb.op('act', lambda: nc.scalar.copy(out=XsQ[:, :, 1, :, 0:CH - 1], in_=zr), R=['Z', 'XsQ'], W=['XsQ'])
                SK = 'XsQ'
                for pp in range(2):
                    for pl in (2 * pp, 2 * pp + 1):
                        pi_ = q * 4 + pl
                        j4 = pi_ % 4
                        bi = pi_ % 2
                        XGb, XGk = XG[bi], 'XG%d' % bi
                        kb.op('pool', lambda XGb=XGb: nc.gpsimd.memset(XGb[:], 0.0), W=[XGk])
                        for d in range(2):
                            def c3(r, d=d, pi_=pi_):
                                v = cfk[:, d, r, pi_, :]
                                return bass.AP(v.tensor, v.offset, [list(v.ap[0]), [0, 8], list(v.ap[1])])

                            def g3(tab, d=d, pi_=pi_):
                                v = tab[:, d, :, pi_]
                                return bass.AP(v.tensor, v.offset, [list(v.ap[0]), list(v.ap[1]), [0, 16]])
                            G0, G1, G2 = s5tmpP[:, 0, :, 0:16], s5tmpP[:, 1, :, 0:16], s5tmpP[:, 2, :, 0:16]
                            RK = ['Epw', 'cfs', 's5tmpP']
                            dv(lambda c3=c3, g3=g3: nc.vector.tensor_tensor(out=G0, in0=c3(0), in1=g3(Eg_re), op=ALU.mult), RK, ['s5tmpP'])
                            dv(lambda c3=c3, g3=g3: nc.vector.tensor_tensor(out=G1, in0=c3(1), in1=g3(Eg_im), op=ALU.mult), RK, ['s5tmpP'])
                            dv(lambda: nc.vector.tensor_tensor(out=G2, in0=G0, in1=G1, op=ALU.subtract), ['s5tmpP'], ['s5tmpP'])
                            for gl in range(2):
                                c0 = j4 * 32 + gl * 16
                                dv(lambda d=d, gl=gl, c0=c0, XGb=XGb: nc.vector.tensor_copy(
                                    out=XGb[gl * 64:(gl + 1) * 64, d, 0, :, c0:c0 + 16], in_=G2[gl * 64:(gl + 1) * 64]), ['s5tmpP', XGk], [XGk])
                            dv(lambda c3=c3, g3=g3: nc.vector.tensor_tensor(out=G0, in0=c3(0), in1=g3(Eg_im), op=ALU.mult), RK, ['s5tmpP'])
                            dv(lambda c3=c3, g3=g3: nc.vector.tensor_tensor(out=G1, in0=c3(1), in1=g3(Eg_re), op=ALU.mult), RK, ['s5tmpP'])
                            dv(lambda: nc.vector.tensor_tensor(out=G2, in0=G0, in1=G1, op=ALU.add), ['s5tmpP'], ['s5tmpP'])
                            for gl in range(2):
                                c0 = j4 * 32 + gl * 16
                                dv(lambda d=d, gl=gl, c0=c0, XGb=XGb: nc.vector.tensor_scalar(
                                    out=XGb[gl * 64:(gl + 1) * 64, d, 1, :, c0:c0 + 16], in0=G2[gl * 64:(gl + 1) * 64], scalar1=-1.0,
                                    scalar2=None, op0=ALU.mult), ['s5tmpP', XGk], [XGk])
                    for t in range(8):
                        ps, pk = psum()

                        def f(ps=ps, t=t, pp=pp):
                            first = True
                            if pp == 0:
                                for s_ in range(8):
                                    if s_ < t:
                                        ws = [BDq[:, 0, t - s_, :]]
                                    elif s_ > t:
                                        ws = [BDq[:, 1, s_ - t, :]]
                                    else:
                                        ws = [BDq[:, 0, 0, :], BDq[:, 1, 0, :]]
                                    for w_ in ws:
                                        nc.tensor.matmul(ps[:, 0:CH], lhsT=w_, rhs=uS[:, s_, :], start=first, stop=False)
                                        first = False
                            for pl in (2 * pp, 2 * pp + 1):
                                XGb = XG[pl % 2]
                                for r in range(2):
                                    nc.tensor.matmul(ps[:, 0:CH], lhsT=XGb[:, 0, r, t, :], rhs=XsQ[:, pl, 0, r, :], start=first, stop=False)
                                    first = False
                                for r in range(2):
                                    nc.tensor.matmul(ps[:, 32:CH], lhsT=XGb[:, 1, r, t, :], rhs=XsQ[:, pl, 1, r, 0:256],
                                                     start=False, stop=False)
                                    ins = nc.tensor.matmul(ps[:, 0:32], lhsT=XGb[:, 1, r, t, :], rhs=XsQ[:, pl, 1, r, 256:CH],
                                                           start=False, stop=(r == 1 and pl == 2 * pp + 1))
                            return ins
                        kb.op('pe', f, R=['XG0', 'XG1', SK, 'BDq', 'uS'], W=[pk])
                        dv(lambda ps=ps, t=t: nc.vector.tensor_tensor(out=yacc[:, t * CH:(t + 1) * CH], in0=yacc[:, t * CH:(t + 1) * CH],
                                                                      in1=ps[:, 0:CH], op=ALU.add), [pk, 'yacc'], ['yacc'])
                kb.op('act', lambda: nc.scalar.activation(out=gsc[:], in_=yacc[:], func=AF.Square), R=['yacc', 'gsc'], W=['gsc'])
                dv(lambda: nc.vector.tensor_scalar(out=gsc[:], in0=gsc[:], scalar1=0.044715, scalar2=1.0, op0=ALU.mult,
                                                   op1=ALU.add), ['gsc'], ['gsc'])
                dv(lambda: nc.vector.tensor_tensor(out=gsc[:], in0=gsc[:], in1=yacc[:], op=ALU.mult), ['yacc', 'gsc'], ['gsc'])
                kb.op('act', lambda: nc.scalar.activation(out=gsc[:], in_=gsc[:], func=AF.Sigmoid,
                                                          scale=2.0 * float(np.sqrt(2.0 / np.pi))), R=['gsc'], W=['gsc'])
                dv(lambda q=q: nc.vector.tensor_tensor(out=uT[:, q, :].rearrange("p (c s) -> p s c", s=8),
                                                       in0=yacc[:].rearrange("p (s c) -> p s c", s=8),
                                                       in1=gsc[:].rearrange("p (s c) -> p s c", s=8), op=ALU.mult),
                   ['yacc', 'gsc', 'uS'], [('uT', q)])
            UK = [('uT', q) for q in range(4)]
            cnt = [0]

            def ev_glu(m, tc0, n, ps, pk):
                i = cnt[0] % 2
                cnt[0] += 1
                kb.op('act', lambda: nc.scalar.activation(out=sg[i][:, 0:n], in_=ps[:, 0:n], func=AF.Sigmoid,
                                                          bias=bgluT[:, m:m + 1]), R=[pk, 'bgluT'], W=['gsc'])
                dv(lambda: nc.vector.tensor_tensor(out=ys5T[:, m, tc0:tc0 + n], in0=uT[:, m, tc0:tc0 + n],
                                                   in1=sg[i][:, 0:n], op=ALU.mult), ['gsc', ('uT', m)], [('ys5T', m)])
            proj_fm(w_glu[l], 0, 512, uT, UK, 4, ev_glu)


        w_rot = din("w_rot", [DEPTH, D, 512])
        thB_d = din("thB", [DEPTH, 128, 8])
        thQ_d = din("thQ", [DEPTH, 128, 4])
        rotc_d = din("rotc", [128, NT])
        rots_d = din("rots", [128, NT])
        retc_d = din("retc", [128, 2, 128])
        iq_d = din("iq", [128, 2, 128])
        ek_d = din("ek", [128, 2])
        onesF = sb("onesF", [128, 128], F32)
        kb.op('pool', lambda: nc.gpsimd.memset(onesF[:], 1.0 / 128.0), W=['onesF'])

        def ret_mixer(l):
            T_ = ['rtab']
            kb.dma(thB[:], thB_d[l], W=['thB'])
            kb.dma(thQ[:], thQ_d[l], W=['thQ'])
            kb.dma(retc[:], retc_d[:, :, :], W=['retc'])
            kb.dma(iq[:], iq_d[:, :, :], W=['iq'])
            kb.dma(ek[:], ek_d[:, :], W=['ek'])
            kb.dma(rotc[:], rotc_d[:, :], W=['rotc'])
            kb.dma(rots[:], rots_d[:, :], W=['rots'])
            for (src, dst, key) in ((thB, lgB, 'thB'), (thQ, lgQ, 'thQ')):
                kb.op('act', lambda src=src, dst=dst: nc.scalar.activation(out=dst[:], in_=src[:], func=AF.Exp, scale=-1.0),
                      R=[key], W=T_)
                dv(lambda dst=dst: nc.vector.tensor_scalar(out=dst[:], in0=dst[:], scalar1=1.0, scalar2=None, op0=ALU.add),
                   T_, T_)
                kb.op('act', lambda dst=dst: nc.scalar.activation(out=dst[:], in_=dst[:], func=AF.Ln), R=T_, W=T_)
                dv(lambda dst=dst: nc.vector.tensor_scalar(out=dst[:], in0=dst[:], scalar1=-1.0, scalar2=None, op0=ALU.mult),
                   T_, T_)
            for d in range(2):
                for h in range(4):
                    c = d * 4 + h
                    kb.op('act', lambda d=d, c=c: nc.scalar.activation(out=Dm[:, c, :], in_=retc[:, d, :], func=AF.Exp,
                                                                       scale=lgB[:, c:c + 1]), R=T_ + ['retc'], W=T_)
                for t in range(2):
                    c = d * 2 + t
                    kb.op('act', lambda d=d, t=t, c=c: nc.scalar.activation(out=qdtab[:, d, t, :], in_=iq[:, d, :],
                                                                            func=AF.Exp, scale=lgQ[:, c:c + 1]),
                          R=T_ + ['iq'], W=T_)
                kb.op('act', lambda d=d: nc.scalar.activation(out=kd[:, d * 4:(d + 1) * 4], in_=lgB[:, d * 4:(d + 1) * 4],
                                                              func=AF.Exp, scale=ek[:, d:d + 1]), R=T_ + ['ek'], W=T_)
            kb.op('act', lambda: nc.scalar.activation(out=cdecQ[:], in_=lgQ[:], func=AF.Exp, scale=128.0), R=T_, W=T_)

            def mk_ev(dst, key, scale):
                def ev(m, tc0, n, ps, pk):
                    kb.op('act', lambda: nc.scalar.activation(out=dst[:, m, tc0:tc0 + n], in_=ps[:, 0:n], func=AF.Copy,
                                                              scale=scale), R=[pk], W=[(key, m)])
                return ev
            for (dst, key, col0, rc0, scale) in ((kT, 'kT', 512, 0, 0.125), (qT, 'qT', 2304, 256, 1.0)):
                proj_fm(w_in[l], col0, 256, hT, HT_KEYS, 8, mk_ev(dst, key, scale))
                proj_fm(w_rot[l], rc0, 256, hT, HT_KEYS, 8, mk_ev(tmpT, 'tmpT', scale))
                for m in range(2):
                    dv(lambda m=m, dst=dst: nc.vector.tensor_tensor(out=dst[:, m, :], in0=dst[:, m, :], in1=rotc[:],
                                                                    op=ALU.mult), [(key, m), 'rotc'], [(key, m)])
                    dv(lambda m=m: nc.vector.tensor_tensor(out=tmpT[:, m, :], in0=tmpT[:, m, :], in1=rots[:],
                                                           op=ALU.mult), [('tmpT', m), 'rots'], [('tmpT', m)])
                    dv(lambda m=m, dst=dst: nc.vector.tensor_tensor(out=dst[:, m, :], in0=dst[:, m, :], in1=tmpT[:, m, :],
                                                                    op=ALU.add), [(key, m), ('tmpT', m)], [(key, m)])
            for cb in range(2):
                wt, wk = load_wk(w_in[l][:, 768 + cb * 256:768 + (cb + 1) * 256], 8, 256)
                for n in range(NTT):
                    ps, pk = psum()

                    def f(ps=ps, wt=wt, n=n):
                        for k in range(8):
                            ins = nc.tensor.matmul(ps[:, 0:256], lhsT=hT[:, k, n * 128:(n + 1) * 128], rhs=wt[:, k, 0:256],
                                                   start=(k == 0), stop=(k == 7))
                        return ins
                    kb.op('pe', f, R=[wk, ('hT', n)], W=[pk])
                    kb.op('act', lambda ps=ps, n=n, cb=cb: nc.scalar.copy(out=vtok[:, n, cb * 256:(cb + 1) * 256],
                                                                          in_=ps[:, 0:256]), R=[pk], W=[('vtok', n, cb)])
        def ret_part2(l):
            T_ = ['rtab']
            rb = [0]
            chains = [(t, d) for t in range(2) for d in range(2)]
            orders = {0: list(range(NTT)), 1: [1, 0] + list(range(NTT - 1, 1, -1))}
            written = set()
            for c in range(4):
                dv(lambda c=c: nc.vector.memset(sst[c][:], 0.0), ['sst%d' % c], ['sst%d' % c])
            for idx in range(NTT):
                for c, (t, d) in enumerate(chains):
                    n = orders[d][idx]
                    tsl = slice(n * 128, (n + 1) * 128)
                    i = rb[0] % 4
                    rb[0] += 1
                    ps, pk = psum()
                    pb = ps[:].bitcast(BF16)
                    kb.op('pe', lambda pb=pb, t=t, tsl=tsl: nc.tensor.transpose(pb[:, 0:128], kT[:, t, tsl], ident[:]),
                          R=[('kT', t), 'ident'], W=[pk])
                    kdv = kd[:, d * 4 + 2 * t:d * 4 + 2 * t + 2]
                    kdb = bass.AP(kdv.tensor, kdv.offset, [list(kdv.ap[0]), [1, 2], [0, 64]])
                    dv(lambda pb=pb, i=i, kdb=kdb: nc.vector.tensor_tensor(
                        out=kdec[i][:].rearrange("p (a b) -> p a b", a=2), in0=pb[:, 0:128].rearrange("p (a b) -> p a b", a=2),
                        in1=kdb, op=ALU.mult), [pk] + T_, ['kdec%d' % i])
                    dv(lambda i=i, t=t, tsl=tsl, d=d: nc.vector.tensor_tensor(out=qdt[i][:], in0=qT[:, t, tsl],
                                                                             in1=qdtab[:, d, t, :], op=ALU.mult),
                       [('qT', t)] + T_, ['qdt%d' % i])
                    for hh in range(2):
                        h = 2 * t + hh
                        psl = slice(64 * hh, 64 * hh + 64)
                        ps1, pk1 = psum()
                        kb.op('pe', lambda ps1=ps1, psl=psl, t=t, tsl=tsl: nc.tensor.matmul(
                            ps1[:, 0:128], lhsT=kT[psl, t, tsl], rhs=qT[psl, t, tsl], start=True, stop=True),
                            R=[('kT', t), ('qT', t)], W=[pk1])
                        dv(lambda ps1=ps1, i=i, hh=hh, d=d, h=h: nc.vector.tensor_tensor(
                            out=pT[i][:, hh, :], in0=ps1[:, 0:128], in1=Dm[:, d * 4 + h, :], op=ALU.mult),
                           [pk1] + T_, [('pT%d' % i, hh)])
                        ps2, pk2 = psum()

                        def f(ps2=ps2, n=n, h=h, hh=hh, i=i, psl=psl, idx=idx, c=c):
                            ins = nc.tensor.matmul(ps2[:, 0:128], lhsT=vtok[:, n, h * 128:(h + 1) * 128],
                                                   rhs=pT[i][:, hh, :], start=True, stop=(idx == 0))
                            if idx > 0:
                                ins = nc.tensor.matmul(ps2[:, 0:128], lhsT=sbf[c][psl, hh * 128:(hh + 1) * 128],
                                                       rhs=qdt[i][psl, :], start=False, stop=True)
                            return ins
                        kb.op('pe', f, R=[('vtok', n, h // 2), ('pT%d' % i, hh), 'sbf%d' % c, 'qdt%d' % i], W=[pk2])
                        if (h, n) not in written:
                            written.add((h, n))
                            kb.op('act', lambda ps2=ps2, h=h, tsl=tsl: nc.scalar.copy(out=oT[:, h, tsl], in_=ps2[:, 0:128]),
                                  R=[pk2], W=[('oT', h, n)])
                        else:
                            dv(lambda ps2=ps2, h=h, tsl=tsl: nc.vector.tensor_tensor(
                                out=oT[:, h, tsl], in0=oT[:, h, tsl], in1=ps2[:, 0:128], op=ALU.add),
                               [pk2, ('oT', h, n)], [('oT', h, n)])
                    if idx < NTT - 1:
                        ps3, pk3 = psum()
                        kb.op('pe', lambda ps3=ps3, i=i, n=n, t=t: nc.tensor.matmul(
                            ps3[:, 0:256], lhsT=kdec[i][:], rhs=vtok[:, n, t * 256:(t + 1) * 256], start=True, stop=True),
                            R=['kdec%d' % i, ('vtok', n, t)], W=[pk3])
                        dv(lambda ps3=ps3, d=d, t=t, c=c: nc.vector.scalar_tensor_tensor(
                            out=sst[c][:], in0=sst[c][:], scalar=cdecQ[:, d * 2 + t:d * 2 + t + 1], in1=ps3[:, 0:256],
                            op0=ALU.mult, op1=ALU.add), [pk3, 'sst%d' % c] + T_, ['sst%d' % c])
                        kb.op('act', lambda c=c: nc.scalar.copy(out=sbf[c][:], in_=sst[c][:]), R=['sst%d' % c], W=['sbf%d' % c])
            units = [(h, bi_, tc0, n) for h in range(4) for bi_, (tc0, n) in enumerate(TB)]
            WT = {}
            NU = len(units)

            def ctx_(j):
                h, bi_, tc0, n = units[j]
                OK_ = [('oT', h, tt) for tt in range(tc0 // 128, (tc0 + n) // 128)]
                return h, tc0, n, OK_, oT[:, h, tc0:tc0 + n], sg[j % 3], 'sg%d' % (j % 3), sg[3 + j % 2], 'sg%d' % (3 + j % 2)

            def st1(j):
                h, tc0, n, OK_, osl, a_, ak, b_, bk_ = ctx_(j)
                if h not in WT:
                    WT[h] = load_wk(w_in[l][:, 2560 + h * 128:2560 + (h + 1) * 128], 8, 128)
                ps, pk = psum()
                kb.op('pe', lambda: nc.tensor.matmul(ps[:, 0:n], lhsT=onesF[:], rhs=osl, start=True, stop=True),
                      R=OK_ + ['onesF'], W=[pk])
                dv(lambda: nc.vector.tensor_tensor(out=osl, in0=osl, in1=ps[:, 0:n], op=ALU.subtract), OK_ + [pk], OK_)

            def st2(j):
                h, tc0, n, OK_, osl, a_, ak, b_, bk_ = ctx_(j)
                dv(lambda: nc.vector.tensor_tensor(out=a_[:, 0:n], in0=osl, in1=osl, op=ALU.mult), OK_ + [ak], [ak])
                ps, pk = psum()
                kb.op('pe', lambda: nc.tensor.matmul(ps[:, 0:n], lhsT=onesF[:], rhs=a_[:, 0:n], start=True, stop=True),
                      R=[ak, 'onesF'], W=[pk])
                kb.op('act', lambda: nc.scalar.activation(out=a_[:, 0:n], in_=ps[:, 0:n], func=AF.Sqrt, bias=eps_t[:, 1:2]),
                      R=[pk, 'eps_t', ak], W=[ak])

            def st3(j):
                h, tc0, n, OK_, osl, a_, ak, b_, bk_ = ctx_(j)
                dv(lambda: nc.vector.reciprocal(out=a_[:, 0:n], in_=a_[:, 0:n]), [ak], [ak])
                dv(lambda: nc.vector.tensor_tensor(out=osl, in0=osl, in1=a_[:, 0:n], op=ALU.mult), OK_ + [ak], OK_)
                wt, wk = WT[h]
                ps, pk = psum()

                def f():
                    for k in range(8):
                        ins = nc.tensor.matmul(ps[:, 0:n], lhsT=wt[:, k, 0:128], rhs=hT[:, k, tc0:tc0 + n],
                                               start=(k == 0), stop=(k == 7))
                    return ins
                kb.op('pe', f, R=[wk] + HT_KEYS, W=[pk])
                kb.op('act', lambda: nc.scalar.activation(out=b_[:, 0:n], in_=ps[:, 0:n], func=AF.Silu), R=[pk, bk_], W=[bk_])

            def st4(j):
                h, tc0, n, OK_, osl, a_, ak, b_, bk_ = ctx_(j)
                dv(lambda: nc.vector.tensor_tensor(out=yretT[:, h, tc0:tc0 + n], in0=osl, in1=b_[:, 0:n], op=ALU.mult),
                   OK_ + [bk_], [('yretT', h)])

            for j in range(NU + 3):
                if j < NU:
                    st1(j)
                if 0 <= j - 1 < NU:
                    st2(j - 1)
                if 0 <= j - 2 < NU:
                    st3(j - 2)
                if 0 <= j - 3 < NU:
                    st4(j - 3)

        nab_d = din("nab", [DEPTH, 8, 128, 15 * 64])
        colmask_d = din("colmask", [128, 64])
        onesB = sb("onesB", [128, 128], BF16)
        kb.op('pool', lambda: nc.gpsimd.memset(onesB[:], 1.0), W=['onesB'])

        def _os_skip():
            import os
            return os.environ.get('NA_SKIPCTX', '0') == '1'

        def na_mixer(l, need_ctx):
            kb.dma(colmask[:], colmask_d[:, :], W=['colmask'])
            cmb = bass.AP(colmask[:].tensor, colmask[:].offset, [list(colmask[:].ap[0]), [0, 15], [1, 64]])
            for h in range(8):
                i = 0
                kb.dma(nst[i], nab_d[l, h], W=['nst0'])
                kb.op('act', lambda i=i: nc.scalar.activation(out=nst[i], in_=nst[i], func=AF.Exp),
                      R=['nst0'], W=['nst0'])
                dv(lambda i=i, h=h: nc.vector.tensor_tensor(out=TT[:, h, :].rearrange("p (a b) -> p a b", b=64),
                                                            in0=nst[i].rearrange("p (a b) -> p a b", b=64), in1=cmb,
                                                            op=ALU.mult), ['nst0', 'colmask'], ['TT'])

            def mk_ev(dst, key, scale):
                def ev(m, tc0, n, ps, pk):
                    kb.op('act', lambda: nc.scalar.activation(out=dst[:, m, tc0:tc0 + n], in_=ps[:, 0:n], func=AF.Copy,
                                                              scale=scale), R=[pk], W=[(key, m)])
                return ev
            proj_fm(w_in[l], 1280, 512, hT, HT_KEYS, 8, mk_ev(nkT, 'nkT', 1.0))
            proj_fm(w_in[l], 3072, 512, hT, HT_KEYS, 8, mk_ev(nqT, 'nqT', 0.125))
            for cb in range(2):
                wt, wk = load_wk(w_in[l][:, 1792 + cb * 256:1792 + (cb + 1) * 256], 8, 256)
                for n in range(NTT):
                    ps, pk = psum()

                    def f(ps=ps, wt=wt, n=n):
                        for k in range(8):
                            ins = nc.tensor.matmul(ps[:, 0:256], lhsT=hT[:, k, n * 128:(n + 1) * 128], rhs=wt[:, k, 0:256],
                                                   start=(k == 0), stop=(k == 7))
                        return ins
                    kb.op('pe', f, R=[wk, ('hT', n)], W=[pk])
                    kb.op('act', lambda ps=ps, n=n, cb=cb: nc.scalar.copy(out=nvtok[:, n, cb * 256:(cb + 1) * 256],
                                                                          in_=ps[:, 0:256]), R=[pk], W=[('nvtok', n, cb)])
            import os as _os
            NR = int(_os.environ.get('NA_ROWS', '32'))
            VK_ = lambda t: [('nvtok', n, t // 2) for n in range(NTT)]

            def ctx_queries(h):
                t, hh = h // 2, h % 2
                psl = slice(64 * hh, 64 * hh + 64)
                KQ = [('nkT', t), ('nqT', t)]
                ps, pk = psum()

                def f(ps=ps):
                    for j in range(2):
                        ins = nc.tensor.matmul(ps[:, j * 256:(j + 1) * 256], lhsT=nkT[psl, t, j * 128:(j + 1) * 128],
                                               rhs=nqT[psl, t, 0:256], start=True, stop=True)
                    return ins
                kb.op('pe', f, R=KQ, W=[pk])
                kb.op('act', lambda: nc.scalar.activation(out=pc[:], in_=ps[:, 0:512], func=AF.Exp), R=[pk, 'pc'], W=['pc'])
                pso, pko = psum()
                psd, pkd = psum()

                def f(pso=pso, psd=psd):
                    for j in range(2):
                        nc.tensor.matmul(pso[:, 0:256], lhsT=nvtok[:, j, t * 128:(t + 1) * 128],
                                         rhs=pc[:, j * 256:(j + 1) * 256], start=(j == 0), stop=(j == 1))
                    for j in range(2):
                        ins = nc.tensor.matmul(psd[:, 0:256], lhsT=onesB[:], rhs=pc[:, j * 256:(j + 1) * 256],
                                               start=(j == 0), stop=(j == 1))
                    return ins
                kb.op('pe', f, R=VK_(t) + ['pc', 'onesB'], W=[pko, pkd])
                dv(lambda: nc.vector.reciprocal(out=rcc[psl, 0:256], in_=psd[psl, 0:256]), [pkd, 'rcc'], ['rcc'])
                dv(lambda: nc.vector.tensor_tensor(out=ynaT[psl, t, 0:256], in0=pso[psl, 0:256], in1=rcc[psl, 0:256],
                                                   op=ALU.mult), [pko, 'rcc'], [('ynaT', t)])

            units = [(h, r) for h in range(8) for r in range(NR)]
            U = {}

            def s_unit(k):
                h, r = units[k]
                t, hh = h // 2, h % 2
                psl = slice(64 * hh, 64 * hh + 64)
                rs = min(max(r - 4, 0), 24)
                q0 = NCTX + r * 64
                rows = list(range(rs, rs + 8))
                par_rows = [[rr for rr in rows if rr % 2 == p] for p in range(2)]
                ps, pk = psum()

                def f():
                    for p in range(2):
                        for sl_, rr in enumerate(par_rows[p]):
                            k0 = NCTX + rr * 64
                            nc.tensor.matmul(ps[64 * p:64 * p + 64, sl_ * 64:(sl_ + 1) * 64], lhsT=nkT[psl, t, k0:k0 + 64],
                                             rhs=nqT[psl, t, q0:q0 + 64], start=True, stop=True)
                    for j in range(2):
                        ins = nc.tensor.matmul(ps[:, 256 + j * 64:256 + (j + 1) * 64], lhsT=nkT[psl, t, j * 128:(j + 1) * 128],
                                               rhs=nqT[psl, t, q0:q0 + 64], start=True, stop=True)
                    return ins
                kb.op('pe', f, R=[('nkT', t), ('nqT', t)], W=[pk])
                U[k] = (ps, pk, par_rows, q0, psl, t, h, r)

            def b_unit(k):
                ps, pk, par_rows, q0, psl, t, h, r = U[k]
                i = k % 3
                kb.op('act', lambda: nc.scalar.activation(out=e1[i][:], in_=ps[:, 0:256], func=AF.Exp),
                      R=[pk, 'e1%d' % i], W=['e1%d' % i])
                kb.op('act', lambda: nc.scalar.activation(out=p1[i][:, 256:384], in_=ps[:, 256:384], func=AF.Exp),
                      R=[pk], W=[('p1', i, 'c')])
                for p in range(2):
                    i0 = par_rows[p][0] - r + 7
                    tv = TT[64 * p:64 * p + 64, h, i0 * 64:(i0 + 7) * 64].rearrange("p (a b) -> p a b", b=64)[:, 0:7:2, :]
                    eng_ = (nc.vector, 'dve') if p == 0 else (nc.gpsimd, 'pool')
                    kb.op(eng_[1], lambda p=p, tv=tv, eng_=eng_: eng_[0].tensor_tensor(
                        out=p1[i][64 * p:64 * p + 64, 0:256].rearrange("p (a b) -> p a b", b=64),
                        in0=e1[i][64 * p:64 * p + 64, :].rearrange("p (a b) -> p a b", b=64), in1=tv, op=ALU.mult),
                        R=['e1%d' % i, 'TT'], W=[('p1', i, p)])

            def c_unit(k):
                ps, pk, par_rows, q0, psl, t, h, r = U[k]
                i = k % 3
                pso, pko = psum()
                psb, pkb = psum()

                def f():
                    for p, acc in ((0, pso), (1, psb)):
                        for sl_, rr in enumerate(par_rows[p]):
                            nc.tensor.matmul(acc[:, 0:64], lhsT=nvtok[64 * p:64 * p + 64, 2 + rr // 2, t * 128:(t + 1) * 128],
                                             rhs=p1[i][64 * p:64 * p + 64, sl_ * 64:(sl_ + 1) * 64], start=(sl_ == 0),
                                             stop=(p == 1 and sl_ == 3))
                    for j in range(2):
                        nc.tensor.matmul(pso[:, 0:64], lhsT=nvtok[:, j, t * 128:(t + 1) * 128],
                                         rhs=p1[i][:, 256 + j * 64:256 + (j + 1) * 64], start=False, stop=(j == 1))
                    for j in range(6):
                        ins = nc.tensor.matmul(pso[:, 64:128], lhsT=onesB[:], rhs=p1[i][:, j * 64:(j + 1) * 64],
                                               start=(j == 0), stop=(j == 5), skip_group_check=True)
                    return ins
                kb.op('pe', f, R=VK_(t) + [('p1', i, 0), ('p1', i, 1), ('p1', i, 'c'), 'onesB'], W=[pko, pkb])
                U[k] = U[k] + (pso, pko, psb, pkb)

            def d_unit(k):
                ps, pk, par_rows, q0, psl, t, h, r, pso, pko, psb, pkb = U.pop(k)
                hh = h % 2
                i = k % 3
                rk = 'rc%d' % i
                nk_, dk_ = ('numb', hh), ('denb', hh)
                rr_ = r % 16
                kb.op('act', lambda: nc.scalar.copy(out=rc[i][psl, 0:64], in_=psb[psl, 0:64]), R=[pkb, rk], W=[rk])
                kb.op('act', lambda: nc.scalar.copy(out=denb[psl, rr_ * 64:(rr_ + 1) * 64], in_=pso[psl, 64:128]),
                      R=[pko, dk_], W=[dk_] + (['nst0'] if k < 2 else []))
                dv(lambda: nc.vector.tensor_tensor(out=numb[psl, rr_ * 64:(rr_ + 1) * 64], in0=pso[psl, 0:64], in1=rc[i][psl, 0:64],
                                                   op=ALU.add), [pko, rk, nk_], [nk_])
                if rr_ == 15 or r == NR - 1:
                    nn = (rr_ + 1) * 64
                    c0 = NCTX + (r - rr_) * 64
                    dv(lambda: nc.vector.reciprocal(out=denb[psl, 0:nn], in_=denb[psl, 0:nn]), [dk_], [dk_])
                    dv(lambda: nc.vector.tensor_tensor(out=ynaT[psl, t, c0:c0 + nn], in0=numb[psl, 0:nn],
                                                       in1=denb[psl, 0:nn], op=ALU.mult), [dk_, nk_], [('ynaT', t)])

            if need_ctx and not _os_skip():
                for h in range(8):
                    ctx_queries(h)
            nu = len(units)
            for j in range(nu + 3):
                if j < nu:
                    s_unit(j)
                if 0 <= j - 1 < nu:
                    b_unit(j - 1)
                if 0 <= j - 2 < nu:
                    c_unit(j - 2)
                if 0 <= j - 3 < nu:
                    d_unit(j - 3)

        w_bs = [din("w_b%d" % i, [DEPTH, 512, D]) for i in range(3)]
        w_out_d = din("w_out", [DEPTH, D, D])
        w_fg = din("w_fg", [DEPTH, D, FFH])
        w_fu = din("w_fu", [DEPTH, D, FFH])
        w_fd = din("w_fd", [DEPTH, FFH, D])
        fnrow_d = din("fnrow", [128, D])
        fnrow = sb("fnrow_t", [128, D], F32)
        kb.dma(fnrow[:], fnrow_d[:, :], W=['fnrow'])

        def blocks_of(l):
            return TB if l < DEPTH - 1 else TB[1:]

        def merge_m(l):
            ys = ((ys5T, 'ys5T'), (yretT, 'yretT'), (ynaT, 'ynaT'))
            c = [0]
            for j in range(8):
                for br in range(3):
                    wg, wgk = load_wk(w_in[l][:, 3584 + br * 1024 + j * 128:3584 + br * 1024 + (j + 1) * 128], 8, 128)
                    wb, wbk = load_wk(w_bs[br][l][:, j * 128:(j + 1) * 128], 4, 128)
                    yt_, yk = ys[br]
                    YK = [(yk, m) for m in range(4)]
                    for (tc0, n) in blocks_of(l):
                        i = c[0] % 2
                        c[0] += 1
                        psg, pkg = psum()
                        psb, pkb = psum()

                        def f(psg=psg, wg=wg, tc0=tc0, n=n):
                            for k in range(8):
                                ins = nc.tensor.matmul(psg[:, 0:n], lhsT=wg[:, k, :], rhs=hT[:, k, tc0:tc0 + n],
                                                       start=(k == 0), stop=(k == 7))
                            return ins
                        kb.op('pe', f, R=[wgk] + HT_KEYS, W=[pkg])

                        def f(psb=psb, wb=wb, yt_=yt_, tc0=tc0, n=n):
                            for k in range(4):
                                ins = nc.tensor.matmul(psb[:, 0:n], lhsT=wb[:, k, :], rhs=yt_[:, k, tc0:tc0 + n],
                                                       start=(k == 0), stop=(k == 3))
                            return ins
                        kb.op('pe', f, R=[wbk] + YK, W=[pkb])
                        kb.op('act', lambda psg=psg, i=i, n=n: nc.scalar.activation(out=sg[i][:, 0:n], in_=psg[:, 0:n],
                                                                                    func=AF.Sigmoid),
                              R=[pkg, 'sg%d' % i], W=['sg%d' % i])
                        if br == 0:
                            dv(lambda psb=psb, i=i, tc0=tc0, n=n: nc.vector.tensor_tensor(
                                out=macc[:, tc0:tc0 + n], in0=sg[i][:, 0:n], in1=psb[:, 0:n], op=ALU.mult),
                               [pkb, 'sg%d' % i, 'macc'], ['macc'])
                        else:
                            dv(lambda psb=psb, i=i, n=n: nc.vector.tensor_tensor(
                                out=sg[i][:, 0:n], in0=sg[i][:, 0:n], in1=psb[:, 0:n], op=ALU.mult),
                               [pkb, 'sg%d' % i], ['sg%d' % i])
                            if br == 1:
                                dv(lambda i=i, tc0=tc0, n=n: nc.vector.tensor_tensor(
                                    out=macc[:, tc0:tc0 + n], in0=macc[:, tc0:tc0 + n], in1=sg[i][:, 0:n], op=ALU.add),
                                   ['sg%d' % i, 'macc'], ['macc'])
                            else:
                                dv(lambda i=i, tc0=tc0, n=n, j=j: nc.vector.tensor_tensor(
                                    out=mT[:, j, tc0:tc0 + n], in0=macc[:, tc0:tc0 + n], in1=sg[i][:, 0:n], op=ALU.add),
                                   ['sg%d' % i, 'macc'], [('mT', j)])

        def delta_apply(l, tc0, n, gate_off, src_dram, after, dsrc=None, doff=0):
            for t in range(tc0 // 128, (tc0 + n) // 128):
                i = t % 2
                kb.dma(xt[i][:], src_dram[t * 128:(t + 1) * 128, :], W=['xt%d' % i])
                o0 = t * 128 - tc0 + doff
                dsr = dT if dsrc is None else dsrc
                for half in range(2):
                    ps, pk = psum()

                    def f(ps=ps, half=half, o0=o0):
                        for kk in range(4):
                            ins = nc.tensor.transpose(ps[:, kk * 128:(kk + 1) * 128], dsr[:, half * 4 + kk, o0:o0 + 128],
                                                      ident_f[:])
                        return ins
                    kb.op('pe', f, R=['dT', 'ident_f'], W=[pk])
                    dv(lambda ps=ps, i=i, half=half: nc.vector.tensor_tensor(
                        out=xt[i][:, half * 512:(half + 1) * 512], in0=xt[i][:, half * 512:(half + 1) * 512], in1=ps[:, :],
                        op=ALU.add), [pk, 'xt%d' % i], ['xt%d' % i])
                after(i, t)

        def merge_out(l):
            def after(i, t):
                kb.dma(xw[t * 128:(t + 1) * 128, :], xt[i][:], R=['xt%d' % i])
                norm_tile(i, t, 24, 32)
            MK = [('mT', j) for j in range(8)]
            for (tc0, n) in blocks_of(l):
                v = 1 if tc0 == 0 else 0
                for dt_ in range(8):
                    wo, wok = load_wk(w_out_d[l][:, dt_ * 128:(dt_ + 1) * 128], 8, 128)
                    ps, pk = psum()

                    def f(ps=ps, wo=wo, tc0=tc0, n=n):
                        for k in range(8):
                            ins = nc.tensor.matmul(ps[:, 0:n], lhsT=wo[:, k, :], rhs=mT[:, k, tc0:tc0 + n],
                                                   start=(k == 0), stop=(k == 7))
                        return ins
                    kb.op('pe', f, R=[wok] + MK, W=[pk])
                    dv(lambda ps=ps, dt_=dt_, n=n, v=v: nc.vector.tensor_scalar(
                        out=dT[:, dt_, 0:n], in0=ps[:, 0:n], scalar1=modT[:, 16 + dt_, v:v + 1], scalar2=None, op0=ALU.mult),
                       [pk, 'modT', 'dT'], ['dT'])
                delta_apply(l, tc0, n, 16, xin if l == 0 else xw, after)

        def ffn(l, groups):
            last = (l == DEPTH - 1)

            def after(i, t):
                if not last:
                    kb.dma(xw[t * 128:(t + 1) * 128, :], xt[i][:], R=['xt%d' % i])
                else:
                    rstd_tile(i)
                    dv(lambda i=i: nc.vector.tensor_scalar(out=xt[i][:], in0=xt[i][:], scalar1=ss[:, 2:3], scalar2=None,
                                                           op0=ALU.mult), ['xt%d' % i, 'ss'], ['xt%d' % i])
                    dv(lambda i=i: nc.vector.tensor_tensor(out=xt[i][:], in0=xt[i][:], in1=fnrow[:], op=ALU.mult),
                       ['xt%d' % i, 'fnrow'], ['xt%d' % i])
                    kb.dma(out[(t - 2) * 128:(t - 1) * 128, :], xt[i][:], R=['xt%d' % i])
            c = [0]
            for grp in groups:
                g0 = grp[0][0]
                for j in range(FFH // 128):
                    wg, wgk = load_wk(w_fg[l][:, j * 128:(j + 1) * 128], 8, 128)
                    wu, wuk = load_wk(w_fu[l][:, j * 128:(j + 1) * 128], 8, 128)
                    for (tc0, n) in grp:
                        i = c[0] % 2
                        c[0] += 1
                        psg, pkg = psum()
                        psu, pku = psum()

                        def f(psg=psg, wg=wg, tc0=tc0, n=n):
                            for k in range(8):
                                ins = nc.tensor.matmul(psg[:, 0:n], lhsT=wg[:, k, :], rhs=hT[:, k, tc0:tc0 + n],
                                                       start=(k == 0), stop=(k == 7))
                            return ins
                        kb.op('pe', f, R=[wgk] + HT_KEYS, W=[pkg])

                        def f(psu=psu, wu=wu, tc0=tc0, n=n):
                            for k in range(8):
                                ins = nc.tensor.matmul(psu[:, 0:n], lhsT=wu[:, k, :], rhs=hT[:, k, tc0:tc0 + n],
                                                       start=(k == 0), stop=(k == 7))
                            return ins
                        kb.op('pe', f, R=[wuk] + HT_KEYS, W=[pku])
                        kb.op('act', lambda psg=psg, i=i, n=n: nc.scalar.activation(out=sg[i][:, 0:n], in_=psg[:, 0:n],
                                                                                    func=AF.Silu),
                              R=[pkg, 'sg%d' % i], W=['sg%d' % i])
                        dv(lambda psu=psu, i=i, j=j, tc0=tc0, n=n: nc.vector.tensor_tensor(
                            out=aT[:, j, tc0 - g0:tc0 - g0 + n], in0=sg[i][:, 0:n], in1=psu[:, 0:n], op=ALU.mult),
                           [pku, 'sg%d' % i], [('aT', j)])
                AK = [('aT', j) for j in range(FFH // 128)]
                for dt_ in range(8):
                    wa, wak = load_wk(w_fd[l][0:2048, dt_ * 128:(dt_ + 1) * 128], 16, 128)
                    wb_, wbk = load_wk(w_fd[l][2048:FFH, dt_ * 128:(dt_ + 1) * 128], 6, 128)
                    for (tc0, n) in grp:
                        v = 1 if tc0 == 0 else 0
                        ps, pk = psum()

                        def f(ps=ps, wa=wa, wb_=wb_, tc0=tc0, n=n):
                            for k in range(22):
                                w_ = wa[:, k, :] if k < 16 else wb_[:, k - 16, :]
                                ins = nc.tensor.matmul(ps[:, 0:n], lhsT=w_, rhs=aT[:, k, tc0 - g0:tc0 - g0 + n],
                                                       start=(k == 0), stop=(k == 21))
                            return ins
                        kb.op('pe', f, R=[wak, wbk] + AK, W=[pk])
                        dv(lambda ps=ps, dt_=dt_, n=n, v=v, tc0=tc0: nc.vector.tensor_scalar(
                            out=dTg[:, dt_, tc0 - g0:tc0 - g0 + n], in0=ps[:, 0:n], scalar1=modT[:, 40 + dt_, v:v + 1], scalar2=None,
                            op0=ALU.mult), [pk, 'modT', 'dT'], ['dT'])
                for (tc0, n) in grp:
                    delta_apply(l, tc0, n, 40, xw, after, dsrc=dTg, doff=tc0 - g0)

        dbgbuf = [None]

        def dump_fm(tile_, nk, keys):
            for k in range(nk):
                kb.op('act', lambda k=k: nc.scalar.copy(out=dbgbuf[0][:], in_=tile_[:, k, :]), R=list(keys) + ['dbgb'], W=['dbgb'])
                kb.dma(dbg_out[:, k, :], dbgbuf[0][:], R=['dbgb'])

        stop = False
        for l in range(DEPTH):
          with ExitStack() as LS:
            with ExitStack() as ph:
                cur[0] = ph
                xt = [sb("xt%d" % i, [128, D], F32) for i in range(2)]
                xn = [sb("xn%d" % i, [128, D], BF16) for i in range(2)]
                junk = sb("junk", [128, D], F32)
                wst = [sb("wst%d" % i, [128, 2048], F32) for i in range(2)]
                xnall = sb("xnall", [128, NTT, D], BF16)
                norm_pre(xin if l == 0 else xw)
                adaln(l)
                norm_post(0, 8)
                kb.barrier()
            cur[0] = LS
            ys5T = sb("ys5T", [128, 4, NT], BF16)
            if dbg and dbg[0] == 'hT':
                dbgbuf[0] = sb("dbgb", [128, NT], F32)
                dump_fm(hT, 8, HT_KEYS)
                break
            with ExitStack() as ph:
                cur[0] = ph
                uT = sb("uT", [128, 4, NT], BF16)
                gsc = sb("gsc", [128, NT], F32)
                XG = [sb("XG%d" % i, [128, 2, 2, 8, 128], BF16) for i in range(2)]
                Sx = sb("Sx", [128, 32, 128], BF16)
                Z = sb("Z", [128, 4, 2, 2, NT // 8], F32)
                XsQ = sb("XsQ", [128, 4, 2, 2, NT // 8], BF16)
                T2s = [None]
                CQ = sb("CQ", [128, 12, 8, 3], F32)
                BDq = sb("BDq", [128, 2, 8, 128], BF16)
                uS = sb("uS", [128, 8, NT // 8], BF16)
                Bt = [sb("Bt%d" % i, [128, 4, 32], F32) for i in range(2)]
                s5tmpP = sb("s5tmpP", [128, 3, 8, 32], F32)
                cfk = sb("cfk", [128, 2, 2, 16, 16], F32)
                Epw_re = sb("Epw_re", [128, 2, 9, 16], F32)
                Epw_im = sb("Epw_im", [128, 2, 9, 16], F32)
                Eg_re = sb("Eg_re", [128, 2, 8, 16], F32)
                Eg_im = sb("Eg_im", [128, 2, 8, 16], F32)
                yacc = sb("yacc", [128, NT], F32)
                Rwp = [sb("Rwp%d" % i, [128, 2, 2, 128], BF16) for i in range(2)]
                prm = sb("prm", [128, 2, 3, 16], F32)
                cfl = sb("cfl", [128, 2, 2, 16, 16], F32)
                APre = sb("APre", [128, 2, 12, 16], F32)
                APim = sb("APim", [128, 2, 12, 16], F32)
                APnim = sb("APnim", [128, 2, 12, 16], F32)
                sm = sb("sm", [128, 12, 16], F32)
                cfs = sb("cfs", [128, 2, 16, 16], F32)
                s5dT = sb("s5dT_t", [128, 4], F32)
                bgluT = sb("bgluT_t", [128, 4], F32)
                sg = [gsc[:, 0:512], gsc[:, 512:1024]]
                s5_mixer(l)
                kb.barrier()
            cur[0] = LS
            yretT = sb("yretT", [128, 4, NT], BF16)
            if dbg and dbg[0] == 'ys5':
                dbgbuf[0] = sb("dbgb", [128, NT], F32)
                dump_fm(ys5T, 4, [('ys5T', m) for m in range(4)])
                break
            with ExitStack() as ph:
                cur[0] = ph
                qT = sb("qT", [128, 2, NT], BF16)
                kT = sb("kT", [128, 2, NT], BF16)
                vtok = sb("vtok", [128, NTT, 512], BF16)
                thB = sb("thB_t", [128, 8], F32)
                thQ = sb("thQ_t", [128, 4], F32)
                lgB = sb("lgB", [128, 8], F32)
                lgQ = sb("lgQ", [128, 4], F32)
                retc = sb("retc_t", [128, 2, 128], F32)
                iq = sb("iq_t", [128, 2, 128], F32)
                ek = sb("ek_t", [128, 2], F32)
                Dm = sb("Dm", [128, 8, 128], F32)
                qdtab = sb("qdtab", [128, 2, 2, 128], F32)
                kd = sb("kd", [128, 8], F32)
                cdecQ = sb("cdecQ", [128, 4], F32)
                with ExitStack() as ph2:
                    cur[0] = ph2
                    tmpT = sb("tmpT", [128, 2, NT], BF16)
                    rotc = sb("rotc_t", [128, NT], F32)
                    rots = sb("rots_t", [128, NT], F32)
                    ret_mixer(l)
                    kb.barrier()
                cur[0] = ph
                oT = sb("oT", [128, 4, NT], F32)
                sst = [sb("sst%d" % i, [128, 256], F32) for i in range(4)]
                sbf = [sb("sbf%d" % i, [128, 256], BF16) for i in range(4)]
                kdec = [sb("kdec%d" % i, [128, 128], BF16) for i in range(4)]
                qdt = [sb("qdt%d" % i, [128, 128], BF16) for i in range(4)]
                pT = [sb("pT%d" % i, [128, 2, 128], BF16) for i in range(4)]
                sg = [sb("sg%d" % i, [128, 512], F32) for i in range(5)]
                ret_part2(l)
                kb.barrier()
            cur[0] = LS
            ynaT = sb("ynaT", [128, 4, NT], BF16)
            if dbg and dbg[0] == 'yret':
                dbgbuf[0] = sb("dbgb", [128, NT], F32)
                dump_fm(yretT, 4, [('yretT', m) for m in range(4)])
                break
            with ExitStack() as ph:
                cur[0] = ph
                nkT = sb("nkT", [128, 4, NT], BF16)
                nqT = sb("nqT", [128, 4, NT], BF16)
                nvtok = sb("nvtok", [128, NTT, 512], BF16)
                TT = sb("TT", [128, 8, 15 * 64], BF16)
                nst0 = sb("nst0", [128, 1024], F32)
                nst = [nst0[:, 0:960], nst0[:, 0:960]]
                colmask = sb("colmask_t", [128, 64], F32)
                e1 = [sb("e1%d" % i, [128, 256], F32) for i in range(3)]
                p1 = [sb("p1%d" % i, [128, 384], BF16) for i in range(3)]
                pc = sb("pc", [128, 512], BF16)
                rcc = sb("rcc", [128, 256], F32)
                rc = [sb("rc%d" % i, [128, 64], F32) for i in range(3)]
                numb = sb("numb", [128, 1024], F32)
                denb = nst0
                na_mixer(l, l < DEPTH - 1)
                kb.barrier()
            cur[0] = LS
            if dbg and dbg[0] == 'yna':
                dbgbuf[0] = sb("dbgb", [128, NT], F32)
                dump_fm(ynaT, 4, [('ynaT', m) for m in range(4)])
                break
            mT = sb("mT", [128, 8, NT], BF16)
            with ExitStack() as ph:
                cur[0] = ph
                macc = sb("macc", [128, NT], F32)
                sg = [sb("sg%d" % i, [128, 512], F32) for i in range(2)]
                merge_m(l)
                kb.barrier()
            with ExitStack() as ph:
                cur[0] = ph
                dT = sb("dT", [128, 8, 512], F32)
                xt = [sb("xt%d" % i, [128, D], F32) for i in range(2)]
                xn = [sb("xn%d" % i, [128, D], BF16) for i in range(2)]
                junk = sb("junk", [128, D], F32)
                merge_out(l)
                kb.barrier()
            cur[0] = LS
          cur[0] = es
          with ExitStack() as ph:
              cur[0] = ph
              if l < DEPTH - 1:
                  groups = [TB[0:3], TB[3:5]]
              else:
                  groups = [TB[1:3], TB[3:5]]
              aT = sb("aT", [128, FFH // 128, 1280], BF16)
              dTg = sb("dTg", [128, 8, 1280], F32)
              sg = [sb("sg%d" % i, [128, 512], F32) for i in range(2)]
              xt = [sb("xt%d" % i, [128, D], F32) for i in range(2)]
              xn = [sb("xn%d" % i, [128, D], BF16) for i in range(2)]
              junk = sb("junk", [128, D], F32)
              ffn(l, groups)
              kb.barrier()
          cur[0] = es
        kb.finish()
    return nc


def host_inputs(inputs, b, shared=None):
    f = np.float32
    d = {}
    d['xin'] = np.ascontiguousarray(np.concatenate([inputs['ctx'][b], inputs['x'][b]], axis=0).astype(f))
    cc = np.stack([inputs['c'][b], inputs['c_ctx']], axis=-1)
    d['cT'] = np.ascontiguousarray(cc.reshape(8, 128, 2).transpose(1, 0, 2).astype(f))
    if shared is not None:
        d.update(shared)
        return d
    d['w_ada'] = np.ascontiguousarray(inputs['w_ada'].astype(f))
    d['b_adaT'] = np.ascontiguousarray(inputs['b_ada'].reshape(DEPTH, 48, 128).transpose(0, 2, 1).astype(f))
    d['w_in'] = np.ascontiguousarray(inputs['w_in'].astype(f))
    d['ident_d'] = np.eye(128, dtype=f)
    def layA(a):
        return a.reshape(DEPTH, 2, 16, 2, 64).transpose(0, 1, 3, 4, 2).reshape(DEPTH, 2, 128, 16)
    ldt = np.repeat(inputs['s5_log_dt'][..., None], 64, axis=-1)
    d['s5p'] = np.ascontiguousarray(np.stack([layA(inputs['s5_lam_re']), layA(inputs['s5_lam_im']), layA(ldt)],
                                             axis=3).astype(f))
    def layC(a):
        return a.reshape(DEPTH, 2, 16, 2, 16, 64).transpose(0, 1, 3, 5, 2, 4).reshape(DEPTH, 2, 128, 16, 16)
    d['s5c'] = np.ascontiguousarray(np.stack([layC(inputs['s5_c_re']), layC(inputs['s5_c_im'])], axis=3).astype(f))
    ba = np.zeros((DEPTH, 16, 128, 4, 32), f)
    for r, nm in enumerate(('s5_b_re', 's5_b_im')):
        b = inputs[nm]
        for pi_ in range(16):
            for gl in range(2):
                g = 2 * pi_ + gl
                for dd in range(2):
                    ba[:, pi_, gl * 64:(gl + 1) * 64, dd * 2 + r, gl * 16:(gl + 1) * 16] = b[:, dd, g]
    d['s5ba'] = ba
    d['s5dT'] = np.ascontiguousarray(inputs['s5_d'].reshape(DEPTH, 4, 128).transpose(0, 2, 1).astype(f))
    d['bgluT'] = np.ascontiguousarray(inputs['s5_b_glu'].reshape(DEPTH, 4, 128).transpose(0, 2, 1).astype(f))
    d['w_glu'] = np.ascontiguousarray(inputs['s5_w_glu'].astype(f))
    perm = np.arange(64)
    perm = np.where((perm % 32) < 16, perm + 16, perm - 16)
    kcols = 512 + (np.arange(256) // 64) * 64 + perm[np.arange(256) % 64]
    qcols = 2304 + (np.arange(256) // 64) * 64 + perm[np.arange(256) % 64]
    d['w_rot'] = np.ascontiguousarray(inputs['w_in'][:, :, np.concatenate([kcols, qcols])].astype(f))
    th = inputs['ret_theta'].astype(f)
    d['thB'] = np.ascontiguousarray(np.broadcast_to(th.reshape(DEPTH, 1, 8), (DEPTH, 128, 8)))
    thq = th.reshape(DEPTH, 2, 2, 2)
    d['thQ'] = np.ascontiguousarray(np.repeat(thq.transpose(0, 3, 1, 2), 64, axis=1).reshape(DEPTH, 128, 4))
    kc = np.arange(64)[:, None]
    qc = np.arange(64)[None, :]
    jidx = np.clip(kc - qc + 15, 0, 30)
    g_ = inputs['na_rpb'][:, :, :, jidx]
    g_ = g_.transpose(0, 1, 3, 2, 4)
    d['nab'] = np.ascontiguousarray(np.concatenate([g_, g_], axis=2).reshape(DEPTH, 8, 128, 15 * 64).astype(f))
    for i, nm in enumerate(('w_branch_s5', 'w_branch_ret', 'w_branch_na')):
        d['w_b%d' % i] = np.ascontiguousarray(inputs[nm].astype(f))
    d['w_out'] = np.ascontiguousarray(inputs['w_out'].astype(f))
    d['w_fg'] = np.ascontiguousarray(inputs['w_ffn_gate'].astype(f))
    d['w_fu'] = np.ascontiguousarray(inputs['w_ffn_up'].astype(f))
    d['w_fd'] = np.ascontiguousarray(inputs['w_ffn_down'].astype(f))
    d['fnrow'] = np.ascontiguousarray(np.broadcast_to(inputs['final_norm'].astype(f)[None, :], (128, D)))
    d.update(_consts())
    return d


_CONSTS = {}


def _consts():
    if _CONSTS:
        return _CONSTS
    f = np.float32
    pos = np.arange(LAT)
    inv_freq = (10000.0 ** (-np.arange(16, dtype=np.float32) / 16)).astype(np.float32)
    dd = np.arange(128) % 64
    coord = np.where((dd // 32)[:, None] == 0, (pos // 64)[None, :], (pos % 64)[None, :]).astype(np.float32)
    ang = coord * inv_freq[dd % 16][:, None]
    sign = np.where((dd % 32) < 16, -1.0, 1.0)[:, None]
    rotc = np.ones((128, NT), f)
    rots = np.zeros((128, NT), f)
    rotc[:, NCTX:] = np.cos(ang)
    rots[:, NCTX:] = np.sin(ang) * sign
    jj = np.arange(128)[:, None].astype(f)
    ii = np.arange(128)[None, :].astype(f)
    BIG = 1.0e6
    retc = np.stack([np.where(ii >= jj, ii - jj, BIG), np.where(jj > ii, jj - ii, BIG)], axis=1)
    iq = np.stack([np.broadcast_to(ii + 1.0, (128, 128)), np.broadcast_to(128.0 - ii, (128, 128))], axis=1)
    ek = np.stack([127.0 - np.arange(128), np.arange(128)], axis=1)
    kc = np.arange(64)[:, None]
    qc = np.arange(64)[None, :]
    cs = np.clip(qc - 8, 0, 48)
    cm = ((kc >= cs) & (kc < cs + 16)).astype(f)
    _CONSTS['colmask'] = np.ascontiguousarray(np.concatenate([cm, cm], axis=0))
    _CONSTS.update(rotc=rotc, rots=rots.astype(f), retc=np.ascontiguousarray(retc.astype(f)),
                   iq=np.ascontiguousarray(iq.astype(f)), ek=np.ascontiguousarray(ek.astype(f)))
    return _CONSTS


def kernel(**inputs):
    inputs = {k: np.asarray(v) for k, v in inputs.items()}
    nc = build()
    first = host_inputs(inputs, 0)
    shared = {k: v for k, v in first.items() if k not in ('xin', 'cT')}
    in_maps = [first] + [host_inputs(inputs, b, shared) for b in range(1, 8)]
    res = run_bass_kernel_spmd(nc, in_maps, core_ids=list(range(8)))
    return np.stack([r["out"] for r in res.results], axis=0).astype(np.float32)
```

```python
import numpy as np
from contextlib import ExitStack
import concourse.bass as bass
import concourse.mybir as mybir
from concourse.bass_utils import run_bass_kernel_spmd

F32 = mybir.dt.float32
BF16 = mybir.dt.bfloat16
AF = mybir.ActivationFunctionType
ALU = mybir.AluOpType

D = 1024
LAT = 2048
NCTX = 256
NT = LAT + NCTX
NTT = NT // 128
DEPTH = 2
NIN = 6656
FFH = 2816
TB = [(0, 256), (256, 512), (768, 512), (1280, 512), (1792, 512)]
RMS_EPS = 1e-6
GN_EPS = 1e-5


class KB:
    def __init__(self, nc, es):
        self.nc = nc
        self.es = es
        self.eng = {'pe': nc.tensor, 'act': nc.scalar, 'dve': nc.vector, 'pool': nc.gpsimd, 'sp': nc.sync}
        self.sem = {e: es.enter_context(nc.semaphore('sem_' + e)) for e in self.eng}
        self.cnt = {e: 0 for e in self.eng}
        self.seen = {e: {} for e in self.eng}
        self.ND = 24
        self.dsem = [es.enter_context(nc.semaphore('dsem%d' % i)) for i in range(self.ND)]
        self.dcnt = [0] * self.ND
        self.dslots = {'sp': list(range(0, 14)), 'pool': list(range(14, 24))}
        self.dnext = {'sp': 0, 'pool': 0}
        self.w = {}
        self.r = {}
        self.nps = 0
        self.ninst = 0

    def _semh(self, s):
        return self.sem[s] if isinstance(s, str) else self.dsem[s[1]]

    def _wait(self, e, s, v):
        if s == e and e == 'pe':
            return
        if self.seen[e].get(s, 0) >= v:
            return
        self.eng[e].wait_ge(self._semh(s), v)
        self.seen[e][s] = v

    def _deps(self, e, R, W):
        for k in list(R) + list(W):
            for s, v in self.w.get(k, {}).items():
                self._wait(e, s, v)
        for k in W:
            for s, v in self.r.get(k, {}).items():
                self._wait(e, s, v)

    def _mark(self, tok, R, W):
        for k in W:
            self.w[k] = {tok[0]: tok[1]}
            self.r[k] = {}
        for k in R:
            d = self.r.setdefault(k, {})
            if d.get(tok[0], 0) < tok[1]:
                d[tok[0]] = tok[1]

    def op(self, e, fn, R=(), W=()):
        self._deps(e, R, W)
        inst = fn()
        self.cnt[e] += 1
        inst.then_inc(self.sem[e], 1)
        self._mark((e, self.cnt[e]), R, W)
        self.ninst += 1

    def dma(self, out, in_, R=(), W=(), q='sp', **kw):
        sl = self.dslots[q]
        i = sl[self.dnext[q] % len(sl)]
        self.dnext[q] += 1
        if self.dcnt[i]:
            self._wait(q, ('d', i), 16 * self.dcnt[i])
        self._deps(q, R, W)
        self.eng[q].dma_start(out=out, in_=in_, **kw).then_inc(self.dsem[i], 16)
        self.dcnt[i] += 1
        self._mark((('d', i), 16 * self.dcnt[i]), R, W)
        self.ninst += 1

    def merge(self, newkey, keys):
        d = {}
        for k in keys:
            for s, v in self.w.get(k, {}).items():
                if d.get(s, 0) < v:
                    d[s] = v
        self.w[newkey] = d
        self.r[newkey] = {}

    def barrier(self):
        for e in self.eng:
            for e2 in self.eng:
                if e2 != e and self.cnt[e2]:
                    self._wait(e, e2, self.cnt[e2])
            for i in range(self.ND):
                if self.dcnt[i]:
                    self._wait(e, ('d', i), 16 * self.dcnt[i])
        self.w = {}
        self.r = {}

    def finish(self):
        for i in range(self.ND):
            if self.dcnt[i]:
                self._wait('sp', ('d', i), 16 * self.dcnt[i])


def pbc(ap, n):
    return ap.to_broadcast([ap.shape[0], n])


def build(dbg=None):
    nc = bass.Bass("TRN2", target_bir_lowering=False)
    dr = {}

    def din(name, shape):
        dr[name] = nc.dram_tensor(name, list(shape), F32, kind="ExternalInput").ap()
        return dr[name]

    xin = din("xin", [NT, D])
    cT = din("cT", [128, 8, 2])
    w_ada = din("w_ada", [DEPTH, D, 6 * D])
    b_adaT = din("b_adaT", [DEPTH, 128, 48])
    w_in = din("w_in", [DEPTH, D, NIN])
    out = nc.dram_tensor("out", [LAT, D], F32, kind="ExternalOutput").ap()
    xw = nc.dram_tensor("xw", [NT, D], F32, kind="Internal").ap()
    if dbg:
        dbg_out = nc.dram_tensor("dbg", list(dbg[1]), F32, kind="ExternalOutput").ap()

    with ExitStack() as es:
        kb = KB(nc, es)

        cur = [es]

        nuniq = [0]

        def sb(name, shape, dt):
            nuniq[0] += 1
            return cur[0].enter_context(nc.sbuf_tensor("%s_%d" % (name, nuniq[0]), list(shape), dt))

        PS = [es.enter_context(nc.psum_tensor("ps%d" % i, [128, 512], F32)) for i in range(8)]

        def psum():
            i = kb.nps % 8
            kb.nps += 1
            return PS[i], 'ps%d' % i

        ident_f = sb("ident_f", [128, 128], F32)
        ident = sb("ident", [128, 128], BF16)
        hT = sb("hT", [128, 8, NT], BF16)
        modT = sb("modT", [128, 48, 2], F32)
        siluT = sb("siluT", [128, 8, 2], F32)
        NWB = 4
        wbf = [sb("wbf%d" % i, [128, 2048], BF16) for i in range(NWB)]
        ss = sb("ss", [128, 4], F32)
        badaT = sb("badaT", [128, 48], F32)
        ident_d = din("ident_d", [128, 128])

        kb.dma(ident_f[:], ident_d[:, :], W=['ident_f'])
        kb.op('dve', lambda: nc.vector.tensor_copy(out=ident[:], in_=ident_f[:]), R=['ident_f'], W=['ident'])
        kb.dma(siluT[:], cT[:, :, :], W=['siluT'])
        kb.op('act', lambda: nc.scalar.activation(out=siluT[:], in_=siluT[:], func=AF.Silu), R=['siluT'], W=['siluT'])

        wctr = [0, 0]

        def load_wk(src_ap, kt, ncols, cast=True):
            if not cast:
                i = wctr[1] % 2
                wctr[1] += 1
                sv = wst[i][:, 0:kt * ncols].rearrange("p (k n) -> p k n", n=ncols)
                kb.dma(sv, src_ap.rearrange("(k p) n -> p k n", p=128), W=['wst%d' % i])
                return sv, 'wst%d' % i
            i = wctr[0] % NWB
            wctr[0] += 1
            bv = wbf[i][:, 0:kt * ncols].rearrange("p (k n) -> p k n", n=ncols)
            kb.dma(bv, src_ap.rearrange("(k p) n -> p k n", p=128), W=['wbf%d' % i], q='pool')
            return bv, 'wbf%d' % i

        def load_w(src_ap, ncols, cast=True):
            return load_wk(src_ap, 8, ncols, cast)

        def adaln(l):
            kb.dma(badaT[:], b_adaT[l], W=['badaT'])
            ps, pk = psum()
            psv = ps[:, 0:96].rearrange("p (j v) -> p j v", v=2)
            for cb in range(24):
                wt, wk = load_w(w_ada[l][:, cb * 256:(cb + 1) * 256], 256, cast=False)
                for jj in range(2):
                    j = cb * 2 + jj

                    def f(j=j, jj=jj, wt=wt):
                        for k in range(8):
                            ins = nc.tensor.matmul(psv[:, j, :], lhsT=wt[:, k, jj * 128:(jj + 1) * 128],
                                                   rhs=siluT[:, k, :], start=(k == 0), stop=(k == 7))
                        return ins
                    kb.op('pe', f, R=[wk, 'siluT'], W=[pk])
            for v in range(2):
                kb.op('dve', lambda v=v: nc.vector.tensor_tensor(out=modT[:, :, v], in0=psv[:, :, v], in1=badaT[:],
                                                                 op=ALU.add), R=[pk, 'badaT'], W=['modT'])
            for a in (8, 32):
                kb.op('dve', lambda a=a: nc.vector.tensor_scalar(out=modT[:, a:a + 8, :], in0=modT[:, a:a + 8, :],
                                                                 scalar1=1.0, scalar2=None, op0=ALU.add),
                      R=['modT'], W=['modT'])

        def rstd_tile(i):
            kb.op('act', lambda i=i: nc.scalar.activation(out=junk[:], in_=xt[i][:], func=AF.Square),
                  R=['xt%d' % i], W=['junk'])
            kb.op('dve', lambda: nc.vector.tensor_reduce(out=ss[:, 0:1], in_=junk[:], axis=mybir.AxisListType.X, op=ALU.add),
                  R=['junk', 'ss'], W=['ss'])
            kb.op('act', lambda: nc.scalar.activation(out=ss[:, 1:2], in_=ss[:, 0:1], func=AF.Sqrt,
                                                      scale=1.0 / D, bias=eps_t[:, 0:1]), R=['ss', 'eps_t'], W=['ss'])
            kb.op('dve', lambda: nc.vector.reciprocal(out=ss[:, 2:3], in_=ss[:, 1:2]), R=['ss'], W=['ss'])

        def norm_tile(i, t, sh_off, sc_off):
            v = 1 if t < 2 else 0
            rstd_tile(i)
            kb.op('dve', lambda i=i: nc.vector.tensor_scalar(out=xn[i][:], in0=xt[i][:], scalar1=ss[:, 2:3],
                                                             scalar2=None, op0=ALU.mult),
                  R=['xt%d' % i, 'ss'], W=['xn%d' % i])
            ps, pk = psum()
            pb = ps[:].bitcast(BF16)

            def f(i=i, pb=pb):
                for k in range(8):
                    ins = nc.tensor.transpose(pb[:, k * 128:(k + 1) * 128], xn[i][:, k * 128:(k + 1) * 128], ident[:])
                return ins
            kb.op('pe', f, R=['xn%d' % i, 'ident'], W=[pk])
            for k in range(8):
                kb.op('dve', lambda k=k, pb=pb, t=t, v=v: nc.vector.tensor_scalar(
                    out=hT[:, k, t * 128:(t + 1) * 128], in0=pb[:, k * 128:(k + 1) * 128],
                    scalar1=modT[:, sc_off + k, v:v + 1], scalar2=modT[:, sh_off + k, v:v + 1],
                    op0=ALU.mult, op1=ALU.add), R=[pk, 'modT'], W=[('hT', t)])

        def norm_to_hT(src_dram, sh_off, sc_off):
            for t in range(NTT):
                i = t % 2
                kb.dma(xt[i][:], src_dram[t * 128:(t + 1) * 128, :], W=['xt%d' % i])
                norm_tile(i, t, sh_off, sc_off)

        def norm_pre(src_dram):
            for t in range(NTT):
                i = t % 2
                kb.dma(xt[i][:], src_dram[t * 128:(t + 1) * 128, :], W=['xt%d' % i])
                rstd_tile(i)
                kb.op('dve', lambda i=i, t=t: nc.vector.tensor_scalar(out=xnall[:, t, :], in0=xt[i][:], scalar1=ss[:, 2:3],
                                                                      scalar2=None, op0=ALU.mult),
                      R=['xt%d' % i, 'ss'], W=[('xnall', t)])

        def norm_post(sh_off, sc_off):
            for t in range(NTT):
                v = 1 if t < 2 else 0
                ps, pk = psum()
                pb = ps[:].bitcast(BF16)

                def f(pb=pb, t=t):
                    for k in range(8):
                        ins = nc.tensor.transpose(pb[:, k * 128:(k + 1) * 128], xnall[:, t, k * 128:(k + 1) * 128], ident[:])
                    return ins
                kb.op('pe', f, R=[('xnall', t), 'ident'], W=[pk])
                for k in range(8):
                    kb.op('dve', lambda k=k, pb=pb, t=t, v=v: nc.vector.tensor_scalar(
                        out=hT[:, k, t * 128:(t + 1) * 128], in0=pb[:, k * 128:(k + 1) * 128],
                        scalar1=modT[:, sc_off + k, v:v + 1], scalar2=modT[:, sh_off + k, v:v + 1],
                        op0=ALU.mult, op1=ALU.add), R=[pk, 'modT'], W=[('hT', t)])

        eps_t = sb("eps_t", [128, 2], F32)
        kb.op('dve', lambda: nc.vector.memset(eps_t[:, 0:1], RMS_EPS), W=['eps_t'])
        kb.op('dve', lambda: nc.vector.memset(eps_t[:, 1:2], GN_EPS), R=['eps_t'], W=['eps_t'])


        HT_KEYS = [('hT', t) for t in range(NTT)]

        def proj_fm(wsrc, col0, ncols, src, srckeys, kt, evac, blocks=TB):
            for cb in range(0, ncols, 256):
                nc_ = min(256, ncols - cb)
                wt, wk = load_wk(wsrc[:, col0 + cb:col0 + cb + nc_], kt, nc_)
                for mm in range(nc_ // 128):
                    m = (cb // 128) + mm
                    for (tc0, n) in blocks:
                        ps, pk = psum()

                        def f(ps=ps, wt=wt, mm=mm, tc0=tc0, n=n):
                            for k in range(kt):
                                ins = nc.tensor.matmul(ps[:, 0:n], lhsT=wt[:, k, mm * 128:(mm + 1) * 128],
                                                       rhs=src[:, k, tc0:tc0 + n], start=(k == 0), stop=(k == kt - 1))
                            return ins
                        kb.op('pe', f, R=[wk] + list(srckeys), W=[pk])
                        evac(m, tc0, n, ps, pk)

        s5p = din("s5p", [DEPTH, 2, 128, 3, 16])
        s5c = din("s5c", [DEPTH, 2, 128, 2, 16, 16])
        s5ba = din("s5ba", [DEPTH, 16, 128, 4, 32])
        s5dT_d = din("s5dT", [DEPTH, 128, 4])
        bgluT_d = din("bgluT", [DEPTH, 128, 4])
        w_glu = din("w_glu", [DEPTH, 512, 512])

        PI = float(np.pi)

        def dv(fn, R, W):
            kb.op('dve', fn, R=R, W=W)

        def pv(fn, R, W):
            kb.op('pool', fn, R=R, W=W)

        def s5_prep(l):
            kb.dma(prm[:], s5p[l].rearrange("d p a b -> p d a b"), W=['prm'])
            kb.dma(cfl[:], s5c[l].rearrange("d p r a b -> p d r a b"), W=['cfl'])
            kb.dma(s5dT[:], s5dT_d[l], W=['s5dT'])
            kb.dma(bgluT[:], bgluT_d[l], W=['bgluT'])
            S = ['sm']
            for d in range(2):
                lre, lim, ldt = prm[:, d, 0, :], prm[:, d, 1, :], prm[:, d, 2, :]
                dtv, mag, th, tmp, sn, cs, th2 = (sm[:, i, :] for i in range(7))
                kb.op('act', lambda: nc.scalar.activation(out=dtv, in_=ldt, func=AF.Exp), R=['prm'], W=S)
                dv(lambda: nc.vector.tensor_tensor(out=mag, in0=lre, in1=dtv, op=ALU.mult), ['prm'] + S, S)
                kb.op('act', lambda: nc.scalar.activation(out=mag, in_=mag, func=AF.Exp), R=S, W=S)
                dv(lambda: nc.vector.tensor_tensor(out=th, in0=lim, in1=dtv, op=ALU.mult), ['prm'] + S, S)
                for _ in range(6):
                    dv(lambda: nc.vector.tensor_scalar(out=tmp, in0=th, scalar1=PI, scalar2=2 * PI, op0=ALU.is_gt,
                                                       op1=ALU.mult), S, S)
                    dv(lambda: nc.vector.tensor_tensor(out=th, in0=th, in1=tmp, op=ALU.subtract), S, S)
                for _ in range(2):
                    dv(lambda: nc.vector.tensor_scalar(out=tmp, in0=th, scalar1=-PI, scalar2=2 * PI, op0=ALU.is_lt,
                                                       op1=ALU.mult), S, S)
                    dv(lambda: nc.vector.tensor_tensor(out=th, in0=th, in1=tmp, op=ALU.add), S, S)
                dv(lambda: nc.vector.tensor_scalar(out=th2, in0=th, scalar1=PI / 2, scalar2=None, op0=ALU.add), S, S)
                dv(lambda: nc.vector.tensor_scalar(out=tmp, in0=th2, scalar1=PI, scalar2=2 * PI, op0=ALU.is_gt,
                                                   op1=ALU.mult), S, S)
                dv(lambda: nc.vector.tensor_tensor(out=th2, in0=th2, in1=tmp, op=ALU.subtract), S, S)
                kb.op('act', lambda: nc.scalar.activation(out=sn, in_=th, func=AF.Sin), R=S, W=S)
                kb.op('act', lambda: nc.scalar.activation(out=cs, in_=th2, func=AF.Sin), R=S, W=S)
                A = ['APW']
                dv(lambda: nc.vector.tensor_tensor(out=APre[:, d, 0, :], in0=mag, in1=cs, op=ALU.mult), S, A)
                dv(lambda: nc.vector.tensor_tensor(out=APim[:, d, 0, :], in0=mag, in1=sn, op=ALU.mult), S, A)
                t1, t2 = sm[:, 7, :], sm[:, 8, :]
                for k in range(1, 12):
                    re0, im0 = APre[:, d, k - 1, :], APim[:, d, k - 1, :]
                    dv(lambda re0=re0: nc.vector.tensor_tensor(out=t1, in0=re0, in1=re0, op=ALU.mult), A + S, S)
                    dv(lambda im0=im0: nc.vector.tensor_tensor(out=t2, in0=im0, in1=im0, op=ALU.mult), A + S, S)
                    dv(lambda k=k: nc.vector.tensor_tensor(out=APre[:, d, k, :], in0=t1, in1=t2, op=ALU.subtract), S, A)
                    dv(lambda re0=re0, im0=im0: nc.vector.tensor_tensor(out=t1, in0=re0, in1=im0, op=ALU.mult), A + S, S)
                    dv(lambda k=k: nc.vector.tensor_scalar(out=APim[:, d, k, :], in0=t1, scalar1=2.0, scalar2=None,
                                                           op0=ALU.mult), S, A)
                am1, den, fr, fi = sm[:, 7, :], sm[:, 8, :], sm[:, 9, :], sm[:, 10, :]
                t3 = sm[:, 11, :]
                dv(lambda: nc.vector.tensor_scalar(out=am1, in0=APre[:, d, 0, :], scalar1=-1.0, scalar2=None,
                                                   op0=ALU.add), A + S, S)
                dv(lambda: nc.vector.tensor_tensor(out=den, in0=lre, in1=lre, op=ALU.mult), ['prm'] + S, S)
                dv(lambda: nc.vector.tensor_tensor(out=t3, in0=lim, in1=lim, op=ALU.mult), ['prm'] + S, S)
                dv(lambda: nc.vector.tensor_tensor(out=den, in0=den, in1=t3, op=ALU.add), S, S)
                dv(lambda: nc.vector.reciprocal(out=den, in_=den), S, S)
                dv(lambda: nc.vector.tensor_tensor(out=fr, in0=am1, in1=lre, op=ALU.mult), ['prm'] + S, S)
                dv(lambda: nc.vector.tensor_tensor(out=t3, in0=APim[:, d, 0, :], in1=lim, op=ALU.mult), ['prm'] + A + S, S)
                dv(lambda: nc.vector.tensor_tensor(out=fr, in0=fr, in1=t3, op=ALU.add), S, S)
                dv(lambda: nc.vector.tensor_tensor(out=fr, in0=fr, in1=den, op=ALU.mult), S, S)
                dv(lambda: nc.vector.tensor_tensor(out=fi, in0=APim[:, d, 0, :], in1=lre, op=ALU.mult), ['prm'] + A + S, S)
                dv(lambda: nc.vector.tensor_tensor(out=t3, in0=am1, in1=lim, op=ALU.mult), ['prm'] + S, S)
                dv(lambda: nc.vector.tensor_tensor(out=fi, in0=fi, in1=t3, op=ALU.subtract), S, S)
                dv(lambda: nc.vector.tensor_tensor(out=fi, in0=fi, in1=den, op=ALU.mult), S, S)

                def bc16(v):
                    return bass.AP(v.tensor, v.offset, [list(v.ap[0]), list(v.ap[1]), [0, 16]])
                cre, cim = cfl[:, d, 0], cfl[:, d, 1]
                cfr, cfi, ta, tb_ = cfk[:, d, 0], cfk[:, d, 1], cfs[:, 0], cfs[:, 1]
                C = ['cfs']
                dv(lambda: nc.vector.tensor_tensor(out=cfr, in0=cre, in1=bc16(fr), op=ALU.mult), ['cfl'] + S + C, C)
                dv(lambda: nc.vector.tensor_tensor(out=ta, in0=cim, in1=bc16(fi), op=ALU.mult), ['cfl'] + S + C, C)
                dv(lambda: nc.vector.tensor_tensor(out=cfr, in0=cfr, in1=ta, op=ALU.subtract), C, C)
                dv(lambda: nc.vector.tensor_tensor(out=cfi, in0=cre, in1=bc16(fi), op=ALU.mult), ['cfl'] + S + C, C)
                dv(lambda: nc.vector.tensor_tensor(out=ta, in0=cim, in1=bc16(fr), op=ALU.mult), ['cfl'] + S + C, C)
                dv(lambda: nc.vector.tensor_tensor(out=cfi, in0=cfi, in1=ta, op=ALU.add), C, C)
            dv(lambda: nc.vector.tensor_scalar(out=APnim[:], in0=APim[:], scalar1=-1.0, scalar2=None, op0=ALU.mult),
               ['APW'], ['APW'])
            P_ = ['Epw']
            for d in range(2):
                dv(lambda d=d: nc.vector.memset(Epw_re[:, d, 0, :], 1.0), P_, P_)
                dv(lambda d=d: nc.vector.memset(Epw_im[:, d, 0, :], 0.0), P_, P_)
                dv(lambda d=d: nc.vector.tensor_copy(out=Epw_re[:, d, 1, :], in_=APre[:, d, 0, :]), ['APW'] + P_, P_)
                dv(lambda d=d: nc.vector.tensor_copy(out=Epw_im[:, d, 1, :], in_=APim[:, d, 0, :]), ['APW'] + P_, P_)
                t1, t2 = sm[:, 7, :], sm[:, 8, :]
                S = ['sm']
                for j in range(2, 9):
                    ar, ai = Epw_re[:, d, j - 1, :], Epw_im[:, d, j - 1, :]
                    br, bi_ = APre[:, d, 0, :], APim[:, d, 0, :]
                    dv(lambda ar=ar, br=br: nc.vector.tensor_tensor(out=t1, in0=ar, in1=br, op=ALU.mult), P_ + S + ['APW'], S)
                    dv(lambda ai=ai, bi_=bi_: nc.vector.tensor_tensor(out=t2, in0=ai, in1=bi_, op=ALU.mult), P_ + S + ['APW'], S)
                    dv(lambda d=d, j=j: nc.vector.tensor_tensor(out=Epw_re[:, d, j, :], in0=t1, in1=t2, op=ALU.subtract), S + P_, P_)
                    dv(lambda ar=ar, bi_=bi_: nc.vector.tensor_tensor(out=t1, in0=ar, in1=bi_, op=ALU.mult), P_ + S + ['APW'], S)
                    dv(lambda ai=ai, br=br: nc.vector.tensor_tensor(out=t2, in0=ai, in1=br, op=ALU.mult), P_ + S + ['APW'], S)
                    dv(lambda d=d, j=j: nc.vector.tensor_tensor(out=Epw_im[:, d, j, :], in0=t1, in1=t2, op=ALU.add), S + P_, P_)
                for t in range(8):
                    j = t + 1 if d == 0 else 8 - t
                    dv(lambda d=d, t=t, j=j: nc.vector.tensor_copy(out=Eg_re[:, d, t, :], in_=Epw_re[:, d, j, :]), P_, P_)
                    dv(lambda d=d, t=t, j=j: nc.vector.tensor_copy(out=Eg_im[:, d, t, :], in_=Epw_im[:, d, j, :]), P_, P_)

        def bk_scan_ops(d, pi_, XR, XI, key, n, koff=0):
            rev = (d == 1)
            X = [key]
            ops = []

            def lvl(k, dst0, m):
                s = 1 << k
                step = 2 * s
                if m <= 0:
                    return
                if not rev:
                    d0 = dst0
                    s0 = dst0 - s
                else:
                    d0 = n - 1 - (dst0 + step * (m - 1))
                    s0 = d0 + s
                dsl = slice(d0, d0 + step * (m - 1) + 1, step)
                ssl = slice(s0, s0 + step * (m - 1) + 1, step)
                ar = APre[:, d, k + koff, pi_:pi_ + 1]
                ai = APim[:, d, k + koff, pi_:pi_ + 1]
                nai = APnim[:, d, k + koff, pi_:pi_ + 1]
                for (dst, src, sc) in ((XR, XR, ar), (XR, XI, nai), (XI, XI, ar), (XI, XR, ai)):
                    ops.append(lambda dst=dst, src=src, sc=sc: dv(lambda: nc.vector.scalar_tensor_tensor(
                        out=dst[:, dsl], in0=src[:, ssl], scalar=sc, in1=dst[:, dsl], op0=ALU.mult, op1=ALU.add),
                        X + ['APW'], X))
            kmax = 0
            while (2 << kmax) - 1 < n:
                kmax += 1
            for k in range(kmax):
                s = 1 << k
                m = (n - (2 * s - 1) - 1) // (2 * s) + 1
                lvl(k, 2 * s - 1, m)
            for k in range(kmax - 1, -1, -1):
                s = 1 << k
                if n - 1 < 3 * s - 1:
                    continue
                m = (n - 1 - (3 * s - 1)) // (2 * s) + 1
                lvl(k, 3 * s - 1, m)
            return ops

        def s5_mixer(l):
            s5_prep(l)
            def ev_u(m, tc0, n, ps, pk):
                kb.op('act', lambda: nc.scalar.copy(out=uT[:, m, tc0:tc0 + n], in_=ps[:, 0:n]), R=[pk], W=[('uT', m)])
            proj_fm(w_in[l], 0, 512, hT, HT_KEYS, 8, ev_u)
            CH = NT // 8
            pstep = Z[:].ap[0][0]
            zoff = Z[:].offset
            cqoff = CQ[:].offset

            def z3(start, step, m):
                return bass.AP(Z[:].tensor, zoff + start, [[pstep, 128], [2 * CH, 8], [CH, 2], [step, m]])

            def z2(comp, start, step, m):
                return bass.AP(Z[:].tensor, zoff + comp * CH + start, [[pstep, 128], [2 * CH, 8], [step, m]])

            def cq3(k, c, m):
                return bass.AP(CQ[:].tensor, cqoff + k * 24 + c, [[CQ[:].ap[0][0], 128], [3, 8], [0, 2], [0, m]])

            def cq2(k, c, m):
                return bass.AP(CQ[:].tensor, cqoff + k * 24 + c, [[CQ[:].ap[0][0], 128], [3, 8], [0, m]])

            TQ = [None]

            def scan_level(k, dst0, m):
                s_ = 1 << k
                step = 2 * s_
                src0 = dst0 - s_
                T1v = gsc[:, 0:16 * 144].rearrange("p (a c m) -> p a c m", a=8, c=2)[:, :, :, 0:m]
                T2v = T2s[0][:, :, 0:m]
                ZK = ['Z']
                dv(lambda: nc.vector.tensor_tensor(out=T1v, in0=z3(src0, step, m), in1=cq3(k + 3, 0, m), op=ALU.mult),
                   ZK + ['CQ', 'gsc'], ['gsc'])
                dv(lambda: nc.vector.tensor_tensor(out=T2v, in0=z2(1, src0, step, m), in1=cq2(k + 3, 2, m), op=ALU.mult),
                   ZK + ['CQ', TQ[0]], [TQ[0]])
                dv(lambda: nc.vector.tensor_tensor(out=T1v[:, :, 0, :], in0=T1v[:, :, 0, :], in1=T2v, op=ALU.add),
                   ['gsc', TQ[0]], ['gsc'])
                dv(lambda: nc.vector.tensor_tensor(out=T2v, in0=z2(0, src0, step, m), in1=cq2(k + 3, 1, m), op=ALU.mult),
                   ZK + ['CQ', TQ[0]], [TQ[0]])
                dv(lambda: nc.vector.tensor_tensor(out=T1v[:, :, 1, :], in0=T1v[:, :, 1, :], in1=T2v, op=ALU.add),
                   ['gsc', TQ[0]], ['gsc'])
                dv(lambda: nc.vector.tensor_tensor(out=z3(dst0, step, m), in0=z3(dst0, step, m), in1=T1v, op=ALU.add),
                   ZK + ['gsc'], ZK)

            for q in range(4):
                T2s[0] = uT[:, q, :].bitcast(F32).rearrange("p (a m) -> p a m", a=8)
                TQ[0] = ('uT', q)
                kb.op('pool', lambda q=q: nc.gpsimd.tensor_copy(out=uS[:], in_=uT[:, q, :].rearrange("p (c s) -> p s c", s=8)),
                      R=[('uT', q)], W=['uS'])
                kb.op('act', lambda q=q: nc.scalar.activation(out=yacc[:].rearrange("p (s c) -> p s c", s=8), in_=uS[:],
                                                              func=AF.Copy, scale=s5dT[:, q:q + 1]),
                      R=['uS', 's5dT'], W=['yacc'])
                for d in range(2):
                    for c, tab in enumerate((APre, APim, APnim)):
                        ov = bass.AP(CQ[:].tensor, cqoff + d * 3 + c, [[CQ[:].ap[0][0], 128], [24, 12], [6, 4]])
                        pv(lambda ov=ov, tab=tab, d=d, q=q: nc.gpsimd.tensor_copy(out=ov, in_=tab[:, d, :, q * 4:(q + 1) * 4]),
                           ['APW', 'CQ'], ['CQ'])
                UQ = ['uS']
                TK = ['s5tmp']
                for pl in range(4):
                    pi_ = q * 4 + pl
                    j4 = pi_ % 4
                    bi = pi_ % 2
                    XGb, XGk = XG[bi], 'XG%d' % bi
                    kb.dma(Bt[bi][:], s5ba[l, pi_], W=['Bt%d' % bi])
                    kb.op('pool', lambda XGb=XGb: nc.gpsimd.memset(XGb[:], 0.0), W=[XGk])
                    for d in range(2):
                        def e3(tab, d=d, pi_=pi_):
                            v = tab[:, d, 0:8, pi_]
                            return bass.AP(v.tensor, v.offset, [list(v.ap[0]), list(v.ap[1]), [0, 32]])

                        def b3(r, d=d, bi=bi):
                            v = Bt[bi][:, d * 2 + r, :]
                            return bass.AP(v.tensor, v.offset, [list(v.ap[0]), [0, 8], list(v.ap[1])])
                        T0, T1 = s5tmpP[:, 0], s5tmpP[:, 1]
                        RK = ['Epw', 'Bt%d' % bi] + ['s5tmpP']
                        dv(lambda e3=e3, b3=b3: nc.vector.tensor_tensor(out=T0, in0=e3(Epw_re), in1=b3(0), op=ALU.mult), RK, ['s5tmpP'])
                        dv(lambda e3=e3, b3=b3: nc.vector.tensor_tensor(out=T1, in0=e3(Epw_im), in1=b3(1), op=ALU.mult), RK, ['s5tmpP'])
                        dv(lambda d=d, XGb=XGb, j4=j4: nc.vector.tensor_tensor(out=XGb[:, d, 0, :, j4 * 32:(j4 + 1) * 32], in0=T0, in1=T1,
                                                                               op=ALU.subtract), ['s5tmpP', XGk], [XGk])
                        dv(lambda e3=e3, b3=b3: nc.vector.tensor_tensor(out=T0, in0=e3(Epw_re), in1=b3(1), op=ALU.mult), RK, ['s5tmpP'])
                        dv(lambda e3=e3, b3=b3: nc.vector.tensor_tensor(out=T1, in0=e3(Epw_im), in1=b3(0), op=ALU.mult), RK, ['s5tmpP'])
                        dv(lambda d=d, XGb=XGb, j4=j4: nc.vector.tensor_tensor(out=XGb[:, d, 1, :, j4 * 32:(j4 + 1) * 32], in0=T0, in1=T1,
                                                                               op=ALU.add), ['s5tmpP', XGk], [XGk])
                    RWk = 'Rwp%d' % bi
                    kb.op('pool', lambda bi=bi: nc.gpsimd.memset(Rwp[bi][:], 0.0), W=[RWk])
                    for d in range(2):
                        for r in range(2):
                            for gl in range(2):
                                c0 = j4 * 32 + gl * 16
                                pv(lambda d=d, r=r, gl=gl, c0=c0, bi=bi, pi_=pi_: nc.gpsimd.tensor_scalar(
                                    out=Rwp[bi][gl * 64:(gl + 1) * 64, d, r, c0:c0 + 16], in0=cfk[gl * 64:(gl + 1) * 64, d, r, pi_, :],
                                    scalar1=(1.0 if r == 0 else -1.0), scalar2=0.0, op0=ALU.mult, op1=ALU.add), ['cfs', RWk], [RWk])
                    for g4 in range(4):
                        ps, pk = psum()
                        pb = ps[:].bitcast(BF16)

                        def f(pb=pb, g4=g4, XGb=XGb):
                            for i8 in range(8):
                                idx = g4 * 8 + i8
                                ins = nc.tensor.transpose(pb[:, i8 * 128:(i8 + 1) * 128],
                                                          XGb[:, idx // 16, (idx // 8) % 2, idx % 8, :], ident[:])
                            return ins
                        kb.op('pe', f, R=[XGk, 'ident'], W=[pk])
                        kb.op('act', lambda pb=pb, g4=g4: nc.scalar.copy(
                            out=Sx[:, g4 * 8:(g4 + 1) * 8, :].rearrange("p a b -> p (a b)"), in_=pb[:, :]), R=[pk, 'Sx'], W=['Sx'])
                    for dd in range(2):
                        for jh in range(2):
                            ps, pk = psum()

                            def f(ps=ps, dd=dd, jh=jh, XGb=XGb, bi=bi):
                                for jj in range(4):
                                    j = jh * 4 + jj
                                    nc.tensor.matmul(ps[:, jj * 128:(jj + 1) * 128], lhsT=XGb[:, dd, 0, j, :], rhs=Rwp[bi][:, dd, 0, :],
                                                     start=True, stop=False)
                                    ins = nc.tensor.matmul(ps[:, jj * 128:(jj + 1) * 128], lhsT=XGb[:, dd, 1, j, :],
                                                           rhs=Rwp[bi][:, dd, 1, :], start=False, stop=True)
                                return ins
                            kb.op('pe', f, R=[XGk, RWk], W=[pk])
                            kb.op('act', lambda ps=ps, dd=dd, jh=jh, j4=j4: nc.scalar.copy(
                                out=BDq[:, dd, jh * 4:(jh + 1) * 4, j4 * 32:(j4 + 1) * 32],
                                in_=ps[:, :].rearrange("p (a b) -> p a b", b=128)[:, :, j4 * 32:(j4 + 1) * 32]),
                                R=[pk, 'BDq'], W=['BDq'])
                    for d in range(2):
                        for r in range(2):
                            ps, pk = psum()

                            def f(ps=ps, d=d, r=r):
                                if d == 0:
                                    for s_ in range(8):
                                        ins = nc.tensor.matmul(ps[:, 0:CH], lhsT=Sx[:, r * 8 + (7 - s_), :], rhs=uS[:, s_, :],
                                                               start=(s_ == 0), stop=(s_ == 7))
                                else:
                                    for s_ in range(8):
                                        nc.tensor.matmul(ps[:, 0:256], lhsT=Sx[:, 16 + r * 8 + s_, :], rhs=uS[:, s_, 32:CH],
                                                         start=(s_ == 0), stop=(s_ == 7))
                                    for s_ in range(8):
                                        ins = nc.tensor.matmul(ps[:, 256:CH], lhsT=Sx[:, 16 + r * 8 + s_, :], rhs=uS[:, s_, 0:32],
                                                               start=(s_ == 0), stop=(s_ == 7), skip_group_check=True)
                                return ins
                            kb.op('pe', f, R=['Sx'] + UQ, W=[pk])
                            if d == 0:
                                src = ps[:, 0:CH]
                            else:
                                v = ps[:, 0:CH]
                                src = bass.AP(v.tensor, v.offset + CH - 1, [list(v.ap[0]), [-1, CH]])
                            kb.op('act', lambda src=src, pl=pl, d=d, r=r: nc.scalar.copy(out=Z[:, pl, d, r, :], in_=src),
                                  R=[pk, 'Z'], W=['Z'])
                kmax = 0
                while (2 << kmax) - 1 < CH:
                    kmax += 1
                for k in range(kmax):
                    s_ = 1 << k
                    scan_level(k, 2 * s_ - 1, (CH - (2 * s_ - 1) - 1) // (2 * s_) + 1)
                for k in range(kmax - 1, -1, -1):
                    s_ = 1 << k
                    if CH - 1 < 3 * s_ - 1:
                        continue
                    scan_level(k, 3 * s_ - 1, (CH - 1 - (3 * s_ - 1)) // (2 * s_) + 1)
                kb.op('pool', lambda: nc.gpsimd.memset(XsQ[:], 0.0), W=['XsQ'])
                kb.op('act', lambda: nc.scalar.copy(out=XsQ[:, :, 0, :, 1:CH], in_=Z[:, :, 0, :, 0:CH - 1]), R=['Z', 'XsQ'], W=['XsQ'])
                zr = bass.AP(Z[:].tensor, zoff + 2 * CH + CH - 2, [[pstep, 128], [4 * CH, 4], [CH, 2], [-1, CH - 1]])
                kb.op('act', lambda: nc.scalar.copy(out=XsQ[:, :, 1, :, 0:CH - 1], in_=zr), R=['Z', 'XsQ'], W=['XsQ'])
                SK = 'XsQ'
                for pp in range(2):
                    for pl in (2 * pp, 2 * pp + 1):
                        pi_ = q * 4 + pl
                        j4 = pi_ % 4
                        bi = pi_ % 2
                        XGb, XGk = XG[bi], 'XG%d' % bi
                        kb.op('pool', lambda XGb=XGb: nc.gpsimd.memset(XGb[:], 0.0), W=[XGk])
                        for d in range(2):
                            def c3(r, d=d, pi_=pi_):
                                v = cfk[:, d, r, pi_, :]
                                return bass.AP(v.tensor, v.offset, [list(v.ap[0]), [0, 8], list(v.ap[1])])

                            def g3(tab, d=d, pi_=pi_):
                                v = tab[:, d, :, pi_]
                                return bass.AP(v.tensor, v.offset, [list(v.ap[0]), list(v.ap[1]), [0, 16]])
                            G0, G1, G2 = s5tmpP[:, 0, :, 0:16], s5tmpP[:, 1, :, 0:16], s5tmpP[:, 2, :, 0:16]
                            RK = ['Epw', 'cfs', 's5tmpP']
                            dv(lambda c3=c3, g3=g3: nc.vector.tensor_tensor(out=G0, in0=c3(0), in1=g3(Eg_re), op=ALU.mult), RK, ['s5tmpP'])
                            dv(lambda c3=c3, g3=g3: nc.vector.tensor_tensor(out=G1, in0=c3(1), in1=g3(Eg_im), op=ALU.mult), RK, ['s5tmpP'])
                            dv(lambda: nc.vector.tensor_tensor(out=G2, in0=G0, in1=G1, op=ALU.subtract), ['s5tmpP'], ['s5tmpP'])
                            for gl in range(2):
                                c0 = j4 * 32 + gl * 16
                                dv(lambda d=d, gl=gl, c0=c0, XGb=XGb: nc.vector.tensor_copy(
                                    out=XGb[gl * 64:(gl + 1) * 64, d, 0, :, c0:c0 + 16], in_=G2[gl * 64:(gl + 1) * 64]), ['s5tmpP', XGk], [XGk])
                            dv(lambda c3=c3, g3=g3: nc.vector.tensor_tensor(out=G0, in0=c3(0), in1=g3(Eg_im), op=ALU.mult), RK, ['s5tmpP'])
                            dv(lambda c3=c3, g3=g3: nc.vector.tensor_tensor(out=G1, in0=c3(1), in1=g3(Eg_re), op=ALU.mult), RK, ['s5tmpP'])
                            dv(lambda: nc.vector.tensor_tensor(out=G2, in0=G0, in1=G1, op=ALU.add), ['s5tmpP'], ['s5tmpP'])
                            for gl in range(2):
                                c0 = j4 * 32 + gl * 16
                                dv(lambda d=d, gl=gl, c0=c0, XGb=XGb: nc.vector.tensor_scalar(
                                    out=XGb[gl * 64:(gl + 1) * 64, d, 1, :, c0:c0 + 16], in0=G2[gl * 64:(gl + 1) * 64], scalar1=-1.0,
                                    scalar2=None, op0=ALU.mult), ['s5tmpP', XGk], [XGk])
                    for t in range(8):
                        ps, pk = psum()

                        def f(ps=ps, t=t, pp=pp):
                            first = True
                            if pp == 0:
                                for s_ in range(8):
                                    if s_ < t:
                                        ws = [BDq[:, 0, t - s_, :]]
                                    elif s_ > t:
                                        ws = [BDq[:, 1, s_ - t, :]]
                                    else:
                                        ws = [BDq[:, 0, 0, :], BDq[:, 1, 0, :]]
                                    for w_ in ws:
                                        nc.tensor.matmul(ps[:, 0:CH], lhsT=w_, rhs=uS[:, s_, :], start=first, stop=False)
                                        first = False
                            for pl in (2 * pp, 2 * pp + 1):
                                XGb = XG[pl % 2]
                                for r in range(2):
                                    nc.tensor.matmul(ps[:, 0:CH], lhsT=XGb[:, 0, r, t, :], rhs=XsQ[:, pl, 0, r, :], start=first, stop=False)
                                    first = False
                                for r in range(2):
                                    nc.tensor.matmul(ps[:, 32:CH], lhsT=XGb[:, 1, r, t, :], rhs=XsQ[:, pl, 1, r, 0:256],
                                                     start=False, stop=False)
                                    ins = nc.tensor.matmul(ps[:, 0:32], lhsT=XGb[:, 1, r, t, :], rhs=XsQ[:, pl, 1, r, 256:CH],
                                                           start=False, stop=(r == 1 and pl == 2 * pp + 1))
                            return ins
                        kb.op('pe', f, R=['XG0', 'XG1', SK, 'BDq', 'uS'], W=[pk])
                        dv(lambda ps=ps, t=t: nc.vector.tensor_tensor(out=yacc[:, t * CH:(t + 1) * CH], in0=yacc[:, t * CH:(t + 1) * CH],
                                                                      in1=ps[:, 0:CH], op=ALU.add), [pk, 'yacc'], ['yacc'])
                kb.op('act', lambda: nc.scalar.activation(out=gsc[:], in_=yacc[:], func=AF.Square), R=['yacc', 'gsc'], W=['gsc'])
                dv(lambda: nc.vector.tensor_scalar(out=gsc[:], in0=gsc[:], scalar1=0.044715, scalar2=1.0, op0=ALU.mult,
                                                   op1=ALU.add), ['gsc'], ['gsc'])
                dv(lambda: nc.vector.tensor_tensor(out=gsc[:], in0=gsc[:], in1=yacc[:], op=ALU.mult), ['yacc', 'gsc'], ['gsc'])
                kb.op('act', lambda: nc.scalar.activation(out=gsc[:], in_=gsc[:], func=AF.Sigmoid,
                                                          scale=2.0 * float(np.sqrt(2.0 / np.pi))), R=['gsc'], W=['gsc'])
                dv(lambda q=q: nc.vector.tensor_tensor(out=uT[:, q, :].rearrange("p (c s) -> p s c", s=8),
                                                       in0=yacc[:].rearrange("p (s c) -> p s c", s=8),
                                                       in1=gsc[:].rearrange("p (s c) -> p s c", s=8), op=ALU.mult),
                   ['yacc', 'gsc', 'uS'], [('uT', q)])
            UK = [('uT', q) for q in range(4)]
            cnt = [0]

            def ev_glu(m, tc0, n, ps, pk):
                i = cnt[0] % 2
                cnt[0] += 1
                kb.op('act', lambda: nc.scalar.activation(out=sg[i][:, 0:n], in_=ps[:, 0:n], func=AF.Sigmoid,
                                                          bias=bgluT[:, m:m + 1]), R=[pk, 'bgluT'], W=['gsc'])
                dv(lambda: nc.vector.tensor_tensor(out=ys5T[:, m, tc0:tc0 + n], in0=uT[:, m, tc0:tc0 + n],
                                                   in1=sg[i][:, 0:n], op=ALU.mult), ['gsc', ('uT', m)], [('ys5T', m)])
            proj_fm(w_glu[l], 0, 512, uT, UK, 4, ev_glu)


        w_rot = din("w_rot", [DEPTH, D, 512])
        thB_d = din("thB", [DEPTH, 128, 8])
        thQ_d = din("thQ", [DEPTH, 128, 4])
        rotc_d = din("rotc", [128, NT])
        rots_d = din("rots", [128, NT])
        retc_d = din("retc", [128, 2, 128])
        iq_d = din("iq", [128, 2, 128])
        ek_d = din("ek", [128, 2])
        onesF = sb("onesF", [128, 128], F32)
        kb.op('pool', lambda: nc.gpsimd.memset(onesF[:], 1.0 / 128.0), W=['onesF'])

        def ret_mixer(l):
            T_ = ['rtab']
            kb.dma(thB[:], thB_d[l], W=['thB'])
            kb.dma(thQ[:], thQ_d[l], W=['thQ'])
            kb.dma(retc[:], retc_d[:, :, :], W=['retc'])
            kb.dma(iq[:], iq_d[:, :, :], W=['iq'])
            kb.dma(ek[:], ek_d[:, :], W=['ek'])
            kb.dma(rotc[:], rotc_d[:, :], W=['rotc'])
            kb.dma(rots[:], rots_d[:, :], W=['rots'])
            for (src, dst, key) in ((thB, lgB, 'thB'), (thQ, lgQ, 'thQ')):
                kb.op('act', lambda src=src, dst=dst: nc.scalar.activation(out=dst[:], in_=src[:], func=AF.Exp, scale=-1.0),
                      R=[key], W=T_)
                dv(lambda dst=dst: nc.vector.tensor_scalar(out=dst[:], in0=dst[:], scalar1=1.0, scalar2=None, op0=ALU.add),
                   T_, T_)
                kb.op('act', lambda dst=dst: nc.scalar.activation(out=dst[:], in_=dst[:], func=AF.Ln), R=T_, W=T_)
                dv(lambda dst=dst: nc.vector.tensor_scalar(out=dst[:], in0=dst[:], scalar1=-1.0, scalar2=None, op0=ALU.mult),
                   T_, T_)
            for d in range(2):
                for h in range(4):
                    c = d * 4 + h
                    kb.op('act', lambda d=d, c=c: nc.scalar.activation(out=Dm[:, c, :], in_=retc[:, d, :], func=AF.Exp,
                                                                       scale=lgB[:, c:c + 1]), R=T_ + ['retc'], W=T_)
                for t in range(2):
                    c = d * 2 + t
                    kb.op('act', lambda d=d, t=t, c=c: nc.scalar.activation(out=qdtab[:, d, t, :], in_=iq[:, d, :],
                                                                            func=AF.Exp, scale=lgQ[:, c:c + 1]),
                          R=T_ + ['iq'], W=T_)
                kb.op('act', lambda d=d: nc.scalar.activation(out=kd[:, d * 4:(d + 1) * 4], in_=lgB[:, d * 4:(d + 1) * 4],
                                                              func=AF.Exp, scale=ek[:, d:d + 1]), R=T_ + ['ek'], W=T_)
            kb.op('act', lambda: nc.scalar.activation(out=cdecQ[:], in_=lgQ[:], func=AF.Exp, scale=128.0), R=T_, W=T_)

            def mk_ev(dst, key, scale):
                def ev(m, tc0, n, ps, pk):
                    kb.op('act', lambda: nc.scalar.activation(out=dst[:, m, tc0:tc0 + n], in_=ps[:, 0:n], func=AF.Copy,
                                                              scale=scale), R=[pk], W=[(key, m)])
                return ev
            for (dst, key, col0, rc0, scale) in ((kT, 'kT', 512, 0, 0.125), (qT, 'qT', 2304, 256, 1.0)):
                proj_fm(w_in[l], col0, 256, hT, HT_KEYS, 8, mk_ev(dst, key, scale))
                proj_fm(w_rot[l], rc0, 256, hT, HT_KEYS, 8, mk_ev(tmpT, 'tmpT', scale))
                for m in range(2):
                    dv(lambda m=m, dst=dst: nc.vector.tensor_tensor(out=dst[:, m, :], in0=dst[:, m, :], in1=rotc[:],
                                                                    op=ALU.mult), [(key, m), 'rotc'], [(key, m)])
                    dv(lambda m=m: nc.vector.tensor_tensor(out=tmpT[:, m, :], in0=tmpT[:, m, :], in1=rots[:],
                                                           op=ALU.mult), [('tmpT', m), 'rots'], [('tmpT', m)])
                    dv(lambda m=m, dst=dst: nc.vector.tensor_tensor(out=dst[:, m, :], in0=dst[:, m, :], in1=tmpT[:, m, :],
                                                                    op=ALU.add), [(key, m), ('tmpT', m)], [(key, m)])
            for cb in range(2):
                wt, wk = load_wk(w_in[l][:, 768 + cb * 256:768 + (cb + 1) * 256], 8, 256)
                for n in range(NTT):
                    ps, pk = psum()

                    def f(ps=ps, wt=wt, n=n):
                        for k in range(8):
                            ins = nc.tensor.matmul(ps[:, 0:256], lhsT=hT[:, k, n * 128:(n + 1) * 128], rhs=wt[:, k, 0:256],
                                                   start=(k == 0), stop=(k == 7))
                        return ins
                    kb.op('pe', f, R=[wk, ('hT', n)], W=[pk])
                    kb.op('act', lambda ps=ps, n=n, cb=cb: nc.scalar.copy(out=vtok[:, n, cb * 256:(cb + 1) * 256],
                                                                          in_=ps[:, 0:256]), R=[pk], W=[('vtok', n, cb)])
        def ret_part2(l):
            T_ = ['rtab']
            rb = [0]
            chains = [(t, d) for t in range(2) for d in range(2)]
            orders = {0: list(range(NTT)), 1: [1, 0] + list(range(NTT - 1, 1, -1))}
            written = set()
            for c in range(4):
                dv(lambda c=c: nc.vector.memset(sst[c][:], 0.0), ['sst%d' % c], ['sst%d' % c])
            steps = [(idx, c) for idx in range(NTT) for c in range(4)]
            CX = {}

            def sA(g):
                idx, c = steps[g]
                t, d = chains[c]
                n = orders[d][idx]
                tsl = slice(n * 128, (n + 1) * 128)
                ps, pk = psum()
                pb = ps[:].bitcast(BF16)
                kb.op('pe', lambda: nc.tensor.transpose(pb[:, 0:128], kT[:, t, tsl], ident[:]), R=[('kT', t), 'ident'], W=[pk])
                CX[g] = dict(idx=idx, c=c, t=t, d=d, n=n, tsl=tsl, i=g % 6, pb=pb, pk=pk)

            def sB(g):
                x = CX[g]
                t, d, i, tsl = x['t'], x['d'], x['i'], x['tsl']
                kdv = kd[:, d * 4 + 2 * t:d * 4 + 2 * t + 2]
                kdb = bass.AP(kdv.tensor, kdv.offset, [list(kdv.ap[0]), [1, 2], [0, 64]])
                dv(lambda: nc.vector.tensor_tensor(
                    out=kdec[i][:].rearrange("p (a b) -> p a b", a=2), in0=x['pb'][:, 0:128].rearrange("p (a b) -> p a b", a=2),
                    in1=kdb, op=ALU.mult), [x['pk']] + T_, ['kdec%d' % i])
                dv(lambda: nc.vector.tensor_tensor(out=qdt[i][:], in0=qT[:, t, tsl], in1=qdtab[:, d, t, :], op=ALU.mult),
                   [('qT', t)] + T_, ['qdt%d' % i])

            def sC(g):
                x = CX[g]
                t, i, tsl, n, idx = x['t'], x['i'], x['tsl'], x['n'], x['idx']
                x['ps1'] = []
                for hh in range(2):
                    psl = slice(64 * hh, 64 * hh + 64)
                    ps1, pk1 = psum()
                    kb.op('pe', lambda ps1=ps1, psl=psl: nc.tensor.matmul(
                        ps1[:, 0:128], lhsT=kT[psl, t, tsl], rhs=qT[psl, t, tsl], start=True, stop=True),
                        R=[('kT', t), ('qT', t)], W=[pk1])
                    x['ps1'].append((ps1, pk1))
                if idx < NTT - 1:
                    ps3, pk3 = psum()
                    kb.op('pe', lambda: nc.tensor.matmul(ps3[:, 0:256], lhsT=kdec[i][:], rhs=vtok[:, n, t * 256:(t + 1) * 256],
                                                         start=True, stop=True), R=['kdec%d' % i, ('vtok', n, t)], W=[pk3])
                    x['ps3'] = (ps3, pk3)

            def sD(g):
                x = CX[g]
                t, d, i, tsl, n, idx, c = x['t'], x['d'], x['i'], x['tsl'], x['n'], x['idx'], x['c']
                x['ps2'] = []
                for hh in range(2):
                    h = 2 * t + hh
                    psl = slice(64 * hh, 64 * hh + 64)
                    ps1, pk1 = x['ps1'][hh]
                    dv(lambda ps1=ps1, hh=hh, h=h: nc.vector.tensor_tensor(
                        out=pT[i][:, hh, :], in0=ps1[:, 0:128], in1=Dm[:, d * 4 + h, :], op=ALU.mult),
                       [pk1] + T_, [('pT%d' % i, hh)])
                    ps2, pk2 = psum()

                    def f(ps2=ps2, h=h, hh=hh, psl=psl):
                        ins = nc.tensor.matmul(ps2[:, 0:128], lhsT=vtok[:, n, h * 128:(h + 1) * 128],
                                               rhs=pT[i][:, hh, :], start=True, stop=(idx == 0))
                        if idx > 0:
                            ins = nc.tensor.matmul(ps2[:, 0:128], lhsT=sbf[c][psl, hh * 128:(hh + 1) * 128],
                                                   rhs=qdt[i][psl, :], start=False, stop=True)
                        return ins
                    kb.op('pe', f, R=[('vtok', n, h // 2), ('pT%d' % i, hh), 'sbf%d' % c, 'qdt%d' % i], W=[pk2])
                    x['ps2'].append((ps2, pk2))
                if idx < NTT - 1:
                    ps3, pk3 = x['ps3']
                    dv(lambda: nc.vector.scalar_tensor_tensor(
                        out=sst[c][:], in0=sst[c][:], scalar=cdecQ[:, d * 2 + t:d * 2 + t + 1], in1=ps3[:, 0:256],
                        op0=ALU.mult, op1=ALU.add), [pk3, 'sst%d' % c] + T_, ['sst%d' % c])
                    kb.op('act', lambda: nc.scalar.copy(out=sbf[c][:], in_=sst[c][:]), R=['sst%d' % c], W=['sbf%d' % c])

            def sE(g):
                x = CX.pop(g)
                t, tsl, n = x['t'], x['tsl'], x['n']
                for hh in range(2):
                    h = 2 * t + hh
                    ps2, pk2 = x['ps2'][hh]
                    if (h, n) not in written:
                        written.add((h, n))
                        kb.op('act', lambda ps2=ps2, h=h: nc.scalar.copy(out=oT[:, h, tsl], in_=ps2[:, 0:128]),
                              R=[pk2], W=[('oT', h, n)])
                    else:
                        dv(lambda ps2=ps2, h=h: nc.vector.tensor_tensor(
                            out=oT[:, h, tsl], in0=oT[:, h, tsl], in1=ps2[:, 0:128], op=ALU.add),
                           [pk2, ('oT', h, n)], [('oT', h, n)])

            NS = len(steps)
            for g in range(NS + 4):
                for st_, off in ((sE, 4), (sD, 3), (sC, 2), (sB, 1), (sA, 0)):
                    if 0 <= g - off < NS:
                        st_(g - off)
            units = [(h, bi_, tc0, n) for h in range(4) for bi_, (tc0, n) in enumerate(TB)]
            WT = {}
            NU = len(units)

            def ctx_(j):
                h, bi_, tc0, n = units[j]
                OK_ = [('oT', h, tt) for tt in range(tc0 // 128, (tc0 + n) // 128)]
                return h, tc0, n, OK_, oT[:, h, tc0:tc0 + n], sg[j % 3], 'sg%d' % (j % 3), sg[3 + j % 2], 'sg%d' % (3 + j % 2)

            def st1(j):
                h, tc0, n, OK_, osl, a_, ak, b_, bk_ = ctx_(j)
                if h not in WT:
                    WT[h] = load_wk(w_in[l][:, 2560 + h * 128:2560 + (h + 1) * 128], 8, 128)
                ps, pk = psum()
                kb.op('pe', lambda: nc.tensor.matmul(ps[:, 0:n], lhsT=onesF[:], rhs=osl, start=True, stop=True),
                      R=OK_ + ['onesF'], W=[pk])
                dv(lambda: nc.vector.tensor_tensor(out=osl, in0=osl, in1=ps[:, 0:n], op=ALU.subtract), OK_ + [pk], OK_)

            def st2(j):
                h, tc0, n, OK_, osl, a_, ak, b_, bk_ = ctx_(j)
                dv(lambda: nc.vector.tensor_tensor(out=a_[:, 0:n], in0=osl, in1=osl, op=ALU.mult), OK_ + [ak], [ak])
                ps, pk = psum()
                kb.op('pe', lambda: nc.tensor.matmul(ps[:, 0:n], lhsT=onesF[:], rhs=a_[:, 0:n], start=True, stop=True),
                      R=[ak, 'onesF'], W=[pk])
                kb.op('act', lambda: nc.scalar.activation(out=a_[:, 0:n], in_=ps[:, 0:n], func=AF.Sqrt, bias=eps_t[:, 1:2]),
                      R=[pk, 'eps_t', ak], W=[ak])

            def st3(j):
                h, tc0, n, OK_, osl, a_, ak, b_, bk_ = ctx_(j)
                dv(lambda: nc.vector.reciprocal(out=a_[:, 0:n], in_=a_[:, 0:n]), [ak], [ak])
                dv(lambda: nc.vector.tensor_tensor(out=osl, in0=osl, in1=a_[:, 0:n], op=ALU.mult), OK_ + [ak], OK_)
                wt, wk = WT[h]
                ps, pk = psum()

                def f():
                    for k in range(8):
                        ins = nc.tensor.matmul(ps[:, 0:n], lhsT=wt[:, k, 0:128], rhs=hT[:, k, tc0:tc0 + n],
                                               start=(k == 0), stop=(k == 7))
                    return ins
                kb.op('pe', f, R=[wk] + HT_KEYS, W=[pk])
                kb.op('act', lambda: nc.scalar.activation(out=b_[:, 0:n], in_=ps[:, 0:n], func=AF.Silu), R=[pk, bk_], W=[bk_])

            def st4(j):
                h, tc0, n, OK_, osl, a_, ak, b_, bk_ = ctx_(j)
                dv(lambda: nc.vector.tensor_tensor(out=yretT[:, h, tc0:tc0 + n], in0=osl, in1=b_[:, 0:n], op=ALU.mult),
                   OK_ + [bk_], [('yretT', h)])

            for j in range(NU + 3):
                if j < NU:
                    st1(j)
                if 0 <= j - 1 < NU:
                    st2(j - 1)
                if 0 <= j - 2 < NU:
                    st3(j - 2)
                if 0 <= j - 3 < NU:
                    st4(j - 3)

        nab_d = din("nab", [DEPTH, 8, 128, 15 * 64])
        colmask_d = din("colmask", [128, 64])
        onesB = sb("onesB", [128, 128], BF16)
        kb.op('pool', lambda: nc.gpsimd.memset(onesB[:], 1.0), W=['onesB'])

        def _os_skip():
            import os
            return os.environ.get('NA_SKIPCTX', '0') == '1'

        def na_mixer(l, need_ctx):
            kb.dma(colmask[:], colmask_d[:, :], W=['colmask'])
            cmb = bass.AP(colmask[:].tensor, colmask[:].offset, [list(colmask[:].ap[0]), [0, 15], [1, 64]])
            for h in range(8):
                i = 0
                kb.dma(nst[i], nab_d[l, h], W=['nst0'])
                kb.op('act', lambda i=i: nc.scalar.activation(out=nst[i], in_=nst[i], func=AF.Exp),
                      R=['nst0'], W=['nst0'])
                dv(lambda i=i, h=h: nc.vector.tensor_tensor(out=TT[:, h, :].rearrange("p (a b) -> p a b", b=64),
                                                            in0=nst[i].rearrange("p (a b) -> p a b", b=64), in1=cmb,
                                                            op=ALU.mult), ['nst0', 'colmask'], ['TT'])

            def mk_ev(dst, key, scale):
                def ev(m, tc0, n, ps, pk):
                    kb.op('act', lambda: nc.scalar.activation(out=dst[:, m, tc0:tc0 + n], in_=ps[:, 0:n], func=AF.Copy,
                                                              scale=scale), R=[pk], W=[(key, m)])
                return ev
            proj_fm(w_in[l], 1280, 512, hT, HT_KEYS, 8, mk_ev(nkT, 'nkT', 1.0))
            proj_fm(w_in[l], 3072, 512, hT, HT_KEYS, 8, mk_ev(nqT, 'nqT', 0.125))
            for cb in range(2):
                wt, wk = load_wk(w_in[l][:, 1792 + cb * 256:1792 + (cb + 1) * 256], 8, 256)
                for n in range(NTT):
                    ps, pk = psum()

                    def f(ps=ps, wt=wt, n=n):
                        for k in range(8):
                            ins = nc.tensor.matmul(ps[:, 0:256], lhsT=hT[:, k, n * 128:(n + 1) * 128], rhs=wt[:, k, 0:256],
                                                   start=(k == 0), stop=(k == 7))
                        return ins
                    kb.op('pe', f, R=[wk, ('hT', n)], W=[pk])
                    kb.op('act', lambda ps=ps, n=n, cb=cb: nc.scalar.copy(out=nvtok[:, n, cb * 256:(cb + 1) * 256],
                                                                          in_=ps[:, 0:256]), R=[pk], W=[('nvtok', n, cb)])
            import os as _os
            NR = int(_os.environ.get('NA_ROWS', '32'))
            VK_ = lambda t: [('nvtok', n, t // 2) for n in range(NTT)]

            def ctx_queries(h):
                t, hh = h // 2, h % 2
                psl = slice(64 * hh, 64 * hh + 64)
                KQ = [('nkT', t), ('nqT', t)]
                ps, pk = psum()

                def f(ps=ps):
                    for j in range(2):
                        ins = nc.tensor.matmul(ps[:, j * 256:(j + 1) * 256], lhsT=nkT[psl, t, j * 128:(j + 1) * 128],
                                               rhs=nqT[psl, t, 0:256], start=True, stop=True)
                    return ins
                kb.op('pe', f, R=KQ, W=[pk])
                kb.op('act', lambda: nc.scalar.activation(out=pc[:], in_=ps[:, 0:512], func=AF.Exp), R=[pk, 'pc'], W=['pc'])
                pso, pko = psum()
                psd, pkd = psum()

                def f(pso=pso, psd=psd):
                    for j in range(2):
                        nc.tensor.matmul(pso[:, 0:256], lhsT=nvtok[:, j, t * 128:(t + 1) * 128],
                                         rhs=pc[:, j * 256:(j + 1) * 256], start=(j == 0), stop=(j == 1))
                    for j in range(2):
                        ins = nc.tensor.matmul(psd[:, 0:256], lhsT=onesB[:], rhs=pc[:, j * 256:(j + 1) * 256],
                                               start=(j == 0), stop=(j == 1))
                    return ins
                kb.op('pe', f, R=VK_(t) + ['pc', 'onesB'], W=[pko, pkd])
                dv(lambda: nc.vector.reciprocal(out=rcc[psl, 0:256], in_=psd[psl, 0:256]), [pkd, 'rcc'], ['rcc'])
                dv(lambda: nc.vector.tensor_tensor(out=ynaT[psl, t, 0:256], in0=pso[psl, 0:256], in1=rcc[psl, 0:256],
                                                   op=ALU.mult), [pko, 'rcc'], [('ynaT', t)])

            units = [(h, r) for h in range(8) for r in range(NR)]
            U = {}

            def s_unit(k):
                h, r = units[k]
                t, hh = h // 2, h % 2
                psl = slice(64 * hh, 64 * hh + 64)
                rs = min(max(r - 4, 0), 24)
                q0 = NCTX + r * 64
                rows = list(range(rs, rs + 8))
                par_rows = [[rr for rr in rows if rr % 2 == p] for p in range(2)]
                ps, pk = psum()

                def f():
                    for p in range(2):
                        for sl_, rr in enumerate(par_rows[p]):
                            k0 = NCTX + rr * 64
                            nc.tensor.matmul(ps[64 * p:64 * p + 64, sl_ * 64:(sl_ + 1) * 64], lhsT=nkT[psl, t, k0:k0 + 64],
                                             rhs=nqT[psl, t, q0:q0 + 64], start=True, stop=True)
                    for j in range(2):
                        ins = nc.tensor.matmul(ps[:, 256 + j * 64:256 + (j + 1) * 64], lhsT=nkT[psl, t, j * 128:(j + 1) * 128],
                                               rhs=nqT[psl, t, q0:q0 + 64], start=True, stop=True)
                    return ins
                kb.op('pe', f, R=[('nkT', t), ('nqT', t)], W=[pk])
                U[k] = (ps, pk, par_rows, q0, psl, t, h, r)

            def b_unit(k):
                ps, pk, par_rows, q0, psl, t, h, r = U[k]
                i = k % 3
                kb.op('act', lambda: nc.scalar.activation(out=e1[i][:], in_=ps[:, 0:256], func=AF.Exp),
                      R=[pk, 'e1%d' % i], W=['e1%d' % i])
                kb.op('act', lambda: nc.scalar.activation(out=p1[i][:, 256:384], in_=ps[:, 256:384], func=AF.Exp),
                      R=[pk], W=[('p1', i, 'c')])
                for p in range(2):
                    i0 = par_rows[p][0] - r + 7
                    tv = TT[64 * p:64 * p + 64, h, i0 * 64:(i0 + 7) * 64].rearrange("p (a b) -> p a b", b=64)[:, 0:7:2, :]
                    eng_ = (nc.vector, 'dve') if p == 0 else (nc.gpsimd, 'pool')
                    kb.op(eng_[1], lambda p=p, tv=tv, eng_=eng_: eng_[0].tensor_tensor(
                        out=p1[i][64 * p:64 * p + 64, 0:256].rearrange("p (a b) -> p a b", b=64),
                        in0=e1[i][64 * p:64 * p + 64, :].rearrange("p (a b) -> p a b", b=64), in1=tv, op=ALU.mult),
                        R=['e1%d' % i, 'TT'], W=[('p1', i, p)])

            def c_unit(k):
                ps, pk, par_rows, q0, psl, t, h, r = U[k]
                i = k % 3
                pso, pko = psum()
                psb, pkb = psum()

                def f():
                    for p, acc in ((0, pso), (1, psb)):
                        for sl_, rr in enumerate(par_rows[p]):
                            nc.tensor.matmul(acc[:, 0:64], lhsT=nvtok[64 * p:64 * p + 64, 2 + rr // 2, t * 128:(t + 1) * 128],
                                             rhs=p1[i][64 * p:64 * p + 64, sl_ * 64:(sl_ + 1) * 64], start=(sl_ == 0),
                                             stop=(p == 1 and sl_ == 3))
                    for j in range(2):
                        nc.tensor.matmul(pso[:, 0:64], lhsT=nvtok[:, j, t * 128:(t + 1) * 128],
                                         rhs=p1[i][:, 256 + j * 64:256 + (j + 1) * 64], start=False, stop=(j == 1))
                    for j in range(6):
                        ins = nc.tensor.matmul(pso[:, 64:128], lhsT=onesB[:], rhs=p1[i][:, j * 64:(j + 1) * 64],
                                               start=(j == 0), stop=(j == 5), skip_group_check=True)
                    return ins
                kb.op('pe', f, R=VK_(t) + [('p1', i, 0), ('p1', i, 1), ('p1', i, 'c'), 'onesB'], W=[pko, pkb])
                U[k] = U[k] + (pso, pko, psb, pkb)

            def d_unit(k):
                ps, pk, par_rows, q0, psl, t, h, r, pso, pko, psb, pkb = U.pop(k)
                hh = h % 2
                i = k % 3
                rk = 'rc%d' % i
                nk_, dk_ = ('numb', hh), ('denb', hh)
                rr_ = r % 16
                kb.op('act', lambda: nc.scalar.copy(out=rc[i][psl, 0:64], in_=psb[psl, 0:64]), R=[pkb, rk], W=[rk])
                kb.op('act', lambda: nc.scalar.copy(out=denb[psl, rr_ * 64:(rr_ + 1) * 64], in_=pso[psl, 64:128]),
                      R=[pko, dk_], W=[dk_] + (['nst0'] if k < 2 else []))
                dv(lambda: nc.vector.tensor_tensor(out=numb[psl, rr_ * 64:(rr_ + 1) * 64], in0=pso[psl, 0:64], in1=rc[i][psl, 0:64],
                                                   op=ALU.add), [pko, rk, nk_], [nk_])
                if rr_ == 15 or r == NR - 1:
                    nn = (rr_ + 1) * 64
                    c0 = NCTX + (r - rr_) * 64
                    dv(lambda: nc.vector.reciprocal(out=denb[psl, 0:nn], in_=denb[psl, 0:nn]), [dk_], [dk_])
                    dv(lambda: nc.vector.tensor_tensor(out=ynaT[psl, t, c0:c0 + nn], in0=numb[psl, 0:nn],
                                                       in1=denb[psl, 0:nn], op=ALU.mult), [dk_, nk_], [('ynaT', t)])

            if need_ctx and not _os_skip():
                for h in range(8):
                    ctx_queries(h)
            nu = len(units)
            for j in range(nu + 3):
                if j < nu:
                    s_unit(j)
                if 0 <= j - 1 < nu:
                    b_unit(j - 1)
                if 0 <= j - 2 < nu:
                    c_unit(j - 2)
                if 0 <= j - 3 < nu:
                    d_unit(j - 3)

        w_bs = [din("w_b%d" % i, [DEPTH, 512, D]) for i in range(3)]
        w_out_d = din("w_out", [DEPTH, D, D])
        w_fg = din("w_fg", [DEPTH, D, FFH])
        w_fu = din("w_fu", [DEPTH, D, FFH])
        w_fd = din("w_fd", [DEPTH, FFH, D])
        fnrow_d = din("fnrow", [128, D])
        fnrow = sb("fnrow_t", [128, D], F32)
        kb.dma(fnrow[:], fnrow_d[:, :], W=['fnrow'])

        def blocks_of(l):
            return TB if l < DEPTH - 1 else TB[1:]

        def merge_m(l):
            ys = ((ys5T, 'ys5T'), (yretT, 'yretT'), (ynaT, 'ynaT'))
            c = [0]
            for j in range(8):
                for br in range(3):
                    wg, wgk = load_wk(w_in[l][:, 3584 + br * 1024 + j * 128:3584 + br * 1024 + (j + 1) * 128], 8, 128)
                    wb, wbk = load_wk(w_bs[br][l][:, j * 128:(j + 1) * 128], 4, 128)
                    yt_, yk = ys[br]
                    YK = [(yk, m) for m in range(4)]
                    for (tc0, n) in blocks_of(l):
                        i = c[0] % 2
                        c[0] += 1
                        psg, pkg = psum()
                        psb, pkb = psum()

                        def f(psg=psg, wg=wg, tc0=tc0, n=n):
                            for k in range(8):
                                ins = nc.tensor.matmul(psg[:, 0:n], lhsT=wg[:, k, :], rhs=hT[:, k, tc0:tc0 + n],
                                                       start=(k == 0), stop=(k == 7))
                            return ins
                        kb.op('pe', f, R=[wgk] + HT_KEYS, W=[pkg])

                        def f(psb=psb, wb=wb, yt_=yt_, tc0=tc0, n=n):
                            for k in range(4):
                                ins = nc.tensor.matmul(psb[:, 0:n], lhsT=wb[:, k, :], rhs=yt_[:, k, tc0:tc0 + n],
                                                       start=(k == 0), stop=(k == 3))
                            return ins
                        kb.op('pe', f, R=[wbk] + YK, W=[pkb])
                        kb.op('act', lambda psg=psg, i=i, n=n: nc.scalar.activation(out=sg[i][:, 0:n], in_=psg[:, 0:n],
                                                                                    func=AF.Sigmoid),
                              R=[pkg, 'sg%d' % i], W=['sg%d' % i])
                        if br == 0:
                            dv(lambda psb=psb, i=i, tc0=tc0, n=n: nc.vector.tensor_tensor(
                                out=macc[:, tc0:tc0 + n], in0=sg[i][:, 0:n], in1=psb[:, 0:n], op=ALU.mult),
                               [pkb, 'sg%d' % i, 'macc'], ['macc'])
                        else:
                            dv(lambda psb=psb, i=i, n=n: nc.vector.tensor_tensor(
                                out=sg[i][:, 0:n], in0=sg[i][:, 0:n], in1=psb[:, 0:n], op=ALU.mult),
                               [pkb, 'sg%d' % i], ['sg%d' % i])
                            if br == 1:
                                dv(lambda i=i, tc0=tc0, n=n: nc.vector.tensor_tensor(
                                    out=macc[:, tc0:tc0 + n], in0=macc[:, tc0:tc0 + n], in1=sg[i][:, 0:n], op=ALU.add),
                                   ['sg%d' % i, 'macc'], ['macc'])
                            else:
                                dv(lambda i=i, tc0=tc0, n=n, j=j: nc.vector.tensor_tensor(
                                    out=mT[:, j, tc0:tc0 + n], in0=macc[:, tc0:tc0 + n], in1=sg[i][:, 0:n], op=ALU.add),
                                   ['sg%d' % i, 'macc'], [('mT', j)])

        def delta_apply(l, tc0, n, gate_off, src_dram, after, dsrc=None, doff=0):
            for t in range(tc0 // 128, (tc0 + n) // 128):
                i = t % 2
                kb.dma(xt[i][:], src_dram[t * 128:(t + 1) * 128, :], W=['xt%d' % i])
                o0 = t * 128 - tc0 + doff
                dsr = dT if dsrc is None else dsrc
                for half in range(2):
                    ps, pk = psum()

                    def f(ps=ps, half=half, o0=o0):
                        for kk in range(4):
                            ins = nc.tensor.transpose(ps[:, kk * 128:(kk + 1) * 128], dsr[:, half * 4 + kk, o0:o0 + 128],
                                                      ident_f[:])
                        return ins
                    kb.op('pe', f, R=['dT', 'ident_f'], W=[pk])
                    dv(lambda ps=ps, i=i, half=half: nc.vector.tensor_tensor(
                        out=xt[i][:, half * 512:(half + 1) * 512], in0=xt[i][:, half * 512:(half + 1) * 512], in1=ps[:, :],
                        op=ALU.add), [pk, 'xt%d' % i], ['xt%d' % i])
                after(i, t)

        def merge_out(l):
            def after(i, t):
                kb.dma(xw[t * 128:(t + 1) * 128, :], xt[i][:], R=['xt%d' % i])
                norm_tile(i, t, 24, 32)
            MK = [('mT', j) for j in range(8)]
            for (tc0, n) in blocks_of(l):
                v = 1 if tc0 == 0 else 0
                for dt_ in range(8):
                    wo, wok = load_wk(w_out_d[l][:, dt_ * 128:(dt_ + 1) * 128], 8, 128)
                    ps, pk = psum()

                    def f(ps=ps, wo=wo, tc0=tc0, n=n):
                        for k in range(8):
                            ins = nc.tensor.matmul(ps[:, 0:n], lhsT=wo[:, k, :], rhs=mT[:, k, tc0:tc0 + n],
                                                   start=(k == 0), stop=(k == 7))
                        return ins
                    kb.op('pe', f, R=[wok] + MK, W=[pk])
                    dv(lambda ps=ps, dt_=dt_, n=n, v=v: nc.vector.tensor_scalar(
                        out=dT[:, dt_, 0:n], in0=ps[:, 0:n], scalar1=modT[:, 16 + dt_, v:v + 1], scalar2=None, op0=ALU.mult),
                       [pk, 'modT', 'dT'], ['dT'])
                delta_apply(l, tc0, n, 16, xin if l == 0 else xw, after)

        def ffn(l, groups):
            last = (l == DEPTH - 1)

            def after(i, t):
                if not last:
                    kb.dma(xw[t * 128:(t + 1) * 128, :], xt[i][:], R=['xt%d' % i])
                else:
                    rstd_tile(i)
                    dv(lambda i=i: nc.vector.tensor_scalar(out=xt[i][:], in0=xt[i][:], scalar1=ss[:, 2:3], scalar2=None,
                                                           op0=ALU.mult), ['xt%d' % i, 'ss'], ['xt%d' % i])
                    dv(lambda i=i: nc.vector.tensor_tensor(out=xt[i][:], in0=xt[i][:], in1=fnrow[:], op=ALU.mult),
                       ['xt%d' % i, 'fnrow'], ['xt%d' % i])
                    kb.dma(out[(t - 2) * 128:(t - 1) * 128, :], xt[i][:], R=['xt%d' % i])
            c = [0]
            for grp in groups:
                g0 = grp[0][0]
                for j in range(FFH // 128):
                    wg, wgk = load_wk(w_fg[l][:, j * 128:(j + 1) * 128], 8, 128)
                    wu, wuk = load_wk(w_fu[l][:, j * 128:(j + 1) * 128], 8, 128)
                    for (tc0, n) in grp:
                        i = c[0] % 2
                        c[0] += 1
                        psg, pkg = psum()
                        psu, pku = psum()

                        def f(psg=psg, wg=wg, tc0=tc0, n=n):
                            for k in range(8):
                                ins = nc.tensor.matmul(psg[:, 0:n], lhsT=wg[:, k, :], rhs=hT[:, k, tc0:tc0 + n],
                                                       start=(k == 0), stop=(k == 7))
                            return ins
                        kb.op('pe', f, R=[wgk] + HT_KEYS, W=[pkg])

                        def f(psu=psu, wu=wu, tc0=tc0, n=n):
                            for k in range(8):
                                ins = nc.tensor.matmul(psu[:, 0:n], lhsT=wu[:, k, :], rhs=hT[:, k, tc0:tc0 + n],
                                                       start=(k == 0), stop=(k == 7))
                            return ins
                        kb.op('pe', f, R=[wuk] + HT_KEYS, W=[pku])
                        kb.op('act', lambda psg=psg, i=i, n=n: nc.scalar.activation(out=sg[i][:, 0:n], in_=psg[:, 0:n],
                                                                                    func=AF.Silu),
                              R=[pkg, 'sg%d' % i], W=['sg%d' % i])
                        dv(lambda psu=psu, i=i, j=j, tc0=tc0, n=n: nc.vector.tensor_tensor(
                            out=aT[:, j, tc0 - g0:tc0 - g0 + n], in0=sg[i][:, 0:n], in1=psu[:, 0:n], op=ALU.mult),
                           [pku, 'sg%d' % i], [('aT', j)])
                AK = [('aT', j) for j in range(FFH // 128)]
                for dt_ in range(8):
                    wa, wak = load_wk(w_fd[l][0:2048, dt_ * 128:(dt_ + 1) * 128], 16, 128)
                    wb_, wbk = load_wk(w_fd[l][2048:FFH, dt_ * 128:(dt_ + 1) * 128], 6, 128)
                    for (tc0, n) in grp:
                        v = 1 if tc0 == 0 else 0
                        ps, pk = psum()

                        def f(ps=ps, wa=wa, wb_=wb_, tc0=tc0, n=n):
                            for k in range(22):
                                w_ = wa[:, k, :] if k < 16 else wb_[:, k - 16, :]
                                ins = nc.tensor.matmul(ps[:, 0:n], lhsT=w_, rhs=aT[:, k, tc0 - g0:tc0 - g0 + n],
                                                       start=(k == 0), stop=(k == 21))
                            return ins
                        kb.op('pe', f, R=[wak, wbk] + AK, W=[pk])
                        dv(lambda ps=ps, dt_=dt_, n=n, v=v, tc0=tc0: nc.vector.tensor_scalar(
                            out=dTg[:, dt_, tc0 - g0:tc0 - g0 + n], in0=ps[:, 0:n], scalar1=modT[:, 40 + dt_, v:v + 1], scalar2=None,
                            op0=ALU.mult), [pk, 'modT', 'dT'], ['dT'])
                for (tc0, n) in grp:
                    delta_apply(l, tc0, n, 40, xw, after, dsrc=dTg, doff=tc0 - g0)

        dbgbuf = [None]

        def dump_fm(tile_, nk, keys):
            for k in range(nk):
                kb.op('act', lambda k=k: nc.scalar.copy(out=dbgbuf[0][:], in_=tile_[:, k, :]), R=list(keys) + ['dbgb'], W=['dbgb'])
                kb.dma(dbg_out[:, k, :], dbgbuf[0][:], R=['dbgb'])

        stop = False
        for l in range(DEPTH):
          with ExitStack() as LS:
            with ExitStack() as ph:
                cur[0] = ph
                xt = [sb("xt%d" % i, [128, D], F32) for i in range(2)]
                xn = [sb("xn%d" % i, [128, D], BF16) for i in range(2)]
                junk = sb("junk", [128, D], F32)
                wst = [sb("wst%d" % i, [128, 2048], F32) for i in range(2)]
                xnall = sb("xnall", [128, NTT, D], BF16)
                norm_pre(xin if l == 0 else xw)
                adaln(l)
                norm_post(0, 8)
                kb.barrier()
            cur[0] = LS
            ys5T = sb("ys5T", [128, 4, NT], BF16)
            if dbg and dbg[0] == 'hT':
                dbgbuf[0] = sb("dbgb", [128, NT], F32)
                dump_fm(hT, 8, HT_KEYS)
                break
            with ExitStack() as ph:
                cur[0] = ph
                uT = sb("uT", [128, 4, NT], BF16)
                gsc = sb("gsc", [128, NT], F32)
                XG = [sb("XG%d" % i, [128, 2, 2, 8, 128], BF16) for i in range(2)]
                Sx = sb("Sx", [128, 32, 128], BF16)
                Z = sb("Z", [128, 4, 2, 2, NT // 8], F32)
                XsQ = sb("XsQ", [128, 4, 2, 2, NT // 8], BF16)
                T2s = [None]
                CQ = sb("CQ", [128, 12, 8, 3], F32)
                BDq = sb("BDq", [128, 2, 8, 128], BF16)
                uS = sb("uS", [128, 8, NT // 8], BF16)
                Bt = [sb("Bt%d" % i, [128, 4, 32], F32) for i in range(2)]
                s5tmpP = sb("s5tmpP", [128, 3, 8, 32], F32)
                cfk = sb("cfk", [128, 2, 2, 16, 16], F32)
                Epw_re = sb("Epw_re", [128, 2, 9, 16], F32)
                Epw_im = sb("Epw_im", [128, 2, 9, 16], F32)
                Eg_re = sb("Eg_re", [128, 2, 8, 16], F32)
                Eg_im = sb("Eg_im", [128, 2, 8, 16], F32)
                yacc = sb("yacc", [128, NT], F32)
                Rwp = [sb("Rwp%d" % i, [128, 2, 2, 128], BF16) for i in range(2)]
                prm = sb("prm", [128, 2, 3, 16], F32)
                cfl = sb("cfl", [128, 2, 2, 16, 16], F32)
                APre = sb("APre", [128, 2, 12, 16], F32)
                APim = sb("APim", [128, 2, 12, 16], F32)
                APnim = sb("APnim", [128, 2, 12, 16], F32)
                sm = sb("sm", [128, 12, 16], F32)
                cfs = sb("cfs", [128, 2, 16, 16], F32)
                s5dT = sb("s5dT_t", [128, 4], F32)
                bgluT = sb("bgluT_t", [128, 4], F32)
                sg = [gsc[:, 0:512], gsc[:, 512:1024]]
                s5_mixer(l)
                kb.barrier()
            cur[0] = LS
            yretT = sb("yretT", [128, 4, NT], BF16)
            if dbg and dbg[0] == 'ys5':
                dbgbuf[0] = sb("dbgb", [128, NT], F32)
                dump_fm(ys5T, 4, [('ys5T', m) for m in range(4)])
                break
            with ExitStack() as ph:
                cur[0] = ph
                qT = sb("qT", [128, 2, NT], BF16)
                kT = sb("kT", [128, 2, NT], BF16)
                vtok = sb("vtok", [128, NTT, 512], BF16)
                thB = sb("thB_t", [128, 8], F32)
                thQ = sb("thQ_t", [128, 4], F32)
                lgB = sb("lgB", [128, 8], F32)
                lgQ = sb("lgQ", [128, 4], F32)
                retc = sb("retc_t", [128, 2, 128], F32)
                iq = sb("iq_t", [128, 2, 128], F32)
                ek = sb("ek_t", [128, 2], F32)
                Dm = sb("Dm", [128, 8, 128], F32)
                qdtab = sb("qdtab", [128, 2, 2, 128], F32)
                kd = sb("kd", [128, 8], F32)
                cdecQ = sb("cdecQ", [128, 4], F32)
                with ExitStack() as ph2:
                    cur[0] = ph2
                    tmpT = sb("tmpT", [128, 2, NT], BF16)
                    rotc = sb("rotc_t", [128, NT], F32)
                    rots = sb("rots_t", [128, NT], F32)
                    ret_mixer(l)
                    kb.barrier()
                cur[0] = ph
                oT = sb("oT", [128, 4, NT], F32)
                sst = [sb("sst%d" % i, [128, 256], F32) for i in range(4)]
                sbf = [sb("sbf%d" % i, [128, 256], BF16) for i in range(4)]
                kdec = [sb("kdec%d" % i, [128, 128], BF16) for i in range(6)]
                qdt = [sb("qdt%d" % i, [128, 128], BF16) for i in range(6)]
                pT = [sb("pT%d" % i, [128, 2, 128], BF16) for i in range(6)]
                sg = [sb("sg%d" % i, [128, 512], F32) for i in range(5)]
                ret_part2(l)
                kb.barrier()
            cur[0] = LS
            ynaT = sb("ynaT", [128, 4, NT], BF16)
            if dbg and dbg[0] == 'yret':
                dbgbuf[0] = sb("dbgb", [128, NT], F32)
                dump_fm(yretT, 4, [('yretT', m) for m in range(4)])
                break
            with ExitStack() as ph:
                cur[0] = ph
                nkT = sb("nkT", [128, 4, NT], BF16)
                nqT = sb("nqT", [128, 4, NT], BF16)
                nvtok = sb("nvtok", [128, NTT, 512], BF16)
                TT = sb("TT", [128, 8, 15 * 64], BF16)
                nst0 = sb("nst0", [128, 1024], F32)
                nst = [nst0[:, 0:960], nst0[:, 0:960]]
                colmask = sb("colmask_t", [128, 64], F32)
                e1 = [sb("e1%d" % i, [128, 256], F32) for i in range(3)]
                p1 = [sb("p1%d" % i, [128, 384], BF16) for i in range(3)]
                pc = sb("pc", [128, 512], BF16)
                rcc = sb("rcc", [128, 256], F32)
                rc = [sb("rc%d" % i, [128, 64], F32) for i in range(3)]
                numb = sb("numb", [128, 1024], F32)
                denb = nst0
                na_mixer(l, l < DEPTH - 1)
                kb.barrier()
            cur[0] = LS
            if dbg and dbg[0] == 'yna':
                dbgbuf[0] = sb("dbgb", [128, NT], F32)
                dump_fm(ynaT, 4, [('ynaT', m) for m in range(4)])
                break
            mT = sb("mT", [128, 8, NT], BF16)
            with ExitStack() as ph:
                cur[0] = ph
                macc = sb("macc", [128, NT], F32)
                sg = [sb("sg%d" % i, [128, 512], F32) for i in range(2)]
                merge_m(l)
                kb.barrier()
            with ExitStack() as ph:
                cur[0] = ph
                dT = sb("dT", [128, 8, 512], F32)
                xt = [sb("xt%d" % i, [128, D], F32) for i in range(2)]
                xn = [sb("xn%d" % i, [128, D], BF16) for i in range(2)]
                junk = sb("junk", [128, D], F32)
                merge_out(l)
                kb.barrier()
            cur[0] = LS
          cur[0] = es
          with ExitStack() as ph:
              cur[0] = ph
              if l < DEPTH - 1:
                  groups = [TB[0:3], TB[3:5]]
              else:
                  groups = [TB[1:3], TB[3:5]]
              aT = sb("aT", [128, FFH // 128, 1280], BF16)
              dTg = sb("dTg", [128, 8, 1280], F32)
              sg = [sb("sg%d" % i, [128, 512], F32) for i in range(2)]
              xt = [sb("xt%d" % i, [128, D], F32) for i in range(2)]
              xn = [sb("xn%d" % i, [128, D], BF16) for i in range(2)]
              junk = sb("junk", [128, D], F32)
              ffn(l, groups)
              kb.barrier()
          cur[0] = es
        kb.finish()
    return nc


def host_inputs(inputs, b, shared=None):
    f = np.float32
    d = {}
    d['xin'] = np.ascontiguousarray(np.concatenate([inputs['ctx'][b], inputs['x'][b]], axis=0).astype(f))
    cc = np.stack([inputs['c'][b], inputs['c_ctx']], axis=-1)
    d['cT'] = np.ascontiguousarray(cc.reshape(8, 128, 2).transpose(1, 0, 2).astype(f))
    if shared is not None:
        d.update(shared)
        return d
    d['w_ada'] = np.ascontiguousarray(inputs['w_ada'].astype(f))
    d['b_adaT'] = np.ascontiguousarray(inputs['b_ada'].reshape(DEPTH, 48, 128).transpose(0, 2, 1).astype(f))
    d['w_in'] = np.ascontiguousarray(inputs['w_in'].astype(f))
    d['ident_d'] = np.eye(128, dtype=f)
    def layA(a):
        return a.reshape(DEPTH, 2, 16, 2, 64).transpose(0, 1, 3, 4, 2).reshape(DEPTH, 2, 128, 16)
    ldt = np.repeat(inputs['s5_log_dt'][..., None], 64, axis=-1)
    d['s5p'] = np.ascontiguousarray(np.stack([layA(inputs['s5_lam_re']), layA(inputs['s5_lam_im']), layA(ldt)],
                                             axis=3).astype(f))
    def layC(a):
        return a.reshape(DEPTH, 2, 16, 2, 16, 64).transpose(0, 1, 3, 5, 2, 4).reshape(DEPTH, 2, 128, 16, 16)
    d['s5c'] = np.ascontiguousarray(np.stack([layC(inputs['s5_c_re']), layC(inputs['s5_c_im'])], axis=3).astype(f))
    ba = np.zeros((DEPTH, 16, 128, 4, 32), f)
    for r, nm in enumerate(('s5_b_re', 's5_b_im')):
        b = inputs[nm]
        for pi_ in range(16):
            for gl in range(2):
                g = 2 * pi_ + gl
                for dd in range(2):
                    ba[:, pi_, gl * 64:(gl + 1) * 64, dd * 2 + r, gl * 16:(gl + 1) * 16] = b[:, dd, g]
    d['s5ba'] = ba
    d['s5dT'] = np.ascontiguousarray(inputs['s5_d'].reshape(DEPTH, 4, 128).transpose(0, 2, 1).astype(f))
    d['bgluT'] = np.ascontiguousarray(inputs['s5_b_glu'].reshape(DEPTH, 4, 128).transpose(0, 2, 1).astype(f))
    d['w_glu'] = np.ascontiguousarray(inputs['s5_w_glu'].astype(f))
    perm = np.arange(64)
    perm = np.where((perm % 32) < 16, perm + 16, perm - 16)
    kcols = 512 + (np.arange(256) // 64) * 64 + perm[np.arange(256) % 64]
    qcols = 2304 + (np.arange(256) // 64) * 64 + perm[np.arange(256) % 64]
    d['w_rot'] = np.ascontiguousarray(inputs['w_in'][:, :, np.concatenate([kcols, qcols])].astype(f))
    th = inputs['ret_theta'].astype(f)
    d['thB'] = np.ascontiguousarray(np.broadcast_to(th.reshape(DEPTH, 1, 8), (DEPTH, 128, 8)))
    thq = th.reshape(DEPTH, 2, 2, 2)
    d['thQ'] = np.ascontiguousarray(np.repeat(thq.transpose(0, 3, 1, 2), 64, axis=1).reshape(DEPTH, 128, 4))
    kc = np.arange(64)[:, None]
    qc = np.arange(64)[None, :]
    jidx = np.clip(kc - qc + 15, 0, 30)
    g_ = inputs['na_rpb'][:, :, :, jidx]
    g_ = g_.transpose(0, 1, 3, 2, 4)
    d['nab'] = np.ascontiguousarray(np.concatenate([g_, g_], axis=2).reshape(DEPTH, 8, 128, 15 * 64).astype(f))
    for i, nm in enumerate(('w_branch_s5', 'w_branch_ret', 'w_branch_na')):
        d['w_b%d' % i] = np.ascontiguousarray(inputs[nm].astype(f))
    d['w_out'] = np.ascontiguousarray(inputs['w_out'].astype(f))
    d['w_fg'] = np.ascontiguousarray(inputs['w_ffn_gate'].astype(f))
    d['w_fu'] = np.ascontiguousarray(inputs['w_ffn_up'].astype(f))
    d['w_fd'] = np.ascontiguousarray(inputs['w_ffn_down'].astype(f))
    d['fnrow'] = np.ascontiguousarray(np.broadcast_to(inputs['final_norm'].astype(f)[None, :], (128, D)))
    d.update(_consts())
    return d


_CONSTS = {}


def _consts():
    if _CONSTS:
        return _CONSTS
    f = np.float32
    pos = np.arange(LAT)
    inv_freq = (10000.0 ** (-np.arange(16, dtype=np.float32) / 16)).astype(np.float32)
    dd = np.arange(128) % 64
    coord = np.where((dd // 32)[:, None] == 0, (pos // 64)[None, :], (pos % 64)[None, :]).astype(np.float32)
    ang = coord * inv_freq[dd % 16][:, None]
    sign = np.where((dd % 32) < 16, -1.0, 1.0)[:, None]
    rotc = np.ones((128, NT), f)
    rots = np.zeros((128, NT), f)
    rotc[:, NCTX:] = np.cos(ang)
    rots[:, NCTX:] = np.sin(ang) * sign
    jj = np.arange(128)[:, None].astype(f)
    ii = np.arange(128)[None, :].astype(f)
    BIG = 1.0e6
    retc = np.stack([np.where(ii >= jj, ii - jj, BIG), np.where(jj > ii, jj - ii, BIG)], axis=1)
    iq = np.stack([np.broadcast_to(ii + 1.0, (128, 128)), np.broadcast_to(128.0 - ii, (128, 128))], axis=1)
    ek = np.stack([127.0 - np.arange(128), np.arange(128)], axis=1)
    kc = np.arange(64)[:, None]
    qc = np.arange(64)[None, :]
    cs = np.clip(qc - 8, 0, 48)
    cm = ((kc >= cs) & (kc < cs + 16)).astype(f)
    _CONSTS['colmask'] = np.ascontiguousarray(np.concatenate([cm, cm], axis=0))
    _CONSTS.update(rotc=rotc, rots=rots.astype(f), retc=np.ascontiguousarray(retc.astype(f)),
                   iq=np.ascontiguousarray(iq.astype(f)), ek=np.ascontiguousarray(ek.astype(f)))
    return _CONSTS


def kernel(**inputs):
    inputs = {k: np.asarray(v) for k, v in inputs.items()}
    nc = build()
    first = host_inputs(inputs, 0)
    shared = {k: v for k, v in first.items() if k not in ('xin', 'cT')}
    in_maps = [first] + [host_inputs(inputs, b, shared) for b in range(1, 8)]
    res = run_bass_kernel_spmd(nc, in_maps, core_ids=list(range(8)))
    return np.stack([r["out"] for r in res.results], axis=0).astype(np.float32)
```

```python
import numpy as np
from contextlib import ExitStack
import concourse.bass as bass
import concourse.mybir as mybir
from concourse.bass_utils import run_bass_kernel_spmd

F32 = mybir.dt.float32
BF16 = mybir.dt.bfloat16
AF = mybir.ActivationFunctionType
ALU = mybir.AluOpType

D = 1024
LAT = 2048
NCTX = 256
NT = LAT + NCTX
NTT = NT // 128
DEPTH = 2
NIN = 6656
FFH = 2816
TB = [(0, 256), (256, 512), (768, 512), (1280, 512), (1792, 512)]
RMS_EPS = 1e-6
GN_EPS = 1e-5


class KB:
    def __init__(self, nc, es):
        self.nc = nc
        self.es = es
        self.eng = {'pe': nc.tensor, 'act': nc.scalar, 'dve': nc.vector, 'pool': nc.gpsimd, 'sp': nc.sync}
        self.sem = {e: es.enter_context(nc.semaphore('sem_' + e)) for e in self.eng}
        self.cnt = {e: 0 for e in self.eng}
        self.seen = {e: {} for e in self.eng}
        self.ND = 24
        self.dsem = [es.enter_context(nc.semaphore('dsem%d' % i)) for i in range(self.ND)]
        self.dcnt = [0] * self.ND
        self.dslots = {'sp': list(range(0, 14)), 'pool': list(range(14, 24))}
        self.dnext = {'sp': 0, 'pool': 0}
        self.w = {}
        self.r = {}
        self.nps = 0
        self.ninst = 0

    def _semh(self, s):
        return self.sem[s] if isinstance(s, str) else self.dsem[s[1]]

    def _wait(self, e, s, v):
        if s == e and e == 'pe':
            return
        if self.seen[e].get(s, 0) >= v:
            return
        self.eng[e].wait_ge(self._semh(s), v)
        self.seen[e][s] = v

    def _deps(self, e, R, W):
        for k in list(R) + list(W):
            for s, v in self.w.get(k, {}).items():
                self._wait(e, s, v)
        for k in W:
            for s, v in self.r.get(k, {}).items():
                self._wait(e, s, v)

    def _mark(self, tok, R, W):
        for k in W:
            self.w[k] = {tok[0]: tok[1]}
            self.r[k] = {}
        for k in R:
            d = self.r.setdefault(k, {})
            if d.get(tok[0], 0) < tok[1]:
                d[tok[0]] = tok[1]

    def op(self, e, fn, R=(), W=()):
        self._deps(e, R, W)
        inst = fn()
        self.cnt[e] += 1
        inst.then_inc(self.sem[e], 1)
        self._mark((e, self.cnt[e]), R, W)
        self.ninst += 1

    def dma(self, out, in_, R=(), W=(), q='sp', **kw):
        sl = self.dslots[q]
        i = sl[self.dnext[q] % len(sl)]
        self.dnext[q] += 1
        if self.dcnt[i]:
            self._wait(q, ('d', i), 16 * self.dcnt[i])
        self._deps(q, R, W)
        self.eng[q].dma_start(out=out, in_=in_, **kw).then_inc(self.dsem[i], 16)
        self.dcnt[i] += 1
        self._mark((('d', i), 16 * self.dcnt[i]), R, W)
        self.ninst += 1

    def merge(self, newkey, keys):
        d = {}
        for k in keys:
            for s, v in self.w.get(k, {}).items():
                if d.get(s, 0) < v:
                    d[s] = v
        self.w[newkey] = d
        self.r[newkey] = {}

    def barrier(self):
        for e in self.eng:
            for e2 in self.eng:
                if e2 != e and self.cnt[e2]:
                    self._wait(e, e2, self.cnt[e2])
            for i in range(self.ND):
                if self.dcnt[i]:
                    self._wait(e, ('d', i), 16 * self.dcnt[i])
        self.w = {}
        self.r = {}

    def finish(self):
        for i in range(self.ND):
            if self.dcnt[i]:
                self._wait('sp', ('d', i), 16 * self.dcnt[i])


def pbc(ap, n):
    return ap.to_broadcast([ap.shape[0], n])


def build(dbg=None):
    nc = bass.Bass("TRN2", target_bir_lowering=False)
    dr = {}

    def din(name, shape):
        dr[name] = nc.dram_tensor(name, list(shape), F32, kind="ExternalInput").ap()
        return dr[name]

    xin = din("xin", [NT, D])
    cT = din("cT", [128, 8, 2])
    w_ada = din("w_ada", [DEPTH, D, 6 * D])
    b_adaT = din("b_adaT", [DEPTH, 128, 48])
    w_in = din("w_in", [DEPTH, D, NIN])
    out = nc.dram_tensor("out", [LAT, D], F32, kind="ExternalOutput").ap()
    xw = nc.dram_tensor("xw", [NT, D], F32, kind="Internal").ap()
    if dbg:
        dbg_out = nc.dram_tensor("dbg", list(dbg[1]), F32, kind="ExternalOutput").ap()

    with ExitStack() as es:
        kb = KB(nc, es)

        cur = [es]

        nuniq = [0]

        def sb(name, shape, dt):
            nuniq[0] += 1
            return cur[0].enter_context(nc.sbuf_tensor("%s_%d" % (name, nuniq[0]), list(shape), dt))

        PS = [es.enter_context(nc.psum_tensor("ps%d" % i, [128, 512], F32)) for i in range(8)]

        def psum():
            i = kb.nps % 8
            kb.nps += 1
            return PS[i], 'ps%d' % i

        ident_f = sb("ident_f", [128, 128], F32)
        ident = sb("ident", [128, 128], BF16)
        hT = sb("hT", [128, 8, NT], BF16)
        modT = sb("modT", [128, 48, 2], F32)
        siluT = sb("siluT", [128, 8, 2], F32)
        NWB = 4
        wbf = [sb("wbf%d" % i, [128, 2048], BF16) for i in range(NWB)]
        ss = sb("ss", [128, 4], F32)
        badaT = sb("badaT", [128, 48], F32)
        ident_d = din("ident_d", [128, 128])

        kb.dma(ident_f[:], ident_d[:, :], W=['ident_f'])
        kb.op('dve', lambda: nc.vector.tensor_copy(out=ident[:], in_=ident_f[:]), R=['ident_f'], W=['ident'])
        kb.dma(siluT[:], cT[:, :, :], W=['siluT'])
        kb.op('act', lambda: nc.scalar.activation(out=siluT[:], in_=siluT[:], func=AF.Silu), R=['siluT'], W=['siluT'])

        wctr = [0, 0]

        def load_wk(src_ap, kt, ncols, cast=True):
            if not cast:
                i = wctr[1] % 2
                wctr[1] += 1
                sv = wst[i][:, 0:kt * ncols].rearrange("p (k n) -> p k n", n=ncols)
                kb.dma(sv, src_ap.rearrange("(k p) n -> p k n", p=128), W=['wst%d' % i])
                return sv, 'wst%d' % i
            i = wctr[0] % NWB
            wctr[0] += 1
            bv = wbf[i][:, 0:kt * ncols].rearrange("p (k n) -> p k n", n=ncols)
            kb.dma(bv, src_ap.rearrange("(k p) n -> p k n", p=128), W=['wbf%d' % i], q='pool')
            return bv, 'wbf%d' % i

        def load_w(src_ap, ncols, cast=True):
            return load_wk(src_ap, 8, ncols, cast)

        def adaln(l):
            kb.dma(badaT[:], b_adaT[l], W=['badaT'])
            ps, pk = psum()
            psv = ps[:, 0:96].rearrange("p (j v) -> p j v", v=2)
            for cb in range(24):
                wt, wk = load_w(w_ada[l][:, cb * 256:(cb + 1) * 256], 256, cast=False)
                for jj in range(2):
                    j = cb * 2 + jj

                    def f(j=j, jj=jj, wt=wt):
                        for k in range(8):
                            ins = nc.tensor.matmul(psv[:, j, :], lhsT=wt[:, k, jj * 128:(jj + 1) * 128],
                                                   rhs=siluT[:, k, :], start=(k == 0), stop=(k == 7))
                        return ins
                    kb.op('pe', f, R=[wk, 'siluT'], W=[pk])
            for v in range(2):
                kb.op('dve', lambda v=v: nc.vector.tensor_tensor(out=modT[:, :, v], in0=psv[:, :, v], in1=badaT[:],
                                                                 op=ALU.add), R=[pk, 'badaT'], W=['modT'])
            for a in (8, 32):
                kb.op('dve', lambda a=a: nc.vector.tensor_scalar(out=modT[:, a:a + 8, :], in0=modT[:, a:a + 8, :],
                                                                 scalar1=1.0, scalar2=None, op0=ALU.add),
                      R=['modT'], W=['modT'])

        def rstd_tile(i):
            kb.op('act', lambda i=i: nc.scalar.activation(out=junk[:], in_=xt[i][:], func=AF.Square),
                  R=['xt%d' % i], W=['junk'])
            kb.op('dve', lambda: nc.vector.tensor_reduce(out=ss[:, 0:1], in_=junk[:], axis=mybir.AxisListType.X, op=ALU.add),
                  R=['junk', 'ss'], W=['ss'])
            kb.op('act', lambda: nc.scalar.activation(out=ss[:, 1:2], in_=ss[:, 0:1], func=AF.Sqrt,
                                                      scale=1.0 / D, bias=eps_t[:, 0:1]), R=['ss', 'eps_t'], W=['ss'])
            kb.op('dve', lambda: nc.vector.reciprocal(out=ss[:, 2:3], in_=ss[:, 1:2]), R=['ss'], W=['ss'])

        def norm_tile(i, t, sh_off, sc_off):
            v = 1 if t < 2 else 0
            rstd_tile(i)
            kb.op('dve', lambda i=i: nc.vector.tensor_scalar(out=xn[i][:], in0=xt[i][:], scalar1=ss[:, 2:3],
                                                             scalar2=None, op0=ALU.mult),
                  R=['xt%d' % i, 'ss'], W=['xn%d' % i])
            ps, pk = psum()
            pb = ps[:].bitcast(BF16)

            def f(i=i, pb=pb):
                for k in range(8):
                    ins = nc.tensor.transpose(pb[:, k * 128:(k + 1) * 128], xn[i][:, k * 128:(k + 1) * 128], ident[:])
                return ins
            kb.op('pe', f, R=['xn%d' % i, 'ident'], W=[pk])
            for k in range(8):
                kb.op('dve', lambda k=k, pb=pb, t=t, v=v: nc.vector.tensor_scalar(
                    out=hT[:, k, t * 128:(t + 1) * 128], in0=pb[:, k * 128:(k + 1) * 128],
                    scalar1=modT[:, sc_off + k, v:v + 1], scalar2=modT[:, sh_off + k, v:v + 1],
                    op0=ALU.mult, op1=ALU.add), R=[pk, 'modT'], W=[('hT', t)])

        def norm_to_hT(src_dram, sh_off, sc_off):
            for t in range(NTT):
                i = t % 2
                kb.dma(xt[i][:], src_dram[t * 128:(t + 1) * 128, :], W=['xt%d' % i])
                norm_tile(i, t, sh_off, sc_off)

        def norm_pre(src_dram):
            for t in range(NTT):
                i = t % 2
                kb.dma(xt[i][:], src_dram[t * 128:(t + 1) * 128, :], W=['xt%d' % i])
                rstd_tile(i)
                kb.op('dve', lambda i=i, t=t: nc.vector.tensor_scalar(out=xnall[:, t, :], in0=xt[i][:], scalar1=ss[:, 2:3],
                                                                      scalar2=None, op0=ALU.mult),
                      R=['xt%d' % i, 'ss'], W=[('xnall', t)])

        def norm_post(sh_off, sc_off):
            for t in range(NTT):
                v = 1 if t < 2 else 0
                ps, pk = psum()
                pb = ps[:].bitcast(BF16)

                def f(pb=pb, t=t):
                    for k in range(8):
                        ins = nc.tensor.transpose(pb[:, k * 128:(k + 1) * 128], xnall[:, t, k * 128:(k + 1) * 128], ident[:])
                    return ins
                kb.op('pe', f, R=[('xnall', t), 'ident'], W=[pk])
                for k in range(8):
                    kb.op('dve', lambda k=k, pb=pb, t=t, v=v: nc.vector.tensor_scalar(
                        out=hT[:, k, t * 128:(t + 1) * 128], in0=pb[:, k * 128:(k + 1) * 128],
                        scalar1=modT[:, sc_off + k, v:v + 1], scalar2=modT[:, sh_off + k, v:v + 1],
                        op0=ALU.mult, op1=ALU.add), R=[pk, 'modT'], W=[('hT', t)])

        eps_t = sb("eps_t", [128, 2], F32)
        kb.op('dve', lambda: nc.vector.memset(eps_t[:, 0:1], RMS_EPS), W=['eps_t'])
        kb.op('dve', lambda: nc.vector.memset(eps_t[:, 1:2], GN_EPS), R=['eps_t'], W=['eps_t'])


        HT_KEYS = [('hT', t) for t in range(NTT)]

        def proj_fm(wsrc, col0, ncols, src, srckeys, kt, evac, blocks=TB):
            for cb in range(0, ncols, 256):
                nc_ = min(256, ncols - cb)
                wt, wk = load_wk(wsrc[:, col0 + cb:col0 + cb + nc_], kt, nc_)
                for mm in range(nc_ // 128):
                    m = (cb // 128) + mm
                    for (tc0, n) in blocks:
                        ps, pk = psum()

                        def f(ps=ps, wt=wt, mm=mm, tc0=tc0, n=n):
                            for k in range(kt):
                                ins = nc.tensor.matmul(ps[:, 0:n], lhsT=wt[:, k, mm * 128:(mm + 1) * 128],
                                                       rhs=src[:, k, tc0:tc0 + n], start=(k == 0), stop=(k == kt - 1))
                            return ins
                        kb.op('pe', f, R=[wk] + list(srckeys), W=[pk])
                        evac(m, tc0, n, ps, pk)

        s5p = din("s5p", [DEPTH, 2, 128, 3, 16])
        s5c = din("s5c", [DEPTH, 2, 128, 2, 16, 16])
        s5ba = din("s5ba", [DEPTH, 16, 128, 4, 32])
        s5dT_d = din("s5dT", [DEPTH, 128, 4])
        bgluT_d = din("bgluT", [DEPTH, 128, 4])
        w_glu = din("w_glu", [DEPTH, 512, 512])

        PI = float(np.pi)

        def dv(fn, R, W):
            kb.op('dve', fn, R=R, W=W)

        def pv(fn, R, W):
            kb.op('pool', fn, R=R, W=W)

        def s5_prep(l):
            kb.dma(prm[:], s5p[l].rearrange("d p a b -> p d a b"), W=['prm'])
            kb.dma(cfl[:], s5c[l].rearrange("d p r a b -> p d r a b"), W=['cfl'])
            kb.dma(s5dT[:], s5dT_d[l], W=['s5dT'])
            kb.dma(bgluT[:], bgluT_d[l], W=['bgluT'])
            S = ['sm']
            for d in range(2):
                lre, lim, ldt = prm[:, d, 0, :], prm[:, d, 1, :], prm[:, d, 2, :]
                dtv, mag, th, tmp, sn, cs, th2 = (sm[:, i, :] for i in range(7))
                kb.op('act', lambda: nc.scalar.activation(out=dtv, in_=ldt, func=AF.Exp), R=['prm'], W=S)
                dv(lambda: nc.vector.tensor_tensor(out=mag, in0=lre, in1=dtv, op=ALU.mult), ['prm'] + S, S)
                kb.op('act', lambda: nc.scalar.activation(out=mag, in_=mag, func=AF.Exp), R=S, W=S)
                dv(lambda: nc.vector.tensor_tensor(out=th, in0=lim, in1=dtv, op=ALU.mult), ['prm'] + S, S)
                for _ in range(6):
                    dv(lambda: nc.vector.tensor_scalar(out=tmp, in0=th, scalar1=PI, scalar2=2 * PI, op0=ALU.is_gt,
                                                       op1=ALU.mult), S, S)
                    dv(lambda: nc.vector.tensor_tensor(out=th, in0=th, in1=tmp, op=ALU.subtract), S, S)
                for _ in range(2):
                    dv(lambda: nc.vector.tensor_scalar(out=tmp, in0=th, scalar1=-PI, scalar2=2 * PI, op0=ALU.is_lt,
                                                       op1=ALU.mult), S, S)
                    dv(lambda: nc.vector.tensor_tensor(out=th, in0=th, in1=tmp, op=ALU.add), S, S)
                dv(lambda: nc.vector.tensor_scalar(out=th2, in0=th, scalar1=PI / 2, scalar2=None, op0=ALU.add), S, S)
                dv(lambda: nc.vector.tensor_scalar(out=tmp, in0=th2, scalar1=PI, scalar2=2 * PI, op0=ALU.is_gt,
                                                   op1=ALU.mult), S, S)
                dv(lambda: nc.vector.tensor_tensor(out=th2, in0=th2, in1=tmp, op=ALU.subtract), S, S)
                kb.op('act', lambda: nc.scalar.activation(out=sn, in_=th, func=AF.Sin), R=S, W=S)
                kb.op('act', lambda: nc.scalar.activation(out=cs, in_=th2, func=AF.Sin), R=S, W=S)
                A = ['APW']
                dv(lambda: nc.vector.tensor_tensor(out=APre[:, d, 0, :], in0=mag, in1=cs, op=ALU.mult), S, A)
                dv(lambda: nc.vector.tensor_tensor(out=APim[:, d, 0, :], in0=mag, in1=sn, op=ALU.mult), S, A)
                t1, t2 = sm[:, 7, :], sm[:, 8, :]
                for k in range(1, 12):
                    re0, im0 = APre[:, d, k - 1, :], APim[:, d, k - 1, :]
                    dv(lambda re0=re0: nc.vector.tensor_tensor(out=t1, in0=re0, in1=re0, op=ALU.mult), A + S, S)
                    dv(lambda im0=im0: nc.vector.tensor_tensor(out=t2, in0=im0, in1=im0, op=ALU.mult), A + S, S)
                    dv(lambda k=k: nc.vector.tensor_tensor(out=APre[:, d, k, :], in0=t1, in1=t2, op=ALU.subtract), S, A)
                    dv(lambda re0=re0, im0=im0: nc.vector.tensor_tensor(out=t1, in0=re0, in1=im0, op=ALU.mult), A + S, S)
                    dv(lambda k=k: nc.vector.tensor_scalar(out=APim[:, d, k, :], in0=t1, scalar1=2.0, scalar2=None,
                                                           op0=ALU.mult), S, A)
                am1, den, fr, fi = sm[:, 7, :], sm[:, 8, :], sm[:, 9, :], sm[:, 10, :]
                t3 = sm[:, 11, :]
                dv(lambda: nc.vector.tensor_scalar(out=am1, in0=APre[:, d, 0, :], scalar1=-1.0, scalar2=None,
                                                   op0=ALU.add), A + S, S)
                dv(lambda: nc.vector.tensor_tensor(out=den, in0=lre, in1=lre, op=ALU.mult), ['prm'] + S, S)
                dv(lambda: nc.vector.tensor_tensor(out=t3, in0=lim, in1=lim, op=ALU.mult), ['prm'] + S, S)
                dv(lambda: nc.vector.tensor_tensor(out=den, in0=den, in1=t3, op=ALU.add), S, S)
                dv(lambda: nc.vector.reciprocal(out=den, in_=den), S, S)
                dv(lambda: nc.vector.tensor_tensor(out=fr, in0=am1, in1=lre, op=ALU.mult), ['prm'] + S, S)
                dv(lambda: nc.vector.tensor_tensor(out=t3, in0=APim[:, d, 0, :], in1=lim, op=ALU.mult), ['prm'] + A + S, S)
                dv(lambda: nc.vector.tensor_tensor(out=fr, in0=fr, in1=t3, op=ALU.add), S, S)
                dv(lambda: nc.vector.tensor_tensor(out=fr, in0=fr, in1=den, op=ALU.mult), S, S)
                dv(lambda: nc.vector.tensor_tensor(out=fi, in0=APim[:, d, 0, :], in1=lre, op=ALU.mult), ['prm'] + A + S, S)
                dv(lambda: nc.vector.tensor_tensor(out=t3, in0=am1, in1=lim, op=ALU.mult), ['prm'] + S, S)
                dv(lambda: nc.vector.tensor_tensor(out=fi, in0=fi, in1=t3, op=ALU.subtract), S, S)
                dv(lambda: nc.vector.tensor_tensor(out=fi, in0=fi, in1=den, op=ALU.mult), S, S)

                def bc16(v):
                    return bass.AP(v.tensor, v.offset, [list(v.ap[0]), list(v.ap[1]), [0, 16]])
                cre, cim = cfl[:, d, 0], cfl[:, d, 1]
                cfr, cfi, ta, tb_ = cfk[:, d, 0], cfk[:, d, 1], cfs[:, 0], cfs[:, 1]
                C = ['cfs']
                dv(lambda: nc.vector.tensor_tensor(out=cfr, in0=cre, in1=bc16(fr), op=ALU.mult), ['cfl'] + S + C, C)
                dv(lambda: nc.vector.tensor_tensor(out=ta, in0=cim, in1=bc16(fi), op=ALU.mult), ['cfl'] + S + C, C)
                dv(lambda: nc.vector.tensor_tensor(out=cfr, in0=cfr, in1=ta, op=ALU.subtract), C, C)
                dv(lambda: nc.vector.tensor_tensor(out=cfi, in0=cre, in1=bc16(fi), op=ALU.mult), ['cfl'] + S + C, C)
                dv(lambda: nc.vector.tensor_tensor(out=ta, in0=cim, in1=bc16(fr), op=ALU.mult), ['cfl'] + S + C, C)
                dv(lambda: nc.vector.tensor_tensor(out=cfi, in0=cfi, in1=ta, op=ALU.add), C, C)
            dv(lambda: nc.vector.tensor_scalar(out=APnim[:], in0=APim[:], scalar1=-1.0, scalar2=None, op0=ALU.mult),
               ['APW'], ['APW'])
            P_ = ['Epw']
            for d in range(2):
                dv(lambda d=d: nc.vector.memset(Epw_re[:, d, 0, :], 1.0), P_, P_)
                dv(lambda d=d: nc.vector.memset(Epw_im[:, d, 0, :], 0.0), P_, P_)
                dv(lambda d=d: nc.vector.tensor_copy(out=Epw_re[:, d, 1, :], in_=APre[:, d, 0, :]), ['APW'] + P_, P_)
                dv(lambda d=d: nc.vector.tensor_copy(out=Epw_im[:, d, 1, :], in_=APim[:, d, 0, :]), ['APW'] + P_, P_)
                t1, t2 = sm[:, 7, :], sm[:, 8, :]
                S = ['sm']
                for j in range(2, 9):
                    ar, ai = Epw_re[:, d, j - 1, :], Epw_im[:, d, j - 1, :]
                    br, bi_ = APre[:, d, 0, :], APim[:, d, 0, :]
                    dv(lambda ar=ar, br=br: nc.vector.tensor_tensor(out=t1, in0=ar, in1=br, op=ALU.mult), P_ + S + ['APW'], S)
                    dv(lambda ai=ai, bi_=bi_: nc.vector.tensor_tensor(out=t2, in0=ai, in1=bi_, op=ALU.mult), P_ + S + ['APW'], S)
                    dv(lambda d=d, j=j: nc.vector.tensor_tensor(out=Epw_re[:, d, j, :], in0=t1, in1=t2, op=ALU.subtract), S + P_, P_)
                    dv(lambda ar=ar, bi_=bi_: nc.vector.tensor_tensor(out=t1, in0=ar, in1=bi_, op=ALU.mult), P_ + S + ['APW'], S)
                    dv(lambda ai=ai, br=br: nc.vector.tensor_tensor(out=t2, in0=ai, in1=br, op=ALU.mult), P_ + S + ['APW'], S)
                    dv(lambda d=d, j=j: nc.vector.tensor_tensor(out=Epw_im[:, d, j, :], in0=t1, in1=t2, op=ALU.add), S + P_, P_)
                for t in range(8):
                    j = t + 1 if d == 0 else 8 - t
                    dv(lambda d=d, t=t, j=j: nc.vector.tensor_copy(out=Eg_re[:, d, t, :], in_=Epw_re[:, d, j, :]), P_, P_)
                    dv(lambda d=d, t=t, j=j: nc.vector.tensor_copy(out=Eg_im[:, d, t, :], in_=Epw_im[:, d, j, :]), P_, P_)

        def bk_scan_ops(d, pi_, XR, XI, key, n, koff=0):
            rev = (d == 1)
            X = [key]
            ops = []

            def lvl(k, dst0, m):
                s = 1 << k
                step = 2 * s
                if m <= 0:
                    return
                if not rev:
                    d0 = dst0
                    s0 = dst0 - s
                else:
                    d0 = n - 1 - (dst0 + step * (m - 1))
                    s0 = d0 + s
                dsl = slice(d0, d0 + step * (m - 1) + 1, step)
                ssl = slice(s0, s0 + step * (m - 1) + 1, step)
                ar = APre[:, d, k + koff, pi_:pi_ + 1]
                ai = APim[:, d, k + koff, pi_:pi_ + 1]
                nai = APnim[:, d, k + koff, pi_:pi_ + 1]
                for (dst, src, sc) in ((XR, XR, ar), (XR, XI, nai), (XI, XI, ar), (XI, XR, ai)):
                    ops.append(lambda dst=dst, src=src, sc=sc: dv(lambda: nc.vector.scalar_tensor_tensor(
                        out=dst[:, dsl], in0=src[:, ssl], scalar=sc, in1=dst[:, dsl], op0=ALU.mult, op1=ALU.add),
                        X + ['APW'], X))
            kmax = 0
            while (2 << kmax) - 1 < n:
                kmax += 1
            for k in range(kmax):
                s = 1 << k
                m = (n - (2 * s - 1) - 1) // (2 * s) + 1
                lvl(k, 2 * s - 1, m)
            for k in range(kmax - 1, -1, -1):
                s = 1 << k
                if n - 1 < 3 * s - 1:
                    continue
                m = (n - 1 - (3 * s - 1)) // (2 * s) + 1
                lvl(k, 3 * s - 1, m)
            return ops

        def s5_mixer(l):
            s5_prep(l)
            def ev_u(m, tc0, n, ps, pk):
                kb.op('act', lambda: nc.scalar.copy(out=uT[:, m, tc0:tc0 + n], in_=ps[:, 0:n]), R=[pk], W=[('uT', m)])
            proj_fm(w_in[l], 0, 512, hT, HT_KEYS, 8, ev_u)
            CH = NT // 8
            pstep = Z[:].ap[0][0]
            zoff = Z[:].offset
            cqoff = CQ[:].offset

            def z3(start, step, m):
                return bass.AP(Z[:].tensor, zoff + start, [[pstep, 128], [2 * CH, 8], [CH, 2], [step, m]])

            def z2(comp, start, step, m):
                return bass.AP(Z[:].tensor, zoff + comp * CH + start, [[pstep, 128], [2 * CH, 8], [step, m]])

            def cq3(k, c, m):
                return bass.AP(CQ[:].tensor, cqoff + k * 24 + c, [[CQ[:].ap[0][0], 128], [3, 8], [0, 2], [0, m]])

            def cq2(k, c, m):
                return bass.AP(CQ[:].tensor, cqoff + k * 24 + c, [[CQ[:].ap[0][0], 128], [3, 8], [0, m]])

            TQ = [None]

            def scan_level(k, dst0, m):
                s_ = 1 << k
                step = 2 * s_
                src0 = dst0 - s_
                T1v = gsc[:, 0:16 * 144].rearrange("p (a c m) -> p a c m", a=8, c=2)[:, :, :, 0:m]
                T2v = T2s[0][:, :, 0:m]
                ZK = ['Z']
                dv(lambda: nc.vector.tensor_tensor(out=T1v, in0=z3(src0, step, m), in1=cq3(k + 3, 0, m), op=ALU.mult),
                   ZK + ['CQ', 'gsc'], ['gsc'])
                dv(lambda: nc.vector.tensor_tensor(out=T2v, in0=z2(1, src0, step, m), in1=cq2(k + 3, 2, m), op=ALU.mult),
                   ZK + ['CQ', TQ[0]], [TQ[0]])
                dv(lambda: nc.vector.tensor_tensor(out=T1v[:, :, 0, :], in0=T1v[:, :, 0, :], in1=T2v, op=ALU.add),
                   ['gsc', TQ[0]], ['gsc'])
                dv(lambda: nc.vector.tensor_tensor(out=T2v, in0=z2(0, src0, step, m), in1=cq2(k + 3, 1, m), op=ALU.mult),
                   ZK + ['CQ', TQ[0]], [TQ[0]])
                dv(lambda: nc.vector.tensor_tensor(out=T1v[:, :, 1, :], in0=T1v[:, :, 1, :], in1=T2v, op=ALU.add),
                   ['gsc', TQ[0]], ['gsc'])
                dv(lambda: nc.vector.tensor_tensor(out=z3(dst0, step, m), in0=z3(dst0, step, m), in1=T1v, op=ALU.add),
                   ZK + ['gsc'], ZK)

            for q in range(4):
                T2s[0] = uT[:, q, :].bitcast(F32).rearrange("p (a m) -> p a m", a=8)
                TQ[0] = ('uT', q)
                kb.op('pool', lambda q=q: nc.gpsimd.tensor_copy(out=uS[:], in_=uT[:, q, :].rearrange("p (c s) -> p s c", s=8)),
                      R=[('uT', q)], W=['uS'])
                kb.op('act', lambda q=q: nc.scalar.activation(out=yacc[:].rearrange("p (s c) -> p s c", s=8), in_=uS[:],
                                                              func=AF.Copy, scale=s5dT[:, q:q + 1]),
                      R=['uS', 's5dT'], W=['yacc'])
                for d in range(2):
                    for c, tab in enumerate((APre, APim, APnim)):
                        ov = bass.AP(CQ[:].tensor, cqoff + d * 3 + c, [[CQ[:].ap[0][0], 128], [24, 12], [6, 4]])
                        pv(lambda ov=ov, tab=tab, d=d, q=q: nc.gpsimd.tensor_copy(out=ov, in_=tab[:, d, :, q * 4:(q + 1) * 4]),
                           ['APW', 'CQ'], ['CQ'])
                UQ = ['uS']
                TK = ['s5tmp']
                for pl in range(4):
                    pi_ = q * 4 + pl
                    j4 = pi_ % 4
                    bi = pi_ % 2
                    XGb, XGk = XG[bi], 'XG%d' % bi
                    kb.dma(Bt[bi][:], s5ba[l, pi_], W=['Bt%d' % bi])
                    kb.op('pool', lambda XGb=XGb: nc.gpsimd.memset(XGb[:], 0.0), W=[XGk])
                    for d in range(2):
                        def e3(tab, d=d, pi_=pi_):
                            v = tab[:, d, 0:8, pi_]
                            return bass.AP(v.tensor, v.offset, [list(v.ap[0]), list(v.ap[1]), [0, 32]])

                        def b3(r, d=d, bi=bi):
                            v = Bt[bi][:, d * 2 + r, :]
                            return bass.AP(v.tensor, v.offset, [list(v.ap[0]), [0, 8], list(v.ap[1])])
                        T0, T1 = s5tmpP[:, 0], s5tmpP[:, 1]
                        RK = ['Epw', 'Bt%d' % bi] + ['s5tmpP']
                        dv(lambda e3=e3, b3=b3: nc.vector.tensor_tensor(out=T0, in0=e3(Epw_re), in1=b3(0), op=ALU.mult), RK, ['s5tmpP'])
                        dv(lambda e3=e3, b3=b3: nc.vector.tensor_tensor(out=T1, in0=e3(Epw_im), in1=b3(1), op=ALU.mult), RK, ['s5tmpP'])
                        dv(lambda d=d, XGb=XGb, j4=j4: nc.vector.tensor_tensor(out=XGb[:, d, 0, :, j4 * 32:(j4 + 1) * 32], in0=T0, in1=T1,
                                                                               op=ALU.subtract), ['s5tmpP', XGk], [XGk])
                        dv(lambda e3=e3, b3=b3: nc.vector.tensor_tensor(out=T0, in0=e3(Epw_re), in1=b3(1), op=ALU.mult), RK, ['s5tmpP'])
                        dv(lambda e3=e3, b3=b3: nc.vector.tensor_tensor(out=T1, in0=e3(Epw_im), in1=b3(0), op=ALU.mult), RK, ['s5tmpP'])
                        dv(lambda d=d, XGb=XGb, j4=j4: nc.vector.tensor_tensor(out=XGb[:, d, 1, :, j4 * 32:(j4 + 1) * 32], in0=T0, in1=T1,
                                                                               op=ALU.add), ['s5tmpP', XGk], [XGk])
                    RWk = 'Rwp%d' % bi
                    kb.op('pool', lambda bi=bi: nc.gpsimd.memset(Rwp[bi][:], 0.0), W=[RWk])
                    for d in range(2):
                        for r in range(2):
                            for gl in range(2):
                                c0 = j4 * 32 + gl * 16
                                pv(lambda d=d, r=r, gl=gl, c0=c0, bi=bi, pi_=pi_: nc.gpsimd.tensor_scalar(
                                    out=Rwp[bi][gl * 64:(gl + 1) * 64, d, r, c0:c0 + 16], in0=cfk[gl * 64:(gl + 1) * 64, d, r, pi_, :],
                                    scalar1=(1.0 if r == 0 else -1.0), scalar2=0.0, op0=ALU.mult, op1=ALU.add), ['cfs', RWk], [RWk])
                    for g4 in range(4):
                        ps, pk = psum()
                        pb = ps[:].bitcast(BF16)

                        def f(pb=pb, g4=g4, XGb=XGb):
                            for i8 in range(8):
                                idx = g4 * 8 + i8
                                ins = nc.tensor.transpose(pb[:, i8 * 128:(i8 + 1) * 128],
                                                          XGb[:, idx // 16, (idx // 8) % 2, idx % 8, :], ident[:])
                            return ins
                        kb.op('pe', f, R=[XGk, 'ident'], W=[pk])
                        kb.op('act', lambda pb=pb, g4=g4: nc.scalar.copy(
                            out=Sx[:, g4 * 8:(g4 + 1) * 8, :].rearrange("p a b -> p (a b)"), in_=pb[:, :]), R=[pk, 'Sx'], W=['Sx'])
                    for dd in range(2):
                        for jh in range(2):
                            ps, pk = psum()

                            def f(ps=ps, dd=dd, jh=jh, XGb=XGb, bi=bi):
                                for jj in range(4):
                                    j = jh * 4 + jj
                                    nc.tensor.matmul(ps[:, jj * 128:(jj + 1) * 128], lhsT=XGb[:, dd, 0, j, :], rhs=Rwp[bi][:, dd, 0, :],
                                                     start=True, stop=False)
                                    ins = nc.tensor.matmul(ps[:, jj * 128:(jj + 1) * 128], lhsT=XGb[:, dd, 1, j, :],
                                                           rhs=Rwp[bi][:, dd, 1, :], start=False, stop=True)
                                return ins
                            kb.op('pe', f, R=[XGk, RWk], W=[pk])
                            kb.op('act', lambda ps=ps, dd=dd, jh=jh, j4=j4: nc.scalar.copy(
                                out=BDq[:, dd, jh * 4:(jh + 1) * 4, j4 * 32:(j4 + 1) * 32],
                                in_=ps[:, :].rearrange("p (a b) -> p a b", b=128)[:, :, j4 * 32:(j4 + 1) * 32]),
                                R=[pk, 'BDq'], W=['BDq'])
                    for d in range(2):
                        for r in range(2):
                            ps, pk = psum()

                            def f(ps=ps, d=d, r=r):
                                if d == 0:
                                    for s_ in range(8):
                                        ins = nc.tensor.matmul(ps[:, 0:CH], lhsT=Sx[:, r * 8 + (7 - s_), :], rhs=uS[:, s_, :],
                                                               start=(s_ == 0), stop=(s_ == 7))
                                else:
                                    for s_ in range(8):
                                        nc.tensor.matmul(ps[:, 0:256], lhsT=Sx[:, 16 + r * 8 + s_, :], rhs=uS[:, s_, 32:CH],
                                                         start=(s_ == 0), stop=(s_ == 7))
                                    for s_ in range(8):
                                        ins = nc.tensor.matmul(ps[:, 256:CH], lhsT=Sx[:, 16 + r * 8 + s_, :], rhs=uS[:, s_, 0:32],
                                                               start=(s_ == 0), stop=(s_ == 7), skip_group_check=True)
                                return ins
                            kb.op('pe', f, R=['Sx'] + UQ, W=[pk])
                            if d == 0:
                                src = ps[:, 0:CH]
                            else:
                                v = ps[:, 0:CH]
                                src = bass.AP(v.tensor, v.offset + CH - 1, [list(v.ap[0]), [-1, CH]])
                            kb.op('act', lambda src=src, pl=pl, d=d, r=r: nc.scalar.copy(out=Z[:, pl, d, r, :], in_=src),
                                  R=[pk, 'Z'], W=['Z'])
                kmax = 0
                while (2 << kmax) - 1 < CH:
                    kmax += 1
                for k in range(kmax):
                    s_ = 1 << k
                    scan_level(k, 2 * s_ - 1, (CH - (2 * s_ - 1) - 1) // (2 * s_) + 1)
                for k in range(kmax - 1, -1, -1):
                    s_ = 1 << k
                    if CH - 1 < 3 * s_ - 1:
                        continue
                    scan_level(k, 3 * s_ - 1, (CH - 1 - (3 * s_ - 1)) // (2 * s_) + 1)
                kb.op('pool', lambda: nc.gpsimd.memset(XsQ[:], 0.0), W=['XsQ'])
                kb.op('act', lambda: nc.scalar.copy(out=XsQ[:, :, 0, :, 1:CH], in_=Z[:, :, 0, :, 0:CH - 1]), R=['Z', 'XsQ'], W=['XsQ'])
                zr = bass.AP(Z[:].tensor, zoff + 2 * CH + CH - 2, [[pstep, 128], [4 * CH, 4], [CH, 2], [-1, CH - 1]])
                kb.op('act', lambda: nc.scalar.copy(out=XsQ[:, :, 1, :, 0:CH - 1], in_=zr), R=['Z', 'XsQ'], W=['XsQ'])
                SK = 'XsQ'
                for pp in range(2):
                    for pl in (2 * pp, 2 * pp + 1):
                        pi_ = q * 4 + pl
                        j4 = pi_ % 4
                        bi = pi_ % 2
                        XGb, XGk = XG[bi], 'XG%d' % bi
                        kb.op('pool', lambda XGb=XGb: nc.gpsimd.memset(XGb[:], 0.0), W=[XGk])
                        for d in range(2):
                            def c3(r, d=d, pi_=pi_):
                                v = cfk[:, d, r, pi_, :]
                                return bass.AP(v.tensor, v.offset, [list(v.ap[0]), [0, 8], list(v.ap[1])])

                            def g3(tab, d=d, pi_=pi_):
                                v = tab[:, d, :, pi_]
                                return bass.AP(v.tensor, v.offset, [list(v.ap[0]), list(v.ap[1]), [0, 16]])
                            G0, G1, G2 = s5tmpP[:, 0, :, 0:16], s5tmpP[:, 1, :, 0:16], s5tmpP[:, 2, :, 0:16]
                            RK = ['Epw', 'cfs', 's5tmpP']
                            dv(lambda c3=c3, g3=g3: nc.vector.tensor_tensor(out=G0, in0=c3(0), in1=g3(Eg_re), op=ALU.mult), RK, ['s5tmpP'])
                            dv(lambda c3=c3, g3=g3: nc.vector.tensor_tensor(out=G1, in0=c3(1), in1=g3(Eg_im), op=ALU.mult), RK, ['s5tmpP'])
                            dv(lambda: nc.vector.tensor_tensor(out=G2, in0=G0, in1=G1, op=ALU.subtract), ['s5tmpP'], ['s5tmpP'])
                            for gl in range(2):
                                c0 = j4 * 32 + gl * 16
                                dv(lambda d=d, gl=gl, c0=c0, XGb=XGb: nc.vector.tensor_copy(
                                    out=XGb[gl * 64:(gl + 1) * 64, d, 0, :, c0:c0 + 16], in_=G2[gl * 64:(gl + 1) * 64]), ['s5tmpP', XGk], [XGk])
                            dv(lambda c3=c3, g3=g3: nc.vector.tensor_tensor(out=G0, in0=c3(0), in1=g3(Eg_im), op=ALU.mult), RK, ['s5tmpP'])
                            dv(lambda c3=c3, g3=g3: nc.vector.tensor_tensor(out=G1, in0=c3(1), in1=g3(Eg_re), op=ALU.mult), RK, ['s5tmpP'])
                            dv(lambda: nc.vector.tensor_tensor(out=G2, in0=G0, in1=G1, op=ALU.add), ['s5tmpP'], ['s5tmpP'])
                            for gl in range(2):
                                c0 = j4 * 32 + gl * 16
                                dv(lambda d=d, gl=gl, c0=c0, XGb=XGb: nc.vector.tensor_scalar(
                                    out=XGb[gl * 64:(gl + 1) * 64, d, 1, :, c0:c0 + 16], in0=G2[gl * 64:(gl + 1) * 64], scalar1=-1.0,
                                    scalar2=None, op0=ALU.mult), ['s5tmpP', XGk], [XGk])
                    for t in range(8):
                        ps, pk = psum()

                        def f(ps=ps, t=t, pp=pp):
                            first = True
                            if pp == 0:
                                for s_ in range(8):
                                    if s_ < t:
                                        ws = [BDq[:, 0, t - s_, :]]
                                    elif s_ > t:
                                        ws = [BDq[:, 1, s_ - t, :]]
                                    else:
                                        ws = [BDq[:, 0, 0, :], BDq[:, 1, 0, :]]
                                    for w_ in ws:
                                        nc.tensor.matmul(ps[:, 0:CH], lhsT=w_, rhs=uS[:, s_, :], start=first, stop=False)
                                        first = False
                            for pl in (2 * pp, 2 * pp + 1):
                                XGb = XG[pl % 2]
                                for r in range(2):
                                    nc.tensor.matmul(ps[:, 0:CH], lhsT=XGb[:, 0, r, t, :], rhs=XsQ[:, pl, 0, r, :], start=first, stop=False)
                                    first = False
                                for r in range(2):
                                    nc.tensor.matmul(ps[:, 32:CH], lhsT=XGb[:, 1, r, t, :], rhs=XsQ[:, pl, 1, r, 0:256],
                                                     start=False, stop=False)
                                    ins = nc.tensor.matmul(ps[:, 0:32], lhsT=XGb[:, 1, r, t, :], rhs=XsQ[:, pl, 1, r, 256:CH],
                                                           start=False, stop=(r == 1 and pl == 2 * pp + 1))
                            return ins
                        kb.op('pe', f, R=['XG0', 'XG1', SK, 'BDq', 'uS'], W=[pk])
                        dv(lambda ps=ps, t=t: nc.vector.tensor_tensor(out=yacc[:, t * CH:(t + 1) * CH], in0=yacc[:, t * CH:(t + 1) * CH],
                                                                      in1=ps[:, 0:CH], op=ALU.add), [pk, 'yacc'], ['yacc'])
                kb.op('act', lambda: nc.scalar.activation(out=gsc[:], in_=yacc[:], func=AF.Square), R=['yacc', 'gsc'], W=['gsc'])
                dv(lambda: nc.vector.tensor_scalar(out=gsc[:], in0=gsc[:], scalar1=0.044715, scalar2=1.0, op0=ALU.mult,
                                                   op1=ALU.add), ['gsc'], ['gsc'])
                dv(lambda: nc.vector.tensor_tensor(out=gsc[:], in0=gsc[:], in1=yacc[:], op=ALU.mult), ['yacc', 'gsc'], ['gsc'])
                kb.op('act', lambda: nc.scalar.activation(out=gsc[:], in_=gsc[:], func=AF.Sigmoid,
                                                          scale=2.0 * float(np.sqrt(2.0 / np.pi))), R=['gsc'], W=['gsc'])
                dv(lambda q=q: nc.vector.tensor_tensor(out=uT[:, q, :].rearrange("p (c s) -> p s c", s=8),
                                                       in0=yacc[:].rearrange("p (s c) -> p s c", s=8),
                                                       in1=gsc[:].rearrange("p (s c) -> p s c", s=8), op=ALU.mult),
                   ['yacc', 'gsc', 'uS'], [('uT', q)])
            UK = [('uT', q) for q in range(4)]
            cnt = [0]

            def ev_glu(m, tc0, n, ps, pk):
                i = cnt[0] % 2
                cnt[0] += 1
                kb.op('act', lambda: nc.scalar.activation(out=sg[i][:, 0:n], in_=ps[:, 0:n], func=AF.Sigmoid,
                                                          bias=bgluT[:, m:m + 1]), R=[pk, 'bgluT'], W=['gsc'])
                dv(lambda: nc.vector.tensor_tensor(out=ys5T[:, m, tc0:tc0 + n], in0=uT[:, m, tc0:tc0 + n],
                                                   in1=sg[i][:, 0:n], op=ALU.mult), ['gsc', ('uT', m)], [('ys5T', m)])
            proj_fm(w_glu[l], 0, 512, uT, UK, 4, ev_glu)


        w_rot = din("w_rot", [DEPTH, D, 512])
        thB_d = din("thB", [DEPTH, 128, 8])
        thQ_d = din("thQ", [DEPTH, 128, 4])
        rotc_d = din("rotc", [128, NT])
        rots_d = din("rots", [128, NT])
        retc_d = din("retc", [128, 2, 128])
        iq_d = din("iq", [128, 2, 128])
        ek_d = din("ek", [128, 2])
        onesF = sb("onesF", [128, 128], F32)
        kb.op('pool', lambda: nc.gpsimd.memset(onesF[:], 1.0 / 128.0), W=['onesF'])

        def ret_mixer(l):
            T_ = ['rtab']
            kb.dma(thB[:], thB_d[l], W=['thB'])
            kb.dma(thQ[:], thQ_d[l], W=['thQ'])
            kb.dma(retc[:], retc_d[:, :, :], W=['retc'])
            kb.dma(iq[:], iq_d[:, :, :], W=['iq'])
            kb.dma(ek[:], ek_d[:, :], W=['ek'])
            kb.dma(rotc[:], rotc_d[:, :], W=['rotc'])
            kb.dma(rots[:], rots_d[:, :], W=['rots'])
            for (src, dst, key) in ((thB, lgB, 'thB'), (thQ, lgQ, 'thQ')):
                kb.op('act', lambda src=src, dst=dst: nc.scalar.activation(out=dst[:], in_=src[:], func=AF.Exp, scale=-1.0),
                      R=[key], W=T_)
                dv(lambda dst=dst: nc.vector.tensor_scalar(out=dst[:], in0=dst[:], scalar1=1.0, scalar2=None, op0=ALU.add),
                   T_, T_)
                kb.op('act', lambda dst=dst: nc.scalar.activation(out=dst[:], in_=dst[:], func=AF.Ln), R=T_, W=T_)
                dv(lambda dst=dst: nc.vector.tensor_scalar(out=dst[:], in0=dst[:], scalar1=-1.0, scalar2=None, op0=ALU.mult),
                   T_, T_)
            for d in range(2):
                for h in range(4):
                    c = d * 4 + h
                    kb.op('act', lambda d=d, c=c: nc.scalar.activation(out=Dm[:, c, :], in_=retc[:, d, :], func=AF.Exp,
                                                                       scale=lgB[:, c:c + 1]), R=T_ + ['retc'], W=T_)
                for t in range(2):
                    c = d * 2 + t
                    kb.op('act', lambda d=d, t=t, c=c: nc.scalar.activation(out=qdtab[:, d, t, :], in_=iq[:, d, :],
                                                                            func=AF.Exp, scale=lgQ[:, c:c + 1]),
                          R=T_ + ['iq'], W=T_)
                kb.op('act', lambda d=d: nc.scalar.activation(out=kd[:, d * 4:(d + 1) * 4], in_=lgB[:, d * 4:(d + 1) * 4],
                                                              func=AF.Exp, scale=ek[:, d:d + 1]), R=T_ + ['ek'], W=T_)
            kb.op('act', lambda: nc.scalar.activation(out=cdecQ[:], in_=lgQ[:], func=AF.Exp, scale=128.0), R=T_, W=T_)

            def mk_ev(dst, key, scale):
                def ev(m, tc0, n, ps, pk):
                    kb.op('act', lambda: nc.scalar.activation(out=dst[:, m, tc0:tc0 + n], in_=ps[:, 0:n], func=AF.Copy,
                                                              scale=scale), R=[pk], W=[(key, m)])
                return ev
            for (dst, key, col0, rc0, scale) in ((kT, 'kT', 512, 0, 0.125), (qT, 'qT', 2304, 256, 1.0)):
                proj_fm(w_in[l], col0, 256, hT, HT_KEYS, 8, mk_ev(dst, key, scale))
                proj_fm(w_rot[l], rc0, 256, hT, HT_KEYS, 8, mk_ev(tmpT, 'tmpT', scale))
                for m in range(2):
                    dv(lambda m=m, dst=dst: nc.vector.tensor_tensor(out=dst[:, m, :], in0=dst[:, m, :], in1=rotc[:],
                                                                    op=ALU.mult), [(key, m), 'rotc'], [(key, m)])
                    dv(lambda m=m: nc.vector.tensor_tensor(out=tmpT[:, m, :], in0=tmpT[:, m, :], in1=rots[:],
                                                           op=ALU.mult), [('tmpT', m), 'rots'], [('tmpT', m)])
                    dv(lambda m=m, dst=dst: nc.vector.tensor_tensor(out=dst[:, m, :], in0=dst[:, m, :], in1=tmpT[:, m, :],
                                                                    op=ALU.add), [(key, m), ('tmpT', m)], [(key, m)])
            for cb in range(2):
                wt, wk = load_wk(w_in[l][:, 768 + cb * 256:768 + (cb + 1) * 256], 8, 256)
                for n in range(NTT):
                    ps, pk = psum()

                    def f(ps=ps, wt=wt, n=n):
                        for k in range(8):
                            ins = nc.tensor.matmul(ps[:, 0:256], lhsT=hT[:, k, n * 128:(n + 1) * 128], rhs=wt[:, k, 0:256],
                                                   start=(k == 0), stop=(k == 7))
                        return ins
                    kb.op('pe', f, R=[wk, ('hT', n)], W=[pk])
                    kb.op('act', lambda ps=ps, n=n, cb=cb: nc.scalar.copy(out=vtok[:, n, cb * 256:(cb + 1) * 256],
                                                                          in_=ps[:, 0:256]), R=[pk], W=[('vtok', n, cb)])
        def ret_part2(l):
            T_ = ['rtab']
            rb = [0]
            chains = [(t, d) for t in range(2) for d in range(2)]
            orders = {0: list(range(NTT)), 1: [1, 0] + list(range(NTT - 1, 1, -1))}
            written = set()
            for c in range(4):
                dv(lambda c=c: nc.vector.memset(sst[c][:], 0.0), ['sst%d' % c], ['sst%d' % c])
            steps = [(idx, c) for idx in range(NTT) for c in range(4)]
            CX = {}

            def sA(g):
                idx, c = steps[g]
                t, d = chains[c]
                n = orders[d][idx]
                tsl = slice(n * 128, (n + 1) * 128)
                ps, pk = psum()
                pb = ps[:].bitcast(BF16)
                kb.op('pe', lambda: nc.tensor.transpose(pb[:, 0:128], kT[:, t, tsl], ident[:]), R=[('kT', t), 'ident'], W=[pk])
                CX[g] = dict(idx=idx, c=c, t=t, d=d, n=n, tsl=tsl, i=g % 6, pb=pb, pk=pk)

            def sB(g):
                x = CX[g]
                t, d, i, tsl = x['t'], x['d'], x['i'], x['tsl']
                kdv = kd[:, d * 4 + 2 * t:d * 4 + 2 * t + 2]
                kdb = bass.AP(kdv.tensor, kdv.offset, [list(kdv.ap[0]), [1, 2], [0, 64]])
                dv(lambda: nc.vector.tensor_tensor(
                    out=kdec[i][:].rearrange("p (a b) -> p a b", a=2), in0=x['pb'][:, 0:128].rearrange("p (a b) -> p a b", a=2),
                    in1=kdb, op=ALU.mult), [x['pk']] + T_, ['kdec%d' % i])
                dv(lambda: nc.vector.tensor_tensor(out=qdt[i][:], in0=qT[:, t, tsl], in1=qdtab[:, d, t, :], op=ALU.mult),
                   [('qT', t)] + T_, ['qdt%d' % i])

            def sC(g):
                x = CX[g]
                t, i, tsl, n, idx = x['t'], x['i'], x['tsl'], x['n'], x['idx']
                x['ps1'] = []
                for hh in range(2):
                    psl = slice(64 * hh, 64 * hh + 64)
                    ps1, pk1 = psum()
                    kb.op('pe', lambda ps1=ps1, psl=psl: nc.tensor.matmul(
                        ps1[:, 0:128], lhsT=kT[psl, t, tsl], rhs=qT[psl, t, tsl], start=True, stop=True),
                        R=[('kT', t), ('qT', t)], W=[pk1])
                    x['ps1'].append((ps1, pk1))
                if idx < NTT - 1:
                    ps3, pk3 = psum()
                    kb.op('pe', lambda: nc.tensor.matmul(ps3[:, 0:256], lhsT=kdec[i][:], rhs=vtok[:, n, t * 256:(t + 1) * 256],
                                                         start=True, stop=True), R=['kdec%d' % i, ('vtok', n, t)], W=[pk3])
                    x['ps3'] = (ps3, pk3)

            def sD(g):
                x = CX[g]
                t, d, i, tsl, n, idx, c = x['t'], x['d'], x['i'], x['tsl'], x['n'], x['idx'], x['c']
                x['ps2'] = []
                for hh in range(2):
                    h = 2 * t + hh
                    psl = slice(64 * hh, 64 * hh + 64)
                    ps1, pk1 = x['ps1'][hh]
                    dv(lambda ps1=ps1, hh=hh, h=h: nc.vector.tensor_tensor(
                        out=pT[i][:, hh, :], in0=ps1[:, 0:128], in1=Dm[:, d * 4 + h, :], op=ALU.mult),
                       [pk1] + T_, [('pT%d' % i, hh)])
                    ps2, pk2 = psum()

                    def f(ps2=ps2, h=h, hh=hh, psl=psl):
                        ins = nc.tensor.matmul(ps2[:, 0:128], lhsT=vtok[:, n, h * 128:(h + 1) * 128],
                                               rhs=pT[i][:, hh, :], start=True, stop=(idx == 0))
                        if idx > 0:
                            ins = nc.tensor.matmul(ps2[:, 0:128], lhsT=sbf[c][psl, hh * 128:(hh + 1) * 128],
                                                   rhs=qdt[i][psl, :], start=False, stop=True)
                        return ins
                    kb.op('pe', f, R=[('vtok', n, h // 2), ('pT%d' % i, hh), 'sbf%d' % c, 'qdt%d' % i], W=[pk2])
                    x['ps2'].append((ps2, pk2))
                if idx < NTT - 1:
                    ps3, pk3 = x['ps3']
                    dv(lambda: nc.vector.scalar_tensor_tensor(
                        out=sst[c][:], in0=sst[c][:], scalar=cdecQ[:, d * 2 + t:d * 2 + t + 1], in1=ps3[:, 0:256],
                        op0=ALU.mult, op1=ALU.add), [pk3, 'sst%d' % c] + T_, ['sst%d' % c])
                    kb.op('act', lambda: nc.scalar.copy(out=sbf[c][:], in_=sst[c][:]), R=['sst%d' % c], W=['sbf%d' % c])

            def sE(g):
                x = CX.pop(g)
                t, tsl, n = x['t'], x['tsl'], x['n']
                for hh in range(2):
                    h = 2 * t + hh
                    ps2, pk2 = x['ps2'][hh]
                    if (h, n) not in written:
                        written.add((h, n))
                        kb.op('act', lambda ps2=ps2, h=h: nc.scalar.copy(out=oT[:, h, tsl], in_=ps2[:, 0:128]),
                              R=[pk2], W=[('oT', h, n)])
                    else:
                        dv(lambda ps2=ps2, h=h: nc.vector.tensor_tensor(
                            out=oT[:, h, tsl], in0=oT[:, h, tsl], in1=ps2[:, 0:128], op=ALU.add),
                           [pk2, ('oT', h, n)], [('oT', h, n)])

            NS = len(steps)
            for g in range(NS + 4):
                for st_, off in ((sE, 4), (sD, 3), (sC, 2), (sB, 1), (sA, 0)):
                    if 0 <= g - off < NS:
                        st_(g - off)
            units = [(h, bi_, tc0, n) for h in range(4) for bi_, (tc0, n) in enumerate(TB)]
            WT = {}
            NU = len(units)

            def ctx_(j):
                h, bi_, tc0, n = units[j]
                OK_ = [('oT', h, tt) for tt in range(tc0 // 128, (tc0 + n) // 128)]
                return h, tc0, n, OK_, oT[:, h, tc0:tc0 + n], sg[j % 3], 'sg%d' % (j % 3), sg[3 + j % 2], 'sg%d' % (3 + j % 2)

            def st1(j):
                h, tc0, n, OK_, osl, a_, ak, b_, bk_ = ctx_(j)
                if h not in WT:
                    WT[h] = load_wk(w_in[l][:, 2560 + h * 128:2560 + (h + 1) * 128], 8, 128)
                ps, pk = psum()
                kb.op('pe', lambda: nc.tensor.matmul(ps[:, 0:n], lhsT=onesF[:], rhs=osl, start=True, stop=True),
                      R=OK_ + ['onesF'], W=[pk])
                dv(lambda: nc.vector.tensor_tensor(out=osl, in0=osl, in1=ps[:, 0:n], op=ALU.subtract), OK_ + [pk], OK_)

            def st2(j):
                h, tc0, n, OK_, osl, a_, ak, b_, bk_ = ctx_(j)
                dv(lambda: nc.vector.tensor_tensor(out=a_[:, 0:n], in0=osl, in1=osl, op=ALU.mult), OK_ + [ak], [ak])
                ps, pk = psum()
                kb.op('pe', lambda: nc.tensor.matmul(ps[:, 0:n], lhsT=onesF[:], rhs=a_[:, 0:n], start=True, stop=True),
                      R=[ak, 'onesF'], W=[pk])
                kb.op('act', lambda: nc.scalar.activation(out=a_[:, 0:n], in_=ps[:, 0:n], func=AF.Sqrt, bias=eps_t[:, 1:2]),
                      R=[pk, 'eps_t', ak], W=[ak])

            def st3(j):
                h, tc0, n, OK_, osl, a_, ak, b_, bk_ = ctx_(j)
                dv(lambda: nc.vector.reciprocal(out=a_[:, 0:n], in_=a_[:, 0:n]), [ak], [ak])
                dv(lambda: nc.vector.tensor_tensor(out=osl, in0=osl, in1=a_[:, 0:n], op=ALU.mult), OK_ + [ak], OK_)
                wt, wk = WT[h]
                ps, pk = psum()

                def f():
                    for k in range(8):
                        ins = nc.tensor.matmul(ps[:, 0:n], lhsT=wt[:, k, 0:128], rhs=hT[:, k, tc0:tc0 + n],
                                               start=(k == 0), stop=(k == 7))
                    return ins
                kb.op('pe', f, R=[wk] + HT_KEYS, W=[pk])
                kb.op('act', lambda: nc.scalar.activation(out=b_[:, 0:n], in_=ps[:, 0:n], func=AF.Silu), R=[pk, bk_], W=[bk_])

            def st4(j):
                h, tc0, n, OK_, osl, a_, ak, b_, bk_ = ctx_(j)
                dv(lambda: nc.vector.tensor_tensor(out=yretT[:, h, tc0:tc0 + n], in0=osl, in1=b_[:, 0:n], op=ALU.mult),
                   OK_ + [bk_], [('yretT', h)])

            for j in range(NU + 3):
                if j < NU:
                    st1(j)
                if 0 <= j - 1 < NU:
                    st2(j - 1)
                if 0 <= j - 2 < NU:
                    st3(j - 2)
                if 0 <= j - 3 < NU:
                    st4(j - 3)

        nab_d = din("nab", [DEPTH, 8, 128, 15 * 64])
        colmask_d = din("colmask", [128, 64])
        onesB = sb("onesB", [128, 128], BF16)
        kb.op('pool', lambda: nc.gpsimd.memset(onesB[:], 1.0), W=['onesB'])

        def _os_skip():
            import os
            return os.environ.get('NA_SKIPCTX', '0') == '1'

        def na_mixer(l, need_ctx):
            kb.dma(colmask[:], colmask_d[:, :], W=['colmask'])
            cmb = bass.AP(colmask[:].tensor, colmask[:].offset, [list(colmask[:].ap[0]), [0, 15], [1, 64]])
            for h in range(8):
                i = 0
                kb.dma(nst[i], nab_d[l, h], W=['nst0'])
                kb.op('act', lambda i=i: nc.scalar.activation(out=nst[i], in_=nst[i], func=AF.Exp),
                      R=['nst0'], W=['nst0'])
                dv(lambda i=i, h=h: nc.vector.tensor_tensor(out=TT[:, h, :].rearrange("p (a b) -> p a b", b=64),
                                                            in0=nst[i].rearrange("p (a b) -> p a b", b=64), in1=cmb,
                                                            op=ALU.mult), ['nst0', 'colmask'], ['TT'])

            def mk_ev(dst, key, scale):
                def ev(m, tc0, n, ps, pk):
                    kb.op('act', lambda: nc.scalar.activation(out=dst[:, m, tc0:tc0 + n], in_=ps[:, 0:n], func=AF.Copy,
                                                              scale=scale), R=[pk], W=[(key, m)])
                return ev
            proj_fm(w_in[l], 1280, 512, hT, HT_KEYS, 8, mk_ev(nkT, 'nkT', 1.0))
            proj_fm(w_in[l], 3072, 512, hT, HT_KEYS, 8, mk_ev(nqT, 'nqT', 0.125))
            for cb in range(2):
                wt, wk = load_wk(w_in[l][:, 1792 + cb * 256:1792 + (cb + 1) * 256], 8, 256)
                for n in range(NTT):
                    ps, pk = psum()

                    def f(ps=ps, wt=wt, n=n):
                        for k in range(8):
                            ins = nc.tensor.matmul(ps[:, 0:256], lhsT=hT[:, k, n * 128:(n + 1) * 128], rhs=wt[:, k, 0:256],
                                                   start=(k == 0), stop=(k == 7))
                        return ins
                    kb.op('pe', f, R=[wk, ('hT', n)], W=[pk])
                    kb.op('act', lambda ps=ps, n=n, cb=cb: nc.scalar.copy(out=nvtok[:, n, cb * 256:(cb + 1) * 256],
                                                                          in_=ps[:, 0:256]), R=[pk], W=[('nvtok', n, cb)])
            import os as _os
            NR = int(_os.environ.get('NA_ROWS', '32'))
            VK_ = lambda t: [('nvtok', n, t // 2) for n in range(NTT)]

            def ctx_queries(h):
                t, hh = h // 2, h % 2
                psl = slice(64 * hh, 64 * hh + 64)
                KQ = [('nkT', t), ('nqT', t)]
                ps, pk = psum()

                def f(ps=ps):
                    for j in range(2):
                        ins = nc.tensor.matmul(ps[:, j * 256:(j + 1) * 256], lhsT=nkT[psl, t, j * 128:(j + 1) * 128],
                                               rhs=nqT[psl, t, 0:256], start=True, stop=True)
                    return ins
                kb.op('pe', f, R=KQ, W=[pk])
                kb.op('act', lambda: nc.scalar.activation(out=pc[:], in_=ps[:, 0:512], func=AF.Exp), R=[pk, 'pc'], W=['pc'])
                pso, pko = psum()
                psd, pkd = psum()

                def f(pso=pso, psd=psd):
                    for j in range(2):
                        nc.tensor.matmul(pso[:, 0:256], lhsT=nvtok[:, j, t * 128:(t + 1) * 128],
                                         rhs=pc[:, j * 256:(j + 1) * 256], start=(j == 0), stop=(j == 1))
                    for j in range(2):
                        ins = nc.tensor.matmul(psd[:, 0:256], lhsT=onesB[:], rhs=pc[:, j * 256:(j + 1) * 256],
                                               start=(j == 0), stop=(j == 1))
                    return ins
                kb.op('pe', f, R=VK_(t) + ['pc', 'onesB'], W=[pko, pkd])
                dv(lambda: nc.vector.reciprocal(out=rcc[psl, 0:256], in_=psd[psl, 0:256]), [pkd, 'rcc'], ['rcc'])
                dv(lambda: nc.vector.tensor_tensor(out=ynaT[psl, t, 0:256], in0=pso[psl, 0:256], in1=rcc[psl, 0:256],
                                                   op=ALU.mult), [pko, 'rcc'], [('ynaT', t)])

            units = [(h, r) for h in range(8) for r in range(NR)]
            U = {}

            def s_unit(k):
                h, r = units[k]
                t, hh = h // 2, h % 2
                psl = slice(64 * hh, 64 * hh + 64)
                rs = min(max(r - 4, 0), 24)
                q0 = NCTX + r * 64
                rows = list(range(rs, rs + 8))
                par_rows = [[rr for rr in rows if rr % 2 == p] for p in range(2)]
                ps, pk = psum()

                def f():
                    for p in range(2):
                        for sl_, rr in enumerate(par_rows[p]):
                            k0 = NCTX + rr * 64
                            nc.tensor.matmul(ps[64 * p:64 * p + 64, sl_ * 64:(sl_ + 1) * 64], lhsT=nkT[psl, t, k0:k0 + 64],
                                             rhs=nqT[psl, t, q0:q0 + 64], start=True, stop=True)
                    for j in range(2):
                        ins = nc.tensor.matmul(ps[:, 256 + j * 64:256 + (j + 1) * 64], lhsT=nkT[psl, t, j * 128:(j + 1) * 128],
                                               rhs=nqT[psl, t, q0:q0 + 64], start=True, stop=True)
                    return ins
                kb.op('pe', f, R=[('nkT', t), ('nqT', t)], W=[pk])
                U[k] = (ps, pk, par_rows, q0, psl, t, h, r)

            def b_unit(k):
                ps, pk, par_rows, q0, psl, t, h, r = U[k]
                i = k % 3
                kb.op('act', lambda: nc.scalar.activation(out=e1[i][:], in_=ps[:, 0:256], func=AF.Exp),
                      R=[pk, 'e1%d' % i], W=['e1%d' % i])
                kb.op('act', lambda: nc.scalar.activation(out=p1[i][:, 256:384], in_=ps[:, 256:384], func=AF.Exp),
                      R=[pk], W=[('p1', i, 'c')])
                for p in range(2):
                    i0 = par_rows[p][0] - r + 7
                    tv = TT[64 * p:64 * p + 64, h, i0 * 64:(i0 + 7) * 64].rearrange("p (a b) -> p a b", b=64)[:, 0:7:2, :]
                    eng_ = (nc.vector, 'dve') if p == 0 else (nc.gpsimd, 'pool')
                    kb.op(eng_[1], lambda p=p, tv=tv, eng_=eng_: eng_[0].tensor_tensor(
                        out=p1[i][64 * p:64 * p + 64, 0:256].rearrange("p (a b) -> p a b", b=64),
                        in0=e1[i][64 * p:64 * p + 64, :].rearrange("p (a b) -> p a b", b=64), in1=tv, op=ALU.mult),
                        R=['e1%d' % i, 'TT'], W=[('p1', i, p)])

            def c_unit(k):
                ps, pk, par_rows, q0, psl, t, h, r = U[k]
                i = k % 3
                pso, pko = psum()
                psb, pkb = psum()

                def f():
                    for p, acc in ((0, pso), (1, psb)):
                        for sl_, rr in enumerate(par_rows[p]):
                            nc.tensor.matmul(acc[:, 0:64], lhsT=nvtok[64 * p:64 * p + 64, 2 + rr // 2, t * 128:(t + 1) * 128],
                                             rhs=p1[i][64 * p:64 * p + 64, sl_ * 64:(sl_ + 1) * 64], start=(sl_ == 0),
                                             stop=(p == 1 and sl_ == 3))
                    for j in range(2):
                        nc.tensor.matmul(pso[:, 0:64], lhsT=nvtok[:, j, t * 128:(t + 1) * 128],
                                         rhs=p1[i][:, 256 + j * 64:256 + (j + 1) * 64], start=False, stop=(j == 1))
                    for j in range(6):
                        ins = nc.tensor.matmul(pso[:, 64:128], lhsT=onesB[:], rhs=p1[i][:, j * 64:(j + 1) * 64],
                                               start=(j == 0), stop=(j == 5), skip_group_check=True)
                    return ins
                kb.op('pe', f, R=VK_(t) + [('p1', i, 0), ('p1', i, 1), ('p1', i, 'c'), 'onesB'], W=[pko, pkb])
                U[k] = U[k] + (pso, pko, psb, pkb)

            def d_unit(k):
                ps, pk, par_rows, q0, psl, t, h, r, pso, pko, psb, pkb = U.pop(k)
                hh = h % 2
                i = k % 3
                rk = 'rc%d' % i
                nk_, dk_ = ('numb', hh), ('denb', hh)
                rr_ = r % 16
                kb.op('act', lambda: nc.scalar.copy(out=rc[i][psl, 0:64], in_=psb[psl, 0:64]), R=[pkb, rk], W=[rk])
                kb.op('act', lambda: nc.scalar.copy(out=denb[psl, rr_ * 64:(rr_ + 1) * 64], in_=pso[psl, 64:128]),
                      R=[pko, dk_], W=[dk_] + (['nst0'] if k < 2 else []))
                dv(lambda: nc.vector.tensor_tensor(out=numb[psl, rr_ * 64:(rr_ + 1) * 64], in0=pso[psl, 0:64], in1=rc[i][psl, 0:64],
                                                   op=ALU.add), [pko, rk, nk_], [nk_])
                if rr_ == 15 or r == NR - 1:
                    nn = (rr_ + 1) * 64
                    c0 = NCTX + (r - rr_) * 64
                    dv(lambda: nc.vector.reciprocal(out=denb[psl, 0:nn], in_=denb[psl, 0:nn]), [dk_], [dk_])
                    dv(lambda: nc.vector.tensor_tensor(out=ynaT[psl, t, c0:c0 + nn], in0=numb[psl, 0:nn],
                                                       in1=denb[psl, 0:nn], op=ALU.mult), [dk_, nk_], [('ynaT', t)])

            if need_ctx and not _os_skip():
                for h in range(8):
                    ctx_queries(h)
            nu = len(units)
            for j in range(nu + 3):
                if j < nu:
                    s_unit(j)
                if 0 <= j - 1 < nu:
                    b_unit(j - 1)
                if 0 <= j - 2 < nu:
                    c_unit(j - 2)
                if 0 <= j - 3 < nu:
                    d_unit(j - 3)

        w_bs = [din("w_b%d" % i, [DEPTH, 512, D]) for i in range(3)]
        w_out_d = din("w_out", [DEPTH, D, D])
        w_fg = din("w_fg", [DEPTH, D, FFH])
        w_fu = din("w_fu", [DEPTH, D, FFH])
        w_fd = din("w_fd", [DEPTH, FFH, D])
        fnrow_d = din("fnrow", [128, D])
        fnrow = sb("fnrow_t", [128, D], F32)
        kb.dma(fnrow[:], fnrow_d[:, :], W=['fnrow'])

        def blocks_of(l):
            return TB if l < DEPTH - 1 else TB[1:]

        def merge_m(l):
            ys = ((ys5T, 'ys5T'), (yretT, 'yretT'), (ynaT, 'ynaT'))
            c = [0]
            for j in range(8):
                for br in range(3):
                    wg, wgk = load_wk(w_in[l][:, 3584 + br * 1024 + j * 128:3584 + br * 1024 + (j + 1) * 128], 8, 128)
                    wb, wbk = load_wk(w_bs[br][l][:, j * 128:(j + 1) * 128], 4, 128)
                    yt_, yk = ys[br]
                    YK = [(yk, m) for m in range(4)]
                    for (tc0, n) in blocks_of(l):
                        i = c[0] % 2
                        c[0] += 1
                        psg, pkg = psum()
                        psb, pkb = psum()

                        def f(psg=psg, wg=wg, tc0=tc0, n=n):
                            for k in range(8):
                                ins = nc.tensor.matmul(psg[:, 0:n], lhsT=wg[:, k, :], rhs=hT[:, k, tc0:tc0 + n],
                                                       start=(k == 0), stop=(k == 7))
                            return ins
                        kb.op('pe', f, R=[wgk] + HT_KEYS, W=[pkg])

                        def f(psb=psb, wb=wb, yt_=yt_, tc0=tc0, n=n):
                            for k in range(4):
                                ins = nc.tensor.matmul(psb[:, 0:n], lhsT=wb[:, k, :], rhs=yt_[:, k, tc0:tc0 + n],
                                                       start=(k == 0), stop=(k == 3))
                            return ins
                        kb.op('pe', f, R=[wbk] + YK, W=[pkb])
                        kb.op('act', lambda psg=psg, i=i, n=n: nc.scalar.activation(out=sg[i][:, 0:n], in_=psg[:, 0:n],
                                                                                    func=AF.Sigmoid),
                              R=[pkg, 'sg%d' % i], W=['sg%d' % i])
                        if br == 0:
                            dv(lambda psb=psb, i=i, tc0=tc0, n=n: nc.vector.tensor_tensor(
                                out=macc[:, tc0:tc0 + n], in0=sg[i][:, 0:n], in1=psb[:, 0:n], op=ALU.mult),
                               [pkb, 'sg%d' % i, 'macc'], ['macc'])
                        else:
                            dv(lambda psb=psb, i=i, n=n: nc.vector.tensor_tensor(
                                out=sg[i][:, 0:n], in0=sg[i][:, 0:n], in1=psb[:, 0:n], op=ALU.mult),
                               [pkb, 'sg%d' % i], ['sg%d' % i])
                            if br == 1:
                                dv(lambda i=i, tc0=tc0, n=n: nc.vector.tensor_tensor(
                                    out=macc[:, tc0:tc0 + n], in0=macc[:, tc0:tc0 + n], in1=sg[i][:, 0:n], op=ALU.add),
                                   ['sg%d' % i, 'macc'], ['macc'])
                            else:
                                dv(lambda i=i, tc0=tc0, n=n, j=j: nc.vector.tensor_tensor(
                                    out=mT[:, j, tc0:tc0 + n], in0=macc[:, tc0:tc0 + n], in1=sg[i][:, 0:n], op=ALU.add),
                                   ['sg%d' % i, 'macc'], [('mT', j)])

        def delta_tiles(tc0, n, src_dram, after, dsr, doff, dkey):
            out_ = []
            for t in range(tc0 // 128, (tc0 + n) // 128):
                def th(t=t):
                    i = t % 2
                    kb.dma(xt[i][:], src_dram[t * 128:(t + 1) * 128, :], W=['xt%d' % i])
                    o0 = t * 128 - tc0 + doff
                    for half in range(2):
                        ps, pk = psum()

                        def f(ps=ps, half=half, o0=o0):
                            for kk in range(4):
                                ins = nc.tensor.transpose(ps[:, kk * 128:(kk + 1) * 128], dsr[:, half * 4 + kk, o0:o0 + 128],
                                                          ident_f[:])
                            return ins
                        kb.op('pe', f, R=[dkey, 'ident_f'], W=[pk])
                        dv(lambda ps=ps, i=i, half=half: nc.vector.tensor_tensor(
                            out=xt[i][:, half * 512:(half + 1) * 512], in0=xt[i][:, half * 512:(half + 1) * 512], in1=ps[:, :],
                            op=ALU.add), [pk, 'xt%d' % i], ['xt%d' % i])
                    after(i, t)
                out_.append(th)
            return out_

        def merge_out(l):
            def after(i, t):
                kb.dma(xw[t * 128:(t + 1) * 128, :], xt[i][:], R=['xt%d' % i])
                norm_tile(i, t, 24, 32)
            MK = [('mT', j) for j in range(8)]
            pend = []
            for b_, (tc0, n) in enumerate(blocks_of(l)):
                v = 1 if tc0 == 0 else 0
                dTb, dkey = dTm[b_ % 2], 'dT%d' % (b_ % 2)
                for dt_ in range(8):
                    wo, wok = load_wk(w_out_d[l][:, dt_ * 128:(dt_ + 1) * 128], 8, 128)
                    ps, pk = psum()

                    def f(ps=ps, wo=wo, tc0=tc0, n=n):
                        for k in range(8):
                            ins = nc.tensor.matmul(ps[:, 0:n], lhsT=wo[:, k, :], rhs=mT[:, k, tc0:tc0 + n],
                                                   start=(k == 0), stop=(k == 7))
                        return ins
                    kb.op('pe', f, R=[wok] + MK, W=[pk])
                    dv(lambda ps=ps, dt_=dt_, n=n, v=v, dTb=dTb: nc.vector.tensor_scalar(
                        out=dTb[:, dt_, 0:n], in0=ps[:, 0:n], scalar1=modT[:, 16 + dt_, v:v + 1], scalar2=None, op0=ALU.mult),
                       [pk, 'modT', dkey], [dkey])
                    if pend and dt_ % 2 == 1:
                        pend.pop(0)()
                while pend:
                    pend.pop(0)()
                pend = delta_tiles(tc0, n, xin if l == 0 else xw, after, dTb, 0, dkey)
            while pend:
                pend.pop(0)()

        def ffn(l, groups):
            last = (l == DEPTH - 1)

            def after(i, t):
                if not last:
                    kb.dma(xw[t * 128:(t + 1) * 128, :], xt[i][:], R=['xt%d' % i])
                else:
                    rstd_tile(i)
                    dv(lambda i=i: nc.vector.tensor_scalar(out=xt[i][:], in0=xt[i][:], scalar1=ss[:, 2:3], scalar2=None,
                                                           op0=ALU.mult), ['xt%d' % i, 'ss'], ['xt%d' % i])
                    dv(lambda i=i: nc.vector.tensor_tensor(out=xt[i][:], in0=xt[i][:], in1=fnrow[:], op=ALU.mult),
                       ['xt%d' % i, 'fnrow'], ['xt%d' % i])
                    kb.dma(out[(t - 2) * 128:(t - 1) * 128, :], xt[i][:], R=['xt%d' % i])
            c = [0]
            pend = []
            for gi_, grp in enumerate(groups):
                g0 = grp[0][0]
                dTg, dgk = dTgs[gi_ % 2], 'dTg%d' % (gi_ % 2)
                for j in range(FFH // 128):
                    wg, wgk = load_wk(w_fg[l][:, j * 128:(j + 1) * 128], 8, 128)
                    wu, wuk = load_wk(w_fu[l][:, j * 128:(j + 1) * 128], 8, 128)
                    for (tc0, n) in grp:
                        i = c[0] % 2
                        c[0] += 1
                        psg, pkg = psum()
                        psu, pku = psum()

                        def f(psg=psg, wg=wg, tc0=tc0, n=n):
                            for k in range(8):
                                ins = nc.tensor.matmul(psg[:, 0:n], lhsT=wg[:, k, :], rhs=hT[:, k, tc0:tc0 + n],
                                                       start=(k == 0), stop=(k == 7))
                            return ins
                        kb.op('pe', f, R=[wgk] + HT_KEYS, W=[pkg])

                        def f(psu=psu, wu=wu, tc0=tc0, n=n):
                            for k in range(8):
                                ins = nc.tensor.matmul(psu[:, 0:n], lhsT=wu[:, k, :], rhs=hT[:, k, tc0:tc0 + n],
                                                       start=(k == 0), stop=(k == 7))
                            return ins
                        kb.op('pe', f, R=[wuk] + HT_KEYS, W=[pku])
                        kb.op('act', lambda psg=psg, i=i, n=n: nc.scalar.activation(out=sg[i][:, 0:n], in_=psg[:, 0:n],
                                                                                    func=AF.Silu),
                              R=[pkg, 'sg%d' % i], W=['sg%d' % i])
                        dv(lambda psu=psu, i=i, j=j, tc0=tc0, n=n: nc.vector.tensor_tensor(
                            out=aT[:, j, tc0 - g0:tc0 - g0 + n], in0=sg[i][:, 0:n], in1=psu[:, 0:n], op=ALU.mult),
                           [pku, 'sg%d' % i], [('aT', j)])
                    if pend:
                        pend.pop(0)()
                while pend:
                    pend.pop(0)()
                AK = [('aT', j) for j in range(FFH // 128)]
                for dt_ in range(8):
                    wa, wak = load_wk(w_fd[l][0:2048, dt_ * 128:(dt_ + 1) * 128], 16, 128)
                    wb_, wbk = load_wk(w_fd[l][2048:FFH, dt_ * 128:(dt_ + 1) * 128], 6, 128)
                    for (tc0, n) in grp:
                        v = 1 if tc0 == 0 else 0
                        ps, pk = psum()

                        def f(ps=ps, wa=wa, wb_=wb_, tc0=tc0, n=n):
                            for k in range(22):
                                w_ = wa[:, k, :] if k < 16 else wb_[:, k - 16, :]
                                ins = nc.tensor.matmul(ps[:, 0:n], lhsT=w_, rhs=aT[:, k, tc0 - g0:tc0 - g0 + n],
                                                       start=(k == 0), stop=(k == 21))
                            return ins
                        kb.op('pe', f, R=[wak, wbk] + AK, W=[pk])
                        dv(lambda ps=ps, dt_=dt_, n=n, v=v, tc0=tc0, dTg=dTg: nc.vector.tensor_scalar(
                            out=dTg[:, dt_, tc0 - g0:tc0 - g0 + n], in0=ps[:, 0:n], scalar1=modT[:, 40 + dt_, v:v + 1], scalar2=None,
                            op0=ALU.mult), [pk, 'modT', dgk], [dgk])
                for (tc0, n) in grp:
                    pend += delta_tiles(tc0, n, xw, after, dTg, tc0 - g0, dgk)
            while pend:
                pend.pop(0)()

        dbgbuf = [None]

        def dump_fm(tile_, nk, keys):
            for k in range(nk):
                kb.op('act', lambda k=k: nc.scalar.copy(out=dbgbuf[0][:], in_=tile_[:, k, :]), R=list(keys) + ['dbgb'], W=['dbgb'])
                kb.dma(dbg_out[:, k, :], dbgbuf[0][:], R=['dbgb'])

        stop = False
        for l in range(DEPTH):
          with ExitStack() as LS:
            with ExitStack() as ph:
                cur[0] = ph
                xt = [sb("xt%d" % i, [128, D], F32) for i in range(2)]
                xn = [sb("xn%d" % i, [128, D], BF16) for i in range(2)]
                junk = sb("junk", [128, D], F32)
                wst = [sb("wst%d" % i, [128, 2048], F32) for i in range(2)]
                xnall = sb("xnall", [128, NTT, D], BF16)
                norm_pre(xin if l == 0 else xw)
                adaln(l)
                norm_post(0, 8)
                kb.barrier()
            cur[0] = LS
            ys5T = sb("ys5T", [128, 4, NT], BF16)
            if dbg and dbg[0] == 'hT':
                dbgbuf[0] = sb("dbgb", [128, NT], F32)
                dump_fm(hT, 8, HT_KEYS)
                break
            with ExitStack() as ph:
                cur[0] = ph
                uT = sb("uT", [128, 4, NT], BF16)
                gsc = sb("gsc", [128, NT], F32)
                XG = [sb("XG%d" % i, [128, 2, 2, 8, 128], BF16) for i in range(2)]
                Sx = sb("Sx", [128, 32, 128], BF16)
                Z = sb("Z", [128, 4, 2, 2, NT // 8], F32)
                XsQ = sb("XsQ", [128, 4, 2, 2, NT // 8], BF16)
                T2s = [None]
                CQ = sb("CQ", [128, 12, 8, 3], F32)
                BDq = sb("BDq", [128, 2, 8, 128], BF16)
                uS = sb("uS", [128, 8, NT // 8], BF16)
                Bt = [sb("Bt%d" % i, [128, 4, 32], F32) for i in range(2)]
                s5tmpP = sb("s5tmpP", [128, 3, 8, 32], F32)
                cfk = sb("cfk", [128, 2, 2, 16, 16], F32)
                Epw_re = sb("Epw_re", [128, 2, 9, 16], F32)
                Epw_im = sb("Epw_im", [128, 2, 9, 16], F32)
                Eg_re = sb("Eg_re", [128, 2, 8, 16], F32)
                Eg_im = sb("Eg_im", [128, 2, 8, 16], F32)
                yacc = sb("yacc", [128, NT], F32)
                Rwp = [sb("Rwp%d" % i, [128, 2, 2, 128], BF16) for i in range(2)]
                prm = sb("prm", [128, 2, 3, 16], F32)
                cfl = sb("cfl", [128, 2, 2, 16, 16], F32)
                APre = sb("APre", [128, 2, 12, 16], F32)
                APim = sb("APim", [128, 2, 12, 16], F32)
                APnim = sb("APnim", [128, 2, 12, 16], F32)
                sm = sb("sm", [128, 12, 16], F32)
                cfs = sb("cfs", [128, 2, 16, 16], F32)
                s5dT = sb("s5dT_t", [128, 4], F32)
                bgluT = sb("bgluT_t", [128, 4], F32)
                sg = [gsc[:, 0:512], gsc[:, 512:1024]]
                s5_mixer(l)
                kb.barrier()
            cur[0] = LS
            yretT = sb("yretT", [128, 4, NT], BF16)
            if dbg and dbg[0] == 'ys5':
                dbgbuf[0] = sb("dbgb", [128, NT], F32)
                dump_fm(ys5T, 4, [('ys5T', m) for m in range(4)])
                break
            with ExitStack() as ph:
                cur[0] = ph
                qT = sb("qT", [128, 2, NT], BF16)
                kT = sb("kT", [128, 2, NT], BF16)
                vtok = sb("vtok", [128, NTT, 512], BF16)
                thB = sb("thB_t", [128, 8], F32)
                thQ = sb("thQ_t", [128, 4], F32)
                lgB = sb("lgB", [128, 8], F32)
                lgQ = sb("lgQ", [128, 4], F32)
                retc = sb("retc_t", [128, 2, 128], F32)
                iq = sb("iq_t", [128, 2, 128], F32)
                ek = sb("ek_t", [128, 2], F32)
                Dm = sb("Dm", [128, 8, 128], F32)
                qdtab = sb("qdtab", [128, 2, 2, 128], F32)
                kd = sb("kd", [128, 8], F32)
                cdecQ = sb("cdecQ", [128, 4], F32)
                with ExitStack() as ph2:
                    cur[0] = ph2
                    tmpT = sb("tmpT", [128, 2, NT], BF16)
                    rotc = sb("rotc_t", [128, NT], F32)
                    rots = sb("rots_t", [128, NT], F32)
                    ret_mixer(l)
                    kb.barrier()
                cur[0] = ph
                oT = sb("oT", [128, 4, NT], F32)
                sst = [sb("sst%d" % i, [128, 256], F32) for i in range(4)]
                sbf = [sb("sbf%d" % i, [128, 256], BF16) for i in range(4)]
                kdec = [sb("kdec%d" % i, [128, 128], BF16) for i in range(6)]
                qdt = [sb("qdt%d" % i, [128, 128], BF16) for i in range(6)]
                pT = [sb("pT%d" % i, [128, 2, 128], BF16) for i in range(6)]
                sg = [sb("sg%d" % i, [128, 512], F32) for i in range(5)]
                ret_part2(l)
                kb.barrier()
            cur[0] = LS
            ynaT = sb("ynaT", [128, 4, NT], BF16)
            if dbg and dbg[0] == 'yret':
                dbgbuf[0] = sb("dbgb", [128, NT], F32)
                dump_fm(yretT, 4, [('yretT', m) for m in range(4)])
                break
            with ExitStack() as ph:
                cur[0] = ph
                nkT = sb("nkT", [128, 4, NT], BF16)
                nqT = sb("nqT", [128, 4, NT], BF16)
                nvtok = sb("nvtok", [128, NTT, 512], BF16)
                TT = sb("TT", [128, 8, 15 * 64], BF16)
                nst0 = sb("nst0", [128, 1024], F32)
                nst = [nst0[:, 0:960], nst0[:, 0:960]]
                colmask = sb("colmask_t", [128, 64], F32)
                e1 = [sb("e1%d" % i, [128, 256], F32) for i in range(3)]
                p1 = [sb("p1%d" % i, [128, 384], BF16) for i in range(3)]
                pc = sb("pc", [128, 512], BF16)
                rcc = sb("rcc", [128, 256], F32)
                rc = [sb("rc%d" % i, [128, 64], F32) for i in range(3)]
                numb = sb("numb", [128, 1024], F32)
                denb = nst0
                na_mixer(l, l < DEPTH - 1)
                kb.barrier()
            cur[0] = LS
            if dbg and dbg[0] == 'yna':
                dbgbuf[0] = sb("dbgb", [128, NT], F32)
                dump_fm(ynaT, 4, [('ynaT', m) for m in range(4)])
                break
            mT = sb("mT", [128, 8, NT], BF16)
            with ExitStack() as ph:
                cur[0] = ph
                macc = sb("macc", [128, NT], F32)
                sg = [sb("sg%d" % i, [128, 512], F32) for i in range(2)]
                merge_m(l)
                kb.barrier()
            with ExitStack() as ph:
                cur[0] = ph
                dTm = [sb("dTm%d" % i, [128, 8, 512], F32) for i in range(2)]
                xt = [sb("xt%d" % i, [128, D], F32) for i in range(2)]
                xn = [sb("xn%d" % i, [128, D], BF16) for i in range(2)]
                junk = sb("junk", [128, D], F32)
                merge_out(l)
                kb.barrier()
            cur[0] = LS
          cur[0] = es
          with ExitStack() as ph:
              cur[0] = ph
              if l < DEPTH - 1:
                  groups = [TB[0:3], TB[3:5]]
              else:
                  groups = [TB[1:3], TB[3:5]]
              aT = sb("aT", [128, FFH // 128, 1280], BF16)
              dTgs = [sb("dTg%d" % i, [128, 8, 1280 if i == 0 else 1024], F32) for i in range(2)]
              sg = [sb("sg%d" % i, [128, 512], F32) for i in range(2)]
              xt = [sb("xt%d" % i, [128, D], F32) for i in range(2)]
              xn = [sb("xn%d" % i, [128, D], BF16) for i in range(2)]
              junk = sb("junk", [128, D], F32)
              ffn(l, groups)
              kb.barrier()
          cur[0] = es
        kb.finish()
    return nc


def host_inputs(inputs, b, shared=None):
    f = np.float32
    d = {}
    d['xin'] = np.ascontiguousarray(np.concatenate([inputs['ctx'][b], inputs['x'][b]], axis=0).astype(f))
    cc = np.stack([inputs['c'][b], inputs['c_ctx']], axis=-1)
    d['cT'] = np.ascontiguousarray(cc.reshape(8, 128, 2).transpose(1, 0, 2).astype(f))
    if shared is not None:
        d.update(shared)
        return d
    d['w_ada'] = np.ascontiguousarray(inputs['w_ada'].astype(f))
    d['b_adaT'] = np.ascontiguousarray(inputs['b_ada'].reshape(DEPTH, 48, 128).transpose(0, 2, 1).astype(f))
    d['w_in'] = np.ascontiguousarray(inputs['w_in'].astype(f))
    d['ident_d'] = np.eye(128, dtype=f)
    def layA(a):
        return a.reshape(DEPTH, 2, 16, 2, 64).transpose(0, 1, 3, 4, 2).reshape(DEPTH, 2, 128, 16)
    ldt = np.repeat(inputs['s5_log_dt'][..., None], 64, axis=-1)
    d['s5p'] = np.ascontiguousarray(np.stack([layA(inputs['s5_lam_re']), layA(inputs['s5_lam_im']), layA(ldt)],
                                             axis=3).astype(f))
    def layC(a):
        return a.reshape(DEPTH, 2, 16, 2, 16, 64).transpose(0, 1, 3, 5, 2, 4).reshape(DEPTH, 2, 128, 16, 16)
    d['s5c'] = np.ascontiguousarray(np.stack([layC(inputs['s5_c_re']), layC(inputs['s5_c_im'])], axis=3).astype(f))
    ba = np.zeros((DEPTH, 16, 128, 4, 32), f)
    for r, nm in enumerate(('s5_b_re', 's5_b_im')):
        b = inputs[nm]
        for pi_ in range(16):
            for gl in range(2):
                g = 2 * pi_ + gl
                for dd in range(2):
                    ba[:, pi_, gl * 64:(gl + 1) * 64, dd * 2 + r, gl * 16:(gl + 1) * 16] = b[:, dd, g]
    d['s5ba'] = ba
    d['s5dT'] = np.ascontiguousarray(inputs['s5_d'].reshape(DEPTH, 4, 128).transpose(0, 2, 1).astype(f))
    d['bgluT'] = np.ascontiguousarray(inputs['s5_b_glu'].reshape(DEPTH, 4, 128).transpose(0, 2, 1).astype(f))
    d['w_glu'] = np.ascontiguousarray(inputs['s5_w_glu'].astype(f))
    perm = np.arange(64)
    perm = np.where((perm % 32) < 16, perm + 16, perm - 16)
    kcols = 512 + (np.arange(256) // 64) * 64 + perm[np.arange(256) % 64]
    qcols = 2304 + (np.arange(256) // 64) * 64 + perm[np.arange(256) % 64]
    d['w_rot'] = np.ascontiguousarray(inputs['w_in'][:, :, np.concatenate([kcols, qcols])].astype(f))
    th = inputs['ret_theta'].astype(f)
    d['thB'] = np.ascontiguousarray(np.broadcast_to(th.reshape(DEPTH, 1, 8), (DEPTH, 128, 8)))
    thq = th.reshape(DEPTH, 2, 2, 2)
    d['thQ'] = np.ascontiguousarray(np.repeat(thq.transpose(0, 3, 1, 2), 64, axis=1).reshape(DEPTH, 128, 4))
    kc = np.arange(64)[:, None]
    qc = np.arange(64)[None, :]
    jidx = np.clip(kc - qc + 15, 0, 30)
    g_ = inputs['na_rpb'][:, :, :, jidx]
    g_ = g_.transpose(0, 1, 3, 2, 4)
    d['nab'] = np.ascontiguousarray(np.concatenate([g_, g_], axis=2).reshape(DEPTH, 8, 128, 15 * 64).astype(f))
    for i, nm in enumerate(('w_branch_s5', 'w_branch_ret', 'w_branch_na')):
        d['w_b%d' % i] = np.ascontiguousarray(inputs[nm].astype(f))
    d['w_out'] = np.ascontiguousarray(inputs['w_out'].astype(f))
    d['w_fg'] = np.ascontiguousarray(inputs['w_ffn_gate'].astype(f))
    d['w_fu'] = np.ascontiguousarray(inputs['w_ffn_up'].astype(f))
    d['w_fd'] = np.ascontiguousarray(inputs['w_ffn_down'].astype(f))
    d['fnrow'] = np.ascontiguousarray(np.broadcast_to(inputs['final_norm'].astype(f)[None, :], (128, D)))
    d.update(_consts())
    return d


_CONSTS = {}


def _consts():
    if _CONSTS:
        return _CONSTS
    f = np.float32
    pos = np.arange(LAT)
    inv_freq = (10000.0 ** (-np.arange(16, dtype=np.float32) / 16)).astype(np.float32)
    dd = np.arange(128) % 64
    coord = np.where((dd // 32)[:, None] == 0, (pos // 64)[None, :], (pos % 64)[None, :]).astype(np.float32)
    ang = coord * inv_freq[dd % 16][:, None]
    sign = np.where((dd % 32) < 16, -1.0, 1.0)[:, None]
    rotc = np.ones((128, NT), f)
    rots = np.zeros((128, NT), f)
    rotc[:, NCTX:] = np.cos(ang)
    rots[:, NCTX:] = np.sin(ang) * sign
    jj = np.arange(128)[:, None].astype(f)
    ii = np.arange(128)[None, :].astype(f)
    BIG = 1.0e6
    retc = np.stack([np.where(ii >= jj, ii - jj, BIG), np.where(jj > ii, jj - ii, BIG)], axis=1)
    iq = np.stack([np.broadcast_to(ii + 1.0, (128, 128)), np.broadcast_to(128.0 - ii, (128, 128))], axis=1)
    ek = np.stack([127.0 - np.arange(128), np.arange(128)], axis=1)
    kc = np.arange(64)[:, None]
    qc = np.arange(64)[None, :]
    cs = np.clip(qc - 8, 0, 48)
    cm = ((kc >= cs) & (kc < cs + 16)).astype(f)
    _CONSTS['colmask'] = np.ascontiguousarray(np.concatenate([cm, cm], axis=0))
    _CONSTS.update(rotc=rotc, rots=rots.astype(f), retc=np.ascontiguousarray(retc.astype(f)),
                   iq=np.ascontiguousarray(iq.astype(f)), ek=np.ascontiguousarray(ek.astype(f)))
    return _CONSTS


def kernel(**inputs):
    inputs = {k: np.asarray(v) for k, v in inputs.items()}
    nc = build()
    first = host_inputs(inputs, 0)
    shared = {k: v for k, v in first.items() if k not in ('xin', 'cT')}
    in_maps = [first] + [host_inputs(inputs, b, shared) for b in range(1, 8)]
    res = run_bass_kernel_spmd(nc, in_maps, core_ids=list(range(8)))
    return np.stack([r["out"] for r in res.results], axis=0).astype(np.float32)
```

```python
import numpy as np
from contextlib import ExitStack
import concourse.bass as bass
import concourse.mybir as mybir
from concourse.bass_utils import run_bass_kernel_spmd

F32 = mybir.dt.float32
BF16 = mybir.dt.bfloat16
AF = mybir.ActivationFunctionType
ALU = mybir.AluOpType

D = 1024
LAT = 2048
NCTX = 256
NT = LAT + NCTX
NTT = NT // 128
DEPTH = 2
NIN = 6656
FFH = 2816
TB = [(0, 256), (256, 512), (768, 512), (1280, 512), (1792, 512)]
RMS_EPS = 1e-6
GN_EPS = 1e-5


class KB:
    def __init__(self, nc, es):
        self.nc = nc
        self.es = es
        self.eng = {'pe': nc.tensor, 'act': nc.scalar, 'dve': nc.vector, 'pool': nc.gpsimd, 'sp': nc.sync}
        self.sem = {e: es.enter_context(nc.semaphore('sem_' + e)) for e in self.eng}
        self.cnt = {e: 0 for e in self.eng}
        self.seen = {e: {} for e in self.eng}
        self.ND = 24
        self.dsem = [es.enter_context(nc.semaphore('dsem%d' % i)) for i in range(self.ND)]
        self.dcnt = [0] * self.ND
        self.dslots = {'sp': list(range(0, 14)), 'pool': list(range(14, 24))}
        self.dnext = {'sp': 0, 'pool': 0}
        self.w = {}
        self.r = {}
        self.nps = 0
        self.ninst = 0

    def _semh(self, s):
        return self.sem[s] if isinstance(s, str) else self.dsem[s[1]]

    def _wait(self, e, s, v):
        if s == e and e == 'pe':
            return
        if self.seen[e].get(s, 0) >= v:
            return
        self.eng[e].wait_ge(self._semh(s), v)
        self.seen[e][s] = v

    def _deps(self, e, R, W):
        for k in list(R) + list(W):
            for s, v in self.w.get(k, {}).items():
                self._wait(e, s, v)
        for k in W:
            for s, v in self.r.get(k, {}).items():
                self._wait(e, s, v)

    def _mark(self, tok, R, W):
        for k in W:
            self.w[k] = {tok[0]: tok[1]}
            self.r[k] = {}
        for k in R:
            d = self.r.setdefault(k, {})
            if d.get(tok[0], 0) < tok[1]:
                d[tok[0]] = tok[1]

    def op(self, e, fn, R=(), W=()):
        self._deps(e, R, W)
        inst = fn()
        self.cnt[e] += 1
        inst.then_inc(self.sem[e], 1)
        self._mark((e, self.cnt[e]), R, W)
        self.ninst += 1

    def dma(self, out, in_, R=(), W=(), q='sp', **kw):
        sl = self.dslots[q]
        i = sl[self.dnext[q] % len(sl)]
        self.dnext[q] += 1
        if self.dcnt[i]:
            self._wait(q, ('d', i), 16 * self.dcnt[i])
        self._deps(q, R, W)
        self.eng[q].dma_start(out=out, in_=in_, **kw).then_inc(self.dsem[i], 16)
        self.dcnt[i] += 1
        self._mark((('d', i), 16 * self.dcnt[i]), R, W)
        self.ninst += 1

    def merge(self, newkey, keys):
        d = {}
        for k in keys:
            for s, v in self.w.get(k, {}).items():
                if d.get(s, 0) < v:
                    d[s] = v
        self.w[newkey] = d
        self.r[newkey] = {}

    def barrier(self):
        for e in self.eng:
            for e2 in self.eng:
                if e2 != e and self.cnt[e2]:
                    self._wait(e, e2, self.cnt[e2])
            for i in range(self.ND):
                if self.dcnt[i]:
                    self._wait(e, ('d', i), 16 * self.dcnt[i])
        self.w = {}
        self.r = {}

    def finish(self):
        for i in range(self.ND):
            if self.dcnt[i]:
                self._wait('sp', ('d', i), 16 * self.dcnt[i])


def pbc(ap, n):
    return ap.to_broadcast([ap.shape[0], n])


def build(dbg=None):
    nc = bass.Bass("TRN2", target_bir_lowering=False)
    dr = {}

    def din(name, shape):
        dr[name] = nc.dram_tensor(name, list(shape), F32, kind="ExternalInput").ap()
        return dr[name]

    xin = din("xin", [NT, D])
    cT = din("cT", [128, 8, 2])
    w_ada = din("w_ada", [DEPTH, D, 6 * D])
    b_adaT = din("b_adaT", [DEPTH, 128, 48])
    w_in = din("w_in", [DEPTH, D, NIN])
    out = nc.dram_tensor("out", [LAT, D], F32, kind="ExternalOutput").ap()
    xw = nc.dram_tensor("xw", [NT, D], F32, kind="Internal").ap()
    if dbg:
        dbg_out = nc.dram_tensor("dbg", list(dbg[1]), F32, kind="ExternalOutput").ap()

    with ExitStack() as es:
        kb = KB(nc, es)

        cur = [es]

        nuniq = [0]

        def sb(name, shape, dt):
            nuniq[0] += 1
            return cur[0].enter_context(nc.sbuf_tensor("%s_%d" % (name, nuniq[0]), list(shape), dt))

        PS = [es.enter_context(nc.psum_tensor("ps%d" % i, [128, 512], F32)) for i in range(8)]

        def psum():
            i = kb.nps % 8
            kb.nps += 1
            return PS[i], 'ps%d' % i

        ident_f = sb("ident_f", [128, 128], F32)
        ident = sb("ident", [128, 128], BF16)
        hT = sb("hT", [128, 8, NT], BF16)
        modT = sb("modT", [128, 48, 2], F32)
        siluT = sb("siluT", [128, 8, 2], F32)
        NWB = 4
        wbf = [sb("wbf%d" % i, [128, 2048], BF16) for i in range(NWB)]
        ss = sb("ss", [128, 4], F32)
        badaT = sb("badaT", [128, 48], F32)
        ident_d = din("ident_d", [128, 128])

        kb.dma(ident_f[:], ident_d[:, :], W=['ident_f'])
        kb.op('dve', lambda: nc.vector.tensor_copy(out=ident[:], in_=ident_f[:]), R=['ident_f'], W=['ident'])
        kb.dma(siluT[:], cT[:, :, :], W=['siluT'])
        kb.op('act', lambda: nc.scalar.activation(out=siluT[:], in_=siluT[:], func=AF.Silu), R=['siluT'], W=['siluT'])
        siluB = sb("siluB", [128, 8, 2], BF16)
        kb.op('dve', lambda: nc.vector.tensor_copy(out=siluB[:], in_=siluT[:]), R=['siluT'], W=['siluB'])

        wctr = [0, 0]

        def load_wk(src_ap, kt, ncols, cast=True):
            if not cast:
                i = wctr[1] % 2
                wctr[1] += 1
                sv = wst[i][:, 0:kt * ncols].rearrange("p (k n) -> p k n", n=ncols)
                kb.dma(sv, src_ap.rearrange("(k p) n -> p k n", p=128), W=['wst%d' % i])
                return sv, 'wst%d' % i
            i = wctr[0] % NWB
            wctr[0] += 1
            bv = wbf[i][:, 0:kt * ncols].rearrange("p (k n) -> p k n", n=ncols)
            kb.dma(bv, src_ap.rearrange("(k p) n -> p k n", p=128), W=['wbf%d' % i], q='pool')
            return bv, 'wbf%d' % i

        def load_w(src_ap, ncols, cast=True):
            return load_wk(src_ap, 8, ncols, cast)

        def adaln(l):
            kb.dma(badaT[:], b_adaT[l], W=['badaT'])
            ps, pk = psum()
            psv = ps[:, 0:96].rearrange("p (j v) -> p j v", v=2)
            for cb in range(24):
                wt, wk = load_w(w_ada[l][:, cb * 256:(cb + 1) * 256], 256)
                for jj in range(2):
                    j = cb * 2 + jj

                    def f(j=j, jj=jj, wt=wt):
                        for k in range(8):
                            ins = nc.tensor.matmul(psv[:, j, :], lhsT=wt[:, k, jj * 128:(jj + 1) * 128],
                                                   rhs=siluB[:, k, :], start=(k == 0), stop=(k == 7))
                        return ins
                    kb.op('pe', f, R=[wk, 'siluB'], W=[pk])
            for v in range(2):
                kb.op('dve', lambda v=v: nc.vector.tensor_tensor(out=modT[:, :, v], in0=psv[:, :, v], in1=badaT[:],
                                                                 op=ALU.add), R=[pk, 'badaT'], W=['modT'])
            for a in (8, 32):
                kb.op('dve', lambda a=a: nc.vector.tensor_scalar(out=modT[:, a:a + 8, :], in0=modT[:, a:a + 8, :],
                                                                 scalar1=1.0, scalar2=None, op0=ALU.add),
                      R=['modT'], W=['modT'])

        def rstd_tile(i):
            kb.op('act', lambda i=i: nc.scalar.activation(out=junk[:], in_=xt[i][:], func=AF.Square),
                  R=['xt%d' % i], W=['junk'])
            kb.op('dve', lambda: nc.vector.tensor_reduce(out=ss[:, 0:1], in_=junk[:], axis=mybir.AxisListType.X, op=ALU.add),
                  R=['junk', 'ss'], W=['ss'])
            kb.op('act', lambda: nc.scalar.activation(out=ss[:, 1:2], in_=ss[:, 0:1], func=AF.Sqrt,
                                                      scale=1.0 / D, bias=eps_t[:, 0:1]), R=['ss', 'eps_t'], W=['ss'])
            kb.op('dve', lambda: nc.vector.reciprocal(out=ss[:, 2:3], in_=ss[:, 1:2]), R=['ss'], W=['ss'])

        def norm_tile(i, t, sh_off, sc_off):
            v = 1 if t < 2 else 0
            rstd_tile(i)
            kb.op('dve', lambda i=i: nc.vector.tensor_scalar(out=xn[i][:], in0=xt[i][:], scalar1=ss[:, 2:3],
                                                             scalar2=None, op0=ALU.mult),
                  R=['xt%d' % i, 'ss'], W=['xn%d' % i])
            ps, pk = psum()
            pb = ps[:].bitcast(BF16)

            def f(i=i, pb=pb):
                for k in range(8):
                    ins = nc.tensor.transpose(pb[:, k * 128:(k + 1) * 128], xn[i][:, k * 128:(k + 1) * 128], ident[:])
                return ins
            kb.op('pe', f, R=['xn%d' % i, 'ident'], W=[pk])
            for k in range(8):
                kb.op('dve', lambda k=k, pb=pb, t=t, v=v: nc.vector.tensor_scalar(
                    out=hT[:, k, t * 128:(t + 1) * 128], in0=pb[:, k * 128:(k + 1) * 128],
                    scalar1=modT[:, sc_off + k, v:v + 1], scalar2=modT[:, sh_off + k, v:v + 1],
                    op0=ALU.mult, op1=ALU.add), R=[pk, 'modT'], W=[('hT', t)])

        def norm_to_hT(src_dram, sh_off, sc_off):
            for t in range(NTT):
                i = t % 2
                kb.dma(xt[i][:], src_dram[t * 128:(t + 1) * 128, :], W=['xt%d' % i])
                norm_tile(i, t, sh_off, sc_off)

        def norm_pre(src_dram):
            for t in range(NTT):
                i = t % 2
                kb.dma(xt[i][:], src_dram[t * 128:(t + 1) * 128, :], W=['xt%d' % i])
                rstd_tile(i)
                kb.op('dve', lambda i=i, t=t: nc.vector.tensor_scalar(out=xnall[:, t, :], in0=xt[i][:], scalar1=ss[:, 2:3],
                                                                      scalar2=None, op0=ALU.mult),
                      R=['xt%d' % i, 'ss'], W=[('xnall', t)])

        def norm_post(sh_off, sc_off):
            for t in range(NTT):
                v = 1 if t < 2 else 0
                ps, pk = psum()
                pb = ps[:].bitcast(BF16)

                def f(pb=pb, t=t):
                    for k in range(8):
                        ins = nc.tensor.transpose(pb[:, k * 128:(k + 1) * 128], xnall[:, t, k * 128:(k + 1) * 128], ident[:])
                    return ins
                kb.op('pe', f, R=[('xnall', t), 'ident'], W=[pk])
                for k in range(8):
                    kb.op('dve', lambda k=k, pb=pb, t=t, v=v: nc.vector.tensor_scalar(
                        out=hT[:, k, t * 128:(t + 1) * 128], in0=pb[:, k * 128:(k + 1) * 128],
                        scalar1=modT[:, sc_off + k, v:v + 1], scalar2=modT[:, sh_off + k, v:v + 1],
                        op0=ALU.mult, op1=ALU.add), R=[pk, 'modT'], W=[('hT', t)])

        eps_t = sb("eps_t", [128, 2], F32)
        kb.op('dve', lambda: nc.vector.memset(eps_t[:, 0:1], RMS_EPS), W=['eps_t'])
        kb.op('dve', lambda: nc.vector.memset(eps_t[:, 1:2], GN_EPS), R=['eps_t'], W=['eps_t'])


        HT_KEYS = [('hT', t) for t in range(NTT)]

        def proj_fm(wsrc, col0, ncols, src, srckeys, kt, evac, blocks=TB):
            for cb in range(0, ncols, 256):
                nc_ = min(256, ncols - cb)
                wt, wk = load_wk(wsrc[:, col0 + cb:col0 + cb + nc_], kt, nc_)
                for mm in range(nc_ // 128):
                    m = (cb // 128) + mm
                    for (tc0, n) in blocks:
                        ps, pk = psum()

                        def f(ps=ps, wt=wt, mm=mm, tc0=tc0, n=n):
                            for k in range(kt):
                                ins = nc.tensor.matmul(ps[:, 0:n], lhsT=wt[:, k, mm * 128:(mm + 1) * 128],
                                                       rhs=src[:, k, tc0:tc0 + n], start=(k == 0), stop=(k == kt - 1))
                            return ins
                        kb.op('pe', f, R=[wk] + list(srckeys), W=[pk])
                        evac(m, tc0, n, ps, pk)

        s5p = din("s5p", [DEPTH, 2, 128, 3, 16])
        s5c = din("s5c", [DEPTH, 2, 128, 2, 16, 16])
        s5ba = din("s5ba", [DEPTH, 16, 128, 4, 32])
        s5dT_d = din("s5dT", [DEPTH, 128, 4])
        bgluT_d = din("bgluT", [DEPTH, 128, 4])
        w_glu = din("w_glu", [DEPTH, 512, 512])

        PI = float(np.pi)

        def dv(fn, R, W):
            kb.op('dve', fn, R=R, W=W)

        def pv(fn, R, W):
            kb.op('pool', fn, R=R, W=W)

        def s5_prep(l):
            kb.dma(prm[:], s5p[l].rearrange("d p a b -> p d a b"), W=['prm'])
            kb.dma(cfl[:], s5c[l].rearrange("d p r a b -> p d r a b"), W=['cfl'])
            kb.dma(s5dT[:], s5dT_d[l], W=['s5dT'])
            kb.dma(bgluT[:], bgluT_d[l], W=['bgluT'])
            S = ['sm']
            for d in range(2):
                lre, lim, ldt = prm[:, d, 0, :], prm[:, d, 1, :], prm[:, d, 2, :]
                dtv, mag, th, tmp, sn, cs, th2 = (sm[:, i, :] for i in range(7))
                kb.op('act', lambda: nc.scalar.activation(out=dtv, in_=ldt, func=AF.Exp), R=['prm'], W=S)
                dv(lambda: nc.vector.tensor_tensor(out=mag, in0=lre, in1=dtv, op=ALU.mult), ['prm'] + S, S)
                kb.op('act', lambda: nc.scalar.activation(out=mag, in_=mag, func=AF.Exp), R=S, W=S)
                dv(lambda: nc.vector.tensor_tensor(out=th, in0=lim, in1=dtv, op=ALU.mult), ['prm'] + S, S)
                for _ in range(6):
                    dv(lambda: nc.vector.tensor_scalar(out=tmp, in0=th, scalar1=PI, scalar2=2 * PI, op0=ALU.is_gt,
                                                       op1=ALU.mult), S, S)
                    dv(lambda: nc.vector.tensor_tensor(out=th, in0=th, in1=tmp, op=ALU.subtract), S, S)
                for _ in range(2):
                    dv(lambda: nc.vector.tensor_scalar(out=tmp, in0=th, scalar1=-PI, scalar2=2 * PI, op0=ALU.is_lt,
                                                       op1=ALU.mult), S, S)
                    dv(lambda: nc.vector.tensor_tensor(out=th, in0=th, in1=tmp, op=ALU.add), S, S)
                dv(lambda: nc.vector.tensor_scalar(out=th2, in0=th, scalar1=PI / 2, scalar2=None, op0=ALU.add), S, S)
                dv(lambda: nc.vector.tensor_scalar(out=tmp, in0=th2, scalar1=PI, scalar2=2 * PI, op0=ALU.is_gt,
                                                   op1=ALU.mult), S, S)
                dv(lambda: nc.vector.tensor_tensor(out=th2, in0=th2, in1=tmp, op=ALU.subtract), S, S)
                kb.op('act', lambda: nc.scalar.activation(out=sn, in_=th, func=AF.Sin), R=S, W=S)
                kb.op('act', lambda: nc.scalar.activation(out=cs, in_=th2, func=AF.Sin), R=S, W=S)
                A = ['APW']
                dv(lambda: nc.vector.tensor_tensor(out=APre[:, d, 0, :], in0=mag, in1=cs, op=ALU.mult), S, A)
                dv(lambda: nc.vector.tensor_tensor(out=APim[:, d, 0, :], in0=mag, in1=sn, op=ALU.mult), S, A)
                t1, t2 = sm[:, 7, :], sm[:, 8, :]
                for k in range(1, 12):
                    re0, im0 = APre[:, d, k - 1, :], APim[:, d, k - 1, :]
                    dv(lambda re0=re0: nc.vector.tensor_tensor(out=t1, in0=re0, in1=re0, op=ALU.mult), A + S, S)
                    dv(lambda im0=im0: nc.vector.tensor_tensor(out=t2, in0=im0, in1=im0, op=ALU.mult), A + S, S)
                    dv(lambda k=k: nc.vector.tensor_tensor(out=APre[:, d, k, :], in0=t1, in1=t2, op=ALU.subtract), S, A)
                    dv(lambda re0=re0, im0=im0: nc.vector.tensor_tensor(out=t1, in0=re0, in1=im0, op=ALU.mult), A + S, S)
                    dv(lambda k=k: nc.vector.tensor_scalar(out=APim[:, d, k, :], in0=t1, scalar1=2.0, scalar2=None,
                                                           op0=ALU.mult), S, A)
                am1, den, fr, fi = sm[:, 7, :], sm[:, 8, :], sm[:, 9, :], sm[:, 10, :]
                t3 = sm[:, 11, :]
                dv(lambda: nc.vector.tensor_scalar(out=am1, in0=APre[:, d, 0, :], scalar1=-1.0, scalar2=None,
                                                   op0=ALU.add), A + S, S)
                dv(lambda: nc.vector.tensor_tensor(out=den, in0=lre, in1=lre, op=ALU.mult), ['prm'] + S, S)
                dv(lambda: nc.vector.tensor_tensor(out=t3, in0=lim, in1=lim, op=ALU.mult), ['prm'] + S, S)
                dv(lambda: nc.vector.tensor_tensor(out=den, in0=den, in1=t3, op=ALU.add), S, S)
                dv(lambda: nc.vector.reciprocal(out=den, in_=den), S, S)
                dv(lambda: nc.vector.tensor_tensor(out=fr, in0=am1, in1=lre, op=ALU.mult), ['prm'] + S, S)
                dv(lambda: nc.vector.tensor_tensor(out=t3, in0=APim[:, d, 0, :], in1=lim, op=ALU.mult), ['prm'] + A + S, S)
                dv(lambda: nc.vector.tensor_tensor(out=fr, in0=fr, in1=t3, op=ALU.add), S, S)
                dv(lambda: nc.vector.tensor_tensor(out=fr, in0=fr, in1=den, op=ALU.mult), S, S)
                dv(lambda: nc.vector.tensor_tensor(out=fi, in0=APim[:, d, 0, :], in1=lre, op=ALU.mult), ['prm'] + A + S, S)
                dv(lambda: nc.vector.tensor_tensor(out=t3, in0=am1, in1=lim, op=ALU.mult), ['prm'] + S, S)
                dv(lambda: nc.vector.tensor_tensor(out=fi, in0=fi, in1=t3, op=ALU.subtract), S, S)
                dv(lambda: nc.vector.tensor_tensor(out=fi, in0=fi, in1=den, op=ALU.mult), S, S)

                def bc16(v):
                    return bass.AP(v.tensor, v.offset, [list(v.ap[0]), list(v.ap[1]), [0, 16]])
                cre, cim = cfl[:, d, 0], cfl[:, d, 1]
                cfr, cfi, ta, tb_ = cfk[:, d, 0], cfk[:, d, 1], cfs[:, 0], cfs[:, 1]
                C = ['cfs']
                dv(lambda: nc.vector.tensor_tensor(out=cfr, in0=cre, in1=bc16(fr), op=ALU.mult), ['cfl'] + S + C, C)
                dv(lambda: nc.vector.tensor_tensor(out=ta, in0=cim, in1=bc16(fi), op=ALU.mult), ['cfl'] + S + C, C)
                dv(lambda: nc.vector.tensor_tensor(out=cfr, in0=cfr, in1=ta, op=ALU.subtract), C, C)
                dv(lambda: nc.vector.tensor_tensor(out=cfi, in0=cre, in1=bc16(fi), op=ALU.mult), ['cfl'] + S + C, C)
                dv(lambda: nc.vector.tensor_tensor(out=ta, in0=cim, in1=bc16(fr), op=ALU.mult), ['cfl'] + S + C, C)
                dv(lambda: nc.vector.tensor_tensor(out=cfi, in0=cfi, in1=ta, op=ALU.add), C, C)
            dv(lambda: nc.vector.tensor_scalar(out=APnim[:], in0=APim[:], scalar1=-1.0, scalar2=None, op0=ALU.mult),
               ['APW'], ['APW'])
            P_ = ['Epw']
            for d in range(2):
                dv(lambda d=d: nc.vector.memset(Epw_re[:, d, 0, :], 1.0), P_, P_)
                dv(lambda d=d: nc.vector.memset(Epw_im[:, d, 0, :], 0.0), P_, P_)
                dv(lambda d=d: nc.vector.tensor_copy(out=Epw_re[:, d, 1, :], in_=APre[:, d, 0, :]), ['APW'] + P_, P_)
                dv(lambda d=d: nc.vector.tensor_copy(out=Epw_im[:, d, 1, :], in_=APim[:, d, 0, :]), ['APW'] + P_, P_)
                t1, t2 = sm[:, 7, :], sm[:, 8, :]
                S = ['sm']
                for j in range(2, 9):
                    ar, ai = Epw_re[:, d, j - 1, :], Epw_im[:, d, j - 1, :]
                    br, bi_ = APre[:, d, 0, :], APim[:, d, 0, :]
                    dv(lambda ar=ar, br=br: nc.vector.tensor_tensor(out=t1, in0=ar, in1=br, op=ALU.mult), P_ + S + ['APW'], S)
                    dv(lambda ai=ai, bi_=bi_: nc.vector.tensor_tensor(out=t2, in0=ai, in1=bi_, op=ALU.mult), P_ + S + ['APW'], S)
                    dv(lambda d=d, j=j: nc.vector.tensor_tensor(out=Epw_re[:, d, j, :], in0=t1, in1=t2, op=ALU.subtract), S + P_, P_)
                    dv(lambda ar=ar, bi_=bi_: nc.vector.tensor_tensor(out=t1, in0=ar, in1=bi_, op=ALU.mult), P_ + S + ['APW'], S)
                    dv(lambda ai=ai, br=br: nc.vector.tensor_tensor(out=t2, in0=ai, in1=br, op=ALU.mult), P_ + S + ['APW'], S)
                    dv(lambda d=d, j=j: nc.vector.tensor_tensor(out=Epw_im[:, d, j, :], in0=t1, in1=t2, op=ALU.add), S + P_, P_)
                for t in range(8):
                    j = t + 1 if d == 0 else 8 - t
                    dv(lambda d=d, t=t, j=j: nc.vector.tensor_copy(out=Eg_re[:, d, t, :], in_=Epw_re[:, d, j, :]), P_, P_)
                    dv(lambda d=d, t=t, j=j: nc.vector.tensor_copy(out=Eg_im[:, d, t, :], in_=Epw_im[:, d, j, :]), P_, P_)

        def bk_scan_ops(d, pi_, XR, XI, key, n, koff=0):
            rev = (d == 1)
            X = [key]
            ops = []

            def lvl(k, dst0, m):
                s = 1 << k
                step = 2 * s
                if m <= 0:
                    return
                if not rev:
                    d0 = dst0
                    s0 = dst0 - s
                else:
                    d0 = n - 1 - (dst0 + step * (m - 1))
                    s0 = d0 + s
                dsl = slice(d0, d0 + step * (m - 1) + 1, step)
                ssl = slice(s0, s0 + step * (m - 1) + 1, step)
                ar = APre[:, d, k + koff, pi_:pi_ + 1]
                ai = APim[:, d, k + koff, pi_:pi_ + 1]
                nai = APnim[:, d, k + koff, pi_:pi_ + 1]
                for (dst, src, sc) in ((XR, XR, ar), (XR, XI, nai), (XI, XI, ar), (XI, XR, ai)):
                    ops.append(lambda dst=dst, src=src, sc=sc: dv(lambda: nc.vector.scalar_tensor_tensor(
                        out=dst[:, dsl], in0=src[:, ssl], scalar=sc, in1=dst[:, dsl], op0=ALU.mult, op1=ALU.add),
                        X + ['APW'], X))
            kmax = 0
            while (2 << kmax) - 1 < n:
                kmax += 1
            for k in range(kmax):
                s = 1 << k
                m = (n - (2 * s - 1) - 1) // (2 * s) + 1
                lvl(k, 2 * s - 1, m)
            for k in range(kmax - 1, -1, -1):
                s = 1 << k
                if n - 1 < 3 * s - 1:
                    continue
                m = (n - 1 - (3 * s - 1)) // (2 * s) + 1
                lvl(k, 3 * s - 1, m)
            return ops

        def s5_mixer(l):
            s5_prep(l)
            def ev_u(m, tc0, n, ps, pk):
                kb.op('act', lambda: nc.scalar.copy(out=uT[:, m, tc0:tc0 + n], in_=ps[:, 0:n]), R=[pk], W=[('uT', m)])
            proj_fm(w_in[l], 0, 512, hT, HT_KEYS, 8, ev_u)
            CH = NT // 8
            pstep = Z[:].ap[0][0]
            zoff = Z[:].offset
            cqoff = CQ[:].offset

            def z3(start, step, m):
                return bass.AP(Z[:].tensor, zoff + start, [[pstep, 128], [2 * CH, 8], [CH, 2], [step, m]])

            def z2(comp, start, step, m):
                return bass.AP(Z[:].tensor, zoff + comp * CH + start, [[pstep, 128], [2 * CH, 8], [step, m]])

            def cq3(k, c, m):
                return bass.AP(CQ[:].tensor, cqoff + k * 24 + c, [[CQ[:].ap[0][0], 128], [3, 8], [0, 2], [0, m]])

            def cq2(k, c, m):
                return bass.AP(CQ[:].tensor, cqoff + k * 24 + c, [[CQ[:].ap[0][0], 128], [3, 8], [0, m]])

            TQ = [None]

            def scan_level(k, dst0, m):
                s_ = 1 << k
                step = 2 * s_
                src0 = dst0 - s_
                T1v = gsc[:, 0:16 * 144].rearrange("p (a c m) -> p a c m", a=8, c=2)[:, :, :, 0:m]
                T2v = T2s[0][:, :, 0:m]
                ZK = ['Z']
                dv(lambda: nc.vector.tensor_tensor(out=T1v, in0=z3(src0, step, m), in1=cq3(k + 3, 0, m), op=ALU.mult),
                   ZK + ['CQ', 'gsc'], ['gsc'])
                dv(lambda: nc.vector.tensor_tensor(out=T2v, in0=z2(1, src0, step, m), in1=cq2(k + 3, 2, m), op=ALU.mult),
                   ZK + ['CQ', TQ[0]], [TQ[0]])
                dv(lambda: nc.vector.tensor_tensor(out=T1v[:, :, 0, :], in0=T1v[:, :, 0, :], in1=T2v, op=ALU.add),
                   ['gsc', TQ[0]], ['gsc'])
                dv(lambda: nc.vector.tensor_tensor(out=T2v, in0=z2(0, src0, step, m), in1=cq2(k + 3, 1, m), op=ALU.mult),
                   ZK + ['CQ', TQ[0]], [TQ[0]])
                dv(lambda: nc.vector.tensor_tensor(out=T1v[:, :, 1, :], in0=T1v[:, :, 1, :], in1=T2v, op=ALU.add),
                   ['gsc', TQ[0]], ['gsc'])
                dv(lambda: nc.vector.tensor_tensor(out=z3(dst0, step, m), in0=z3(dst0, step, m), in1=T1v, op=ALU.add),
                   ZK + ['gsc'], ZK)

            for q in range(4):
                T2s[0] = uT[:, q, :].bitcast(F32).rearrange("p (a m) -> p a m", a=8)
                TQ[0] = ('uT', q)
                kb.op('pool', lambda q=q: nc.gpsimd.tensor_copy(out=uS[:], in_=uT[:, q, :].rearrange("p (c s) -> p s c", s=8)),
                      R=[('uT', q)], W=['uS'])
                kb.op('act', lambda q=q: nc.scalar.activation(out=yacc[:].rearrange("p (s c) -> p s c", s=8), in_=uS[:],
                                                              func=AF.Copy, scale=s5dT[:, q:q + 1]),
                      R=['uS', 's5dT'], W=['yacc'])
                for d in range(2):
                    for c, tab in enumerate((APre, APim, APnim)):
                        ov = bass.AP(CQ[:].tensor, cqoff + d * 3 + c, [[CQ[:].ap[0][0], 128], [24, 12], [6, 4]])
                        pv(lambda ov=ov, tab=tab, d=d, q=q: nc.gpsimd.tensor_copy(out=ov, in_=tab[:, d, :, q * 4:(q + 1) * 4]),
                           ['APW', 'CQ'], ['CQ'])
                UQ = ['uS']
                TK = ['s5tmp']
                for pl in range(4):
                    pi_ = q * 4 + pl
                    j4 = pi_ % 4
                    bi = pi_ % 2
                    XGb, XGk = XG[bi], 'XG%d' % bi
                    kb.dma(Bt[bi][:], s5ba[l, pi_], W=['Bt%d' % bi])
                    kb.op('pool', lambda XGb=XGb: nc.gpsimd.memset(XGb[:], 0.0), W=[XGk])
                    for d in range(2):
                        def e3(tab, d=d, pi_=pi_):
                            v = tab[:, d, 0:8, pi_]
                            return bass.AP(v.tensor, v.offset, [list(v.ap[0]), list(v.ap[1]), [0, 32]])

                        def b3(r, d=d, bi=bi):
                            v = Bt[bi][:, d * 2 + r, :]
                            return bass.AP(v.tensor, v.offset, [list(v.ap[0]), [0, 8], list(v.ap[1])])
                        T0, T1 = s5tmpP[:, 0], s5tmpP[:, 1]
                        RK = ['Epw', 'Bt%d' % bi] + ['s5tmpP']
                        dv(lambda e3=e3, b3=b3: nc.vector.tensor_tensor(out=T0, in0=e3(Epw_re), in1=b3(0), op=ALU.mult), RK, ['s5tmpP'])
                        dv(lambda e3=e3, b3=b3: nc.vector.tensor_tensor(out=T1, in0=e3(Epw_im), in1=b3(1), op=ALU.mult), RK, ['s5tmpP'])
                        dv(lambda d=d, XGb=XGb, j4=j4: nc.vector.tensor_tensor(out=XGb[:, d, 0, :, j4 * 32:(j4 + 1) * 32], in0=T0, in1=T1,
                                                                               op=ALU.subtract), ['s5tmpP', XGk], [XGk])
                        dv(lambda e3=e3, b3=b3: nc.vector.tensor_tensor(out=T0, in0=e3(Epw_re), in1=b3(1), op=ALU.mult), RK, ['s5tmpP'])
                        dv(lambda e3=e3, b3=b3: nc.vector.tensor_tensor(out=T1, in0=e3(Epw_im), in1=b3(0), op=ALU.mult), RK, ['s5tmpP'])
                        dv(lambda d=d, XGb=XGb, j4=j4: nc.vector.tensor_tensor(out=XGb[:, d, 1, :, j4 * 32:(j4 + 1) * 32], in0=T0, in1=T1,
                                                                               op=ALU.add), ['s5tmpP', XGk], [XGk])
                    RWk = 'Rwp%d' % bi
                    kb.op('pool', lambda bi=bi: nc.gpsimd.memset(Rwp[bi][:], 0.0), W=[RWk])
                    for d in range(2):
                        for r in range(2):
                            for gl in range(2):
                                c0 = j4 * 32 + gl * 16
                                pv(lambda d=d, r=r, gl=gl, c0=c0, bi=bi, pi_=pi_: nc.gpsimd.tensor_scalar(
                                    out=Rwp[bi][gl * 64:(gl + 1) * 64, d, r, c0:c0 + 16], in0=cfk[gl * 64:(gl + 1) * 64, d, r, pi_, :],
                                    scalar1=(1.0 if r == 0 else -1.0), scalar2=0.0, op0=ALU.mult, op1=ALU.add), ['cfs', RWk], [RWk])
                    for g4 in range(4):
                        ps, pk = psum()
                        pb = ps[:].bitcast(BF16)

                        def f(pb=pb, g4=g4, XGb=XGb):
                            for i8 in range(8):
                                idx = g4 * 8 + i8
                                ins = nc.tensor.transpose(pb[:, i8 * 128:(i8 + 1) * 128],
                                                          XGb[:, idx // 16, (idx // 8) % 2, idx % 8, :], ident[:])
                            return ins
                        kb.op('pe', f, R=[XGk, 'ident'], W=[pk])
                        kb.op('act', lambda pb=pb, g4=g4: nc.scalar.copy(
                            out=Sx[:, g4 * 8:(g4 + 1) * 8, :].rearrange("p a b -> p (a b)"), in_=pb[:, :]), R=[pk, 'Sx'], W=['Sx'])
                    for dd in range(2):
                        for jh in range(2):
                            ps, pk = psum()

                            def f(ps=ps, dd=dd, jh=jh, XGb=XGb, bi=bi):
                                for jj in range(4):
                                    j = jh * 4 + jj
                                    nc.tensor.matmul(ps[:, jj * 128:(jj + 1) * 128], lhsT=XGb[:, dd, 0, j, :], rhs=Rwp[bi][:, dd, 0, :],
                                                     start=True, stop=False)
                                    ins = nc.tensor.matmul(ps[:, jj * 128:(jj + 1) * 128], lhsT=XGb[:, dd, 1, j, :],
                                                           rhs=Rwp[bi][:, dd, 1, :], start=False, stop=True)
                                return ins
                            kb.op('pe', f, R=[XGk, RWk], W=[pk])
                            kb.op('act', lambda ps=ps, dd=dd, jh=jh, j4=j4: nc.scalar.copy(
                                out=BDq[:, dd, jh * 4:(jh + 1) * 4, j4 * 32:(j4 + 1) * 32],
                                in_=ps[:, :].rearrange("p (a b) -> p a b", b=128)[:, :, j4 * 32:(j4 + 1) * 32]),
                                R=[pk, 'BDq'], W=['BDq'])
                    for d in range(2):
                        for r in range(2):
                            ps, pk = psum()

                            def f(ps=ps, d=d, r=r):
                                if d == 0:
                                    for s_ in range(8):
                                        ins = nc.tensor.matmul(ps[:, 0:CH], lhsT=Sx[:, r * 8 + (7 - s_), :], rhs=uS[:, s_, :],
                                                               start=(s_ == 0), stop=(s_ == 7))
                                else:
                                    for s_ in range(8):
                                        nc.tensor.matmul(ps[:, 0:256], lhsT=Sx[:, 16 + r * 8 + s_, :], rhs=uS[:, s_, 32:CH],
                                                         start=(s_ == 0), stop=(s_ == 7))
                                    for s_ in range(8):
                                        ins = nc.tensor.matmul(ps[:, 256:CH], lhsT=Sx[:, 16 + r * 8 + s_, :], rhs=uS[:, s_, 0:32],
                                                               start=(s_ == 0), stop=(s_ == 7), skip_group_check=True)
                                return ins
                            kb.op('pe', f, R=['Sx'] + UQ, W=[pk])
                            if d == 0:
                                src = ps[:, 0:CH]
                            else:
                                v = ps[:, 0:CH]
                                src = bass.AP(v.tensor, v.offset + CH - 1, [list(v.ap[0]), [-1, CH]])
                            kb.op('act', lambda src=src, pl=pl, d=d, r=r: nc.scalar.copy(out=Z[:, pl, d, r, :], in_=src),
                                  R=[pk, 'Z'], W=['Z'])
                kmax = 0
                while (2 << kmax) - 1 < CH:
                    kmax += 1
                for k in range(kmax):
                    s_ = 1 << k
                    scan_level(k, 2 * s_ - 1, (CH - (2 * s_ - 1) - 1) // (2 * s_) + 1)
                for k in range(kmax - 1, -1, -1):
                    s_ = 1 << k
                    if CH - 1 < 3 * s_ - 1:
                        continue
                    scan_level(k, 3 * s_ - 1, (CH - 1 - (3 * s_ - 1)) // (2 * s_) + 1)
                kb.op('pool', lambda: nc.gpsimd.memset(XsQ[:], 0.0), W=['XsQ'])
                kb.op('act', lambda: nc.scalar.copy(out=XsQ[:, :, 0, :, 1:CH], in_=Z[:, :, 0, :, 0:CH - 1]), R=['Z', 'XsQ'], W=['XsQ'])
                zr = bass.AP(Z[:].tensor, zoff + 2 * CH + CH - 2, [[pstep, 128], [4 * CH, 4], [CH, 2], [-1, CH - 1]])
                kb.op('act', lambda: nc.scalar.copy(out=XsQ[:, :, 1, :, 0:CH - 1], in_=zr), R=['Z', 'XsQ'], W=['XsQ'])
                SK = 'XsQ'
                for pp in range(2):
                    for pl in (2 * pp, 2 * pp + 1):
                        pi_ = q * 4 + pl
                        j4 = pi_ % 4
                        bi = pi_ % 2
                        XGb, XGk = XG[bi], 'XG%d' % bi
                        kb.op('pool', lambda XGb=XGb: nc.gpsimd.memset(XGb[:], 0.0), W=[XGk])
                        for d in range(2):
                            def c3(r, d=d, pi_=pi_):
                                v = cfk[:, d, r, pi_, :]
                                return bass.AP(v.tensor, v.offset, [list(v.ap[0]), [0, 8], list(v.ap[1])])

                            def g3(tab, d=d, pi_=pi_):
                                v = tab[:, d, :, pi_]
                                return bass.AP(v.tensor, v.offset, [list(v.ap[0]), list(v.ap[1]), [0, 16]])
                            G0, G1, G2 = s5tmpP[:, 0, :, 0:16], s5tmpP[:, 1, :, 0:16], s5tmpP[:, 2, :, 0:16]
                            RK = ['Epw', 'cfs', 's5tmpP']
                            dv(lambda c3=c3, g3=g3: nc.vector.tensor_tensor(out=G0, in0=c3(0), in1=g3(Eg_re), op=ALU.mult), RK, ['s5tmpP'])
                            dv(lambda c3=c3, g3=g3: nc.vector.tensor_tensor(out=G1, in0=c3(1), in1=g3(Eg_im), op=ALU.mult), RK, ['s5tmpP'])
                            dv(lambda: nc.vector.tensor_tensor(out=G2, in0=G0, in1=G1, op=ALU.subtract), ['s5tmpP'], ['s5tmpP'])
                            for gl in range(2):
                                c0 = j4 * 32 + gl * 16
                                dv(lambda d=d, gl=gl, c0=c0, XGb=XGb: nc.vector.tensor_copy(
                                    out=XGb[gl * 64:(gl + 1) * 64, d, 0, :, c0:c0 + 16], in_=G2[gl * 64:(gl + 1) * 64]), ['s5tmpP', XGk], [XGk])
                            dv(lambda c3=c3, g3=g3: nc.vector.tensor_tensor(out=G0, in0=c3(0), in1=g3(Eg_im), op=ALU.mult), RK, ['s5tmpP'])
                            dv(lambda c3=c3, g3=g3: nc.vector.tensor_tensor(out=G1, in0=c3(1), in1=g3(Eg_re), op=ALU.mult), RK, ['s5tmpP'])
                            dv(lambda: nc.vector.tensor_tensor(out=G2, in0=G0, in1=G1, op=ALU.add), ['s5tmpP'], ['s5tmpP'])
                            for gl in range(2):
                                c0 = j4 * 32 + gl * 16
                                dv(lambda d=d, gl=gl, c0=c0, XGb=XGb: nc.vector.tensor_scalar(
                                    out=XGb[gl * 64:(gl + 1) * 64, d, 1, :, c0:c0 + 16], in0=G2[gl * 64:(gl + 1) * 64], scalar1=-1.0,
                                    scalar2=None, op0=ALU.mult), ['s5tmpP', XGk], [XGk])
                    for t in range(8):
                        ps, pk = psum()

                        def f(ps=ps, t=t, pp=pp):
                            first = True
                            if pp == 0:
                                for s_ in range(8):
                                    if s_ < t:
                                        ws = [BDq[:, 0, t - s_, :]]
                                    elif s_ > t:
                                        ws = [BDq[:, 1, s_ - t, :]]
                                    else:
                                        ws = [BDq[:, 0, 0, :], BDq[:, 1, 0, :]]
                                    for w_ in ws:
                                        nc.tensor.matmul(ps[:, 0:CH], lhsT=w_, rhs=uS[:, s_, :], start=first, stop=False)
                                        first = False
                            for pl in (2 * pp, 2 * pp + 1):
                                XGb = XG[pl % 2]
                                for r in range(2):
                                    nc.tensor.matmul(ps[:, 0:CH], lhsT=XGb[:, 0, r, t, :], rhs=XsQ[:, pl, 0, r, :], start=first, stop=False)
                                    first = False
                                for r in range(2):
                                    nc.tensor.matmul(ps[:, 32:CH], lhsT=XGb[:, 1, r, t, :], rhs=XsQ[:, pl, 1, r, 0:256],
                                                     start=False, stop=False)
                                    ins = nc.tensor.matmul(ps[:, 0:32], lhsT=XGb[:, 1, r, t, :], rhs=XsQ[:, pl, 1, r, 256:CH],
                                                           start=False, stop=(r == 1 and pl == 2 * pp + 1))
                            return ins
                        kb.op('pe', f, R=['XG0', 'XG1', SK, 'BDq', 'uS'], W=[pk])
                        dv(lambda ps=ps, t=t: nc.vector.tensor_tensor(out=yacc[:, t * CH:(t + 1) * CH], in0=yacc[:, t * CH:(t + 1) * CH],
                                                                      in1=ps[:, 0:CH], op=ALU.add), [pk, 'yacc'], ['yacc'])
                kb.op('act', lambda: nc.scalar.activation(out=gsc[:], in_=yacc[:], func=AF.Square), R=['yacc', 'gsc'], W=['gsc'])
                dv(lambda: nc.vector.tensor_scalar(out=gsc[:], in0=gsc[:], scalar1=0.044715, scalar2=1.0, op0=ALU.mult,
                                                   op1=ALU.add), ['gsc'], ['gsc'])
                dv(lambda: nc.vector.tensor_tensor(out=gsc[:], in0=gsc[:], in1=yacc[:], op=ALU.mult), ['yacc', 'gsc'], ['gsc'])
                kb.op('act', lambda: nc.scalar.activation(out=gsc[:], in_=gsc[:], func=AF.Sigmoid,
                                                          scale=2.0 * float(np.sqrt(2.0 / np.pi))), R=['gsc'], W=['gsc'])
                dv(lambda q=q: nc.vector.tensor_tensor(out=uT[:, q, :].rearrange("p (c s) -> p s c", s=8),
                                                       in0=yacc[:].rearrange("p (s c) -> p s c", s=8),
                                                       in1=gsc[:].rearrange("p (s c) -> p s c", s=8), op=ALU.mult),
                   ['yacc', 'gsc', 'uS'], [('uT', q)])
            UK = [('uT', q) for q in range(4)]
            cnt = [0]

            def ev_glu(m, tc0, n, ps, pk):
                i = cnt[0] % 2
                cnt[0] += 1
                kb.op('act', lambda: nc.scalar.activation(out=sg[i][:, 0:n], in_=ps[:, 0:n], func=AF.Sigmoid,
                                                          bias=bgluT[:, m:m + 1]), R=[pk, 'bgluT'], W=['gsc'])
                dv(lambda: nc.vector.tensor_tensor(out=ys5T[:, m, tc0:tc0 + n], in0=uT[:, m, tc0:tc0 + n],
                                                   in1=sg[i][:, 0:n], op=ALU.mult), ['gsc', ('uT', m)], [('ys5T', m)])
            proj_fm(w_glu[l], 0, 512, uT, UK, 4, ev_glu)


        w_rot = din("w_rot", [DEPTH, D, 512])
        thB_d = din("thB", [DEPTH, 128, 8])
        thQ_d = din("thQ", [DEPTH, 128, 4])
        rotc_d = din("rotc", [128, NT])
        rots_d = din("rots", [128, NT])
        retc_d = din("retc", [128, 2, 128])
        iq_d = din("iq", [128, 2, 128])
        ek_d = din("ek", [128, 2])
        onesF = sb("onesF", [128, 128], F32)
        kb.op('pool', lambda: nc.gpsimd.memset(onesF[:], 1.0 / 128.0), W=['onesF'])

        def ret_mixer(l):
            T_ = ['rtab']
            kb.dma(thB[:], thB_d[l], W=['thB'])
            kb.dma(thQ[:], thQ_d[l], W=['thQ'])
            kb.dma(retc[:], retc_d[:, :, :], W=['retc'])
            kb.dma(iq[:], iq_d[:, :, :], W=['iq'])
            kb.dma(ek[:], ek_d[:, :], W=['ek'])
            kb.dma(rotc[:], rotc_d[:, :], W=['rotc'])
            kb.dma(rots[:], rots_d[:, :], W=['rots'])
            for (src, dst, key) in ((thB, lgB, 'thB'), (thQ, lgQ, 'thQ')):
                kb.op('act', lambda src=src, dst=dst: nc.scalar.activation(out=dst[:], in_=src[:], func=AF.Exp, scale=-1.0),
                      R=[key], W=T_)
                dv(lambda dst=dst: nc.vector.tensor_scalar(out=dst[:], in0=dst[:], scalar1=1.0, scalar2=None, op0=ALU.add),
                   T_, T_)
                kb.op('act', lambda dst=dst: nc.scalar.activation(out=dst[:], in_=dst[:], func=AF.Ln), R=T_, W=T_)
                dv(lambda dst=dst: nc.vector.tensor_scalar(out=dst[:], in0=dst[:], scalar1=-1.0, scalar2=None, op0=ALU.mult),
                   T_, T_)
            for d in range(2):
                for h in range(4):
                    c = d * 4 + h
                    kb.op('act', lambda d=d, c=c: nc.scalar.activation(out=Dm[:, c, :], in_=retc[:, d, :], func=AF.Exp,
                                                                       scale=lgB[:, c:c + 1]), R=T_ + ['retc'], W=T_)
                for t in range(2):
                    c = d * 2 + t
                    kb.op('act', lambda d=d, t=t, c=c: nc.scalar.activation(out=qdtab[:, d, t, :], in_=iq[:, d, :],
                                                                            func=AF.Exp, scale=lgQ[:, c:c + 1]),
                          R=T_ + ['iq'], W=T_)
                kb.op('act', lambda d=d: nc.scalar.activation(out=kd[:, d * 4:(d + 1) * 4], in_=lgB[:, d * 4:(d + 1) * 4],
                                                              func=AF.Exp, scale=ek[:, d:d + 1]), R=T_ + ['ek'], W=T_)
            kb.op('act', lambda: nc.scalar.activation(out=cdecQ[:], in_=lgQ[:], func=AF.Exp, scale=128.0), R=T_, W=T_)

            def mk_ev(dst, key, scale):
                def ev(m, tc0, n, ps, pk):
                    kb.op('act', lambda: nc.scalar.activation(out=dst[:, m, tc0:tc0 + n], in_=ps[:, 0:n], func=AF.Copy,
                                                              scale=scale), R=[pk], W=[(key, m)])
                return ev
            for (dst, key, col0, rc0, scale) in ((kT, 'kT', 512, 0, 0.125), (qT, 'qT', 2304, 256, 1.0)):
                proj_fm(w_in[l], col0, 256, hT, HT_KEYS, 8, mk_ev(dst, key, scale))
                proj_fm(w_rot[l], rc0, 256, hT, HT_KEYS, 8, mk_ev(tmpT, 'tmpT', scale))
                for m in range(2):
                    dv(lambda m=m, dst=dst: nc.vector.tensor_tensor(out=dst[:, m, :], in0=dst[:, m, :], in1=rotc[:],
                                                                    op=ALU.mult), [(key, m), 'rotc'], [(key, m)])
                    dv(lambda m=m: nc.vector.tensor_tensor(out=tmpT[:, m, :], in0=tmpT[:, m, :], in1=rots[:],
                                                           op=ALU.mult), [('tmpT', m), 'rots'], [('tmpT', m)])
                    dv(lambda m=m, dst=dst: nc.vector.tensor_tensor(out=dst[:, m, :], in0=dst[:, m, :], in1=tmpT[:, m, :],
                                                                    op=ALU.add), [(key, m), ('tmpT', m)], [(key, m)])
            for cb in range(2):
                wt, wk = load_wk(w_in[l][:, 768 + cb * 256:768 + (cb + 1) * 256], 8, 256)
                for n in range(NTT):
                    ps, pk = psum()

                    def f(ps=ps, wt=wt, n=n):
                        for k in range(8):
                            ins = nc.tensor.matmul(ps[:, 0:256], lhsT=hT[:, k, n * 128:(n + 1) * 128], rhs=wt[:, k, 0:256],
                                                   start=(k == 0), stop=(k == 7))
                        return ins
                    kb.op('pe', f, R=[wk, ('hT', n)], W=[pk])
                    kb.op('act', lambda ps=ps, n=n, cb=cb: nc.scalar.copy(out=vtok[:, n, cb * 256:(cb + 1) * 256],
                                                                          in_=ps[:, 0:256]), R=[pk], W=[('vtok', n, cb)])
        def ret_part2(l):
            T_ = ['rtab']
            rb = [0]
            chains = [(t, d) for t in range(2) for d in range(2)]
            orders = {0: list(range(NTT)), 1: [1, 0] + list(range(NTT - 1, 1, -1))}
            written = set()
            for c in range(4):
                dv(lambda c=c: nc.vector.memset(sst[c][:], 0.0), ['sst%d' % c], ['sst%d' % c])
            steps = [(idx, c) for idx in range(NTT) for c in range(4)]
            CX = {}

            def sA(g):
                idx, c = steps[g]
                t, d = chains[c]
                n = orders[d][idx]
                tsl = slice(n * 128, (n + 1) * 128)
                ps, pk = psum()
                pb = ps[:].bitcast(BF16)
                kb.op('pe', lambda: nc.tensor.transpose(pb[:, 0:128], kT[:, t, tsl], ident[:]), R=[('kT', t), 'ident'], W=[pk])
                CX[g] = dict(idx=idx, c=c, t=t, d=d, n=n, tsl=tsl, i=g % 6, pb=pb, pk=pk)

            def sB(g):
                x = CX[g]
                t, d, i, tsl = x['t'], x['d'], x['i'], x['tsl']
                kdv = kd[:, d * 4 + 2 * t:d * 4 + 2 * t + 2]
                kdb = bass.AP(kdv.tensor, kdv.offset, [list(kdv.ap[0]), [1, 2], [0, 64]])
                dv(lambda: nc.vector.tensor_tensor(
                    out=kdec[i][:].rearrange("p (a b) -> p a b", a=2), in0=x['pb'][:, 0:128].rearrange("p (a b) -> p a b", a=2),
                    in1=kdb, op=ALU.mult), [x['pk']] + T_, ['kdec%d' % i])
                dv(lambda: nc.vector.tensor_tensor(out=qdt[i][:], in0=qT[:, t, tsl], in1=qdtab[:, d, t, :], op=ALU.mult),
                   [('qT', t)] + T_, ['qdt%d' % i])

            def sC(g):
                x = CX[g]
                t, i, tsl, n, idx = x['t'], x['i'], x['tsl'], x['n'], x['idx']
                x['ps1'] = []
                for hh in range(2):
                    psl = slice(64 * hh, 64 * hh + 64)
                    ps1, pk1 = psum()
                    kb.op('pe', lambda ps1=ps1, psl=psl: nc.tensor.matmul(
                        ps1[:, 0:128], lhsT=kT[psl, t, tsl], rhs=qT[psl, t, tsl], start=True, stop=True),
                        R=[('kT', t), ('qT', t)], W=[pk1])
                    x['ps1'].append((ps1, pk1))
                if idx < NTT - 1:
                    ps3, pk3 = psum()
                    kb.op('pe', lambda: nc.tensor.matmul(ps3[:, 0:256], lhsT=kdec[i][:], rhs=vtok[:, n, t * 256:(t + 1) * 256],
                                                         start=True, stop=True), R=['kdec%d' % i, ('vtok', n, t)], W=[pk3])
                    x['ps3'] = (ps3, pk3)

            def sD(g):
                x = CX[g]
                t, d, i, tsl, n, idx, c = x['t'], x['d'], x['i'], x['tsl'], x['n'], x['idx'], x['c']
                x['ps2'] = []
                for hh in range(2):
                    h = 2 * t + hh
                    psl = slice(64 * hh, 64 * hh + 64)
                    ps1, pk1 = x['ps1'][hh]
                    dv(lambda ps1=ps1, hh=hh, h=h: nc.vector.tensor_tensor(
                        out=pT[i][:, hh, :], in0=ps1[:, 0:128], in1=Dm[:, d * 4 + h, :], op=ALU.mult),
                       [pk1] + T_, [('pT%d' % i, hh)])
                    ps2, pk2 = psum()

                    def f(ps2=ps2, h=h, hh=hh, psl=psl):
                        ins = nc.tensor.matmul(ps2[:, 0:128], lhsT=vtok[:, n, h * 128:(h + 1) * 128],
                                               rhs=pT[i][:, hh, :], start=True, stop=(idx == 0))
                        if idx > 0:
                            ins = nc.tensor.matmul(ps2[:, 0:128], lhsT=sbf[c][psl, hh * 128:(hh + 1) * 128],
                                                   rhs=qdt[i][psl, :], start=False, stop=True)
                        return ins
                    kb.op('pe', f, R=[('vtok', n, h // 2), ('pT%d' % i, hh), 'sbf%d' % c, 'qdt%d' % i], W=[pk2])
                    x['ps2'].append((ps2, pk2))
                if idx < NTT - 1:
                    ps3, pk3 = x['ps3']
                    dv(lambda: nc.vector.scalar_tensor_tensor(
                        out=sst[c][:], in0=sst[c][:], scalar=cdecQ[:, d * 2 + t:d * 2 + t + 1], in1=ps3[:, 0:256],
                        op0=ALU.mult, op1=ALU.add), [pk3, 'sst%d' % c] + T_, ['sst%d' % c])
                    kb.op('act', lambda: nc.scalar.copy(out=sbf[c][:], in_=sst[c][:]), R=['sst%d' % c], W=['sbf%d' % c])

            def sE(g):
                x = CX.pop(g)
                t, tsl, n = x['t'], x['tsl'], x['n']
                for hh in range(2):
                    h = 2 * t + hh
                    ps2, pk2 = x['ps2'][hh]
                    if (h, n) not in written:
                        written.add((h, n))
                        kb.op('act', lambda ps2=ps2, h=h: nc.scalar.copy(out=oT[:, h, tsl], in_=ps2[:, 0:128]),
                              R=[pk2], W=[('oT', h, n)])
                    else:
                        dv(lambda ps2=ps2, h=h: nc.vector.tensor_tensor(
                            out=oT[:, h, tsl], in0=oT[:, h, tsl], in1=ps2[:, 0:128], op=ALU.add),
                           [pk2, ('oT', h, n)], [('oT', h, n)])

            NS = len(steps)
            for g in range(NS + 4):
                for st_, off in ((sE, 4), (sD, 3), (sC, 2), (sB, 1), (sA, 0)):
                    if 0 <= g - off < NS:
                        st_(g - off)
            units = [(h, bi_, tc0, n) for h in range(4) for bi_, (tc0, n) in enumerate(TB)]
            WT = {}
            NU = len(units)

            def ctx_(j):
                h, bi_, tc0, n = units[j]
                OK_ = [('oT', h, tt) for tt in range(tc0 // 128, (tc0 + n) // 128)]
                return h, tc0, n, OK_, oT[:, h, tc0:tc0 + n], sg[j % 3], 'sg%d' % (j % 3), sg[3 + j % 2], 'sg%d' % (3 + j % 2)

            def st1(j):
                h, tc0, n, OK_, osl, a_, ak, b_, bk_ = ctx_(j)
                if h not in WT:
                    WT[h] = load_wk(w_in[l][:, 2560 + h * 128:2560 + (h + 1) * 128], 8, 128)
                ps, pk = psum()
                kb.op('pe', lambda: nc.tensor.matmul(ps[:, 0:n], lhsT=onesF[:], rhs=osl, start=True, stop=True),
                      R=OK_ + ['onesF'], W=[pk])
                dv(lambda: nc.vector.tensor_tensor(out=osl, in0=osl, in1=ps[:, 0:n], op=ALU.subtract), OK_ + [pk], OK_)

            def st2(j):
                h, tc0, n, OK_, osl, a_, ak, b_, bk_ = ctx_(j)
                dv(lambda: nc.vector.tensor_tensor(out=a_[:, 0:n], in0=osl, in1=osl, op=ALU.mult), OK_ + [ak], [ak])
                ps, pk = psum()
                kb.op('pe', lambda: nc.tensor.matmul(ps[:, 0:n], lhsT=onesF[:], rhs=a_[:, 0:n], start=True, stop=True),
                      R=[ak, 'onesF'], W=[pk])
                kb.op('act', lambda: nc.scalar.activation(out=a_[:, 0:n], in_=ps[:, 0:n], func=AF.Sqrt, bias=eps_t[:, 1:2]),
                      R=[pk, 'eps_t', ak], W=[ak])

            def st3(j):
                h, tc0, n, OK_, osl, a_, ak, b_, bk_ = ctx_(j)
                dv(lambda: nc.vector.reciprocal(out=a_[:, 0:n], in_=a_[:, 0:n]), [ak], [ak])
                dv(lambda: nc.vector.tensor_tensor(out=osl, in0=osl, in1=a_[:, 0:n], op=ALU.mult), OK_ + [ak], OK_)
                wt, wk = WT[h]
                ps, pk = psum()

                def f():
                    for k in range(8):
                        ins = nc.tensor.matmul(ps[:, 0:n], lhsT=wt[:, k, 0:128], rhs=hT[:, k, tc0:tc0 + n],
                                               start=(k == 0), stop=(k == 7))
                    return ins
                kb.op('pe', f, R=[wk] + HT_KEYS, W=[pk])
                kb.op('act', lambda: nc.scalar.activation(out=b_[:, 0:n], in_=ps[:, 0:n], func=AF.Silu), R=[pk, bk_], W=[bk_])

            def st4(j):
                h, tc0, n, OK_, osl, a_, ak, b_, bk_ = ctx_(j)
                dv(lambda: nc.vector.tensor_tensor(out=yretT[:, h, tc0:tc0 + n], in0=osl, in1=b_[:, 0:n], op=ALU.mult),
                   OK_ + [bk_], [('yretT', h)])

            for j in range(NU + 3):
                if j < NU:
                    st1(j)
                if 0 <= j - 1 < NU:
                    st2(j - 1)
                if 0 <= j - 2 < NU:
                    st3(j - 2)
                if 0 <= j - 3 < NU:
                    st4(j - 3)

        nab_d = din("nab", [DEPTH, 8, 128, 15 * 64])
        colmask_d = din("colmask", [128, 64])
        onesB = sb("onesB", [128, 128], BF16)
        kb.op('pool', lambda: nc.gpsimd.memset(onesB[:], 1.0), W=['onesB'])

        def _os_skip():
            import os
            return os.environ.get('NA_SKIPCTX', '0') == '1'

        def na_mixer(l, need_ctx):
            kb.dma(colmask[:], colmask_d[:, :], W=['colmask'])
            cmb = bass.AP(colmask[:].tensor, colmask[:].offset, [list(colmask[:].ap[0]), [0, 15], [1, 64]])
            for h in range(8):
                i = 0
                kb.dma(nst[i], nab_d[l, h], W=['nst0'])
                kb.op('act', lambda i=i: nc.scalar.activation(out=nst[i], in_=nst[i], func=AF.Exp),
                      R=['nst0'], W=['nst0'])
                dv(lambda i=i, h=h: nc.vector.tensor_tensor(out=TT[:, h, :].rearrange("p (a b) -> p a b", b=64),
                                                            in0=nst[i].rearrange("p (a b) -> p a b", b=64), in1=cmb,
                                                            op=ALU.mult), ['nst0', 'colmask'], ['TT'])

            def mk_ev(dst, key, scale):
                def ev(m, tc0, n, ps, pk):
                    kb.op('act', lambda: nc.scalar.activation(out=dst[:, m, tc0:tc0 + n], in_=ps[:, 0:n], func=AF.Copy,
                                                              scale=scale), R=[pk], W=[(key, m)])
                return ev
            proj_fm(w_in[l], 1280, 512, hT, HT_KEYS, 8, mk_ev(nkT, 'nkT', 1.0))
            proj_fm(w_in[l], 3072, 512, hT, HT_KEYS, 8, mk_ev(nqT, 'nqT', 0.125))
            for cb in range(2):
                wt, wk = load_wk(w_in[l][:, 1792 + cb * 256:1792 + (cb + 1) * 256], 8, 256)
                for n in range(NTT):
                    ps, pk = psum()

                    def f(ps=ps, wt=wt, n=n):
                        for k in range(8):
                            ins = nc.tensor.matmul(ps[:, 0:256], lhsT=hT[:, k, n * 128:(n + 1) * 128], rhs=wt[:, k, 0:256],
                                                   start=(k == 0), stop=(k == 7))
                        return ins
                    kb.op('pe', f, R=[wk, ('hT', n)], W=[pk])
                    kb.op('act', lambda ps=ps, n=n, cb=cb: nc.scalar.copy(out=nvtok[:, n, cb * 256:(cb + 1) * 256],
                                                                          in_=ps[:, 0:256]), R=[pk], W=[('nvtok', n, cb)])
            import os as _os
            NR = int(_os.environ.get('NA_ROWS', '32'))
            VK_ = lambda t: [('nvtok', n, t // 2) for n in range(NTT)]

            def ctx_queries(h):
                t, hh = h // 2, h % 2
                psl = slice(64 * hh, 64 * hh + 64)
                KQ = [('nkT', t), ('nqT', t)]
                ps, pk = psum()

                def f(ps=ps):
                    for j in range(2):
                        ins = nc.tensor.matmul(ps[:, j * 256:(j + 1) * 256], lhsT=nkT[psl, t, j * 128:(j + 1) * 128],
                                               rhs=nqT[psl, t, 0:256], start=True, stop=True)
                    return ins
                kb.op('pe', f, R=KQ, W=[pk])
                kb.op('act', lambda: nc.scalar.activation(out=pc[:], in_=ps[:, 0:512], func=AF.Exp), R=[pk, 'pc'], W=['pc'])
                pso, pko = psum()
                psd, pkd = psum()

                def f(pso=pso, psd=psd):
                    for j in range(2):
                        nc.tensor.matmul(pso[:, 0:256], lhsT=nvtok[:, j, t * 128:(t + 1) * 128],
                                         rhs=pc[:, j * 256:(j + 1) * 256], start=(j == 0), stop=(j == 1))
                    for j in range(2):
                        ins = nc.tensor.matmul(psd[:, 0:256], lhsT=onesB[:], rhs=pc[:, j * 256:(j + 1) * 256],
                                               start=(j == 0), stop=(j == 1))
                    return ins
                kb.op('pe', f, R=VK_(t) + ['pc', 'onesB'], W=[pko, pkd])
                dv(lambda: nc.vector.reciprocal(out=rcc[psl, 0:256], in_=psd[psl, 0:256]), [pkd, 'rcc'], ['rcc'])
                dv(lambda: nc.vector.tensor_tensor(out=ynaT[psl, t, 0:256], in0=pso[psl, 0:256], in1=rcc[psl, 0:256],
                                                   op=ALU.mult), [pko, 'rcc'], [('ynaT', t)])

            units = [(h, r) for h in range(8) for r in range(NR)]
            U = {}

            def s_unit(k):
                h, r = units[k]
                t, hh = h // 2, h % 2
                psl = slice(64 * hh, 64 * hh + 64)
                rs = min(max(r - 4, 0), 24)
                q0 = NCTX + r * 64
                rows = list(range(rs, rs + 8))
                par_rows = [[rr for rr in rows if rr % 2 == p] for p in range(2)]
                ps, pk = psum()

                def f():
                    for p in range(2):
                        for sl_, rr in enumerate(par_rows[p]):
                            k0 = NCTX + rr * 64
                            nc.tensor.matmul(ps[64 * p:64 * p + 64, sl_ * 64:(sl_ + 1) * 64], lhsT=nkT[psl, t, k0:k0 + 64],
                                             rhs=nqT[psl, t, q0:q0 + 64], start=True, stop=True)
                    for j in range(2):
                        ins = nc.tensor.matmul(ps[:, 256 + j * 64:256 + (j + 1) * 64], lhsT=nkT[psl, t, j * 128:(j + 1) * 128],
                                               rhs=nqT[psl, t, q0:q0 + 64], start=True, stop=True)
                    return ins
                kb.op('pe', f, R=[('nkT', t), ('nqT', t)], W=[pk])
                U[k] = (ps, pk, par_rows, q0, psl, t, h, r)

            def b_unit(k):
                ps, pk, par_rows, q0, psl, t, h, r = U[k]
                i = k % 3
                kb.op('act', lambda: nc.scalar.activation(out=e1[i][:], in_=ps[:, 0:256], func=AF.Exp),
                      R=[pk, 'e1%d' % i], W=['e1%d' % i])
                kb.op('act', lambda: nc.scalar.activation(out=p1[i][:, 256:384], in_=ps[:, 256:384], func=AF.Exp),
                      R=[pk], W=[('p1', i, 'c')])
                for p in range(2):
                    i0 = par_rows[p][0] - r + 7
                    tv = TT[64 * p:64 * p + 64, h, i0 * 64:(i0 + 7) * 64].rearrange("p (a b) -> p a b", b=64)[:, 0:7:2, :]
                    eng_ = (nc.vector, 'dve') if p == 0 else (nc.gpsimd, 'pool')
                    kb.op(eng_[1], lambda p=p, tv=tv, eng_=eng_: eng_[0].tensor_tensor(
                        out=p1[i][64 * p:64 * p + 64, 0:256].rearrange("p (a b) -> p a b", b=64),
                        in0=e1[i][64 * p:64 * p + 64, :].rearrange("p (a b) -> p a b", b=64), in1=tv, op=ALU.mult),
                        R=['e1%d' % i, 'TT'], W=[('p1', i, p)])

            def c_unit(k):
                ps, pk, par_rows, q0, psl, t, h, r = U[k]
                i = k % 3
                pso, pko = psum()
                psb, pkb = psum()

                def f():
                    for p, acc in ((0, pso), (1, psb)):
                        for sl_, rr in enumerate(par_rows[p]):
                            nc.tensor.matmul(acc[:, 0:64], lhsT=nvtok[64 * p:64 * p + 64, 2 + rr // 2, t * 128:(t + 1) * 128],
                                             rhs=p1[i][64 * p:64 * p + 64, sl_ * 64:(sl_ + 1) * 64], start=(sl_ == 0),
                                             stop=(p == 1 and sl_ == 3))
                    for j in range(2):
                        nc.tensor.matmul(pso[:, 0:64], lhsT=nvtok[:, j, t * 128:(t + 1) * 128],
                                         rhs=p1[i][:, 256 + j * 64:256 + (j + 1) * 64], start=False, stop=(j == 1))
                    for j in range(6):
                        ins = nc.tensor.matmul(pso[:, 64:128], lhsT=onesB[:], rhs=p1[i][:, j * 64:(j + 1) * 64],
                                               start=(j == 0), stop=(j == 5), skip_group_check=True)
                    return ins
                kb.op('pe', f, R=VK_(t) + [('p1', i, 0), ('p1', i, 1), ('p1', i, 'c'), 'onesB'], W=[pko, pkb])
                U[k] = U[k] + (pso, pko, psb, pkb)

            def d_unit(k):
                ps, pk, par_rows, q0, psl, t, h, r, pso, pko, psb, pkb = U.pop(k)
                hh = h % 2
                i = k % 3
                rk = 'rc%d' % i
                nk_, dk_ = ('numb', hh), ('denb', hh)
                rr_ = r % 16
                kb.op('act', lambda: nc.scalar.copy(out=rc[i][psl, 0:64], in_=psb[psl, 0:64]), R=[pkb, rk], W=[rk])
                kb.op('act', lambda: nc.scalar.copy(out=denb[psl, rr_ * 64:(rr_ + 1) * 64], in_=pso[psl, 64:128]),
                      R=[pko, dk_], W=[dk_] + (['nst0'] if k < 2 else []))
                dv(lambda: nc.vector.tensor_tensor(out=numb[psl, rr_ * 64:(rr_ + 1) * 64], in0=pso[psl, 0:64], in1=rc[i][psl, 0:64],
                                                   op=ALU.add), [pko, rk, nk_], [nk_])
                if rr_ == 15 or r == NR - 1:
                    nn = (rr_ + 1) * 64
                    c0 = NCTX + (r - rr_) * 64
                    dv(lambda: nc.vector.reciprocal(out=denb[psl, 0:nn], in_=denb[psl, 0:nn]), [dk_], [dk_])
                    dv(lambda: nc.vector.tensor_tensor(out=ynaT[psl, t, c0:c0 + nn], in0=numb[psl, 0:nn],
                                                       in1=denb[psl, 0:nn], op=ALU.mult), [dk_, nk_], [('ynaT', t)])

            if need_ctx and not _os_skip():
                for h in range(8):
                    ctx_queries(h)
            nu = len(units)
            for j in range(nu + 3):
                if j < nu:
                    s_unit(j)
                if 0 <= j - 1 < nu:
                    b_unit(j - 1)
                if 0 <= j - 2 < nu:
                    c_unit(j - 2)
                if 0 <= j - 3 < nu:
                    d_unit(j - 3)

        w_bs = [din("w_b%d" % i, [DEPTH, 512, D]) for i in range(3)]
        w_out_d = din("w_out", [DEPTH, D, D])
        w_fg = din("w_fg", [DEPTH, D, FFH])
        w_fu = din("w_fu", [DEPTH, D, FFH])
        w_fd = din("w_fd", [DEPTH, FFH, D])
        fnrow_d = din("fnrow", [128, D])
        fnrow = sb("fnrow_t", [128, D], F32)
        kb.dma(fnrow[:], fnrow_d[:, :], W=['fnrow'])

        def blocks_of(l):
            return TB if l < DEPTH - 1 else TB[1:]

        def merge_m(l):
            ys = ((ys5T, 'ys5T'), (yretT, 'yretT'), (ynaT, 'ynaT'))
            c = [0]
            for j in range(8):
                for br in range(3):
                    wg, wgk = load_wk(w_in[l][:, 3584 + br * 1024 + j * 128:3584 + br * 1024 + (j + 1) * 128], 8, 128)
                    wb, wbk = load_wk(w_bs[br][l][:, j * 128:(j + 1) * 128], 4, 128)
                    yt_, yk = ys[br]
                    YK = [(yk, m) for m in range(4)]
                    for (tc0, n) in blocks_of(l):
                        i = c[0] % 2
                        c[0] += 1
                        psg, pkg = psum()
                        psb, pkb = psum()

                        def f(psg=psg, wg=wg, tc0=tc0, n=n):
                            for k in range(8):
                                ins = nc.tensor.matmul(psg[:, 0:n], lhsT=wg[:, k, :], rhs=hT[:, k, tc0:tc0 + n],
                                                       start=(k == 0), stop=(k == 7))
                            return ins
                        kb.op('pe', f, R=[wgk] + HT_KEYS, W=[pkg])

                        def f(psb=psb, wb=wb, yt_=yt_, tc0=tc0, n=n):
                            for k in range(4):
                                ins = nc.tensor.matmul(psb[:, 0:n], lhsT=wb[:, k, :], rhs=yt_[:, k, tc0:tc0 + n],
                                                       start=(k == 0), stop=(k == 3))
                            return ins
                        kb.op('pe', f, R=[wbk] + YK, W=[pkb])
                        kb.op('act', lambda psg=psg, i=i, n=n: nc.scalar.activation(out=sg[i][:, 0:n], in_=psg[:, 0:n],
                                                                                    func=AF.Sigmoid),
                              R=[pkg, 'sg%d' % i], W=['sg%d' % i])
                        if br == 0:
                            dv(lambda psb=psb, i=i, tc0=tc0, n=n: nc.vector.tensor_tensor(
                                out=macc[:, tc0:tc0 + n], in0=sg[i][:, 0:n], in1=psb[:, 0:n], op=ALU.mult),
                               [pkb, 'sg%d' % i, 'macc'], ['macc'])
                        else:
                            dv(lambda psb=psb, i=i, n=n: nc.vector.tensor_tensor(
                                out=sg[i][:, 0:n], in0=sg[i][:, 0:n], in1=psb[:, 0:n], op=ALU.mult),
                               [pkb, 'sg%d' % i], ['sg%d' % i])
                            if br == 1:
                                dv(lambda i=i, tc0=tc0, n=n: nc.vector.tensor_tensor(
                                    out=macc[:, tc0:tc0 + n], in0=macc[:, tc0:tc0 + n], in1=sg[i][:, 0:n], op=ALU.add),
                                   ['sg%d' % i, 'macc'], ['macc'])
                            else:
                                dv(lambda i=i, tc0=tc0, n=n, j=j: nc.vector.tensor_tensor(
                                    out=mT[:, j, tc0:tc0 + n], in0=macc[:, tc0:tc0 + n], in1=sg[i][:, 0:n], op=ALU.add),
                                   ['sg%d' % i, 'macc'], [('mT', j)])

        def delta_tiles(tc0, n, src_dram, after, dsr, doff, dkey):
            out_ = []
            for t in range(tc0 // 128, (tc0 + n) // 128):
                def th(t=t):
                    i = t % 2
                    kb.dma(xt[i][:], src_dram[t * 128:(t + 1) * 128, :], W=['xt%d' % i])
                    o0 = t * 128 - tc0 + doff
                    for half in range(2):
                        ps, pk = psum()

                        def f(ps=ps, half=half, o0=o0):
                            for kk in range(4):
                                ins = nc.tensor.transpose(ps[:, kk * 128:(kk + 1) * 128], dsr[:, half * 4 + kk, o0:o0 + 128],
                                                          ident_f[:])
                            return ins
                        kb.op('pe', f, R=[dkey, 'ident_f'], W=[pk])
                        dv(lambda ps=ps, i=i, half=half: nc.vector.tensor_tensor(
                            out=xt[i][:, half * 512:(half + 1) * 512], in0=xt[i][:, half * 512:(half + 1) * 512], in1=ps[:, :],
                            op=ALU.add), [pk, 'xt%d' % i], ['xt%d' % i])
                    after(i, t)
                out_.append(th)
            return out_

        def merge_out(l):
            def after(i, t):
                kb.dma(xw[t * 128:(t + 1) * 128, :], xt[i][:], R=['xt%d' % i])
                norm_tile(i, t, 24, 32)
            MK = [('mT', j) for j in range(8)]
            pend = []
            for b_, (tc0, n) in enumerate(blocks_of(l)):
                v = 1 if tc0 == 0 else 0
                dTb, dkey = dTm[b_ % 2], 'dT%d' % (b_ % 2)
                for dt_ in range(8):
                    wo, wok = load_wk(w_out_d[l][:, dt_ * 128:(dt_ + 1) * 128], 8, 128)
                    ps, pk = psum()

                    def f(ps=ps, wo=wo, tc0=tc0, n=n):
                        for k in range(8):
                            ins = nc.tensor.matmul(ps[:, 0:n], lhsT=wo[:, k, :], rhs=mT[:, k, tc0:tc0 + n],
                                                   start=(k == 0), stop=(k == 7))
                        return ins
                    kb.op('pe', f, R=[wok] + MK, W=[pk])
                    dv(lambda ps=ps, dt_=dt_, n=n, v=v, dTb=dTb: nc.vector.tensor_scalar(
                        out=dTb[:, dt_, 0:n], in0=ps[:, 0:n], scalar1=modT[:, 16 + dt_, v:v + 1], scalar2=None, op0=ALU.mult),
                       [pk, 'modT', dkey], [dkey])
                    if pend and dt_ % 2 == 1:
                        pend.pop(0)()
                while pend:
                    pend.pop(0)()
                pend = delta_tiles(tc0, n, xin if l == 0 else xw, after, dTb, 0, dkey)
            while pend:
                pend.pop(0)()

        def ffn(l, groups):
            last = (l == DEPTH - 1)

            def after(i, t):
                if not last:
                    kb.dma(xw[t * 128:(t + 1) * 128, :], xt[i][:], R=['xt%d' % i])
                else:
                    rstd_tile(i)
                    dv(lambda i=i: nc.vector.tensor_scalar(out=xt[i][:], in0=xt[i][:], scalar1=ss[:, 2:3], scalar2=None,
                                                           op0=ALU.mult), ['xt%d' % i, 'ss'], ['xt%d' % i])
                    dv(lambda i=i: nc.vector.tensor_tensor(out=xt[i][:], in0=xt[i][:], in1=fnrow[:], op=ALU.mult),
                       ['xt%d' % i, 'fnrow'], ['xt%d' % i])
                    kb.dma(out[(t - 2) * 128:(t - 1) * 128, :], xt[i][:], R=['xt%d' % i])
            c = [0]
            pend = []
            for gi_, grp in enumerate(groups):
                g0 = grp[0][0]
                dTg, dgk = dTgs[gi_ % 2], 'dTg%d' % (gi_ % 2)
                for j in range(FFH // 128):
                    wg, wgk = load_wk(w_fg[l][:, j * 128:(j + 1) * 128], 8, 128)
                    wu, wuk = load_wk(w_fu[l][:, j * 128:(j + 1) * 128], 8, 128)
                    for (tc0, n) in grp:
                        i = c[0] % 2
                        c[0] += 1
                        psg, pkg = psum()
                        psu, pku = psum()

                        def f(psg=psg, wg=wg, tc0=tc0, n=n):
                            for k in range(8):
                                ins = nc.tensor.matmul(psg[:, 0:n], lhsT=wg[:, k, :], rhs=hT[:, k, tc0:tc0 + n],
                                                       start=(k == 0), stop=(k == 7))
                            return ins
                        kb.op('pe', f, R=[wgk] + HT_KEYS, W=[pkg])

                        def f(psu=psu, wu=wu, tc0=tc0, n=n):
                            for k in range(8):
                                ins = nc.tensor.matmul(psu[:, 0:n], lhsT=wu[:, k, :], rhs=hT[:, k, tc0:tc0 + n],
                                                       start=(k == 0), stop=(k == 7))
                            return ins
                        kb.op('pe', f, R=[wuk] + HT_KEYS, W=[pku])
                        kb.op('act', lambda psg=psg, i=i, n=n: nc.scalar.activation(out=sg[i][:, 0:n], in_=psg[:, 0:n],
                                                                                    func=AF.Silu),
                              R=[pkg, 'sg%d' % i], W=['sg%d' % i])
                        dv(lambda psu=psu, i=i, j=j, tc0=tc0, n=n: nc.vector.tensor_tensor(
                            out=aT[:, j, tc0 - g0:tc0 - g0 + n], in0=sg[i][:, 0:n], in1=psu[:, 0:n], op=ALU.mult),
                           [pku, 'sg%d' % i], [('aT', j)])
                    if pend:
                        pend.pop(0)()
                while pend:
                    pend.pop(0)()
                AK = [('aT', j) for j in range(FFH // 128)]
                for dt_ in range(8):
                    wa, wak = load_wk(w_fd[l][0:2048, dt_ * 128:(dt_ + 1) * 128], 16, 128)
                    wb_, wbk = load_wk(w_fd[l][2048:FFH, dt_ * 128:(dt_ + 1) * 128], 6, 128)
                    for (tc0, n) in grp:
                        v = 1 if tc0 == 0 else 0
                        ps, pk = psum()

                        def f(ps=ps, wa=wa, wb_=wb_, tc0=tc0, n=n):
                            for k in range(22):
                                w_ = wa[:, k, :] if k < 16 else wb_[:, k - 16, :]
                                ins = nc.tensor.matmul(ps[:, 0:n], lhsT=w_, rhs=aT[:, k, tc0 - g0:tc0 - g0 + n],
                                                       start=(k == 0), stop=(k == 21))
                            return ins
                        kb.op('pe', f, R=[wak, wbk] + AK, W=[pk])
                        dv(lambda ps=ps, dt_=dt_, n=n, v=v, tc0=tc0, dTg=dTg: nc.vector.tensor_scalar(
                            out=dTg[:, dt_, tc0 - g0:tc0 - g0 + n], in0=ps[:, 0:n], scalar1=modT[:, 40 + dt_, v:v + 1], scalar2=None,
                            op0=ALU.mult), [pk, 'modT', dgk], [dgk])
                for (tc0, n) in grp:
                    pend += delta_tiles(tc0, n, xw, after, dTg, tc0 - g0, dgk)
            while pend:
                pend.pop(0)()

        dbgbuf = [None]

        def dump_fm(tile_, nk, keys):
            for k in range(nk):
                kb.op('act', lambda k=k: nc.scalar.copy(out=dbgbuf[0][:], in_=tile_[:, k, :]), R=list(keys) + ['dbgb'], W=['dbgb'])
                kb.dma(dbg_out[:, k, :], dbgbuf[0][:], R=['dbgb'])

        stop = False
        for l in range(DEPTH):
          with ExitStack() as LS:
            with ExitStack() as ph:
                cur[0] = ph
                xt = [sb("xt%d" % i, [128, D], F32) for i in range(2)]
                xn = [sb("xn%d" % i, [128, D], BF16) for i in range(2)]
                junk = sb("junk", [128, D], F32)
                xnall = sb("xnall", [128, NTT, D], BF16)
                norm_pre(xin if l == 0 else xw)
                adaln(l)
                norm_post(0, 8)
                kb.barrier()
            cur[0] = LS
            ys5T = sb("ys5T", [128, 4, NT], BF16)
            if dbg and dbg[0] == 'hT':
                dbgbuf[0] = sb("dbgb", [128, NT], F32)
                dump_fm(hT, 8, HT_KEYS)
                break
            with ExitStack() as ph:
                cur[0] = ph
                uT = sb("uT", [128, 4, NT], BF16)
                gsc = sb("gsc", [128, NT], F32)
                XG = [sb("XG%d" % i, [128, 2, 2, 8, 128], BF16) for i in range(2)]
                Sx = sb("Sx", [128, 32, 128], BF16)
                Z = sb("Z", [128, 4, 2, 2, NT // 8], F32)
                XsQ = sb("XsQ", [128, 4, 2, 2, NT // 8], BF16)
                T2s = [None]
                CQ = sb("CQ", [128, 12, 8, 3], F32)
                BDq = sb("BDq", [128, 2, 8, 128], BF16)
                uS = sb("uS", [128, 8, NT // 8], BF16)
                Bt = [sb("Bt%d" % i, [128, 4, 32], F32) for i in range(2)]
                s5tmpP = sb("s5tmpP", [128, 3, 8, 32], F32)
                cfk = sb("cfk", [128, 2, 2, 16, 16], F32)
                Epw_re = sb("Epw_re", [128, 2, 9, 16], F32)
                Epw_im = sb("Epw_im", [128, 2, 9, 16], F32)
                Eg_re = sb("Eg_re", [128, 2, 8, 16], F32)
                Eg_im = sb("Eg_im", [128, 2, 8, 16], F32)
                yacc = sb("yacc", [128, NT], F32)
                Rwp = [sb("Rwp%d" % i, [128, 2, 2, 128], BF16) for i in range(2)]
                prm = sb("prm", [128, 2, 3, 16], F32)
                cfl = sb("cfl", [128, 2, 2, 16, 16], F32)
                APre = sb("APre", [128, 2, 12, 16], F32)
                APim = sb("APim", [128, 2, 12, 16], F32)
                APnim = sb("APnim", [128, 2, 12, 16], F32)
                sm = sb("sm", [128, 12, 16], F32)
                cfs = sb("cfs", [128, 2, 16, 16], F32)
                s5dT = sb("s5dT_t", [128, 4], F32)
                bgluT = sb("bgluT_t", [128, 4], F32)
                sg = [gsc[:, 0:512], gsc[:, 512:1024]]
                s5_mixer(l)
                kb.barrier()
            cur[0] = LS
            yretT = sb("yretT", [128, 4, NT], BF16)
            if dbg and dbg[0] == 'ys5':
                dbgbuf[0] = sb("dbgb", [128, NT], F32)
                dump_fm(ys5T, 4, [('ys5T', m) for m in range(4)])
                break
            with ExitStack() as ph:
                cur[0] = ph
                qT = sb("qT", [128, 2, NT], BF16)
                kT = sb("kT", [128, 2, NT], BF16)
                vtok = sb("vtok", [128, NTT, 512], BF16)
                thB = sb("thB_t", [128, 8], F32)
                thQ = sb("thQ_t", [128, 4], F32)
                lgB = sb("lgB", [128, 8], F32)
                lgQ = sb("lgQ", [128, 4], F32)
                retc = sb("retc_t", [128, 2, 128], F32)
                iq = sb("iq_t", [128, 2, 128], F32)
                ek = sb("ek_t", [128, 2], F32)
                Dm = sb("Dm", [128, 8, 128], F32)
                qdtab = sb("qdtab", [128, 2, 2, 128], F32)
                kd = sb("kd", [128, 8], F32)
                cdecQ = sb("cdecQ", [128, 4], F32)
                with ExitStack() as ph2:
                    cur[0] = ph2
                    tmpT = sb("tmpT", [128, 2, NT], BF16)
                    rotc = sb("rotc_t", [128, NT], F32)
                    rots = sb("rots_t", [128, NT], F32)
                    ret_mixer(l)
                    kb.barrier()
                cur[0] = ph
                oT = sb("oT", [128, 4, NT], F32)
                sst = [sb("sst%d" % i, [128, 256], F32) for i in range(4)]
                sbf = [sb("sbf%d" % i, [128, 256], BF16) for i in range(4)]
                kdec = [sb("kdec%d" % i, [128, 128], BF16) for i in range(6)]
                qdt = [sb("qdt%d" % i, [128, 128], BF16) for i in range(6)]
                pT = [sb("pT%d" % i, [128, 2, 128], BF16) for i in range(6)]
                sg = [sb("sg%d" % i, [128, 512], F32) for i in range(5)]
                ret_part2(l)
                kb.barrier()
            cur[0] = LS
            ynaT = sb("ynaT", [128, 4, NT], BF16)
            if dbg and dbg[0] == 'yret':
                dbgbuf[0] = sb("dbgb", [128, NT], F32)
                dump_fm(yretT, 4, [('yretT', m) for m in range(4)])
                break
            with ExitStack() as ph:
                cur[0] = ph
                nkT = sb("nkT", [128, 4, NT], BF16)
                nqT = sb("nqT", [128, 4, NT], BF16)
                nvtok = sb("nvtok", [128, NTT, 512], BF16)
                TT = sb("TT", [128, 8, 15 * 64], BF16)
                nst0 = sb("nst0", [128, 1024], F32)
                nst = [nst0[:, 0:960], nst0[:, 0:960]]
                colmask = sb("colmask_t", [128, 64], F32)
                e1 = [sb("e1%d" % i, [128, 256], F32) for i in range(3)]
                p1 = [sb("p1%d" % i, [128, 384], BF16) for i in range(3)]
                pc = sb("pc", [128, 512], BF16)
                rcc = sb("rcc", [128, 256], F32)
                rc = [sb("rc%d" % i, [128, 64], F32) for i in range(3)]
                numb = sb("numb", [128, 1024], F32)
                denb = nst0
                na_mixer(l, l < DEPTH - 1)
                kb.barrier()
            cur[0] = LS
            if dbg and dbg[0] == 'yna':
                dbgbuf[0] = sb("dbgb", [128, NT], F32)
                dump_fm(ynaT, 4, [('ynaT', m) for m in range(4)])
                break
            mT = sb("mT", [128, 8, NT], BF16)
            with ExitStack() as ph:
                cur[0] = ph
                macc = sb("macc", [128, NT], F32)
                sg = [sb("sg%d" % i, [128, 512], F32) for i in range(2)]
                merge_m(l)
                kb.barrier()
            with ExitStack() as ph:
                cur[0] = ph
                dTm = [sb("dTm%d" % i, [128, 8, 512], F32) for i in range(2)]
                xt = [sb("xt%d" % i, [128, D], F32) for i in range(2)]
                xn = [sb("xn%d" % i, [128, D], BF16) for i in range(2)]
                junk = sb("junk", [128, D], F32)
                merge_out(l)
                kb.barrier()
            cur[0] = LS
          cur[0] = es
          with ExitStack() as ph:
              cur[0] = ph
              if l < DEPTH - 1:
                  groups = [TB[0:3], TB[3:5]]
              else:
                  groups = [TB[1:3], TB[3:5]]
              aT = sb("aT", [128, FFH // 128, 1280], BF16)
              dTgs = [sb("dTg%d" % i, [128, 8, 1280 if i == 0 else 1024], F32) for i in range(2)]
              sg = [sb("sg%d" % i, [128, 512], F32) for i in range(2)]
              xt = [sb("xt%d" % i, [128, D], F32) for i in range(2)]
              xn = [sb("xn%d" % i, [128, D], BF16) for i in range(2)]
              junk = sb("junk", [128, D], F32)
              ffn(l, groups)
              kb.barrier()
          cur[0] = es
        kb.finish()
    return nc


def host_inputs(inputs, b, shared=None):
    f = np.float32
    d = {}
    d['xin'] = np.ascontiguousarray(np.concatenate([inputs['ctx'][b], inputs['x'][b]], axis=0).astype(f))
    cc = np.stack([inputs['c'][b], inputs['c_ctx']], axis=-1)
    d['cT'] = np.ascontiguousarray(cc.reshape(8, 128, 2).transpose(1, 0, 2).astype(f))
    if shared is not None:
        d.update(shared)
        return d
    d['w_ada'] = np.ascontiguousarray(inputs['w_ada'].astype(f))
    d['b_adaT'] = np.ascontiguousarray(inputs['b_ada'].reshape(DEPTH, 48, 128).transpose(0, 2, 1).astype(f))
    d['w_in'] = np.ascontiguousarray(inputs['w_in'].astype(f))
    d['ident_d'] = np.eye(128, dtype=f)
    def layA(a):
        return a.reshape(DEPTH, 2, 16, 2, 64).transpose(0, 1, 3, 4, 2).reshape(DEPTH, 2, 128, 16)
    ldt = np.repeat(inputs['s5_log_dt'][..., None], 64, axis=-1)
    d['s5p'] = np.ascontiguousarray(np.stack([layA(inputs['s5_lam_re']), layA(inputs['s5_lam_im']), layA(ldt)],
                                             axis=3).astype(f))
    def layC(a):
        return a.reshape(DEPTH, 2, 16, 2, 16, 64).transpose(0, 1, 3, 5, 2, 4).reshape(DEPTH, 2, 128, 16, 16)
    d['s5c'] = np.ascontiguousarray(np.stack([layC(inputs['s5_c_re']), layC(inputs['s5_c_im'])], axis=3).astype(f))
    ba = np.zeros((DEPTH, 16, 128, 4, 32), f)
    for r, nm in enumerate(('s5_b_re', 's5_b_im')):
        b = inputs[nm]
        for pi_ in range(16):
            for gl in range(2):
                g = 2 * pi_ + gl
                for dd in range(2):
                    ba[:, pi_, gl * 64:(gl + 1) * 64, dd * 2 + r, gl * 16:(gl + 1) * 16] = b[:, dd, g]
    d['s5ba'] = ba
    d['s5dT'] = np.ascontiguousarray(inputs['s5_d'].reshape(DEPTH, 4, 128).transpose(0, 2, 1).astype(f))
    d['bgluT'] = np.ascontiguousarray(inputs['s5_b_glu'].reshape(DEPTH, 4, 128).transpose(0, 2, 1).astype(f))
    d['w_glu'] = np.ascontiguousarray(inputs['s5_w_glu'].astype(f))
    perm = np.arange(64)
    perm = np.where((perm % 32) < 16, perm + 16, perm - 16)
    kcols = 512 + (np.arange(256) // 64) * 64 + perm[np.arange(256) % 64]
    qcols = 2304 + (np.arange(256) // 64) * 64 + perm[np.arange(256) % 64]
    d['w_rot'] = np.ascontiguousarray(inputs['w_in'][:, :, np.concatenate([kcols, qcols])].astype(f))
    th = inputs['ret_theta'].astype(f)
    d['thB'] = np.ascontiguousarray(np.broadcast_to(th.reshape(DEPTH, 1, 8), (DEPTH, 128, 8)))
    thq = th.reshape(DEPTH, 2, 2, 2)
    d['thQ'] = np.ascontiguousarray(np.repeat(thq.transpose(0, 3, 1, 2), 64, axis=1).reshape(DEPTH, 128, 4))
    kc = np.arange(64)[:, None]
    qc = np.arange(64)[None, :]
    jidx = np.clip(kc - qc + 15, 0, 30)
    g_ = inputs['na_rpb'][:, :, :, jidx]
    g_ = g_.transpose(0, 1, 3, 2, 4)
    d['nab'] = np.ascontiguousarray(np.concatenate([g_, g_], axis=2).reshape(DEPTH, 8, 128, 15 * 64).astype(f))
    for i, nm in enumerate(('w_branch_s5', 'w_branch_ret', 'w_branch_na')):
        d['w_b%d' % i] = np.ascontiguousarray(inputs[nm].astype(f))
    d['w_out'] = np.ascontiguousarray(inputs['w_out'].astype(f))
    d['w_fg'] = np.ascontiguousarray(inputs['w_ffn_gate'].astype(f))
    d['w_fu'] = np.ascontiguousarray(inputs['w_ffn_up'].astype(f))
    d['w_fd'] = np.ascontiguousarray(inputs['w_ffn_down'].astype(f))
    d['fnrow'] = np.ascontiguousarray(np.broadcast_to(inputs['final_norm'].astype(f)[None, :], (128, D)))
    d.update(_consts())
    return d


_CONSTS = {}


def _consts():
    if _CONSTS:
        return _CONSTS
    f = np.float32
    pos = np.arange(LAT)
    inv_freq = (10000.0 ** (-np.arange(16, dtype=np.float32) / 16)).astype(np.float32)
    dd = np.arange(128) % 64
    coord = np.where((dd // 32)[:, None] == 0, (pos // 64)[None, :], (pos % 64)[None, :]).astype(np.float32)
    ang = coord * inv_freq[dd % 16][:, None]
    sign = np.where((dd % 32) < 16, -1.0, 1.0)[:, None]
    rotc = np.ones((128, NT), f)
    rots = np.zeros((128, NT), f)
    rotc[:, NCTX:] = np.cos(ang)
    rots[:, NCTX:] = np.sin(ang) * sign
    jj = np.arange(128)[:, None].astype(f)
    ii = np.arange(128)[None, :].astype(f)
    BIG = 1.0e6
    retc = np.stack([np.where(ii >= jj, ii - jj, BIG), np.where(jj > ii, jj - ii, BIG)], axis=1)
    iq = np.stack([np.broadcast_to(ii + 1.0, (128, 128)), np.broadcast_to(128.0 - ii, (128, 128))], axis=1)
    ek = np.stack([127.0 - np.arange(128), np.arange(128)], axis=1)
    kc = np.arange(64)[:, None]
    qc = np.arange(64)[None, :]
    cs = np.clip(qc - 8, 0, 48)
    cm = ((kc >= cs) & (kc < cs + 16)).astype(f)
    _CONSTS['colmask'] = np.ascontiguousarray(np.concatenate([cm, cm], axis=0))
    _CONSTS.update(rotc=rotc, rots=rots.astype(f), retc=np.ascontiguousarray(retc.astype(f)),
                   iq=np.ascontiguousarray(iq.astype(f)), ek=np.ascontiguousarray(ek.astype(f)))
    return _CONSTS


def kernel(**inputs):
    inputs = {k: np.asarray(v) for k, v in inputs.items()}
    nc = build()
    first = host_inputs(inputs, 0)
    shared = {k: v for k, v in first.items() if k not in ('xin', 'cT')}
    in_maps = [first] + [host_inputs(inputs, b, shared) for b in range(1, 8)]
    res = run_bass_kernel_spmd(nc, in_maps, core_ids=list(range(8)))
    return np.stack([r["out"] for r in res.results], axis=0).astype(np.float32)
```
